# Optimizing a Trainium2 kernel written in Bass

```python
import math
import jax
import jax.numpy as jnp
from jax import lax
import numpy as np

D_MODEL = 1024
BATCH = 16
SEQ = 2048
DEPTH = 2

CTX_LEN = 256
GRID_W = 64
ROPE_THETA = 10000.0
LN_EPS = 1e-6
RMS_EPS = 1e-5
NEG_INF = -1e30
Q_BLOCK = 128

A_GROUPS = 4
A_GROUP_DIM = 128
A_WIDTH = A_GROUPS * A_GROUP_DIM
CHUNK = 128

B_HEADS = 4
B_HEAD_DIM = 64
B_V_DIM = 2 * B_HEAD_DIM
B_QK_WIDTH = B_HEADS * 2 * B_HEAD_DIM
B_WIDTH = B_HEADS * B_V_DIM

C_WIDTH = 512
C_KERNEL = 31

D_HEADS = 8
D_KV_HEADS = 2
D_GROUP = D_HEADS // D_KV_HEADS
D_HEAD_DIM = 64
D_WIDTH = D_HEADS * D_HEAD_DIM
D_KV_WIDTH = D_KV_HEADS * D_HEAD_DIM
WINDOW = 128

AB_CUTS = (A_WIDTH, 2 * A_WIDTH, 3 * A_WIDTH, 3 * A_WIDTH + B_QK_WIDTH,
           3 * A_WIDTH + 2 * B_QK_WIDTH, 3 * A_WIDTH + 2 * B_QK_WIDTH + B_WIDTH)
AB_IN = AB_CUTS[-1] + B_WIDTH
AB_MIX = A_WIDTH + B_WIDTH
CD_CUTS = (C_WIDTH, 2 * C_WIDTH, 3 * C_WIDTH, 3 * C_WIDTH + D_WIDTH,
           3 * C_WIDTH + D_WIDTH + D_KV_WIDTH, 3 * C_WIDTH + D_WIDTH + 2 * D_KV_WIDTH)
CD_IN = CD_CUTS[-1] + D_WIDTH
CD_MIX = C_WIDTH + D_WIDTH

N_AB = (DEPTH + 1) // 2
N_CD = DEPTH // 2

kernel_name = "hybrid_diffusion_gmlp_diffattn_conformer_swa"


def layer_norm(x, g, b):
    xf = x.astype(jnp.float32)
    mu = jnp.mean(xf, axis=-1, keepdims=True)
    var = jnp.mean(jnp.square(xf - mu), axis=-1, keepdims=True)
    return ((xf - mu) * lax.rsqrt(var + LN_EPS)).astype(x.dtype) * g + b


def rms_norm(x, g):
    xf = x.astype(jnp.float32)
    return (xf * lax.rsqrt(jnp.mean(xf * xf, axis=-1, keepdims=True) + RMS_EPS)).astype(x.dtype) * g


def axial_rope_tables(rows, head_dim):
    m = head_dim // 4
    inv = ROPE_THETA ** (-jnp.arange(m, dtype=jnp.float32) / m)
    r, col = jnp.meshgrid(jnp.arange(rows, dtype=jnp.float32), jnp.arange(GRID_W, dtype=jnp.float32), indexing="ij")
    ang_r = r.reshape(-1, 1) * inv
    ang_c = col.reshape(-1, 1) * inv
    return (jnp.cos(ang_r), jnp.sin(ang_r), jnp.cos(ang_c), jnp.sin(ang_c))


def _rotate(x, cos, sin):
    x1, x2 = jnp.split(x, 2, axis=-1)
    cos = cos[None, :, None, :].astype(x.dtype)
    sin = sin[None, :, None, :].astype(x.dtype)
    return jnp.concatenate([x1 * cos - x2 * sin, x2 * cos + x1 * sin], axis=-1)


def apply_axial_rope(x, rope):
    cr, sr, cc, sc = rope
    xr, xc = jnp.split(x, 2, axis=-1)
    return jnp.concatenate([_rotate(xr, cr, sr), _rotate(xc, cc, sc)], axis=-1)


def chunk_gmlp(u, v, w_s, b_s, g, bb):
    bsz, n, _ = v.shape
    u = jax.nn.gelu(u)
    v = layer_norm(jax.nn.gelu(v), g, bb)
    vc = v.reshape(bsz, n // CHUNK, CHUNK, A_GROUPS, A_GROUP_DIM)
    mixed = jnp.einsum("gpq,bcqgd->bcpgd", w_s, vc) + b_s.T[None, None, :, :, None]
    return u * mixed.reshape(bsz, n, A_WIDTH)


def diff_attend(q, k, v, lam):
    s = jnp.einsum("bqhcd,bkhcd->bhcqk", q, k).astype(jnp.float32) * (B_HEAD_DIM ** -0.5)
    p = jax.nn.softmax(s, axis=-1)
    w = p[:, :, 0] - lam * p[:, :, 1]
    return jnp.einsum("bhqk,bkhd->bqhd", w.astype(v.dtype), v)


def diff_heads_out(o, subln_g, lam_init):
    bsz, n = o.shape[0], o.shape[1]
    return (rms_norm(o, subln_g) * (1.0 - lam_init)).reshape(bsz, n, B_WIDTH)


def conformer_conv(a, b, dw_w, dw_b, g, bb):
    h = a * jax.nn.sigmoid(b)
    y = lax.conv_general_dilated(h, dw_w[:, None, :].astype(h.dtype), window_strides=(1,),
                                 padding=[(C_KERNEL // 2, C_KERNEL // 2)],
                                 dimension_numbers=("NWC", "WIO", "NWC"),
                                 feature_group_count=C_WIDTH) + dw_b
    return jax.nn.silu(layer_norm(y, g, bb))


def sink_gqa_attend(q, k, v, mask, sink):
    s = jnp.einsum("bqhgd,bkhd->bhgqk", q, k).astype(jnp.float32) * (D_HEAD_DIM ** -0.5)
    if mask is not None:
        s = jnp.where(mask, s, NEG_INF)
    sink_col = jnp.broadcast_to(sink.astype(jnp.float32)[None, :, :, None, None], s.shape[:-1] + (1,))
    p = jax.nn.softmax(jnp.concatenate([s, sink_col], axis=-1), axis=-1)[..., :-1]
    o = jnp.einsum("bhgqk,bkhd->bqhgd", p.astype(v.dtype), v)
    return o.reshape(o.shape[0], o.shape[1], D_WIDTH)


def ab_sublayer(u_lat, u_ctx, need_ctx, rope, layer, w_in, w_out, w_s, b_s, an_g, an_b,
                lq1, lk1, lq2, lk2, subln_g):
    bsz, n, _ = u_lat.shape
    au, av, ag, bq, bk, bv, bg = jnp.split(u_lat @ w_in, AB_CUTS, axis=-1)
    if need_ctx:
        cau, cav, cag, cbq, cbk, cbv, cbg = jnp.split(u_ctx @ w_in, AB_CUTS, axis=-1)
    else:
        cbk, cbv = jnp.split(u_ctx @ w_in[:, AB_CUTS[3]:AB_CUTS[5]], [B_QK_WIDTH], axis=-1)
    lam_init = 0.8 - 0.6 * math.exp(-0.3 * layer)
    lam = (jnp.exp(jnp.sum(lq1.astype(jnp.float32) * lk1.astype(jnp.float32)))
           - jnp.exp(jnp.sum(lq2.astype(jnp.float32) * lk2.astype(jnp.float32))) + lam_init)
    q = apply_axial_rope(bq.reshape(bsz, n, 2 * B_HEADS, B_HEAD_DIM), rope).reshape(bsz, n, B_HEADS, 2, B_HEAD_DIM)
    k = apply_axial_rope(bk.reshape(bsz, n, 2 * B_HEADS, B_HEAD_DIM), rope).reshape(bsz, n, B_HEADS, 2, B_HEAD_DIM)
    v = bv.reshape(bsz, n, B_HEADS, B_V_DIM)
    ck = cbk.reshape(bsz, CTX_LEN, B_HEADS, 2, B_HEAD_DIM)
    cv = cbv.reshape(bsz, CTX_LEN, B_HEADS, B_V_DIM)
    k_all = jnp.concatenate([ck, k], axis=1)
    v_all = jnp.concatenate([cv, v], axis=1)
    q_blocks = q.reshape(bsz, n // Q_BLOCK, Q_BLOCK, B_HEADS, 2, B_HEAD_DIM).swapaxes(0, 1)
    o = lax.map(lambda qb: diff_attend(qb, k_all, v_all, lam), q_blocks)
    b_lat = diff_heads_out(o.swapaxes(0, 1).reshape(bsz, n, B_HEADS, B_V_DIM), subln_g, lam_init)
    a_lat = chunk_gmlp(au, av, w_s, b_s, an_g, an_b)
    y_lat = jnp.concatenate([a_lat * jax.nn.silu(ag), b_lat * jax.nn.silu(bg)], axis=-1) @ w_out
    y_ctx = None
    if need_ctx:
        cq = cbq.reshape(bsz, CTX_LEN, B_HEADS, 2, B_HEAD_DIM)
        b_ctx = diff_heads_out(diff_attend(cq, ck, cv, lam), subln_g, lam_init)
        a_ctx = chunk_gmlp(cau, cav, w_s, b_s, an_g, an_b)
        y_ctx = jnp.concatenate([a_ctx * jax.nn.silu(cag), b_ctx * jax.nn.silu(cbg)], axis=-1) @ w_out
    return y_lat, y_ctx


def cd_sublayer(u_lat, u_ctx, need_ctx, rope, w_in, w_out, dw_w, dw_b, cn_g, cn_b, sink_logit):
    bsz, n, _ = u_lat.shape
    ca, cb, cg, dq, dk, dv, dg = jnp.split(u_lat @ w_in, CD_CUTS, axis=-1)
    if need_ctx:
        cca, ccb, ccg, cdq, cdk, cdv, cdg = jnp.split(u_ctx @ w_in, CD_CUTS, axis=-1)
    else:
        cdk, cdv = jnp.split(u_ctx @ w_in[:, CD_CUTS[3]:CD_CUTS[5]], [D_KV_WIDTH], axis=-1)
    ck = cdk.reshape(bsz, CTX_LEN, D_KV_HEADS, D_HEAD_DIM)
    cv = cdv.reshape(bsz, CTX_LEN, D_KV_HEADS, D_HEAD_DIM)
    sink = sink_logit.reshape(D_KV_HEADS, D_GROUP)
    q = apply_axial_rope(dq.reshape(bsz, n, D_HEADS, D_HEAD_DIM), rope).reshape(bsz, n, D_KV_HEADS, D_GROUP, D_HEAD_DIM)
    k = apply_axial_rope(dk.reshape(bsz, n, D_KV_HEADS, D_HEAD_DIM), rope)
    v = dv.reshape(bsz, n, D_KV_HEADS, D_HEAD_DIM)
    pad = ((0, 0), (WINDOW, WINDOW), (0, 0), (0, 0))
    kp, vp = jnp.pad(k, pad), jnp.pad(v, pad)
    band_len = Q_BLOCK + 2 * WINDOW
    ctx_mask = jnp.ones((Q_BLOCK, CTX_LEN), dtype=bool)

    def band_block(i):
        start = i * Q_BLOCK
        qb = lax.dynamic_slice_in_dim(q, start, Q_BLOCK, axis=1)
        kb = lax.dynamic_slice_in_dim(kp, start, band_len, axis=1)
        vb = lax.dynamic_slice_in_dim(vp, start, band_len, axis=1)
        qpos = start + jnp.arange(Q_BLOCK)
        kpos = start - WINDOW + jnp.arange(band_len)
        band = ((jnp.abs(qpos[:, None] - kpos[None, :]) <= WINDOW)
                & (kpos >= 0)[None, :] & (kpos < n)[None, :])
        mask = jnp.concatenate([band, ctx_mask], axis=1)
        return sink_gqa_attend(qb, jnp.concatenate([kb, ck], axis=1), jnp.concatenate([vb, cv], axis=1), mask, sink)

    d_lat = lax.map(band_block, jnp.arange(n // Q_BLOCK)).swapaxes(0, 1).reshape(bsz, n, D_WIDTH)
    c_lat = conformer_conv(ca, cb, dw_w, dw_b, cn_g, cn_b)
    y_lat = jnp.concatenate([c_lat * jax.nn.silu(cg), d_lat * jax.nn.silu(dg)], axis=-1) @ w_out
    y_ctx = None
    if need_ctx:
        cq = cdq.reshape(bsz, CTX_LEN, D_KV_HEADS, D_GROUP, D_HEAD_DIM)
        d_ctx = sink_gqa_attend(cq, ck, cv, None, sink)
        c_ctx_out = conformer_conv(cca, ccb, dw_w, dw_b, cn_g, cn_b)
        y_ctx = jnp.concatenate([c_ctx_out * jax.nn.silu(ccg), d_ctx * jax.nn.silu(cdg)], axis=-1) @ w_out
    return y_lat, y_ctx


def setup_inputs(seed: int = 0) -> dict:
    key = jax.random.key(seed)
    ks = jax.random.split(key, 32)
    f32 = jnp.float32
    beta = (8.0 * DEPTH) ** -0.25

    def nrm(k, shape, s):
        return jax.random.normal(k, shape, f32) * s

    return {
        "x": nrm(ks[0], (BATCH, SEQ, D_MODEL), 1.0),
        "c": nrm(ks[1], (BATCH, D_MODEL), 1.0),
        "ctx": nrm(ks[2], (BATCH, CTX_LEN, D_MODEL), 1.0),
        "c_ctx": nrm(ks[3], (D_MODEL,), 1.0),
        "mod_w": nrm(ks[4], (DEPTH, D_MODEL, 3 * D_MODEL), D_MODEL ** -0.5),
        "mod_b": nrm(ks[5], (DEPTH, 3 * D_MODEL), 0.02),
        "ln_g": 1.0 + nrm(ks[6], (DEPTH, D_MODEL), 0.02),
        "ln_b": nrm(ks[7], (DEPTH, D_MODEL), 0.02),
        "ab_w_in": nrm(ks[8], (N_AB, D_MODEL, AB_IN), D_MODEL ** -0.5),
        "ab_w_out": nrm(ks[9], (N_AB, AB_MIX, D_MODEL), beta * AB_MIX ** -0.5),
        "a_w_s": nrm(ks[10], (N_AB, A_GROUPS, CHUNK, CHUNK), CHUNK ** -0.5),
        "a_b_s": 1.0 + nrm(ks[11], (N_AB, A_GROUPS, CHUNK), 0.02),
        "a_norm_g": 1.0 + nrm(ks[12], (N_AB, A_WIDTH), 0.02),
        "a_norm_b": nrm(ks[13], (N_AB, A_WIDTH), 0.02),
        "b_lq1": nrm(ks[14], (N_AB, B_HEAD_DIM), 0.1),
        "b_lk1": nrm(ks[15], (N_AB, B_HEAD_DIM), 0.1),
        "b_lq2": nrm(ks[16], (N_AB, B_HEAD_DIM), 0.1),
        "b_lk2": nrm(ks[17], (N_AB, B_HEAD_DIM), 0.1),
        "b_subln_g": 1.0 + nrm(ks[18], (N_AB, B_V_DIM), 0.02),
        "cd_w_in": nrm(ks[19], (N_CD, D_MODEL, CD_IN), D_MODEL ** -0.5),
        "cd_w_out": nrm(ks[20], (N_CD, CD_MIX, D_MODEL), beta * CD_MIX ** -0.5),
        "c_dw_w": nrm(ks[21], (N_CD, C_KERNEL, C_WIDTH), C_KERNEL ** -0.5),
        "c_dw_b": nrm(ks[22], (N_CD, C_WIDTH), 0.02),
        "c_norm_g": 1.0 + nrm(ks[23], (N_CD, C_WIDTH), 0.02),
        "c_norm_b": nrm(ks[24], (N_CD, C_WIDTH), 0.02),
        "d_sink": nrm(ks[25], (N_CD, D_HEADS), 0.5),
    }


def reference(x, c, ctx, c_ctx, mod_w, mod_b, ln_g, ln_b, ab_w_in, ab_w_out, a_w_s, a_b_s,
              a_norm_g, a_norm_b, b_lq1, b_lk1, b_lq2, b_lk2, b_subln_g, cd_w_in, cd_w_out,
              c_dw_w, c_dw_b, c_norm_g, c_norm_b, d_sink):
    n = x.shape[1]
    rows = n // GRID_W
    rope = axial_rope_tables(rows, B_HEAD_DIM)
    alpha = (2.0 * DEPTH) ** 0.25
    silu_c = jax.nn.silu(c)
    silu_cc = jax.nn.silu(c_ctx)
    h_lat, h_ctx = x, ctx
    for layer in range(DEPTH):
        need_ctx = layer < DEPTH - 1
        shift, scale, gate = jnp.split(silu_c @ mod_w[layer] + mod_b[layer], 3, axis=-1)
        shift_c, scale_c, gate_c = jnp.split(silu_cc @ mod_w[layer] + mod_b[layer], 3, axis=-1)
        u_lat = h_lat * (1.0 + scale[:, None, :]) + shift[:, None, :]
        u_ctx = h_ctx * (1.0 + scale_c) + shift_c
        if layer % 2 == 0:
            i = layer // 2
            y_lat, y_ctx = ab_sublayer(u_lat, u_ctx, need_ctx, rope, layer, ab_w_in[i], ab_w_out[i],
                                       a_w_s[i], a_b_s[i], a_norm_g[i], a_norm_b[i],
                                       b_lq1[i], b_lk1[i], b_lq2[i], b_lk2[i], b_subln_g[i])
        else:
            i = layer // 2
            y_lat, y_ctx = cd_sublayer(u_lat, u_ctx, need_ctx, rope, cd_w_in[i], cd_w_out[i],
                                       c_dw_w[i], c_dw_b[i], c_norm_g[i], c_norm_b[i], d_sink[i])
        new_lat = layer_norm(alpha * h_lat + gate[:, None, :] * y_lat, ln_g[layer], ln_b[layer])
        if need_ctx:
            h_ctx = layer_norm(alpha * h_ctx + gate_c * y_ctx, ln_g[layer], ln_b[layer])
        h_lat = new_lat
    return h_lat
```

```python
import math
import numpy as np
from contextlib import ExitStack
import concourse.bass as bass
import concourse.mybir as mybir
from concourse.bass_utils import run_bass_kernel_spmd

F32 = mybir.dt.float32
BF16 = mybir.dt.bfloat16
AF = mybir.ActivationFunctionType
ALU = mybir.AluOpType

NT = 2304
NCTX = 256
NLAT = 2048
D = 1024
LN_EPS = 1e-6
RMS_EPS = 1e-5
ALPHA = (2.0 * 2) ** 0.25
GC0 = math.sqrt(2.0 / math.pi)
GC1 = 0.044715
SEGS = [(0, 256), (256, 512), (768, 512), (1280, 512), (1792, 512)]
LSEGS = SEGS[1:]


class Trk:
    __slots__ = ("name", "w", "r")

    def __init__(self, name=""):
        self.name = name
        self.w = None
        self.r = {}


class DSem:
    def __init__(self, sem):
        self.sem = sem
        self.count = 0


class KB:
    def __init__(self, nc, es):
        self.nc = nc
        self.es = es
        self.eng = {"pe": nc.tensor, "act": nc.scalar, "dve": nc.vector, "pool": nc.gpsimd, "sp": nc.sync}
        self.esem = {}
        self.ecnt = {}
        for e in ("pe", "act", "dve", "pool"):
            self.esem[e] = es.enter_context(nc.semaphore("s_" + e))
            self.ecnt[e] = 0
        self.seen = {e: {} for e in self.eng}
        self.nsem = 0

    def dsem(self):
        self.nsem += 1
        return DSem(self.es.enter_context(self.nc.semaphore(f"d{self.nsem}")))

    def _wait(self, e, reads, writes):
        evs = {}

        def add(ev):
            if ev is None:
                return
            k = id(ev[0])
            if k not in evs or evs[k][1] < ev[1]:
                evs[k] = ev

        for t in reads:
            add(t.w)
        for t in writes:
            add(t.w)
            for ev in t.r.values():
                add(ev)
        seen = self.seen[e]
        own = self.esem.get(e)
        for k, (sem, val) in evs.items():
            if e == "pe" and sem is own:
                continue
            if seen.get(k, 0) >= val:
                continue
            self.eng[e].wait_ge(sem, val)
            seen[k] = val

    def _post(self, ev, reads, writes):
        k = id(ev[0])
        for t in writes:
            t.w = ev
            t.r = {}
        for t in reads:
            t.r[k] = ev

    def op(self, e, fn, reads=(), writes=()):
        self._wait(e, reads, writes)
        ins = fn(self.eng[e])
        self.ecnt[e] += 1
        ins.then_inc(self.esem[e], 1)
        ev = (self.esem[e], self.ecnt[e])
        self._post(ev, reads, writes)
        return ev

    def dma(self, q, out, in_, ds, reads=(), writes=()):
        self._wait(q, reads, writes)
        ins = self.eng[q].dma_start(out=out, in_=in_)
        ds.count += 16
        ins.then_inc(ds.sem, 16)
        ev = (ds.sem, ds.count)
        self._post(ev, reads, writes)
        return ev

    def barrier(self):
        for e in self.eng:
            seen = self.seen[e]
            for f in ("pe", "act", "dve", "pool"):
                if f == e or self.ecnt[f] == 0:
                    continue
                k = id(self.esem[f])
                if seen.get(k, 0) >= self.ecnt[f]:
                    continue
                self.eng[e].wait_ge(self.esem[f], self.ecnt[f])
                seen[k] = self.ecnt[f]


class _Stop(Exception):
    pass


def build_program(debug_h1=False, stage=99):
    nc = bass.Bass("TRN2", target_bir_lowering=False)

    def chk(n):
        if stage <= n:
            raise _Stop()

    def din(name, shape):
        return nc.dram_tensor(name, list(shape), F32, kind="ExternalInput").ap()

    x_d = din("x", [2, NLAT, D])
    ctx_d = din("ctx", [2, NCTX, D])
    cvT_d = din("cvT", [128, 24])
    modw_d = din("mod_w", [2, D, 3 * D])
    modbT_d = din("mod_bT", [128, 48])
    lng_d = din("ln_g", [2, D])
    lnb_d = din("ln_b", [2, D])
    abwin_d = din("ab_w_in", [D, 3584])
    abwout_d = din("ab_w_out", [D, D])
    awsT_d = din("a_w_sT", [128, 512])
    abs_d = din("a_b_s", [1, 512])
    ang_d = din("a_norm_g", [1, 512])
    anb_d = din("a_norm_b", [1, 512])
    lam_d = din("lam_in", [1, 256])
    subg_d = din("b_subln_g", [1, 128])
    cdwin_d = din("cd_w_in", [D, 2816])
    cdwout_d = din("cd_w_out", [D, D])
    dwT_d = din("dwT", [128, 124])
    cvec_d = din("cvecs", [128, 12])
    sink_d = din("d_sink", [1, 8])
    ident_d = din("ident", [128, 128])
    rmat_d = din("rmat", [128, 128])
    rope_d = din("ropeT", [128, 2 * NT])
    mask_d = din("masks", [128, 512])
    out_d = nc.dram_tensor("out", [2, NLAT, D], F32, kind="ExternalOutput").ap()
    if debug_h1:
        h1_d = nc.dram_tensor("h1", [2, NT, D], F32, kind="ExternalOutput").ap()
    else:
        h1_d = nc.dram_tensor("h1", [2, NT, D], F32).ap()

    with ExitStack() as es:
        kb = KB(nc, es)
        op = kb.op

        def sb(name, shape, dt=F32):
            return es.enter_context(nc.sbuf_tensor("sb_" + name, list(shape), dt))

        PB = [es.enter_context(nc.psum_tensor(f"pb{i}", [128, 512], F32)) for i in range(8)]
        TPB = [Trk(f"pb{i}") for i in range(8)]

        d_const = kb.dsem()
        T_const = Trk("const")
        ident = sb("ident", [128, 128])
        identb = sb("identb", [128, 128], BF16)
        rmat = sb("rmat", [128, 128], BF16)
        ropeT = sb("ropeT", [128, 2, NT])
        masks = sb("masks", [128, 2, 256], BF16)
        cvT = sb("cvT", [128, 24])
        modbT = sb("modbT", [128, 48])
        wsT = sb("wsT", [128, 4, 128], BF16)
        lamt = sb("lamt", [128, 256])
        subg = sb("subg", [128, 128])
        dwT = sb("dwT", [128, 124])
        cvecs = sb("cvecs", [128, 12])
        sinkt = sb("sinkt", [128, 8])
        ones_f = sb("ones_f", [128, 128])
        onesdiv = sb("onesdiv", [128, 128])
        sTb = sb("sTb", [128, 24], BF16)
        modT = sb("modT", [128, 144])
        small = sb("small", [128, 64])
        epsln = small[:, 0:1]
        epsrms = small[:, 1:2]
        epsln4 = small[:, 2:3]
        neglam = small[:, 3:4]

        def cdma(q, dst, src):
            kb.dma(q, dst, src, d_const, writes=[T_const])

        cdma("sp", ident[:], ident_d)
        cdma("pool", rmat[:], rmat_d)
        cdma("sp", ropeT[:], rope_d.rearrange("p (a t) -> p a t", a=2))
        cdma("pool", masks[:], mask_d.rearrange("p (a t) -> p a t", a=2))
        cdma("sp", cvT[:], cvT_d)
        cdma("sp", modbT[:], modbT_d)
        cdma("pool", wsT[:], awsT_d.rearrange("p (g q) -> p g q", g=4))
        cdma("sp", lamt[:], lam_d.partition_broadcast(128))
        cdma("sp", subg[:], subg_d.partition_broadcast(128))
        cdma("sp", dwT[:], dwT_d)
        cdma("sp", cvecs[:], cvec_d)
        cdma("sp", sinkt[:], sink_d.partition_broadcast(128))

        T_c2 = Trk("c2")
        RC = [T_const, T_c2]
        op("dve", lambda e: e.memset(ones_f[:], 1.0), writes=[T_c2])
        op("dve", lambda e: e.memset(onesdiv[:], 1.0 / 512.0), writes=[T_c2])
        op("dve", lambda e: e.memset(small[:, 0:1], LN_EPS), writes=[T_c2])
        op("dve", lambda e: e.memset(small[:, 1:2], RMS_EPS), writes=[T_c2])
        op("dve", lambda e: e.memset(small[:, 2:3], 4.0 * LN_EPS), writes=[T_c2])
        op("dve", lambda e: e.tensor_copy(identb[:], ident[:]), reads=[T_const], writes=[T_c2])
        lam_init0 = 0.8 - 0.6 * math.exp(-0.3 * 0)
        lprod = sb("lprod", [128, 128])
        op("dve", lambda e: e.tensor_tensor(lprod[:, 0:64], lamt[:, 0:64], lamt[:, 64:128], ALU.mult),
           reads=[T_const], writes=[T_c2])
        op("dve", lambda e: e.tensor_tensor(lprod[:, 64:128], lamt[:, 128:192], lamt[:, 192:256], ALU.mult),
           reads=[T_c2, T_const], writes=[T_c2])
        op("dve", lambda e: e.reduce_sum(small[:, 4:6], lprod[:].rearrange("p (a b) -> p a b", a=2),
                                        mybir.AxisListType.X), reads=[T_c2], writes=[T_c2])
        op("act", lambda e: e.activation(small[:, 6:8], small[:, 4:6], AF.Exp), reads=[T_c2], writes=[T_c2])
        op("dve", lambda e: e.scalar_tensor_tensor(small[:, 3:4], small[:, 7:8], -lam_init0, small[:, 6:7],
                                                  ALU.add, ALU.subtract), reads=[T_c2], writes=[T_c2])
        op("dve", lambda e: e.tensor_scalar(subg[:], subg[:], (1.0 - lam_init0) * 0.5, None, ALU.mult),
           reads=[T_const, T_c2], writes=[T_c2])
        op("act", lambda e: e.activation(sinkt[:], sinkt[:], AF.Exp), reads=[T_const, T_c2], writes=[T_c2])
        op("dve", lambda e: e.tensor_scalar(dwT[:], dwT[:], 0.5, None, ALU.mult), reads=[T_const, T_c2], writes=[T_c2])
        sct = sb("sct", [128, 24])
        op("act", lambda e: e.activation(sct[:], cvT[:], AF.Tanh, scale=0.5), reads=[T_const], writes=[T_c2])
        op("dve", lambda e: e.scalar_tensor_tensor(sct[:], sct[:], 1.0, cvT[:], ALU.add, ALU.mult),
           reads=[T_c2, T_const], writes=[T_c2])
        op("dve", lambda e: e.tensor_scalar(sTb[:], sct[:], 0.5, None, ALU.mult), reads=[T_c2], writes=[T_c2])

        NW = 3
        wslot = [sb(f"wslot{i}", [128, 8, 512], BF16) for i in range(NW)]
        T_w = [Trk(f"w{i}") for i in range(NW)]
        d_w = [kb.dsem() for _ in range(NW)]
        wctr = [0]

        def wload(src_ap, ncols):
            i = wctr[0] % NW
            wctr[0] += 1
            kb.dma("pool", wslot[i][:, :, 0:ncols], src_ap.rearrange("(j p) c -> p j c", p=128), d_w[i],
                   writes=[T_w[i]])
            return wslot[i], T_w[i]

        NS = 2
        srct = [sb(f"srct{i}", [128, D]) for i in range(NS)]
        T_s = [Trk(f"s{i}") for i in range(NS)]
        d_s = [kb.dsem() for _ in range(NS)]
        sctr = [0]

        def sload(src_ap):
            i = sctr[0] % NS
            sctr[0] += 1
            kb.dma("sp", srct[i][:], src_ap, d_s[i], writes=[T_s[i]])
            return srct[i], T_s[i]

        NO = 2
        outt = [sb(f"outt{i}", [128, D]) for i in range(NO)]
        T_o = [Trk(f"o{i}") for i in range(NO)]
        d_o = [kb.dsem() for _ in range(NO)]
        octr = [0]
        T_h1 = [[Trk(f"h1_{b}_{i}") for i in range(18)] for b in range(2)]

        T_mod = Trk("mod")
        for l in range(2):
            for g in range(3):
                for half in range(2):
                    c0 = g * 1024 + half * 512
                    ws, tw = wload(modw_d[l, :, c0:c0 + 512], 512)
                    for cc in range(4):
                        for j in range(8):
                            op("pe", lambda e: e.matmul(PB[7][:, cc * 4:cc * 4 + 3], ws[:, j, cc * 128:(cc + 1) * 128],
                                                        sTb[:, j * 3:(j + 1) * 3], start=(j == 0), stop=(j == 7)),
                               reads=[tw, T_c2], writes=[TPB[7]])
                    for cc in range(4):
                        k = g * 8 + half * 4 + cc
                        o0 = (l * 24 + k) * 3
                        op("dve", lambda e: e.tensor_scalar(modT[:, o0:o0 + 3], PB[7][:, cc * 4:cc * 4 + 3],
                                                           modbT[:, l * 24 + k:l * 24 + k + 1],
                                                           1.0 if g == 1 else 0.0, ALU.add, ALU.add),
                           reads=[TPB[7], T_const], writes=[T_mod])

        def mod_ap(l, k, r):
            o0 = (l * 24 + k) * 3 + r
            return modT[:, o0:o0 + 1]

        uT = sb("uT", [128, 8, NT], BF16)
        T_uT = [Trk(f"uT{i}") for i in range(18)]
        mixT = sb("mixT", [128, 8, NT], BF16)
        T_mix = Trk("mixT")
        T_gbc = Trk("gbc")
        T_diagf = [Trk(), Trk()]
        T_lngb = Trk("lngb")
        d_lngb = kb.dsem()
        T_ab = Trk("ab")
        d_ab = kb.dsem()
        ARENA = 68 * 1024
        arena = sb("arena", [128, ARENA // 4])

        class Carver:
            def __init__(self):
                self.off = 0

            def get(self, nelem, dt=F32):
                nbytes = nelem * (4 if dt == F32 else 2)
                nbytes = (nbytes + 31) // 32 * 32
                assert self.off + nbytes <= ARENA, (self.off, nbytes)
                a = arena[:, self.off // 4:(self.off + nbytes) // 4]
                self.off += nbytes
                if dt == BF16:
                    a = a.bitcast(BF16)[:, 0:nelem]
                return a

        def tile_rows(l, b, i):
            if l == 0:
                return ctx_d[b, i * 128:(i + 1) * 128, :] if i < 2 else x_d[b, (i - 2) * 128:(i - 1) * 128, :]
            return h1_d[b, i * 128:(i + 1) * 128, :]

        def seg_tiles(t0, n):
            return list(range(t0 // 128, (t0 + n) // 128))

        def proj_fm(ws, tw, c0, t0, n, bank):
            for j in range(8):
                op("pe", lambda e: e.matmul(PB[bank][:, 0:n], ws[:, j, c0:c0 + 128], uT[:, j, t0:t0 + n],
                                            start=(j == 0), stop=(j == 7)),
                   reads=[tw] + [T_uT[i] for i in seg_tiles(t0, n)], writes=[TPB[bank]])

        def proj_tm(ws, tw, c0, ncols, i, bank):
            for j in range(8):
                op("pe", lambda e: e.matmul(PB[bank][:, 0:ncols], uT[:, j, i * 128:(i + 1) * 128], ws[:, j, c0:c0 + ncols],
                                            start=(j == 0), stop=(j == 7)),
                   reads=[tw, T_uT[i]], writes=[TPB[bank]])

        def rsqrt_small(dst, src, eps_ap, trk, scale=1.0):
            op("act", lambda e: e.activation(dst, src, AF.Sqrt, bias=eps_ap, scale=scale), reads=[trk, T_c2], writes=[trk])
            op("dve", lambda e: e.reciprocal(dst, dst), reads=[trk], writes=[trk])

        def rope_evac(bank, n, t0, dst, T_dst, rb, qraw, T_qraw, t1, T_t1, t2, T_t2):
            op("act", lambda e: e.copy(qraw[:, 0:n], PB[bank][:, 0:n]), reads=[TPB[bank]], writes=[T_qraw])
            op("pe", lambda e: e.matmul(PB[rb][:, 0:n], rmat[:], qraw[:, 0:n], start=True, stop=True),
               reads=[T_qraw, T_const], writes=[TPB[rb]])
            op("dve", lambda e: e.tensor_tensor(t1[:, 0:n], qraw[:, 0:n], ropeT[:, 0, t0:t0 + n], ALU.mult),
               reads=[T_qraw, T_const], writes=[T_t1])
            op("dve", lambda e: e.tensor_tensor(t2[:, 0:n], PB[rb][:, 0:n], ropeT[:, 1, t0:t0 + n], ALU.mult),
               reads=[TPB[rb], T_const], writes=[T_t2])
            if not isinstance(dst, list):
                dst = [(dst, 0, 128)]
            for (dap, p0, p1) in dst:
                op("pool", lambda e: e.tensor_tensor(dap, t1[p0:p1, 0:n], t2[p0:p1, 0:n], ALU.add),
                   reads=[T_t1, T_t2], writes=[T_dst])

        def build_gate_bc(l, r, slot, gbc, diagf):
            for cc in range(8):
                dg = diagf[cc % 2]
                op("dve", lambda e: e.tensor_scalar(dg[:], ident[:], mod_ap(l, 16 + cc, r), None, ALU.mult),
                   reads=[T_const, T_mod], writes=[T_diagf[cc % 2]])
                op("pe", lambda e: e.matmul(PB[7][:, (cc % 4) * 128:(cc % 4 + 1) * 128], ones_f[:], dg[:],
                                            start=True, stop=True),
                   reads=[T_c2, T_diagf[cc % 2]], writes=[TPB[7]])
                if cc % 4 == 3:
                    h = cc // 4
                    op("act", lambda e: e.copy(gbc[:, slot, h * 512:(h + 1) * 512], PB[7][:, :]),
                       reads=[TPB[7]], writes=[T_gbc])

        def sload_dep(l, b, i):
            idx = sctr[0] % NS
            sctr[0] += 1
            rd = [T_h1[b][i]] if l == 1 else []
            kb.dma("sp", srct[idx][:], tile_rows(l, b, i), d_s[idx], reads=rd, writes=[T_s[idx]])
            return srct[idx], T_s[idx]

        def phase_transposes(l, b):
            for i in range(18):
                st, ts = sload_dep(l, b, i)
                r = 2 if i < 2 else b
                for half in range(2):
                    bank = (2 * i + half) % 4
                    for q in range(4):
                        j = half * 4 + q
                        op("pe", lambda e: e.transpose(PB[bank][:, q * 128:(q + 1) * 128], st[:, j * 128:(j + 1) * 128], ident[:]),
                           reads=[ts, T_const], writes=[TPB[bank]])
                    for q in range(4):
                        j = half * 4 + q
                        op("dve", lambda e: e.tensor_scalar(uT[:, j, i * 128:(i + 1) * 128], PB[bank][:, q * 128:(q + 1) * 128],
                                                           mod_ap(l, 8 + j, r), mod_ap(l, j, r), ALU.mult, ALU.add),
                           reads=[TPB[bank], T_mod], writes=[T_uT[i]])

        def phase_outproj(l, b, wout_d, tiles):
            kb.barrier()
            cv = Carver()
            zt = [cv.get(D) for _ in range(2)]
            T_z = [Trk(), Trk()]
            stats = cv.get(64)
            T_st = Trk()
            lngb = cv.get(2 * D).rearrange("p (a d) -> p a d", a=2)
            gbc = cv.get(2 * D).rearrange("p (a d) -> p a d", a=2)
            diagf = [cv.get(128) for _ in range(2)]
            kb.dma("sp", lngb[:, 0, :], lng_d[l:l + 1, :].partition_broadcast(128), d_lngb, writes=[T_lngb])
            kb.dma("sp", lngb[:, 1, :], lnb_d[l:l + 1, :].partition_broadcast(128), d_lngb, writes=[T_lngb])
            build_gate_bc(l, b, 0, gbc, diagf)
            if l == 0:
                build_gate_bc(l, 2, 1, gbc, diagf)
            w0, tw0 = wload(wout_d[:, 0:512], 512)
            w1, tw1 = wload(wout_d[:, 512:1024], 512)
            for n_i, i in enumerate(tiles):
                st, ts = sload_dep(l, b, i)
                gs = 1 if i < 2 else 0
                pb0 = (n_i % 2) * 2
                for half, (ws, tw) in enumerate(((w0, tw0), (w1, tw1))):
                    for j in range(8):
                        op("pe", lambda e: e.matmul(PB[pb0 + half][:, :], mixT[:, j, i * 128:(i + 1) * 128], ws[:, j, :],
                                                    start=(j == 0), stop=(j == 7)),
                           reads=[T_mix, tw], writes=[TPB[pb0 + half]])
                z = zt[n_i % 2]
                tz = T_z[n_i % 2]
                for half in range(2):
                    hs = slice(half * 512, (half + 1) * 512)
                    op("dve", lambda e: e.tensor_tensor(z[:, hs], PB[pb0 + half][:, :], gbc[:, gs, hs], ALU.mult),
                       reads=[TPB[pb0 + half], T_gbc], writes=[tz])
                    op("dve", lambda e: e.scalar_tensor_tensor(z[:, hs], st[:, hs], ALPHA, z[:, hs], ALU.mult, ALU.add),
                       reads=[ts, tz], writes=[tz])
                    op("dve", lambda e: e.bn_stats(stats[:, half * 6:(half + 1) * 6], z[:, hs]), reads=[tz], writes=[T_st])
                op("dve", lambda e: e.bn_aggr(stats[:, 16:18], stats[:, 0:12].rearrange("p (a b) -> p a b", a=2)),
                   reads=[T_st], writes=[T_st])
                rsqrt_small(stats[:, 18:19], stats[:, 17:18], epsln, T_st)
                op("dve", lambda e: e.scalar_tensor_tensor(stats[:, 19:20], stats[:, 16:17], -1.0, stats[:, 18:19],
                                                          ALU.mult, ALU.mult), reads=[T_st], writes=[T_st])
                oi = octr[0] % NO
                octr[0] += 1
                ot, to = outt[oi], T_o[oi]
                op("act", lambda e: e.activation(z[:, :], z[:, :], AF.Identity, bias=stats[:, 19:20], scale=stats[:, 18:19]),
                   reads=[tz, T_st], writes=[tz])
                op("pool", lambda e: e.tensor_tensor(z[:, :], z[:, :], lngb[:, 0, :], ALU.mult), reads=[tz, T_lngb], writes=[tz])
                op("pool", lambda e: e.tensor_tensor(ot[:, :], z[:, :], lngb[:, 1, :], ALU.add), reads=[tz, T_lngb], writes=[to])
                if l == 0:
                    kb.dma("sp", h1_d[b, i * 128:(i + 1) * 128, :], ot[:, :], d_o[oi], reads=[to], writes=[T_h1[b][i]])
                else:
                    kb.dma("sp", out_d[b, (i - 2) * 128:(i - 1) * 128, :], ot[:, :], d_o[oi], reads=[to])

        def gelu2(dst, T_dst, bank, n, sq, T_sq, tt, T_tt):
            op("act", lambda e: e.activation(sq[:, 0:n], PB[bank][:, 0:n], AF.Square), reads=[TPB[bank]], writes=[T_sq])
            op("dve", lambda e: e.tensor_scalar(sq[:, 0:n], sq[:, 0:n], GC1, 1.0, ALU.mult, ALU.add), reads=[T_sq], writes=[T_sq])
            op("dve", lambda e: e.tensor_tensor(sq[:, 0:n], sq[:, 0:n], PB[bank][:, 0:n], ALU.mult),
               reads=[T_sq, TPB[bank]], writes=[T_sq])
            op("act", lambda e: e.activation(tt[:, 0:n], sq[:, 0:n], AF.Tanh, scale=GC0), reads=[T_sq], writes=[T_tt])
            return op("dve", lambda e: e.scalar_tensor_tensor(dst, tt[:, 0:n], 1.0, PB[bank][:, 0:n], ALU.add, ALU.mult),
                      reads=[T_tt, TPB[bank]], writes=[T_dst])

        def pass_layer0(b):
            l = 0
            kb.barrier()
            phase_transposes(l, b)
            chk(1)
            kb.barrier()
            cv = Carver()
            vn = cv.get(18 * 512, BF16)
            vn3 = vn.rearrange("p (i c) -> p i c", i=18)
            T_vn = [Trk() for _ in range(18)]
            Gu = cv.get(NT)
            T_Gu = Trk()
            sq = [cv.get(512) for _ in range(2)]
            T_sq = [Trk(), Trk()]
            tt = [cv.get(512) for _ in range(2)]
            T_tt = [Trk(), Trk()]
            g2 = [cv.get(512) for _ in range(2)]
            T_g2 = [Trk(), Trk()]
            stats = cv.get(64)
            T_st = Trk()
            bsrep = cv.get(4 * 512).rearrange("p (g q) -> p g q", g=4)
            angb = cv.get(2 * 512).rearrange("p (a q) -> p a q", a=2)
            for rep in range(4):
                kb.dma("sp", bsrep[:, :, rep * 128:(rep + 1) * 128],
                       abs_d.rearrange("o (g q) -> o g q", g=4).partition_broadcast(128), d_ab, writes=[T_ab])
            kb.dma("sp", angb[:, 0, :], ang_d.partition_broadcast(128), d_ab, writes=[T_ab])
            kb.dma("sp", angb[:, 1, :], anb_d.partition_broadcast(128), d_ab, writes=[T_ab])
            wv, twv = wload(abwin_d[:, 512:1024], 512)
            wu, twu = wload(abwin_d[:, 0:512], 512)
            wg, twg = wload(abwin_d[:, 1024:1536], 512)
            for i in range(18):
                bank = i % 4
                k2 = i % 2
                proj_tm(wv, twv, 0, 512, i, bank)
                gelu2(g2[k2][:, :], T_g2[k2], bank, 512, sq[k2], T_sq[k2], tt[k2], T_tt[k2])
                op("dve", lambda e: e.bn_stats(stats[:, 0:6], g2[k2][:, :]), reads=[T_g2[k2]], writes=[T_st])
                op("dve", lambda e: e.bn_aggr(stats[:, 16:18], stats[:, 0:6]), reads=[T_st], writes=[T_st])
                rsqrt_small(stats[:, 18:19], stats[:, 17:18], epsln4, T_st)
                op("dve", lambda e: e.tensor_scalar(g2[k2][:, :], g2[k2][:, :], stats[:, 16:17], stats[:, 18:19],
                                                   ALU.subtract, ALU.mult), reads=[T_g2[k2], T_st], writes=[T_g2[k2]])
                op("pool", lambda e: e.tensor_tensor(g2[k2][:, :], g2[k2][:, :], angb[:, 0, :], ALU.mult),
                   reads=[T_g2[k2], T_ab], writes=[T_g2[k2]])
                op("pool", lambda e: e.tensor_tensor(vn3[:, i, :], g2[k2][:, :], angb[:, 1, :], ALU.add),
                   reads=[T_g2[k2], T_ab], writes=[T_vn[i]])
            for g in range(4):
                for si, (t0, n) in enumerate(SEGS):
                    bank = si % 4
                    k2 = si % 2
                    proj_fm(wu, twu, g * 128, t0, n, bank)
                    gelu2(Gu[:, t0:t0 + n], T_Gu, bank, n, sq[k2], T_sq[k2], tt[k2], T_tt[k2])
                for si, (t0, n) in enumerate(SEGS):
                    bank = si % 4
                    k2 = si % 2
                    proj_fm(wg, twg, g * 128, t0, n, bank)
                    op("act", lambda e: e.activation(tt[k2][:, 0:n], PB[bank][:, 0:n], AF.Tanh, scale=0.5),
                       reads=[TPB[bank]], writes=[T_tt[k2]])
                    op("dve", lambda e: e.scalar_tensor_tensor(tt[k2][:, 0:n], tt[k2][:, 0:n], 1.0, PB[bank][:, 0:n], ALU.add, ALU.mult),
                       reads=[T_tt[k2], TPB[bank]], writes=[T_tt[k2]])
                    op("dve", lambda e: e.scalar_tensor_tensor(Gu[:, t0:t0 + n], tt[k2][:, 0:n], 0.25, Gu[:, t0:t0 + n], ALU.mult, ALU.mult),
                       reads=[T_tt[k2], T_Gu], writes=[T_Gu])
                    mb = 4 + si % 2
                    tl = seg_tiles(t0, n)
                    for ci, c in enumerate(tl):
                        op("pe", lambda e: e.matmul(PB[mb][:, ci * 128:(ci + 1) * 128], vn3[:, c, g * 128:(g + 1) * 128], wsT[:, g, :],
                                                    start=True, stop=True),
                           reads=[T_vn[c], T_const], writes=[TPB[mb]])
                    op("dve", lambda e: e.tensor_tensor(sq[k2][:, 0:n], PB[mb][:, 0:n], bsrep[:, g, 0:n], ALU.add),
                       reads=[TPB[mb], T_ab], writes=[T_sq[k2]])
                    op("pool", lambda e: e.tensor_tensor(mixT[:, g, t0:t0 + n], sq[k2][:, 0:n], Gu[:, t0:t0 + n], ALU.mult),
                       reads=[T_sq[k2], T_Gu], writes=[T_mix])
            chk(2)
            kb.barrier()
            cv = Carver()
            V1 = cv.get(18 * 4 * 144, BF16)
            V14 = V1.rearrange("p (i h c) -> p i h c", i=18, h=4)
            T_V = [Trk() for _ in range(18)]
            qT0 = cv.get(NT, BF16)
            qT1 = cv.get(NT, BF16)
            qTc = [qT0, qT1]
            kT = cv.get(NT, BF16)
            T_qT, T_kT = Trk(), Trk()
            op("pool", lambda e: e.memset(qT0[:, :], 0.0), writes=[T_qT])
            op("pool", lambda e: e.memset(qT1[:, :], 0.0), writes=[T_qT])
            sbg = cv.get(NT)
            T_sbg = Trk()
            qraw = [cv.get(512, BF16) for _ in range(2)]
            T_qraw = [Trk(), Trk()]
            t1 = [cv.get(512) for _ in range(2)]
            T_t1 = [Trk(), Trk()]
            t2 = [cv.get(512) for _ in range(2)]
            T_t2 = [Trk(), Trk()]
            pT = [cv.get(512, BF16) for _ in range(4)]
            T_pT = [Trk() for _ in range(4)]
            o1n = [cv.get(128) for _ in range(2)]
            T_o1n = [Trk(), Trk()]
            ocomb = [cv.get(128) for _ in range(2)]
            T_oc = [Trk(), Trk()]
            stats = cv.get(64)
            T_st = Trk()
            op("dve", lambda e: e.memset(V1[:, :], 1.0), writes=T_V)
            wv, twv = wload(abwin_d[:, 2560:3072], 512)
            wq, twq = wload(abwin_d[:, 1536:2048], 512)
            wk, twk = wload(abwin_d[:, 2048:2560], 512)
            for i in range(18):
                bank = i % 4
                proj_tm(wv, twv, 0, 512, i, bank)
                op("act", lambda e: e.copy(V14[:, i, :, 0:128], PB[bank][:, :].rearrange("p (h c) -> p h c", h=4)),
                   reads=[TPB[bank]], writes=[T_V[i]])
            chk(2.2)
            wg, twg = wload(abwin_d[:, 3072:3584], 512)
            pctr = [0]
            for h in range(4):
                for si, (t0, n) in enumerate(SEGS):
                    k2 = si % 2
                    proj_fm(wq, twq, h * 128, t0, n, si % 4)
                    rope_evac(si % 4, n, t0, [(qT0[0:64, t0:t0 + n], 0, 64), (qT1[64:128, t0:t0 + n], 64, 128)], T_qT, 4 + k2, qraw[k2], T_qraw[k2], t1[k2], T_t1[k2], t2[k2], T_t2[k2])
                for si, (t0, n) in enumerate(SEGS):
                    k2 = si % 2
                    proj_fm(wk, twk, h * 128, t0, n, si % 4)
                    rope_evac(si % 4, n, t0, kT[:, t0:t0 + n], T_kT, 4 + k2, qraw[k2], T_qraw[k2], t1[k2], T_t1[k2], t2[k2], T_t2[k2])
                for si, (t0, n) in enumerate(SEGS):
                    k2 = si % 2
                    bank = si % 4
                    proj_fm(wg, twg, h * 128, t0, n, bank)
                    op("act", lambda e: e.activation(t1[k2][:, 0:n], PB[bank][:, 0:n], AF.Tanh, scale=0.5),
                       reads=[TPB[bank]], writes=[T_t1[k2]])
                    op("dve", lambda e: e.scalar_tensor_tensor(sbg[:, t0:t0 + n], t1[k2][:, 0:n], 1.0, PB[bank][:, 0:n], ALU.add, ALU.mult),
                       reads=[T_t1[k2], TPB[bank]], writes=[T_sbg])
                chk(2.4)
                qtiles = [(0, [0, 1])] + [(256 + 256 * qi, list(range(18))) for qi in range(8)]
                for qn, (q0, kts) in enumerate(qtiles):
                    ob = 4 + 2 * (qn % 2)

                    def qk(kt, sbank):
                        for c in range(2):
                            op("pe", lambda e: e.matmul(PB[sbank][:, c * 256:(c + 1) * 256],
                                                        kT[:, kt * 128:(kt + 1) * 128],
                                                        qTc[c][:, q0:q0 + 256], start=True, stop=True),
                               reads=[T_kT, T_qT], writes=[TPB[sbank]])

                    qk(kts[0], 0)
                    for ki, kt in enumerate(kts):
                        sbank = ki % 3
                        if ki + 1 < len(kts):
                            qk(kts[ki + 1], (ki + 1) % 3)
                        pi = pctr[0] % 4
                        pctr[0] += 1
                        op("act", lambda e: e.activation(pT[pi][:, :], PB[sbank][:, :], AF.Exp, scale=0.125),
                           reads=[TPB[sbank]], writes=[T_pT[pi]])
                        chk(2.5)
                        for c in range(2):
                            for s in range(2):
                                first = (ki == 0 and s == 0)
                                op("pe", lambda e: e.matmul(PB[ob + c][:, s * 144:s * 144 + 130],
                                                            pT[pi][:, c * 256 + s * 128:c * 256 + (s + 1) * 128],
                                                            V14[:, kt, h, 0:130], start=first, stop=(ki == len(kts) - 1),
                                                            skip_group_check=True),
                                   reads=[T_pT[pi], T_V[kt]], writes=[TPB[ob + c]])
                    chk(2.6)
                    for s in range(2):
                        k2 = s
                        tok0 = q0 + s * 128
                        op("dve", lambda e: e.reciprocal(stats[:, 0:1], PB[ob][:, s * 144 + 128:s * 144 + 129]),
                           reads=[TPB[ob]], writes=[T_st])
                        op("dve", lambda e: e.reciprocal(stats[:, 1:2], PB[ob + 1][:, s * 144 + 128:s * 144 + 129]),
                           reads=[TPB[ob + 1]], writes=[T_st])
                        op("dve", lambda e: e.tensor_scalar(o1n[k2][:, :], PB[ob + 1][:, s * 144:s * 144 + 128], stats[:, 1:2], neglam,
                                                           ALU.mult, ALU.mult), reads=[TPB[ob + 1], T_st, T_c2], writes=[T_o1n[k2]])
                        op("dve", lambda e: e.scalar_tensor_tensor(ocomb[k2][:, :], PB[ob][:, s * 144:s * 144 + 128], stats[:, 0:1], o1n[k2][:, :],
                                                                  ALU.mult, ALU.add), reads=[TPB[ob], T_st, T_o1n[k2]], writes=[T_oc[k2]])
                        op("dve", lambda e: e.bn_stats(stats[:, 8:14], ocomb[k2][:, :]), reads=[T_oc[k2]], writes=[T_st])
                        op("dve", lambda e: e.bn_aggr(stats[:, 16:18], stats[:, 8:14]), reads=[T_st], writes=[T_st])
                        op("dve", lambda e: e.scalar_tensor_tensor(stats[:, 18:19], stats[:, 16:17], stats[:, 16:17], stats[:, 17:18],
                                                                  ALU.mult, ALU.add), reads=[T_st], writes=[T_st])
                        rsqrt_small(stats[:, 19:20], stats[:, 18:19], epsrms, T_st)
                        op("dve", lambda e: e.scalar_tensor_tensor(ocomb[k2][:, :], ocomb[k2][:, :], stats[:, 19:20], subg[:, :],
                                                                  ALU.mult, ALU.mult), reads=[T_oc[k2], T_st, T_c2], writes=[T_oc[k2]])
                        op("pe", lambda e: e.transpose(PB[3][:, s * 128:(s + 1) * 128], ocomb[k2][:, :], ident[:]),
                           reads=[T_oc[k2], T_const], writes=[TPB[3]])
                        op("dve", lambda e: e.tensor_tensor(mixT[:, 4 + h, tok0:tok0 + 128], PB[3][:, s * 128:(s + 1) * 128],
                                                           sbg[:, tok0:tok0 + 128], ALU.mult),
                           reads=[TPB[3], T_sbg], writes=[T_mix])
            chk(3)
            phase_outproj(l, b, abwout_d, list(range(18)))
            chk(4)

        def pass_layer1(b):
            l = 1
            kb.barrier()
            phase_transposes(l, b)
            kb.barrier()
            cv = Carver()
            yT = cv.get(4 * NLAT)
            yT3 = yT.rearrange("p (j t) -> p j t", j=4)
            T_yT = [Trk() for _ in range(4)]
            hpad = [cv.get(NLAT + 32, BF16) for _ in range(2)]
            T_hp = [Trk(), Trk()]
            caS = [cv.get(512) for _ in range(2)]
            T_ca = [Trk(), Trk()]
            tt = [cv.get(512) for _ in range(2)]
            T_tt = [Trk(), Trk()]
            diag = cv.get(31 * 128, BF16)
            diag3 = diag.rearrange("p (k c) -> p k c", k=31)
            T_dg = Trk()
            wa, twa = wload(cdwin_d[:, 0:512], 512)
            wb, twb = wload(cdwin_d[:, 512:1024], 512)
            for k2 in range(2):
                op("dve", lambda e: e.memset(hpad[k2][:, :], 0.0), writes=[T_hp[k2]])
            for j in range(4):
                hp = hpad[j % 2]
                thp = T_hp[j % 2]
                for k in range(31):
                    op("dve", lambda e: e.tensor_scalar(diag3[:, k, :], identb[:, :], dwT[:, j * 31 + k:j * 31 + k + 1], None, ALU.mult),
                       reads=[T_c2], writes=[T_dg])
                for si, (t0, n) in enumerate(LSEGS):
                    k2 = si % 2
                    lt0 = t0 - NCTX
                    proj_fm(wa, twa, j * 128, t0, n, si % 2)
                    op("act", lambda e: e.copy(caS[k2][:, :], PB[si % 2][:, :]), reads=[TPB[si % 2]], writes=[T_ca[k2]])
                    proj_fm(wb, twb, j * 128, t0, n, 2 + si % 2)
                    op("act", lambda e: e.activation(tt[k2][:, :], PB[2 + si % 2][:, :], AF.Tanh, scale=0.5),
                       reads=[TPB[2 + si % 2]], writes=[T_tt[k2]])
                    op("dve", lambda e: e.scalar_tensor_tensor(hp[:, 15 + lt0:15 + lt0 + n], tt[k2][:, :], 1.0, caS[k2][:, :], ALU.add, ALU.mult),
                       reads=[T_tt[k2], T_ca[k2]], writes=[thp])
                for si, (t0, n) in enumerate(LSEGS):
                    lt0 = t0 - NCTX
                    cb = 4 + si % 2
                    for k in range(31):
                        op("pe", lambda e: e.matmul(PB[cb][:, :], diag3[:, k, :], hp[:, lt0 + k:lt0 + k + 512],
                                                    start=(k == 0), stop=(k == 30)),
                           reads=[T_dg, thp], writes=[TPB[cb]])
                    op("act", lambda e: e.activation(yT3[:, j, lt0:lt0 + 512], PB[cb][:, :], AF.Identity, bias=cvecs[:, j:j + 1]),
                       reads=[TPB[cb], T_const], writes=[T_yT[j]])
            kb.barrier()
            cv2 = Carver()
            cv2.off = 4 * NLAT * 4
            mean_bc = cv2.get(NLAT)
            rstd_bc = cv2.get(NLAT)
            T_mr = Trk()
            ysq = [cv2.get(512) for _ in range(2)]
            T_ysq = [Trk(), Trk()]
            tt = [cv2.get(512) for _ in range(2)]
            T_tt = [Trk(), Trk()]
            sg = [cv2.get(512) for _ in range(2)]
            T_sg = [Trk(), Trk()]
            yn = [cv2.get(512) for _ in range(2)]
            T_yn = [Trk(), Trk()]
            for si, (t0, n) in enumerate(LSEGS):
                lt0 = t0 - NCTX
                ts_ = slice(lt0, lt0 + 512)
                mbk, sbk = 0 + 2 * (si % 2), 1 + 2 * (si % 2)
                for j in range(4):
                    op("pe", lambda e: e.matmul(PB[mbk][:, :], onesdiv[:, :], yT3[:, j, ts_], start=(j == 0), stop=(j == 3)),
                       reads=[T_c2, T_yT[j]], writes=[TPB[mbk]])
                for j in range(4):
                    k2 = j % 2
                    op("act", lambda e: e.activation(ysq[k2][:, :], yT3[:, j, ts_], AF.Square), reads=[T_yT[j]], writes=[T_ysq[k2]])
                    op("pe", lambda e: e.matmul(PB[sbk][:, :], onesdiv[:, :], ysq[k2][:, :], start=(j == 0), stop=(j == 3)),
                       reads=[T_c2, T_ysq[k2]], writes=[TPB[sbk]])
                op("act", lambda e: e.copy(mean_bc[:, ts_], PB[mbk][:, :]), reads=[TPB[mbk]], writes=[T_mr])
                op("dve", lambda e: e.tensor_tensor(rstd_bc[:, ts_], mean_bc[:, ts_], mean_bc[:, ts_], ALU.mult), reads=[T_mr], writes=[T_mr])
                op("dve", lambda e: e.tensor_tensor(rstd_bc[:, ts_], PB[sbk][:, :], rstd_bc[:, ts_], ALU.subtract),
                   reads=[TPB[sbk], T_mr], writes=[T_mr])
                op("act", lambda e: e.activation(rstd_bc[:, ts_], rstd_bc[:, ts_], AF.Sqrt, bias=epsln), reads=[T_mr, T_c2], writes=[T_mr])
                op("dve", lambda e: e.reciprocal(rstd_bc[:, ts_], rstd_bc[:, ts_]), reads=[T_mr], writes=[T_mr])
            wg, twg = wload(cdwin_d[:, 1024:1536], 512)
            cnt = 0
            for j in range(4):
                for si, (t0, n) in enumerate(LSEGS):
                    lt0 = t0 - NCTX
                    ts_ = slice(lt0, lt0 + 512)
                    k2 = cnt % 2
                    bank = 4 + cnt % 4
                    cnt += 1
                    proj_fm(wg, twg, j * 128, t0, n, bank)
                    op("act", lambda e: e.activation(tt[k2][:, :], PB[bank][:, :], AF.Tanh, scale=0.5), reads=[TPB[bank]], writes=[T_tt[k2]])
                    op("dve", lambda e: e.scalar_tensor_tensor(sg[k2][:, :], tt[k2][:, :], 1.0, PB[bank][:, :], ALU.add, ALU.mult),
                       reads=[T_tt[k2], TPB[bank]], writes=[T_sg[k2]])
                    op("pool", lambda e: e.tensor_tensor(yn[k2][:, :], yT3[:, j, ts_], mean_bc[:, ts_], ALU.subtract),
                       reads=[T_yT[j], T_mr], writes=[T_yn[k2]])
                    op("pool", lambda e: e.tensor_tensor(yn[k2][:, :], yn[k2][:, :], rstd_bc[:, ts_], ALU.mult),
                       reads=[T_yn[k2], T_mr], writes=[T_yn[k2]])
                    op("dve", lambda e: e.tensor_scalar(yn[k2][:, :], yn[k2][:, :], cvecs[:, 4 + j:5 + j], cvecs[:, 8 + j:9 + j], ALU.mult, ALU.add),
                       reads=[T_yn[k2], T_const], writes=[T_yn[k2]])
                    op("act", lambda e: e.activation(tt[k2][:, :], yn[k2][:, :], AF.Tanh, scale=0.5), reads=[T_yn[k2], T_sg[k2]], writes=[T_tt[k2]])
                    op("dve", lambda e: e.scalar_tensor_tensor(yn[k2][:, :], tt[k2][:, :], 1.0, yn[k2][:, :], ALU.add, ALU.mult),
                       reads=[T_tt[k2], T_yn[k2]], writes=[T_yn[k2]])
                    op("dve", lambda e: e.scalar_tensor_tensor(mixT[:, j, t0:t0 + n], yn[k2][:, :], 0.25, sg[k2][:, :], ALU.mult, ALU.mult),
                       reads=[T_yn[k2], T_sg[k2]], writes=[T_mix])
            kb.barrier()
            cv = Carver()
            V1 = cv.get(18 * 2 * 80, BF16)
            V14 = V1.rearrange("p (i h c) -> p i h c", i=18, h=2)
            T_V = [Trk() for _ in range(18)]
            kT2 = cv.get(2 * NT, BF16)
            kT23 = kT2.rearrange("p (h t) -> p h t", h=2)
            T_kT = Trk()
            qT0 = cv.get(NLAT, BF16)
            qT1 = cv.get(NLAT, BF16)
            qTc = [qT0, qT1]
            T_qT = Trk()
            op("pool", lambda e: e.memset(qT0[:, :], 0.0), writes=[T_qT])
            op("pool", lambda e: e.memset(qT1[:, :], 0.0), writes=[T_qT])
            sdg = cv.get(NLAT)
            T_sdg = Trk()
            wk2 = cv.get(8 * 2 * 128, BF16)
            wk24 = wk2.rearrange("p (j h c) -> p j h c", j=8, h=2)
            T_wk2 = Trk()
            d_wk2 = kb.dsem()
            qraw = [cv.get(512, BF16) for _ in range(2)]
            T_qraw = [Trk(), Trk()]
            t1 = [cv.get(512) for _ in range(2)]
            T_t1 = [Trk(), Trk()]
            t2 = [cv.get(512) for _ in range(2)]
            T_t2 = [Trk(), Trk()]
            pT = [cv.get(256, BF16) for _ in range(4)]
            T_pT = [Trk() for _ in range(4)]
            ocomb = [cv.get(128) for _ in range(2)]
            T_oc = [Trk(), Trk()]
            stats = cv.get(64)
            T_st = Trk()
            op("dve", lambda e: e.memset(V1[:, :], 1.0), writes=T_V)
            for kvh in range(2):
                for dup in range(2):
                    kb.dma("pool", wk24[:, :, kvh, dup * 64:(dup + 1) * 64],
                           cdwin_d[:, 2048 + kvh * 64:2048 + (kvh + 1) * 64].rearrange("(j p) c -> p j c", p=128),
                           d_wk2, writes=[T_wk2])
            wq, twq = wload(cdwin_d[:, 1536:2048], 512)
            wkv, twkv = wload(cdwin_d[:, 2048:2560], 512)
            wg2, twg2 = wload(cdwin_d[:, 2560:2816], 256)
            for i in range(18):
                bank = i % 4
                proj_tm(wkv, twkv, 128, 128, i, bank)
                op("act", lambda e: e.copy(V14[:, i, :, 0:64], PB[bank][:, 0:128].rearrange("p (h c) -> p h c", h=2)),
                   reads=[TPB[bank]], writes=[T_V[i]])
            for kvh in range(2):
                for si, (t0, n) in enumerate(SEGS):
                    k2 = si % 2
                    bank = si % 4
                    for j in range(8):
                        op("pe", lambda e: e.matmul(PB[bank][:, 0:n], wk24[:, j, kvh, :], uT[:, j, t0:t0 + n], start=(j == 0), stop=(j == 7)),
                           reads=[T_wk2] + [T_uT[i] for i in seg_tiles(t0, n)], writes=[TPB[bank]])
                    rope_evac(bank, n, t0, kT23[:, kvh, t0:t0 + n], T_kT, 4 + k2, qraw[k2], T_qraw[k2], t1[k2], T_t1[k2], t2[k2], T_t2[k2])
            pctr = [0]
            for cc in range(4):
                kvh = cc // 2
                for si, (t0, n) in enumerate(LSEGS):
                    k2 = si % 2
                    lt0 = t0 - NCTX
                    proj_fm(wq, twq, cc * 128, t0, n, si % 4)
                    rope_evac(si % 4, n, t0, [(qT0[0:64, lt0:lt0 + n], 0, 64), (qT1[64:128, lt0:lt0 + n], 64, 128)], T_qT, 4 + k2, qraw[k2], T_qraw[k2], t1[k2], T_t1[k2], t2[k2], T_t2[k2])
                for si, (t0, n) in enumerate(LSEGS):
                    k2 = si % 2
                    bank = si % 4
                    lt0 = t0 - NCTX
                    if cc < 2:
                        proj_fm(wkv, twkv, 256 + cc * 128, t0, n, bank)
                    else:
                        proj_fm(wg2, twg2, (cc - 2) * 128, t0, n, bank)
                    op("act", lambda e: e.activation(t1[k2][:, 0:n], PB[bank][:, 0:n], AF.Tanh, scale=0.5),
                       reads=[TPB[bank]], writes=[T_t1[k2]])
                    op("dve", lambda e: e.scalar_tensor_tensor(sdg[:, lt0:lt0 + n], t1[k2][:, 0:n], 1.0, PB[bank][:, 0:n], ALU.add, ALU.mult),
                       reads=[T_t1[k2], TPB[bank]], writes=[T_sdg])
                for qb in range(16):
                    q0 = qb * 128
                    kts = []
                    if qb > 0:
                        kts.append((2 + qb - 1, 0))
                    kts.append((2 + qb, None))
                    if qb < 15:
                        kts.append((2 + qb + 1, 1))
                    kts += [(0, None), (1, None)]
                    ob = 4 + qb % 2
                    for ki, (kt, mk) in enumerate(kts):
                        sbank = ki % 4
                        for hl in range(2):
                            op("pe", lambda e: e.matmul(PB[sbank][:, hl * 128:(hl + 1) * 128],
                                                        kT23[:, kvh, kt * 128:(kt + 1) * 128],
                                                        qTc[hl][:, q0:q0 + 128], start=True, stop=True),
                               reads=[T_kT, T_qT], writes=[TPB[sbank]])
                        pi = pctr[0] % 4
                        pctr[0] += 1
                        op("act", lambda e: e.activation(pT[pi][:, :], PB[sbank][:, 0:256], AF.Exp, scale=0.125),
                           reads=[TPB[sbank]], writes=[T_pT[pi]])
                        if mk is not None:
                            op("pool", lambda e: e.tensor_tensor(pT[pi][:, :], pT[pi][:, :], masks[:, mk, :], ALU.mult),
                               reads=[T_pT[pi], T_const], writes=[T_pT[pi]])
                        for hl in range(2):
                            first = (ki == 0 and hl == 0)
                            op("pe", lambda e: e.matmul(PB[ob][:, hl * 80:hl * 80 + 66], pT[pi][:, hl * 128:(hl + 1) * 128],
                                                        V14[:, kt, kvh, 0:66], start=first, stop=(ki == len(kts) - 1),
                                                        skip_group_check=True),
                               reads=[T_pT[pi], T_V[kt]], writes=[TPB[ob]])
                    k2 = qb % 2
                    for hl in range(2):
                        hd = cc * 2 + hl
                        op("dve", lambda e: e.tensor_scalar(stats[:, hl:hl + 1], PB[ob][:, hl * 80 + 64:hl * 80 + 65], sinkt[:, hd:hd + 1], None, ALU.add),
                           reads=[TPB[ob], T_c2], writes=[T_st])
                        op("dve", lambda e: e.reciprocal(stats[:, 2 + hl:3 + hl], stats[:, hl:hl + 1]), reads=[T_st], writes=[T_st])
                        op("dve", lambda e: e.tensor_scalar(ocomb[k2][:, hl * 64:(hl + 1) * 64], PB[ob][:, hl * 80:hl * 80 + 64],
                                                           stats[:, 2 + hl:3 + hl], None, ALU.mult),
                           reads=[TPB[ob], T_st], writes=[T_oc[k2]])
                    tb = 6 + qb % 2
                    op("pe", lambda e: e.transpose(PB[tb][:, 0:128], ocomb[k2][:, :], ident[:]), reads=[T_oc[k2], T_const], writes=[TPB[tb]])
                    op("dve", lambda e: e.scalar_tensor_tensor(mixT[:, 4 + cc, NCTX + q0:NCTX + q0 + 128], PB[tb][:, 0:128], 0.5,
                                                              sdg[:, q0:q0 + 128], ALU.mult, ALU.mult),
                       reads=[TPB[tb], T_sdg], writes=[T_mix])
            phase_outproj(l, b, cdwout_d, list(range(2, 18)))

        try:
            chk(0)
            for b in range(2):
                pass_layer0(b)
            if not debug_h1:
                for b in range(2):
                    pass_layer1(b)
        except _Stop:
            pass
        kb.barrier()
        for ds in d_o:
            if ds.count:
                nc.sync.wait_ge(ds.sem, ds.count)
    return nc


def _consts():
    ident = np.eye(128, dtype=np.float32)
    rmat = np.zeros((128, 128), np.float32)
    for dp in range(128):
        partner = dp + 16 if (dp % 32) < 16 else dp - 16
        rmat[partner, dp] = 1.0
    m = 16
    inv = (10000.0 ** (-np.arange(m, dtype=np.float32) / m)).astype(np.float32)
    t = np.arange(NLAT)
    row = (t // 64).astype(np.float32)
    col = (t % 64).astype(np.float32)
    ang_r = (row[:, None] * inv[None, :]).astype(np.float32)
    ang_c = (col[:, None] * inv[None, :]).astype(np.float32)
    cos_t = np.ones((128, NT), np.float32)
    sin_t = np.zeros((128, NT), np.float32)
    for p in range(128):
        d = p % 64
        ang = ang_r if d < 32 else ang_c
        f = d % 16
        sign = -1.0 if (d % 32) < 16 else 1.0
        cos_t[p, NCTX:] = np.cos(ang[:, f])
        sin_t[p, NCTX:] = sign * np.sin(ang[:, f])
    ropeT = np.concatenate([cos_t, sin_t], axis=1)
    kk = np.arange(128)[:, None]
    qq = np.arange(128)[None, :]
    mp = (kk >= qq).astype(np.float32)
    mn = (kk <= qq).astype(np.float32)
    masks = np.concatenate([mp, mp, mn, mn], axis=1)
    return ident, rmat, ropeT, masks


_CACHE = {}


def _core_inputs(core, x, c, ctx, c_ctx, mod_w, mod_b, ln_g, ln_b, ab_w_in, ab_w_out, a_w_s, a_b_s,
                 a_norm_g, a_norm_b, b_lq1, b_lk1, b_lq2, b_lk2, b_subln_g, cd_w_in, cd_w_out,
                 c_dw_w, c_dw_b, c_norm_g, c_norm_b, d_sink, shared):
    b0 = 2 * core
    cvec = np.stack([c[b0], c[b0 + 1], c_ctx], axis=0)
    cvT = np.ascontiguousarray(cvec.reshape(3, 8, 128).transpose(2, 1, 0)).reshape(128, 24)
    d = dict(shared)
    d["x"] = np.ascontiguousarray(x[b0:b0 + 2])
    d["ctx"] = np.ascontiguousarray(ctx[b0:b0 + 2])
    d["cvT"] = cvT
    return d


def kernel(x, c, ctx, c_ctx, mod_w, mod_b, ln_g, ln_b, ab_w_in, ab_w_out, a_w_s, a_b_s,
           a_norm_g, a_norm_b, b_lq1, b_lk1, b_lq2, b_lk2, b_subln_g, cd_w_in, cd_w_out,
           c_dw_w, c_dw_b, c_norm_g, c_norm_b, d_sink, _debug_h1=False, _stage=99):
    f = lambda a: np.ascontiguousarray(np.asarray(a, dtype=np.float32))
    x, c, ctx, c_ctx = f(x), f(c), f(ctx), f(c_ctx)
    ident, rmat, ropeT, masks = _consts()
    shared = {
        "mod_w": f(mod_w),
        "mod_bT": np.ascontiguousarray(f(mod_b).reshape(2, 24, 128).transpose(2, 0, 1)).reshape(128, 48),
        "ln_g": f(ln_g), "ln_b": f(ln_b),
        "ab_w_in": f(ab_w_in)[0], "ab_w_out": f(ab_w_out)[0],
        "a_w_sT": np.ascontiguousarray(f(a_w_s)[0].transpose(2, 0, 1)).reshape(128, 512),
        "a_b_s": f(a_b_s)[0].reshape(1, 512),
        "a_norm_g": f(a_norm_g).reshape(1, 512), "a_norm_b": f(a_norm_b).reshape(1, 512),
        "lam_in": np.concatenate([f(b_lq1)[0], f(b_lk1)[0], f(b_lq2)[0], f(b_lk2)[0]]).reshape(1, 256),
        "b_subln_g": f(b_subln_g).reshape(1, 128),
        "cd_w_in": f(cd_w_in)[0], "cd_w_out": f(cd_w_out)[0],
        "dwT": np.ascontiguousarray(f(c_dw_w)[0].reshape(31, 4, 128).transpose(2, 1, 0)).reshape(128, 124),
        "cvecs": np.ascontiguousarray(np.stack([f(c_dw_b)[0], f(c_norm_g)[0], f(c_norm_b)[0]], 0)
                                      .reshape(3, 4, 128).transpose(2, 0, 1)).reshape(128, 12),
        "d_sink": f(d_sink).reshape(1, 8),
        "ident": ident, "rmat": rmat, "ropeT": ropeT, "masks": masks,
    }
    in_maps = []
    for core in range(8):
        in_maps.append(_core_inputs(core, x, c, ctx, c_ctx, None, None, None, None, None, None, None, None,
                                    None, None, None, None, None, None, None, None, None,
                                    None, None, None, None, None, shared))
    key = (bool(_debug_h1), _stage)
    if key not in _CACHE:
        _CACHE[key] = build_program(debug_h1=key[0], stage=_stage)
    nc = _CACHE[key]
    res = run_bass_kernel_spmd(nc, in_maps, core_ids=list(range(8)))
    if _debug_h1:
        return np.concatenate([r["h1"] for r in res.results], axis=0)
    return np.concatenate([r["out"] for r in res.results], axis=0).astype(np.float32)
```

```python
import math
import numpy as np
from contextlib import ExitStack
import concourse.bass as bass
import concourse.mybir as mybir
from concourse.bass_utils import run_bass_kernel_spmd

F32 = mybir.dt.float32
BF16 = mybir.dt.bfloat16
AF = mybir.ActivationFunctionType
ALU = mybir.AluOpType

NT = 2304
NCTX = 256
NLAT = 2048
D = 1024
LN_EPS = 1e-6
RMS_EPS = 1e-5
ALPHA = (2.0 * 2) ** 0.25
GC0 = math.sqrt(2.0 / math.pi)
GC1 = 0.044715
SEGS = [(0, 256), (256, 512), (768, 512), (1280, 512), (1792, 512)]
LSEGS = SEGS[1:]


class Trk:
    __slots__ = ("name", "w", "r")

    def __init__(self, name=""):
        self.name = name
        self.w = None
        self.r = {}


class DSem:
    def __init__(self, sem):
        self.sem = sem
        self.count = 0


class KB:
    def __init__(self, nc, es):
        self.nc = nc
        self.es = es
        self.eng = {"pe": nc.tensor, "act": nc.scalar, "dve": nc.vector, "pool": nc.gpsimd, "sp": nc.sync}
        self.esem = {}
        self.ecnt = {}
        for e in ("pe", "act", "dve", "pool"):
            self.esem[e] = es.enter_context(nc.semaphore("s_" + e))
            self.ecnt[e] = 0
        self.seen = {e: {} for e in self.eng}
        self.nsem = 0
        self.hooks = []
        self.bar_dsems = []

    def dsem(self):
        self.nsem += 1
        return DSem(self.es.enter_context(self.nc.semaphore(f"d{self.nsem}")))

    def _wait(self, e, reads, writes):
        evs = {}

        def add(ev):
            if ev is None:
                return
            k = id(ev[0])
            if k not in evs or evs[k][1] < ev[1]:
                evs[k] = ev

        for t in reads:
            add(t.w)
        for t in writes:
            add(t.w)
            for ev in t.r.values():
                add(ev)
        seen = self.seen[e]
        own = self.esem.get(e)
        for k, (sem, val) in evs.items():
            if e == "pe" and sem is own:
                continue
            if seen.get(k, 0) >= val:
                continue
            self.eng[e].wait_ge(sem, val)
            seen[k] = val

    def _post(self, ev, reads, writes):
        k = id(ev[0])
        for t in writes:
            t.w = ev
            t.r = {}
        for t in reads:
            t.r[k] = ev

    def op(self, e, fn, reads=(), writes=()):
        self._wait(e, reads, writes)
        ins = fn(self.eng[e])
        self.ecnt[e] += 1
        ins.then_inc(self.esem[e], 1)
        ev = (self.esem[e], self.ecnt[e])
        self._post(ev, reads, writes)
        return ev

    def dma(self, q, out, in_, ds, reads=(), writes=()):
        self._wait(q, reads, writes)
        ins = self.eng[q].dma_start(out=out, in_=in_)
        ds.count += 16
        ins.then_inc(ds.sem, 16)
        ev = (ds.sem, ds.count)
        self._post(ev, reads, writes)
        return ev

    def barrier(self):
        for h in self.hooks:
            h()
        for e in self.eng:
            seen = self.seen[e]
            for ds in self.bar_dsems:
                k = id(ds.sem)
                if ds.count and seen.get(k, 0) < ds.count:
                    self.eng[e].wait_ge(ds.sem, ds.count)
                    seen[k] = ds.count
            for f in ("pe", "act", "dve", "pool"):
                if f == e or self.ecnt[f] == 0:
                    continue
                k = id(self.esem[f])
                if seen.get(k, 0) >= self.ecnt[f]:
                    continue
                self.eng[e].wait_ge(self.esem[f], self.ecnt[f])
                seen[k] = self.ecnt[f]


class _Stop(Exception):
    pass


LAG_ON = True


class Lag:
    def __init__(self):
        self.p = None

    def push(self, fn):
        if not LAG_ON:
            fn()
            return
        old, self.p = self.p, fn
        if old:
            old()

    def flush(self):
        old, self.p = self.p, None
        if old:
            old()


def build_program(debug_h1=False, stage=99):
    nc = bass.Bass("TRN2", target_bir_lowering=False)

    def chk(n):
        if stage <= n:
            raise _Stop()

    def din(name, shape):
        return nc.dram_tensor(name, list(shape), F32, kind="ExternalInput").ap()

    x_d = din("x", [2, NLAT, D])
    ctx_d = din("ctx", [2, NCTX, D])
    cvT_d = din("cvT", [128, 24])
    modw_d = din("mod_w", [2, D, 3 * D])
    modbT_d = din("mod_bT", [128, 48])
    lng_d = din("ln_g", [2, D])
    lnb_d = din("ln_b", [2, D])
    abwin_d = din("ab_w_in", [D, 3584])
    abwout_d = din("ab_w_out", [D, D])
    awsT_d = din("a_w_sT", [128, 512])
    abs_d = din("a_b_s", [1, 512])
    ang_d = din("a_norm_g", [1, 512])
    anb_d = din("a_norm_b", [1, 512])
    lam_d = din("lam_in", [1, 256])
    subg_d = din("b_subln_g", [1, 128])
    cdwin_d = din("cd_w_in", [D, 2816])
    cdwout_d = din("cd_w_out", [D, D])
    dwT_d = din("dwT", [128, 124])
    cvec_d = din("cvecs", [128, 12])
    sink_d = din("d_sink", [1, 8])
    ident_d = din("ident", [128, 128])
    rmat_d = din("rmat", [128, 128])
    rope_d = din("ropeT", [128, 2 * NT])
    mask_d = din("masks", [128, 512])
    out_d = nc.dram_tensor("out", [2, NLAT, D], F32, kind="ExternalOutput").ap()
    if debug_h1:
        h1_d = nc.dram_tensor("h1", [2, NT, D], F32, kind="ExternalOutput").ap()
    else:
        h1_d = nc.dram_tensor("h1", [2, NT, D], F32).ap()

    with ExitStack() as es:
        kb = KB(nc, es)
        op = kb.op

        def sb(name, shape, dt=F32):
            return es.enter_context(nc.sbuf_tensor("sb_" + name, list(shape), dt))

        PB = [es.enter_context(nc.psum_tensor(f"pb{i}", [128, 512], F32)) for i in range(8)]
        TPB = [Trk(f"pb{i}") for i in range(8)]

        d_const = kb.dsem()
        T_const = Trk("const")
        d_constp = kb.dsem()
        T_constp = Trk("constp")
        ident = sb("ident", [128, 128])
        identb = sb("identb", [128, 128], BF16)
        rmat = sb("rmat", [128, 128], BF16)
        ropeT = sb("ropeT", [128, 2, NT])
        masks = sb("masks", [128, 2, 256], BF16)
        cvT = sb("cvT", [128, 24])
        modbT = sb("modbT", [128, 48])
        wsT = sb("wsT", [128, 4, 128], BF16)
        lamt = sb("lamt", [128, 256])
        subg = sb("subg", [128, 128])
        dwT = sb("dwT", [128, 124])
        cvecs = sb("cvecs", [128, 12])
        sinkt = sb("sinkt", [128, 8])
        ones_f = sb("ones_f", [128, 128])
        onesdiv = sb("onesdiv", [128, 128])
        sTb = sb("sTb", [128, 24], BF16)
        modT = sb("modT", [128, 144])
        small = sb("small", [128, 64])
        epsln = small[:, 0:1]
        epsrms = small[:, 1:2]
        epsln4 = small[:, 2:3]
        neglam = small[:, 3:4]

        def cdma(q, dst, src):
            if q == "pool":
                kb.dma(q, dst, src, d_constp, writes=[T_constp])
            else:
                kb.dma(q, dst, src, d_const, writes=[T_const])

        cdma("sp", ident[:], ident_d)
        cdma("pool", rmat[:], rmat_d)
        cdma("sp", ropeT[:], rope_d.rearrange("p (a t) -> p a t", a=2))
        cdma("pool", masks[:], mask_d.rearrange("p (a t) -> p a t", a=2))
        cdma("sp", cvT[:], cvT_d)
        cdma("sp", modbT[:], modbT_d)
        cdma("pool", wsT[:], awsT_d.rearrange("p (g q) -> p g q", g=4))
        cdma("sp", lamt[:], lam_d.partition_broadcast(128))
        cdma("sp", subg[:], subg_d.partition_broadcast(128))
        cdma("sp", dwT[:], dwT_d)
        cdma("sp", cvecs[:], cvec_d)
        cdma("sp", sinkt[:], sink_d.partition_broadcast(128))

        T_c2 = Trk("c2")
        RC = [T_const, T_c2]
        op("dve", lambda e: e.memset(ones_f[:], 1.0), writes=[T_c2])
        op("dve", lambda e: e.memset(onesdiv[:], 1.0 / 512.0), writes=[T_c2])
        op("dve", lambda e: e.memset(small[:, 0:1], LN_EPS), writes=[T_c2])
        op("dve", lambda e: e.memset(small[:, 1:2], RMS_EPS), writes=[T_c2])
        op("dve", lambda e: e.memset(small[:, 2:3], 4.0 * LN_EPS), writes=[T_c2])
        op("dve", lambda e: e.tensor_copy(identb[:], ident[:]), reads=[T_const], writes=[T_c2])
        lam_init0 = 0.8 - 0.6 * math.exp(-0.3 * 0)
        lprod = sb("lprod", [128, 128])
        op("dve", lambda e: e.tensor_tensor(lprod[:, 0:64], lamt[:, 0:64], lamt[:, 64:128], ALU.mult),
           reads=[T_const], writes=[T_c2])
        op("dve", lambda e: e.tensor_tensor(lprod[:, 64:128], lamt[:, 128:192], lamt[:, 192:256], ALU.mult),
           reads=[T_c2, T_const], writes=[T_c2])
        op("dve", lambda e: e.reduce_sum(small[:, 4:6], lprod[:].rearrange("p (a b) -> p a b", a=2),
                                        mybir.AxisListType.X), reads=[T_c2], writes=[T_c2])
        op("act", lambda e: e.activation(small[:, 6:8], small[:, 4:6], AF.Exp), reads=[T_c2], writes=[T_c2])
        op("dve", lambda e: e.scalar_tensor_tensor(small[:, 3:4], small[:, 7:8], -lam_init0, small[:, 6:7],
                                                  ALU.add, ALU.subtract), reads=[T_c2], writes=[T_c2])
        op("dve", lambda e: e.tensor_scalar(subg[:], subg[:], (1.0 - lam_init0) * 0.5, None, ALU.mult),
           reads=[T_const, T_c2], writes=[T_c2])
        op("act", lambda e: e.activation(sinkt[:], sinkt[:], AF.Exp), reads=[T_const, T_c2], writes=[T_c2])
        op("dve", lambda e: e.tensor_scalar(dwT[:], dwT[:], 0.5, None, ALU.mult), reads=[T_const, T_c2], writes=[T_c2])
        sct = sb("sct", [128, 24])
        op("act", lambda e: e.activation(sct[:], cvT[:], AF.Tanh, scale=0.5), reads=[T_const], writes=[T_c2])
        op("dve", lambda e: e.scalar_tensor_tensor(sct[:], sct[:], 1.0, cvT[:], ALU.add, ALU.mult),
           reads=[T_c2, T_const], writes=[T_c2])
        op("dve", lambda e: e.tensor_scalar(sTb[:], sct[:], 0.5, None, ALU.mult), reads=[T_c2], writes=[T_c2])

        NW = 3
        wslot = [sb(f"wslot{i}", [128, 8, 512], BF16) for i in range(NW)]
        T_w = [Trk(f"w{i}") for i in range(NW)]
        d_w = [kb.dsem() for _ in range(NW)]
        wctr = [0]

        def wload(src_ap, ncols):
            i = wctr[0] % NW
            wctr[0] += 1
            kb.dma("pool", wslot[i][:, :, 0:ncols], src_ap.rearrange("(j p) c -> p j c", p=128), d_w[i],
                   writes=[T_w[i]])
            return wslot[i], T_w[i]

        NS = 3
        srct = [None] * NS
        T_s = [Trk(f"s{i}") for i in range(NS)]
        d_s = [kb.dsem() for _ in range(NS)]
        sctr = [0]

        def sload(src_ap):
            i = sctr[0] % NS
            sctr[0] += 1
            kb.dma("sp", srct[i][:], src_ap, d_s[i], writes=[T_s[i]])
            return srct[i], T_s[i]

        NO = 3
        outt = [None] * NO
        T_o = [Trk(f"o{i}") for i in range(NO)]
        d_o = [kb.dsem() for _ in range(NO)]
        octr = [0]
        kb.bar_dsems = d_o
        T_h1 = [[Trk(f"h1_{b}_{i}") for i in range(18)] for b in range(2)]

        T_mod = Trk("mod")
        for l in range(2):
            for g in range(3):
                for half in range(2):
                    c0 = g * 1024 + half * 512
                    ws, tw = wload(modw_d[l, :, c0:c0 + 512], 512)
                    for cc in range(4):
                        for j in range(8):
                            op("pe", lambda e: e.matmul(PB[7][:, cc * 4:cc * 4 + 3], ws[:, j, cc * 128:(cc + 1) * 128],
                                                        sTb[:, j * 3:(j + 1) * 3], start=(j == 0), stop=(j == 7)),
                               reads=[tw, T_c2], writes=[TPB[7]])
                    for cc in range(4):
                        k = g * 8 + half * 4 + cc
                        o0 = (l * 24 + k) * 3
                        op("dve", lambda e: e.tensor_scalar(modT[:, o0:o0 + 3], PB[7][:, cc * 4:cc * 4 + 3],
                                                           modbT[:, l * 24 + k:l * 24 + k + 1],
                                                           1.0 if g == 1 else 0.0, ALU.add, ALU.add),
                           reads=[TPB[7], T_const], writes=[T_mod])

        def mod_ap(l, k, r):
            o0 = (l * 24 + k) * 3 + r
            return modT[:, o0:o0 + 1]

        uT = sb("uT", [128, 8, NT], BF16)
        T_uT = [Trk(f"uT{i}") for i in range(18)]
        mixT = sb("mixT", [128, 8, NT], BF16)
        T_mix = Trk("mixT")
        T_gbc = Trk("gbc")
        T_diagf = [Trk(), Trk()]
        T_lngb = Trk("lngb")
        d_lngb = kb.dsem()
        T_ab = Trk("ab")
        d_ab = kb.dsem()
        ARENA = 84 * 1024
        arena = sb("arena", [128, ARENA // 4])

        class Carver:
            def __init__(self):
                self.off = 0

            def get(self, nelem, dt=F32):
                nbytes = nelem * (4 if dt == F32 else 2)
                nbytes = (nbytes + 31) // 32 * 32
                assert self.off + nbytes <= ARENA, (self.off, nbytes)
                a = arena[:, self.off // 4:(self.off + nbytes) // 4]
                self.off += nbytes
                if dt == BF16:
                    a = a.bitcast(BF16)[:, 0:nelem]
                return a

        def tile_rows(l, b, i):
            if l == 0:
                return ctx_d[b, i * 128:(i + 1) * 128, :] if i < 2 else x_d[b, (i - 2) * 128:(i - 1) * 128, :]
            return h1_d[b, i * 128:(i + 1) * 128, :]

        def seg_tiles(t0, n):
            return list(range(t0 // 128, (t0 + n) // 128))

        def proj_fm(ws, tw, c0, t0, n, bank):
            for j in range(8):
                op("pe", lambda e: e.matmul(PB[bank][:, 0:n], ws[:, j, c0:c0 + 128], uT[:, j, t0:t0 + n],
                                            start=(j == 0), stop=(j == 7)),
                   reads=[tw] + [T_uT[i] for i in seg_tiles(t0, n)], writes=[TPB[bank]])

        def proj_tm(ws, tw, c0, ncols, i, bank):
            for j in range(8):
                op("pe", lambda e: e.matmul(PB[bank][:, 0:ncols], uT[:, j, i * 128:(i + 1) * 128], ws[:, j, c0:c0 + ncols],
                                            start=(j == 0), stop=(j == 7)),
                   reads=[tw, T_uT[i]], writes=[TPB[bank]])

        def rsqrt_small(dst, src, eps_ap, trk, scale=1.0):
            op("act", lambda e: e.activation(dst, src, AF.Sqrt, bias=eps_ap, scale=scale), reads=[trk, T_c2], writes=[trk])
            op("dve", lambda e: e.reciprocal(dst, dst), reads=[trk], writes=[trk])

        rope_lag = Lag()
        kb.hooks.append(rope_lag.flush)

        rctr = [0]

        def rope_evac(bank, n, t0, dst, T_dst, rb, qraw, T_qraw, t1, T_t1, t2, T_t2):
            k = rctr[0] % 2
            rctr[0] += 1
            rb = 4 + k
            qraw, T_qraw, t1, T_t1, t2, T_t2 = qraw[k], T_qraw[k], t1[k], T_t1[k], t2[k], T_t2[k]
            op("act", lambda e: e.copy(qraw[:, 0:n], PB[bank][:, 0:n]), reads=[TPB[bank]], writes=[T_qraw])
            rope_lag.push(lambda: rope_rest(n, t0, dst, T_dst, rb, qraw, T_qraw, t1, T_t1, t2, T_t2))

        def rope_rest(n, t0, dst, T_dst, rb, qraw, T_qraw, t1, T_t1, t2, T_t2):
            op("pe", lambda e: e.matmul(PB[rb][:, 0:n], rmat[:], qraw[:, 0:n], start=True, stop=True),
               reads=[T_qraw, T_constp], writes=[TPB[rb]])
            op("dve", lambda e: e.tensor_tensor(t1[:, 0:n], qraw[:, 0:n], ropeT[:, 0, t0:t0 + n], ALU.mult),
               reads=[T_qraw, T_const], writes=[T_t1])
            op("dve", lambda e: e.tensor_tensor(t2[:, 0:n], PB[rb][:, 0:n], ropeT[:, 1, t0:t0 + n], ALU.mult),
               reads=[TPB[rb], T_const], writes=[T_t2])
            if not isinstance(dst, list):
                dst = [(dst, 0, 128)]
            for (dap, p0, p1) in dst:
                op("pool", lambda e: e.tensor_tensor(dap, t1[p0:p1, 0:n], t2[p0:p1, 0:n], ALU.add),
                   reads=[T_t1, T_t2], writes=[T_dst])

        def build_gate_bc(l, r, slot, gbc, diagf):
            for cc in range(8):
                dg = diagf[cc % 2]
                op("dve", lambda e: e.tensor_scalar(dg[:], ident[:], mod_ap(l, 16 + cc, r), None, ALU.mult),
                   reads=[T_const, T_mod], writes=[T_diagf[cc % 2]])
                op("pe", lambda e: e.matmul(PB[7][:, (cc % 4) * 128:(cc % 4 + 1) * 128], ones_f[:], dg[:],
                                            start=True, stop=True),
                   reads=[T_c2, T_diagf[cc % 2]], writes=[TPB[7]])
                if cc % 4 == 3:
                    h = cc // 4
                    op("act", lambda e: e.copy(gbc[:, slot, h * 512:(h + 1) * 512], PB[7][:, :]),
                       reads=[TPB[7]], writes=[T_gbc])

        def sload_dep(l, b, i):
            idx = sctr[0] % NS
            sctr[0] += 1
            rd = [T_h1[b][i]] if l == 1 else []
            kb.dma("sp", srct[idx][:], tile_rows(l, b, i), d_s[idx], reads=rd, writes=[T_s[idx]])
            return srct[idx], T_s[idx]

        def phase_transposes(l, b):
            cvt = Carver()
            for k in range(NS):
                srct[k] = cvt.get(D)
            for i in range(18):
                st, ts = sload_dep(l, b, i)
                r = 2 if i < 2 else b
                for half in range(2):
                    bank = (2 * i + half) % 4
                    for q in range(4):
                        j = half * 4 + q
                        op("pe", lambda e: e.transpose(PB[bank][:, q * 128:(q + 1) * 128], st[:, j * 128:(j + 1) * 128], ident[:]),
                           reads=[ts, T_const], writes=[TPB[bank]])
                    for q in range(4):
                        j = half * 4 + q
                        op("dve", lambda e: e.tensor_scalar(uT[:, j, i * 128:(i + 1) * 128], PB[bank][:, q * 128:(q + 1) * 128],
                                                           mod_ap(l, 8 + j, r), mod_ap(l, j, r), ALU.mult, ALU.add),
                           reads=[TPB[bank], T_mod], writes=[T_uT[i]])

        def phase_outproj(l, b, wout_d, tiles):
            kb.barrier()
            cv = Carver()
            for k in range(NS):
                srct[k] = cv.get(D)
            for k in range(NO):
                outt[k] = cv.get(D)
            zt = [cv.get(D) for _ in range(2)]
            T_z = [Trk(), Trk()]
            stats = cv.get(64)
            T_st = Trk()
            lngb = cv.get(2 * D).rearrange("p (a d) -> p a d", a=2)
            gbc = cv.get(2 * D).rearrange("p (a d) -> p a d", a=2)
            diagf = [cv.get(128) for _ in range(2)]
            kb.dma("sp", lngb[:, 0, :], lng_d[l:l + 1, :].partition_broadcast(128), d_lngb, writes=[T_lngb])
            kb.dma("sp", lngb[:, 1, :], lnb_d[l:l + 1, :].partition_broadcast(128), d_lngb, writes=[T_lngb])
            build_gate_bc(l, b, 0, gbc, diagf)
            if l == 0:
                build_gate_bc(l, 2, 1, gbc, diagf)
            w0, tw0 = wload(wout_d[:, 0:512], 512)
            w1, tw1 = wload(wout_d[:, 512:1024], 512)
            for n_i, i in enumerate(tiles):
                st, ts = sload_dep(l, b, i)
                gs = 1 if i < 2 else 0
                pb0 = (n_i % 2) * 2
                for half, (ws, tw) in enumerate(((w0, tw0), (w1, tw1))):
                    for j in range(8):
                        op("pe", lambda e: e.matmul(PB[pb0 + half][:, :], mixT[:, j, i * 128:(i + 1) * 128], ws[:, j, :],
                                                    start=(j == 0), stop=(j == 7)),
                           reads=[T_mix, tw], writes=[TPB[pb0 + half]])
                z = zt[n_i % 2]
                tz = T_z[n_i % 2]
                for half in range(2):
                    hs = slice(half * 512, (half + 1) * 512)
                    op("dve", lambda e: e.tensor_tensor(z[:, hs], PB[pb0 + half][:, :], gbc[:, gs, hs], ALU.mult),
                       reads=[TPB[pb0 + half], T_gbc], writes=[tz])
                    op("dve", lambda e: e.scalar_tensor_tensor(z[:, hs], st[:, hs], ALPHA, z[:, hs], ALU.mult, ALU.add),
                       reads=[ts, tz], writes=[tz])
                    op("dve", lambda e: e.bn_stats(stats[:, half * 6:(half + 1) * 6], z[:, hs]), reads=[tz], writes=[T_st])
                op("dve", lambda e: e.bn_aggr(stats[:, 16:18], stats[:, 0:12].rearrange("p (a b) -> p a b", a=2)),
                   reads=[T_st], writes=[T_st])
                rsqrt_small(stats[:, 18:19], stats[:, 17:18], epsln, T_st)
                op("dve", lambda e: e.scalar_tensor_tensor(stats[:, 19:20], stats[:, 16:17], -1.0, stats[:, 18:19],
                                                          ALU.mult, ALU.mult), reads=[T_st], writes=[T_st])
                oi = octr[0] % NO
                octr[0] += 1
                ot, to = outt[oi], T_o[oi]
                op("act", lambda e: e.activation(z[:, :], z[:, :], AF.Identity, bias=stats[:, 19:20], scale=stats[:, 18:19]),
                   reads=[tz, T_st], writes=[tz])
                op("pool", lambda e: e.tensor_tensor(z[:, :], z[:, :], lngb[:, 0, :], ALU.mult), reads=[tz, T_lngb], writes=[tz])
                op("pool", lambda e: e.tensor_tensor(ot[:, :], z[:, :], lngb[:, 1, :], ALU.add), reads=[tz, T_lngb], writes=[to])
                if l == 0:
                    kb.dma("sp", h1_d[b, i * 128:(i + 1) * 128, :], ot[:, :], d_o[oi], reads=[to], writes=[T_h1[b][i]])
                else:
                    kb.dma("sp", out_d[b, (i - 2) * 128:(i - 1) * 128, :], ot[:, :], d_o[oi], reads=[to])

        def gelu2(dst, T_dst, bank, n, sq, T_sq, tt, T_tt):
            op("act", lambda e: e.activation(sq[:, 0:n], PB[bank][:, 0:n], AF.Square), reads=[TPB[bank]], writes=[T_sq])
            op("dve", lambda e: e.tensor_scalar(sq[:, 0:n], sq[:, 0:n], GC1, 1.0, ALU.mult, ALU.add), reads=[T_sq], writes=[T_sq])
            op("dve", lambda e: e.tensor_tensor(sq[:, 0:n], sq[:, 0:n], PB[bank][:, 0:n], ALU.mult),
               reads=[T_sq, TPB[bank]], writes=[T_sq])
            op("act", lambda e: e.activation(tt[:, 0:n], sq[:, 0:n], AF.Tanh, scale=GC0), reads=[T_sq], writes=[T_tt])
            return op("dve", lambda e: e.scalar_tensor_tensor(dst, tt[:, 0:n], 1.0, PB[bank][:, 0:n], ALU.add, ALU.mult),
                      reads=[T_tt, TPB[bank]], writes=[T_dst])

        def pass_layer0(b):
            l = 0
            kb.barrier()
            phase_transposes(l, b)
            chk(1)
            kb.barrier()
            cv = Carver()
            vn = cv.get(18 * 512, BF16)
            vn3 = vn.rearrange("p (i c) -> p i c", i=18)
            T_vn = [Trk() for _ in range(18)]
            Gu = cv.get(NT)
            T_Gu = Trk()
            sq = [cv.get(512) for _ in range(2)]
            T_sq = [Trk(), Trk()]
            tt = [cv.get(512) for _ in range(2)]
            T_tt = [Trk(), Trk()]
            g2 = [cv.get(512) for _ in range(2)]
            T_g2 = [Trk(), Trk()]
            stats = cv.get(64)
            T_st = Trk()
            bsrep = cv.get(4 * 512).rearrange("p (g q) -> p g q", g=4)
            angb = cv.get(2 * 512).rearrange("p (a q) -> p a q", a=2)
            for rep in range(4):
                kb.dma("sp", bsrep[:, :, rep * 128:(rep + 1) * 128],
                       abs_d.rearrange("o (g q) -> o g q", g=4).partition_broadcast(128), d_ab, writes=[T_ab])
            kb.dma("sp", angb[:, 0, :], ang_d.partition_broadcast(128), d_ab, writes=[T_ab])
            kb.dma("sp", angb[:, 1, :], anb_d.partition_broadcast(128), d_ab, writes=[T_ab])
            wv, twv = wload(abwin_d[:, 512:1024], 512)
            wu, twu = wload(abwin_d[:, 0:512], 512)
            wg, twg = wload(abwin_d[:, 1024:1536], 512)
            for i in range(18):
                bank = i % 4
                k2 = i % 2
                proj_tm(wv, twv, 0, 512, i, bank)
                gelu2(g2[k2][:, :], T_g2[k2], bank, 512, sq[k2], T_sq[k2], tt[k2], T_tt[k2])
                op("dve", lambda e: e.bn_stats(stats[:, 0:6], g2[k2][:, :]), reads=[T_g2[k2]], writes=[T_st])
                op("dve", lambda e: e.bn_aggr(stats[:, 16:18], stats[:, 0:6]), reads=[T_st], writes=[T_st])
                rsqrt_small(stats[:, 18:19], stats[:, 17:18], epsln4, T_st)
                op("dve", lambda e: e.tensor_scalar(g2[k2][:, :], g2[k2][:, :], stats[:, 16:17], stats[:, 18:19],
                                                   ALU.subtract, ALU.mult), reads=[T_g2[k2], T_st], writes=[T_g2[k2]])
                op("pool", lambda e: e.tensor_tensor(g2[k2][:, :], g2[k2][:, :], angb[:, 0, :], ALU.mult),
                   reads=[T_g2[k2], T_ab], writes=[T_g2[k2]])
                op("pool", lambda e: e.tensor_tensor(vn3[:, i, :], g2[k2][:, :], angb[:, 1, :], ALU.add),
                   reads=[T_g2[k2], T_ab], writes=[T_vn[i]])
            for g in range(4):
                for si, (t0, n) in enumerate(SEGS):
                    bank = si % 4
                    k2 = si % 2
                    proj_fm(wu, twu, g * 128, t0, n, bank)
                    gelu2(Gu[:, t0:t0 + n], T_Gu, bank, n, sq[k2], T_sq[k2], tt[k2], T_tt[k2])
                for si, (t0, n) in enumerate(SEGS):
                    bank = si % 4
                    k2 = si % 2
                    proj_fm(wg, twg, g * 128, t0, n, bank)
                    op("act", lambda e: e.activation(tt[k2][:, 0:n], PB[bank][:, 0:n], AF.Tanh, scale=0.5),
                       reads=[TPB[bank]], writes=[T_tt[k2]])
                    op("dve", lambda e: e.scalar_tensor_tensor(tt[k2][:, 0:n], tt[k2][:, 0:n], 1.0, PB[bank][:, 0:n], ALU.add, ALU.mult),
                       reads=[T_tt[k2], TPB[bank]], writes=[T_tt[k2]])
                    op("dve", lambda e: e.scalar_tensor_tensor(Gu[:, t0:t0 + n], tt[k2][:, 0:n], 0.25, Gu[:, t0:t0 + n], ALU.mult, ALU.mult),
                       reads=[T_tt[k2], T_Gu], writes=[T_Gu])
                    mb = 4 + si % 2
                    tl = seg_tiles(t0, n)
                    for ci, c in enumerate(tl):
                        op("pe", lambda e: e.matmul(PB[mb][:, ci * 128:(ci + 1) * 128], vn3[:, c, g * 128:(g + 1) * 128], wsT[:, g, :],
                                                    start=True, stop=True),
                           reads=[T_vn[c], T_constp], writes=[TPB[mb]])
                    op("dve", lambda e: e.tensor_tensor(sq[k2][:, 0:n], PB[mb][:, 0:n], bsrep[:, g, 0:n], ALU.add),
                       reads=[TPB[mb], T_ab], writes=[T_sq[k2]])
                    op("pool", lambda e: e.tensor_tensor(mixT[:, g, t0:t0 + n], sq[k2][:, 0:n], Gu[:, t0:t0 + n], ALU.mult),
                       reads=[T_sq[k2], T_Gu], writes=[T_mix])
            chk(2)
            kb.barrier()
            cv = Carver()
            V1 = cv.get(18 * 4 * 144, BF16)
            V14 = V1.rearrange("p (i h c) -> p i h c", i=18, h=4)
            T_V = [Trk() for _ in range(18)]
            qT0 = cv.get(NT, BF16)
            qT1 = cv.get(NT, BF16)
            qTc = [qT0, qT1]
            kT = cv.get(NT, BF16)
            T_qT, T_kT = Trk(), Trk()
            op("pool", lambda e: e.memset(qT0[:, :], 0.0), writes=[T_qT])
            op("pool", lambda e: e.memset(qT1[:, :], 0.0), writes=[T_qT])
            sbg = cv.get(NT)
            T_sbg = Trk()
            qraw = [cv.get(512, BF16) for _ in range(2)]
            T_qraw = [Trk(), Trk()]
            t1 = [cv.get(512) for _ in range(2)]
            T_t1 = [Trk(), Trk()]
            t2 = [cv.get(512) for _ in range(2)]
            T_t2 = [Trk(), Trk()]
            pT = [cv.get(512, BF16) for _ in range(4)]
            T_pT = [Trk() for _ in range(4)]
            o1n = [cv.get(128) for _ in range(2)]
            T_o1n = [Trk(), Trk()]
            oc_all = cv.get(18 * 128)
            oc3 = oc_all.rearrange("p (i c) -> p i c", i=18)
            T_oca = [Trk() for _ in range(18)]
            mv_all = cv.get(64)
            mv3 = mv_all[:, 0:36].rearrange("p (i c) -> p i c", i=18)
            msr = cv.get(64)
            T_mv = Trk()
            osc = [cv.get(128) for _ in range(2)]
            T_osc = [Trk(), Trk()]
            TQ3 = [Trk() for _ in range(4)]
            stats = cv.get(64)
            T_st = Trk()
            op("dve", lambda e: e.memset(V1[:, :], 1.0), writes=T_V)
            wv, twv = wload(abwin_d[:, 2560:3072], 512)
            wq, twq = wload(abwin_d[:, 1536:2048], 512)
            wk, twk = wload(abwin_d[:, 2048:2560], 512)
            for i in range(18):
                bank = i % 3
                proj_tm(wv, twv, 0, 512, i, bank)
                op("act", lambda e: e.copy(V14[:, i, :, 0:128], PB[bank][:, :].rearrange("p (h c) -> p h c", h=4)),
                   reads=[TPB[bank]], writes=[T_V[i]])
            chk(2.2)
            wg, twg = wload(abwin_d[:, 3072:3584], 512)
            pctr = [0]
            for h in range(4):
                for si, (t0, n) in enumerate(SEGS):
                    k2 = si % 2
                    proj_fm(wq, twq, h * 128, t0, n, si % 3)
                    rope_evac(si % 3, n, t0, [(qT0[0:64, t0:t0 + n], 0, 64), (qT1[64:128, t0:t0 + n], 64, 128)], T_qT, 4 + k2, qraw, T_qraw, t1, T_t1, t2, T_t2)
                for si, (t0, n) in enumerate(SEGS):
                    k2 = si % 2
                    proj_fm(wk, twk, h * 128, t0, n, si % 3)
                    rope_evac(si % 3, n, t0, kT[:, t0:t0 + n], T_kT, 4 + k2, qraw, T_qraw, t1, T_t1, t2, T_t2)
                rope_lag.flush()
                for si, (t0, n) in enumerate(SEGS):
                    k2 = si % 2
                    bank = si % 3
                    proj_fm(wg, twg, h * 128, t0, n, bank)
                    op("act", lambda e: e.activation(t1[k2][:, 0:n], PB[bank][:, 0:n], AF.Tanh, scale=0.5),
                       reads=[TPB[bank]], writes=[T_t1[k2]])
                    op("dve", lambda e: e.scalar_tensor_tensor(sbg[:, t0:t0 + n], t1[k2][:, 0:n], 1.0, PB[bank][:, 0:n], ALU.add, ALU.mult),
                       reads=[T_t1[k2], TPB[bank]], writes=[T_sbg])
                chk(2.4)
                qtiles = [(0, [0, 1])] + [(256 + 256 * qi, list(range(18))) for qi in range(8)]
                for qn, (q0, kts) in enumerate(qtiles):
                    ob = 4 + 2 * (qn % 2)

                    def qk(kt, sbank):
                        for c in range(2):
                            op("pe", lambda e: e.matmul(PB[sbank][:, c * 256:(c + 1) * 256],
                                                        kT[:, kt * 128:(kt + 1) * 128],
                                                        qTc[c][:, q0:q0 + 256], start=True, stop=True),
                               reads=[T_kT, T_qT], writes=[TPB[sbank]])

                    qk(kts[0], 0)
                    for ki, kt in enumerate(kts):
                        sbank = ki % 3
                        if ki + 1 < len(kts):
                            qk(kts[ki + 1], (ki + 1) % 3)
                        pi = pctr[0] % 4
                        pctr[0] += 1
                        op("act", lambda e: e.activation(pT[pi][:, :], PB[sbank][:, :], AF.Exp, scale=0.125),
                           reads=[TPB[sbank]], writes=[T_pT[pi]])
                        chk(2.5)
                        for c in range(2):
                            for s in range(2):
                                first = (ki == 0 and s == 0)
                                op("pe", lambda e: e.matmul(PB[ob + c][:, s * 144:s * 144 + 130],
                                                            pT[pi][:, c * 256 + s * 128:c * 256 + (s + 1) * 128],
                                                            V14[:, kt, h, 0:130], start=first, stop=(ki == len(kts) - 1),
                                                            skip_group_check=True),
                                   reads=[T_pT[pi], T_V[kt]], writes=[TPB[ob + c]])
                    chk(2.6)
                    for s in range(2):
                        k2 = s
                        idx = (q0 + s * 128) // 128
                        op("dve", lambda e: e.reciprocal(stats[:, 0:1], PB[ob][:, s * 144 + 128:s * 144 + 129]),
                           reads=[TPB[ob]], writes=[T_st])
                        op("dve", lambda e: e.reciprocal(stats[:, 1:2], PB[ob + 1][:, s * 144 + 128:s * 144 + 129]),
                           reads=[TPB[ob + 1]], writes=[T_st])
                        op("dve", lambda e: e.tensor_scalar(o1n[k2][:, :], PB[ob + 1][:, s * 144:s * 144 + 128], stats[:, 1:2], neglam,
                                                           ALU.mult, ALU.mult), reads=[TPB[ob + 1], T_st, T_c2], writes=[T_o1n[k2]])
                        op("dve", lambda e: e.scalar_tensor_tensor(oc3[:, idx, :], PB[ob][:, s * 144:s * 144 + 128], stats[:, 0:1], o1n[k2][:, :],
                                                                  ALU.mult, ALU.add), reads=[TPB[ob], T_st, T_o1n[k2]], writes=[T_oca[idx]])
                        op("dve", lambda e: e.bn_stats(stats[:, 8:14], oc3[:, idx, :]), reads=[T_oca[idx]], writes=[T_st])
                        op("dve", lambda e: e.bn_aggr(mv3[:, idx, :], stats[:, 8:14]), reads=[T_st], writes=[T_mv])
                op("dve", lambda e: e.tensor_tensor(msr[:, 0:18], mv3[:, :, 0], mv3[:, :, 0], ALU.mult), reads=[T_mv], writes=[T_mv])
                op("dve", lambda e: e.tensor_tensor(msr[:, 0:18], msr[:, 0:18], mv3[:, :, 1], ALU.add), reads=[T_mv], writes=[T_mv])
                rsqrt_small(msr[:, 32:50], msr[:, 0:18], epsrms, T_mv)
                for idx in range(18):
                    k2 = idx % 2
                    qd = idx % 4
                    op("dve", lambda e: e.scalar_tensor_tensor(osc[k2][:, :], oc3[:, idx, :], msr[:, 32 + idx:33 + idx], subg[:, :],
                                                              ALU.mult, ALU.mult), reads=[T_oca[idx], T_mv, T_c2], writes=[T_osc[k2]])
                    op("pe", lambda e: e.transpose(PB[3][:, qd * 128:(qd + 1) * 128], osc[k2][:, :], ident[:]),
                       reads=[T_osc[k2], T_const], writes=[TQ3[qd]])
                    op("dve", lambda e: e.tensor_tensor(mixT[:, 4 + h, idx * 128:(idx + 1) * 128], PB[3][:, qd * 128:(qd + 1) * 128],
                                                       sbg[:, idx * 128:(idx + 1) * 128], ALU.mult),
                       reads=[TQ3[qd], T_sbg], writes=[T_mix])
            chk(3)
            phase_outproj(l, b, abwout_d, list(range(18)))
            chk(4)

        def pass_layer1(b):
            l = 1
            kb.barrier()
            phase_transposes(l, b)
            kb.barrier()
            cv = Carver()
            yT = cv.get(4 * NLAT)
            yT3 = yT.rearrange("p (j t) -> p j t", j=4)
            T_yT = [Trk() for _ in range(4)]
            hpad = [cv.get(NLAT + 32, BF16) for _ in range(2)]
            T_hp = [Trk(), Trk()]
            caS = [cv.get(512) for _ in range(2)]
            T_ca = [Trk(), Trk()]
            tt = [cv.get(512) for _ in range(2)]
            T_tt = [Trk(), Trk()]
            diag = cv.get(31 * 128, BF16)
            diag3 = diag.rearrange("p (k c) -> p k c", k=31)
            T_dg = Trk()
            wa, twa = wload(cdwin_d[:, 0:512], 512)
            wb, twb = wload(cdwin_d[:, 512:1024], 512)
            for k2 in range(2):
                op("dve", lambda e: e.memset(hpad[k2][:, :], 0.0), writes=[T_hp[k2]])
            for j in range(4):
                hp = hpad[j % 2]
                thp = T_hp[j % 2]
                for k in range(31):
                    op("dve", lambda e: e.tensor_scalar(diag3[:, k, :], identb[:, :], dwT[:, j * 31 + k:j * 31 + k + 1], None, ALU.mult),
                       reads=[T_c2], writes=[T_dg])
                for si, (t0, n) in enumerate(LSEGS):
                    k2 = si % 2
                    lt0 = t0 - NCTX
                    proj_fm(wa, twa, j * 128, t0, n, si % 2)
                    op("act", lambda e: e.copy(caS[k2][:, :], PB[si % 2][:, :]), reads=[TPB[si % 2]], writes=[T_ca[k2]])
                    proj_fm(wb, twb, j * 128, t0, n, 2 + si % 2)
                    op("act", lambda e: e.activation(tt[k2][:, :], PB[2 + si % 2][:, :], AF.Tanh, scale=0.5),
                       reads=[TPB[2 + si % 2]], writes=[T_tt[k2]])
                    op("dve", lambda e: e.scalar_tensor_tensor(hp[:, 15 + lt0:15 + lt0 + n], tt[k2][:, :], 1.0, caS[k2][:, :], ALU.add, ALU.mult),
                       reads=[T_tt[k2], T_ca[k2]], writes=[thp])
                for si, (t0, n) in enumerate(LSEGS):
                    lt0 = t0 - NCTX
                    cb = 4 + si % 2
                    for k in range(31):
                        op("pe", lambda e: e.matmul(PB[cb][:, :], diag3[:, k, :], hp[:, lt0 + k:lt0 + k + 512],
                                                    start=(k == 0), stop=(k == 30)),
                           reads=[T_dg, thp], writes=[TPB[cb]])
                    op("act", lambda e: e.activation(yT3[:, j, lt0:lt0 + 512], PB[cb][:, :], AF.Identity, bias=cvecs[:, j:j + 1]),
                       reads=[TPB[cb], T_const], writes=[T_yT[j]])
            kb.barrier()
            cv2 = Carver()
            cv2.off = 4 * NLAT * 4
            mean_bc = cv2.get(NLAT)
            rstd_bc = cv2.get(NLAT)
            T_mr = Trk()
            ysq = [cv2.get(512) for _ in range(2)]
            T_ysq = [Trk(), Trk()]
            tt = [cv2.get(512) for _ in range(2)]
            T_tt = [Trk(), Trk()]
            sg = [cv2.get(512) for _ in range(2)]
            T_sg = [Trk(), Trk()]
            yn = [cv2.get(512) for _ in range(2)]
            T_yn = [Trk(), Trk()]
            for si, (t0, n) in enumerate(LSEGS):
                lt0 = t0 - NCTX
                ts_ = slice(lt0, lt0 + 512)
                mbk, sbk = 0 + 2 * (si % 2), 1 + 2 * (si % 2)
                for j in range(4):
                    op("pe", lambda e: e.matmul(PB[mbk][:, :], onesdiv[:, :], yT3[:, j, ts_], start=(j == 0), stop=(j == 3)),
                       reads=[T_c2, T_yT[j]], writes=[TPB[mbk]])
                for j in range(4):
                    k2 = j % 2
                    op("act", lambda e: e.activation(ysq[k2][:, :], yT3[:, j, ts_], AF.Square), reads=[T_yT[j]], writes=[T_ysq[k2]])
                    op("pe", lambda e: e.matmul(PB[sbk][:, :], onesdiv[:, :], ysq[k2][:, :], start=(j == 0), stop=(j == 3)),
                       reads=[T_c2, T_ysq[k2]], writes=[TPB[sbk]])
                op("act", lambda e: e.copy(mean_bc[:, ts_], PB[mbk][:, :]), reads=[TPB[mbk]], writes=[T_mr])
                op("dve", lambda e: e.tensor_tensor(rstd_bc[:, ts_], mean_bc[:, ts_], mean_bc[:, ts_], ALU.mult), reads=[T_mr], writes=[T_mr])
                op("dve", lambda e: e.tensor_tensor(rstd_bc[:, ts_], PB[sbk][:, :], rstd_bc[:, ts_], ALU.subtract),
                   reads=[TPB[sbk], T_mr], writes=[T_mr])
                op("act", lambda e: e.activation(rstd_bc[:, ts_], rstd_bc[:, ts_], AF.Sqrt, bias=epsln), reads=[T_mr, T_c2], writes=[T_mr])
                op("dve", lambda e: e.reciprocal(rstd_bc[:, ts_], rstd_bc[:, ts_]), reads=[T_mr], writes=[T_mr])
            wg, twg = wload(cdwin_d[:, 1024:1536], 512)
            cnt = 0
            for j in range(4):
                for si, (t0, n) in enumerate(LSEGS):
                    lt0 = t0 - NCTX
                    ts_ = slice(lt0, lt0 + 512)
                    k2 = cnt % 2
                    bank = 4 + cnt % 4
                    cnt += 1
                    proj_fm(wg, twg, j * 128, t0, n, bank)
                    op("act", lambda e: e.activation(tt[k2][:, :], PB[bank][:, :], AF.Tanh, scale=0.5), reads=[TPB[bank]], writes=[T_tt[k2]])
                    op("dve", lambda e: e.scalar_tensor_tensor(sg[k2][:, :], tt[k2][:, :], 1.0, PB[bank][:, :], ALU.add, ALU.mult),
                       reads=[T_tt[k2], TPB[bank]], writes=[T_sg[k2]])
                    op("pool", lambda e: e.tensor_tensor(yn[k2][:, :], yT3[:, j, ts_], mean_bc[:, ts_], ALU.subtract),
                       reads=[T_yT[j], T_mr], writes=[T_yn[k2]])
                    op("pool", lambda e: e.tensor_tensor(yn[k2][:, :], yn[k2][:, :], rstd_bc[:, ts_], ALU.mult),
                       reads=[T_yn[k2], T_mr], writes=[T_yn[k2]])
                    op("dve", lambda e: e.tensor_scalar(yn[k2][:, :], yn[k2][:, :], cvecs[:, 4 + j:5 + j], cvecs[:, 8 + j:9 + j], ALU.mult, ALU.add),
                       reads=[T_yn[k2], T_const], writes=[T_yn[k2]])
                    op("act", lambda e: e.activation(tt[k2][:, :], yn[k2][:, :], AF.Tanh, scale=0.5), reads=[T_yn[k2], T_sg[k2]], writes=[T_tt[k2]])
                    op("dve", lambda e: e.scalar_tensor_tensor(yn[k2][:, :], tt[k2][:, :], 1.0, yn[k2][:, :], ALU.add, ALU.mult),
                       reads=[T_tt[k2], T_yn[k2]], writes=[T_yn[k2]])
                    op("dve", lambda e: e.scalar_tensor_tensor(mixT[:, j, t0:t0 + n], yn[k2][:, :], 0.25, sg[k2][:, :], ALU.mult, ALU.mult),
                       reads=[T_yn[k2], T_sg[k2]], writes=[T_mix])
            kb.barrier()
            cv = Carver()
            V1 = cv.get(18 * 2 * 80, BF16)
            V14 = V1.rearrange("p (i h c) -> p i h c", i=18, h=2)
            T_V = [Trk() for _ in range(18)]
            kT2 = cv.get(2 * NT, BF16)
            kT23 = kT2.rearrange("p (h t) -> p h t", h=2)
            T_kT = Trk()
            qT0 = cv.get(NLAT, BF16)
            qT1 = cv.get(NLAT, BF16)
            qTc = [qT0, qT1]
            T_qT = Trk()
            op("pool", lambda e: e.memset(qT0[:, :], 0.0), writes=[T_qT])
            op("pool", lambda e: e.memset(qT1[:, :], 0.0), writes=[T_qT])
            sdg = cv.get(NLAT)
            T_sdg = Trk()
            wk2 = cv.get(8 * 2 * 128, BF16)
            wk24 = wk2.rearrange("p (j h c) -> p j h c", j=8, h=2)
            T_wk2 = Trk()
            d_wk2 = kb.dsem()
            qraw = [cv.get(512, BF16) for _ in range(2)]
            T_qraw = [Trk(), Trk()]
            t1 = [cv.get(512) for _ in range(2)]
            T_t1 = [Trk(), Trk()]
            t2 = [cv.get(512) for _ in range(2)]
            T_t2 = [Trk(), Trk()]
            pT = [cv.get(256, BF16) for _ in range(4)]
            T_pT = [Trk() for _ in range(4)]
            ocomb = [cv.get(128) for _ in range(2)]
            T_oc = [Trk(), Trk()]
            stats = cv.get(64)
            T_st = Trk()
            op("dve", lambda e: e.memset(V1[:, :], 1.0), writes=T_V)
            for kvh in range(2):
                for dup in range(2):
                    kb.dma("pool", wk24[:, :, kvh, dup * 64:(dup + 1) * 64],
                           cdwin_d[:, 2048 + kvh * 64:2048 + (kvh + 1) * 64].rearrange("(j p) c -> p j c", p=128),
                           d_wk2, writes=[T_wk2])
            wq, twq = wload(cdwin_d[:, 1536:2048], 512)
            wkv, twkv = wload(cdwin_d[:, 2048:2560], 512)
            wg2, twg2 = wload(cdwin_d[:, 2560:2816], 256)
            for i in range(18):
                bank = i % 4
                proj_tm(wkv, twkv, 128, 128, i, bank)
                op("act", lambda e: e.copy(V14[:, i, :, 0:64], PB[bank][:, 0:128].rearrange("p (h c) -> p h c", h=2)),
                   reads=[TPB[bank]], writes=[T_V[i]])
            for kvh in range(2):
                for si, (t0, n) in enumerate(SEGS):
                    k2 = si % 2
                    bank = si % 4
                    for j in range(8):
                        op("pe", lambda e: e.matmul(PB[bank][:, 0:n], wk24[:, j, kvh, :], uT[:, j, t0:t0 + n], start=(j == 0), stop=(j == 7)),
                           reads=[T_wk2] + [T_uT[i] for i in seg_tiles(t0, n)], writes=[TPB[bank]])
                    rope_evac(bank, n, t0, kT23[:, kvh, t0:t0 + n], T_kT, 4 + k2, qraw, T_qraw, t1, T_t1, t2, T_t2)
            rope_lag.flush()
            pctr = [0]
            att_lag = Lag()
            for cc in range(4):
                kvh = cc // 2
                for si, (t0, n) in enumerate(LSEGS):
                    k2 = si % 2
                    lt0 = t0 - NCTX
                    proj_fm(wq, twq, cc * 128, t0, n, si % 4)
                    rope_evac(si % 4, n, t0, [(qT0[0:64, lt0:lt0 + n], 0, 64), (qT1[64:128, lt0:lt0 + n], 64, 128)], T_qT, 4 + k2, qraw, T_qraw, t1, T_t1, t2, T_t2)
                rope_lag.flush()
                for si, (t0, n) in enumerate(LSEGS):
                    k2 = si % 2
                    bank = si % 4
                    lt0 = t0 - NCTX
                    if cc < 2:
                        proj_fm(wkv, twkv, 256 + cc * 128, t0, n, bank)
                    else:
                        proj_fm(wg2, twg2, (cc - 2) * 128, t0, n, bank)
                    op("act", lambda e: e.activation(t1[k2][:, 0:n], PB[bank][:, 0:n], AF.Tanh, scale=0.5),
                       reads=[TPB[bank]], writes=[T_t1[k2]])
                    op("dve", lambda e: e.scalar_tensor_tensor(sdg[:, lt0:lt0 + n], t1[k2][:, 0:n], 1.0, PB[bank][:, 0:n], ALU.add, ALU.mult),
                       reads=[T_t1[k2], TPB[bank]], writes=[T_sdg])
                for qb in range(16):
                    q0 = qb * 128
                    kts = []
                    if qb > 0:
                        kts.append((2 + qb - 1, 0))
                    kts.append((2 + qb, None))
                    if qb < 15:
                        kts.append((2 + qb + 1, 1))
                    kts += [(0, None), (1, None)]
                    ob = 4 + qb % 2
                    for ki, (kt, mk) in enumerate(kts):
                        sbank = ki % 4
                        for hl in range(2):
                            op("pe", lambda e: e.matmul(PB[sbank][:, hl * 128:(hl + 1) * 128],
                                                        kT23[:, kvh, kt * 128:(kt + 1) * 128],
                                                        qTc[hl][:, q0:q0 + 128], start=True, stop=True),
                               reads=[T_kT, T_qT], writes=[TPB[sbank]])
                        pi = pctr[0] % 4
                        pctr[0] += 1
                        op("act", lambda e: e.activation(pT[pi][:, :], PB[sbank][:, 0:256], AF.Exp, scale=0.125),
                           reads=[TPB[sbank]], writes=[T_pT[pi]])
                        if mk is not None:
                            op("pool", lambda e: e.tensor_tensor(pT[pi][:, :], pT[pi][:, :], masks[:, mk, :], ALU.mult),
                               reads=[T_pT[pi], T_constp], writes=[T_pT[pi]])
                        for hl in range(2):
                            first = (ki == 0 and hl == 0)
                            op("pe", lambda e: e.matmul(PB[ob][:, hl * 80:hl * 80 + 66], pT[pi][:, hl * 128:(hl + 1) * 128],
                                                        V14[:, kt, kvh, 0:66], start=first, stop=(ki == len(kts) - 1),
                                                        skip_group_check=True),
                               reads=[T_pT[pi], T_V[kt]], writes=[TPB[ob]])
                    k2 = qb % 2
                    for hl in range(2):
                        hd = cc * 2 + hl
                        op("dve", lambda e: e.tensor_scalar(stats[:, hl:hl + 1], PB[ob][:, hl * 80 + 64:hl * 80 + 65], sinkt[:, hd:hd + 1], None, ALU.add),
                           reads=[TPB[ob], T_c2], writes=[T_st])
                        op("dve", lambda e: e.reciprocal(stats[:, 2 + hl:3 + hl], stats[:, hl:hl + 1]), reads=[T_st], writes=[T_st])
                        op("dve", lambda e: e.tensor_scalar(ocomb[k2][:, hl * 64:(hl + 1) * 64], PB[ob][:, hl * 80:hl * 80 + 64],
                                                           stats[:, 2 + hl:3 + hl], None, ALU.mult),
                           reads=[TPB[ob], T_st], writes=[T_oc[k2]])
                    def post_b(cc=cc, qb=qb, q0=q0, k2=k2):
                        tb = 6 + qb % 2
                        op("pe", lambda e: e.transpose(PB[tb][:, 0:128], ocomb[k2][:, :], ident[:]), reads=[T_oc[k2], T_const], writes=[TPB[tb]])
                        op("dve", lambda e: e.scalar_tensor_tensor(mixT[:, 4 + cc, NCTX + q0:NCTX + q0 + 128], PB[tb][:, 0:128], 0.5,
                                                                  sdg[:, q0:q0 + 128], ALU.mult, ALU.mult),
                           reads=[TPB[tb], T_sdg], writes=[T_mix])
                    att_lag.push(post_b)
                att_lag.flush()
            phase_outproj(l, b, cdwout_d, list(range(2, 18)))

        try:
            chk(0)
            for b in range(2):
                pass_layer0(b)
            if not debug_h1:
                for b in range(2):
                    pass_layer1(b)
        except _Stop:
            pass
        kb.barrier()
        for ds in d_o:
            if ds.count:
                nc.sync.wait_ge(ds.sem, ds.count)
    return nc


def _consts():
    ident = np.eye(128, dtype=np.float32)
    rmat = np.zeros((128, 128), np.float32)
    for dp in range(128):
        partner = dp + 16 if (dp % 32) < 16 else dp - 16
        rmat[partner, dp] = 1.0
    m = 16
    inv = (10000.0 ** (-np.arange(m, dtype=np.float32) / m)).astype(np.float32)
    t = np.arange(NLAT)
    row = (t // 64).astype(np.float32)
    col = (t % 64).astype(np.float32)
    ang_r = (row[:, None] * inv[None, :]).astype(np.float32)
    ang_c = (col[:, None] * inv[None, :]).astype(np.float32)
    cos_t = np.ones((128, NT), np.float32)
    sin_t = np.zeros((128, NT), np.float32)
    for p in range(128):
        d = p % 64
        ang = ang_r if d < 32 else ang_c
        f = d % 16
        sign = -1.0 if (d % 32) < 16 else 1.0
        cos_t[p, NCTX:] = np.cos(ang[:, f])
        sin_t[p, NCTX:] = sign * np.sin(ang[:, f])
    ropeT = np.concatenate([cos_t, sin_t], axis=1)
    kk = np.arange(128)[:, None]
    qq = np.arange(128)[None, :]
    mp = (kk >= qq).astype(np.float32)
    mn = (kk <= qq).astype(np.float32)
    masks = np.concatenate([mp, mp, mn, mn], axis=1)
    return ident, rmat, ropeT, masks


_CACHE = {}


def _core_inputs(core, x, c, ctx, c_ctx, mod_w, mod_b, ln_g, ln_b, ab_w_in, ab_w_out, a_w_s, a_b_s,
                 a_norm_g, a_norm_b, b_lq1, b_lk1, b_lq2, b_lk2, b_subln_g, cd_w_in, cd_w_out,
                 c_dw_w, c_dw_b, c_norm_g, c_norm_b, d_sink, shared):
    b0 = 2 * core
    cvec = np.stack([c[b0], c[b0 + 1], c_ctx], axis=0)
    cvT = np.ascontiguousarray(cvec.reshape(3, 8, 128).transpose(2, 1, 0)).reshape(128, 24)
    d = dict(shared)
    d["x"] = np.ascontiguousarray(x[b0:b0 + 2])
    d["ctx"] = np.ascontiguousarray(ctx[b0:b0 + 2])
    d["cvT"] = cvT
    return d


def kernel(x, c, ctx, c_ctx, mod_w, mod_b, ln_g, ln_b, ab_w_in, ab_w_out, a_w_s, a_b_s,
           a_norm_g, a_norm_b, b_lq1, b_lk1, b_lq2, b_lk2, b_subln_g, cd_w_in, cd_w_out,
           c_dw_w, c_dw_b, c_norm_g, c_norm_b, d_sink, _debug_h1=False, _stage=99):
    f = lambda a: np.ascontiguousarray(np.asarray(a, dtype=np.float32))
    x, c, ctx, c_ctx = f(x), f(c), f(ctx), f(c_ctx)
    ident, rmat, ropeT, masks = _consts()
    shared = {
        "mod_w": f(mod_w),
        "mod_bT": np.ascontiguousarray(f(mod_b).reshape(2, 24, 128).transpose(2, 0, 1)).reshape(128, 48),
        "ln_g": f(ln_g), "ln_b": f(ln_b),
        "ab_w_in": f(ab_w_in)[0], "ab_w_out": f(ab_w_out)[0],
        "a_w_sT": np.ascontiguousarray(f(a_w_s)[0].transpose(2, 0, 1)).reshape(128, 512),
        "a_b_s": f(a_b_s)[0].reshape(1, 512),
        "a_norm_g": f(a_norm_g).reshape(1, 512), "a_norm_b": f(a_norm_b).reshape(1, 512),
        "lam_in": np.concatenate([f(b_lq1)[0], f(b_lk1)[0], f(b_lq2)[0], f(b_lk2)[0]]).reshape(1, 256),
        "b_subln_g": f(b_subln_g).reshape(1, 128),
        "cd_w_in": f(cd_w_in)[0], "cd_w_out": f(cd_w_out)[0],
        "dwT": np.ascontiguousarray(f(c_dw_w)[0].reshape(31, 4, 128).transpose(2, 1, 0)).reshape(128, 124),
        "cvecs": np.ascontiguousarray(np.stack([f(c_dw_b)[0], f(c_norm_g)[0], f(c_norm_b)[0]], 0)
                                      .reshape(3, 4, 128).transpose(2, 0, 1)).reshape(128, 12),
        "d_sink": f(d_sink).reshape(1, 8),
        "ident": ident, "rmat": rmat, "ropeT": ropeT, "masks": masks,
    }
    in_maps = []
    for core in range(8):
        in_maps.append(_core_inputs(core, x, c, ctx, c_ctx, None, None, None, None, None, None, None, None,
                                    None, None, None, None, None, None, None, None, None,
                                    None, None, None, None, None, shared))
    key = (bool(_debug_h1), _stage)
    if key not in _CACHE:
        _CACHE[key] = build_program(debug_h1=key[0], stage=_stage)
    nc = _CACHE[key]
    res = run_bass_kernel_spmd(nc, in_maps, core_ids=list(range(8)))
    if _debug_h1:
        return np.concatenate([r["h1"] for r in res.results], axis=0)
    return np.concatenate([r["out"] for r in res.results], axis=0).astype(np.float32)
```

```python
import math
import numpy as np
from contextlib import ExitStack
import concourse.bass as bass
import concourse.mybir as mybir
from concourse.bass_utils import run_bass_kernel_spmd

F32 = mybir.dt.float32
BF16 = mybir.dt.bfloat16
AF = mybir.ActivationFunctionType
ALU = mybir.AluOpType

NT = 2304
NCTX = 256
NLAT = 2048
D = 1024
LN_EPS = 1e-6
RMS_EPS = 1e-5
ALPHA = (2.0 * 2) ** 0.25
GC0 = math.sqrt(2.0 / math.pi)
GC1 = 0.044715
SEGS = [(0, 256), (256, 512), (768, 512), (1280, 512), (1792, 512)]
LSEGS = SEGS[1:]


class Trk:
    __slots__ = ("name", "w", "r")

    def __init__(self, name=""):
        self.name = name
        self.w = None
        self.r = {}


class DSem:
    def __init__(self, sem):
        self.sem = sem
        self.count = 0


class KB:
    def __init__(self, nc, es):
        self.nc = nc
        self.es = es
        self.eng = {"pe": nc.tensor, "act": nc.scalar, "dve": nc.vector, "pool": nc.gpsimd, "sp": nc.sync}
        self.esem = {}
        self.ecnt = {}
        for e in ("pe", "act", "dve", "pool"):
            self.esem[e] = es.enter_context(nc.semaphore("s_" + e))
            self.ecnt[e] = 0
        self.seen = {e: {} for e in self.eng}
        self.nsem = 0
        self.hooks = []
        self.bar_dsems = []

    def dsem(self):
        self.nsem += 1
        return DSem(self.es.enter_context(self.nc.semaphore(f"d{self.nsem}")))

    def _wait(self, e, reads, writes):
        evs = {}

        def add(ev):
            if ev is None:
                return
            k = id(ev[0])
            if k not in evs or evs[k][1] < ev[1]:
                evs[k] = ev

        for t in reads:
            add(t.w)
        for t in writes:
            add(t.w)
            for ev in t.r.values():
                add(ev)
        seen = self.seen[e]
        own = self.esem.get(e)
        for k, (sem, val) in evs.items():
            if e == "pe" and sem is own:
                continue
            if seen.get(k, 0) >= val:
                continue
            self.eng[e].wait_ge(sem, val)
            seen[k] = val

    def _post(self, ev, reads, writes):
        k = id(ev[0])
        for t in writes:
            t.w = ev
            t.r = {}
        for t in reads:
            t.r[k] = ev

    def op(self, e, fn, reads=(), writes=()):
        self._wait(e, reads, writes)
        ins = fn(self.eng[e])
        self.ecnt[e] += 1
        ins.then_inc(self.esem[e], 1)
        ev = (self.esem[e], self.ecnt[e])
        self._post(ev, reads, writes)
        return ev

    def dma(self, q, out, in_, ds, reads=(), writes=()):
        self._wait(q, reads, writes)
        ins = self.eng[q].dma_start(out=out, in_=in_)
        ds.count += 16
        ins.then_inc(ds.sem, 16)
        ev = (ds.sem, ds.count)
        self._post(ev, reads, writes)
        return ev

    def barrier(self):
        for h in self.hooks:
            h()
        for e in self.eng:
            seen = self.seen[e]
            for ds in self.bar_dsems:
                k = id(ds.sem)
                if ds.count and seen.get(k, 0) < ds.count:
                    self.eng[e].wait_ge(ds.sem, ds.count)
                    seen[k] = ds.count
            for f in ("pe", "act", "dve", "pool"):
                if f == e or self.ecnt[f] == 0:
                    continue
                k = id(self.esem[f])
                if seen.get(k, 0) >= self.ecnt[f]:
                    continue
                self.eng[e].wait_ge(self.esem[f], self.ecnt[f])
                seen[k] = self.ecnt[f]


class _Stop(Exception):
    pass


LAG_ON = True


class Lag:
    def __init__(self):
        self.p = None

    def push(self, fn):
        if not LAG_ON:
            fn()
            return
        old, self.p = self.p, fn
        if old:
            old()

    def flush(self):
        old, self.p = self.p, None
        if old:
            old()


def build_program(debug_h1=False, stage=99):
    nc = bass.Bass("TRN2", target_bir_lowering=False)

    def chk(n):
        if stage <= n:
            raise _Stop()

    def din(name, shape):
        return nc.dram_tensor(name, list(shape), F32, kind="ExternalInput").ap()

    x_d = din("x", [2, NLAT, D])
    ctx_d = din("ctx", [2, NCTX, D])
    cvT_d = din("cvT", [128, 24])
    modw_d = din("mod_w", [2, D, 3 * D])
    modbT_d = din("mod_bT", [128, 48])
    lng_d = din("ln_g", [2, D])
    lnb_d = din("ln_b", [2, D])
    abwin_d = din("ab_w_in", [D, 3584])
    abwout_d = din("ab_w_out", [D, D])
    awsT_d = din("a_w_sT", [128, 512])
    abs_d = din("a_b_s", [1, 512])
    ang_d = din("a_norm_g", [1, 512])
    anb_d = din("a_norm_b", [1, 512])
    lam_d = din("lam_in", [1, 256])
    subg_d = din("b_subln_g", [1, 128])
    cdwin_d = din("cd_w_in", [D, 2816])
    cdwout_d = din("cd_w_out", [D, D])
    dwT_d = din("dwT", [128, 124])
    cvec_d = din("cvecs", [128, 12])
    sink_d = din("d_sink", [1, 8])
    ident_d = din("ident", [128, 128])
    rmat_d = din("rmat", [128, 128])
    rope_d = din("ropeT", [128, 2 * NT])
    mask_d = din("masks", [128, 512])
    out_d = nc.dram_tensor("out", [2, NLAT, D], F32, kind="ExternalOutput").ap()
    if debug_h1:
        h1_d = nc.dram_tensor("h1", [2, NT, D], F32, kind="ExternalOutput").ap()
    else:
        h1_d = nc.dram_tensor("h1", [2, NT, D], F32).ap()

    with ExitStack() as es:
        kb = KB(nc, es)
        op = kb.op

        def sb(name, shape, dt=F32):
            return es.enter_context(nc.sbuf_tensor("sb_" + name, list(shape), dt))

        PB = [es.enter_context(nc.psum_tensor(f"pb{i}", [128, 512], F32)) for i in range(8)]
        TPB = [Trk(f"pb{i}") for i in range(8)]

        d_const = kb.dsem()
        T_const = Trk("const")
        d_constp = kb.dsem()
        T_constp = Trk("constp")
        ident = sb("ident", [128, 128])
        identb = sb("identb", [128, 128], BF16)
        rmat = sb("rmat", [128, 128], BF16)
        ropeT = sb("ropeT", [128, 2, NT])
        masks = sb("masks", [128, 2, 256], BF16)
        cvT = sb("cvT", [128, 24])
        modbT = sb("modbT", [128, 48])
        wsT = sb("wsT", [128, 4, 128], BF16)
        lamt = sb("lamt", [128, 256])
        subg = sb("subg", [128, 128])
        dwT = sb("dwT", [128, 124])
        cvecs = sb("cvecs", [128, 12])
        sinkt = sb("sinkt", [128, 8])
        ones_f = sb("ones_f", [128, 128])
        onesdiv = sb("onesdiv", [128, 128])
        sTb = sb("sTb", [128, 24], BF16)
        modT = sb("modT", [128, 144])
        small = sb("small", [128, 64])
        epsln = small[:, 0:1]
        epsrms = small[:, 1:2]
        epsln4 = small[:, 2:3]
        neglam = small[:, 3:4]

        def cdma(q, dst, src):
            if q == "pool":
                kb.dma(q, dst, src, d_constp, writes=[T_constp])
            else:
                kb.dma(q, dst, src, d_const, writes=[T_const])

        cdma("sp", ident[:], ident_d)
        cdma("pool", rmat[:], rmat_d)
        cdma("sp", ropeT[:], rope_d.rearrange("p (a t) -> p a t", a=2))
        cdma("pool", masks[:], mask_d.rearrange("p (a t) -> p a t", a=2))
        cdma("sp", cvT[:], cvT_d)
        cdma("sp", modbT[:], modbT_d)
        cdma("pool", wsT[:], awsT_d.rearrange("p (g q) -> p g q", g=4))
        cdma("sp", lamt[:], lam_d.partition_broadcast(128))
        cdma("sp", subg[:], subg_d.partition_broadcast(128))
        cdma("sp", dwT[:], dwT_d)
        cdma("sp", cvecs[:], cvec_d)
        cdma("sp", sinkt[:], sink_d.partition_broadcast(128))

        T_c2 = Trk("c2")
        RC = [T_const, T_c2]
        op("dve", lambda e: e.memset(ones_f[:], 1.0), writes=[T_c2])
        op("dve", lambda e: e.memset(onesdiv[:], 1.0 / 512.0), writes=[T_c2])
        op("dve", lambda e: e.memset(small[:, 0:1], LN_EPS), writes=[T_c2])
        op("dve", lambda e: e.memset(small[:, 1:2], RMS_EPS), writes=[T_c2])
        op("dve", lambda e: e.memset(small[:, 2:3], 4.0 * LN_EPS), writes=[T_c2])
        op("dve", lambda e: e.tensor_copy(identb[:], ident[:]), reads=[T_const], writes=[T_c2])
        lam_init0 = 0.8 - 0.6 * math.exp(-0.3 * 0)
        lprod = sb("lprod", [128, 128])
        op("dve", lambda e: e.tensor_tensor(lprod[:, 0:64], lamt[:, 0:64], lamt[:, 64:128], ALU.mult),
           reads=[T_const], writes=[T_c2])
        op("dve", lambda e: e.tensor_tensor(lprod[:, 64:128], lamt[:, 128:192], lamt[:, 192:256], ALU.mult),
           reads=[T_c2, T_const], writes=[T_c2])
        op("dve", lambda e: e.reduce_sum(small[:, 4:6], lprod[:].rearrange("p (a b) -> p a b", a=2),
                                        mybir.AxisListType.X), reads=[T_c2], writes=[T_c2])
        op("act", lambda e: e.activation(small[:, 6:8], small[:, 4:6], AF.Exp), reads=[T_c2], writes=[T_c2])
        op("dve", lambda e: e.scalar_tensor_tensor(small[:, 3:4], small[:, 7:8], -lam_init0, small[:, 6:7],
                                                  ALU.add, ALU.subtract), reads=[T_c2], writes=[T_c2])
        op("dve", lambda e: e.tensor_scalar(subg[:], subg[:], (1.0 - lam_init0) * 0.5, None, ALU.mult),
           reads=[T_const, T_c2], writes=[T_c2])
        op("act", lambda e: e.activation(sinkt[:], sinkt[:], AF.Exp), reads=[T_const, T_c2], writes=[T_c2])
        op("dve", lambda e: e.tensor_scalar(dwT[:], dwT[:], 0.5, None, ALU.mult), reads=[T_const, T_c2], writes=[T_c2])
        sct = sb("sct", [128, 24])
        op("act", lambda e: e.activation(sct[:], cvT[:], AF.Tanh, scale=0.5), reads=[T_const], writes=[T_c2])
        op("dve", lambda e: e.scalar_tensor_tensor(sct[:], sct[:], 1.0, cvT[:], ALU.add, ALU.mult),
           reads=[T_c2, T_const], writes=[T_c2])
        op("dve", lambda e: e.tensor_scalar(sTb[:], sct[:], 0.5, None, ALU.mult), reads=[T_c2], writes=[T_c2])

        NW = 3
        wslot = [sb(f"wslot{i}", [128, 8, 512], BF16) for i in range(NW)]
        T_w = [Trk(f"w{i}") for i in range(NW)]
        d_w = [kb.dsem() for _ in range(NW)]
        wctr = [0]

        def wload(src_ap, ncols):
            i = wctr[0] % NW
            wctr[0] += 1
            kb.dma("pool", wslot[i][:, :, 0:ncols], src_ap.rearrange("(j p) c -> p j c", p=128), d_w[i],
                   writes=[T_w[i]])
            return wslot[i], T_w[i]

        NS = 3
        srct = [None] * NS
        T_s = [Trk(f"s{i}") for i in range(NS)]
        d_s = [kb.dsem() for _ in range(NS)]
        sctr = [0]

        def sload(src_ap):
            i = sctr[0] % NS
            sctr[0] += 1
            kb.dma("sp", srct[i][:], src_ap, d_s[i], writes=[T_s[i]])
            return srct[i], T_s[i]

        NO = 3
        outt = [None] * NO
        T_o = [Trk(f"o{i}") for i in range(NO)]
        d_o = [kb.dsem() for _ in range(NO)]
        octr = [0]
        kb.bar_dsems = d_o
        T_h1 = [[Trk(f"h1_{b}_{i}") for i in range(18)] for b in range(2)]

        T_mod = Trk("mod")
        for l in range(2):
            for g in range(3):
                for half in range(2):
                    c0 = g * 1024 + half * 512
                    ws, tw = wload(modw_d[l, :, c0:c0 + 512], 512)
                    for cc in range(4):
                        for j in range(8):
                            op("pe", lambda e: e.matmul(PB[7][:, cc * 4:cc * 4 + 3], ws[:, j, cc * 128:(cc + 1) * 128],
                                                        sTb[:, j * 3:(j + 1) * 3], start=(j == 0), stop=(j == 7)),
                               reads=[tw, T_c2], writes=[TPB[7]])
                    for cc in range(4):
                        k = g * 8 + half * 4 + cc
                        o0 = (l * 24 + k) * 3
                        op("dve", lambda e: e.tensor_scalar(modT[:, o0:o0 + 3], PB[7][:, cc * 4:cc * 4 + 3],
                                                           modbT[:, l * 24 + k:l * 24 + k + 1],
                                                           1.0 if g == 1 else 0.0, ALU.add, ALU.add),
                           reads=[TPB[7], T_const], writes=[T_mod])

        def mod_ap(l, k, r):
            o0 = (l * 24 + k) * 3 + r
            return modT[:, o0:o0 + 1]

        uT = sb("uT", [128, 8, NT], BF16)
        T_uT = [Trk(f"uT{i}") for i in range(18)]
        mixT = sb("mixT", [128, 8, NT], BF16)
        T_mix = Trk("mixT")
        T_gbc = Trk("gbc")
        T_diagf = [Trk(), Trk()]
        T_lngb = Trk("lngb")
        d_lngb = kb.dsem()
        T_ab = Trk("ab")
        d_ab = kb.dsem()
        ARENA = 84 * 1024
        arena = sb("arena", [128, ARENA // 4])

        class Carver:
            def __init__(self):
                self.off = 0

            def get(self, nelem, dt=F32):
                nbytes = nelem * (4 if dt == F32 else 2)
                nbytes = (nbytes + 31) // 32 * 32
                assert self.off + nbytes <= ARENA, (self.off, nbytes)
                a = arena[:, self.off // 4:(self.off + nbytes) // 4]
                self.off += nbytes
                if dt == BF16:
                    a = a.bitcast(BF16)[:, 0:nelem]
                return a

        def tile_rows(l, b, i):
            if l == 0:
                return ctx_d[b, i * 128:(i + 1) * 128, :] if i < 2 else x_d[b, (i - 2) * 128:(i - 1) * 128, :]
            return h1_d[b, i * 128:(i + 1) * 128, :]

        def seg_tiles(t0, n):
            return list(range(t0 // 128, (t0 + n) // 128))

        def proj_fm(ws, tw, c0, t0, n, bank):
            for j in range(8):
                op("pe", lambda e: e.matmul(PB[bank][:, 0:n], ws[:, j, c0:c0 + 128], uT[:, j, t0:t0 + n],
                                            start=(j == 0), stop=(j == 7)),
                   reads=[tw] + [T_uT[i] for i in seg_tiles(t0, n)], writes=[TPB[bank]])

        def proj_tm(ws, tw, c0, ncols, i, bank):
            for j in range(8):
                op("pe", lambda e: e.matmul(PB[bank][:, 0:ncols], uT[:, j, i * 128:(i + 1) * 128], ws[:, j, c0:c0 + ncols],
                                            start=(j == 0), stop=(j == 7)),
                   reads=[tw, T_uT[i]], writes=[TPB[bank]])

        def rsqrt_small(dst, src, eps_ap, trk, scale=1.0):
            op("act", lambda e: e.activation(dst, src, AF.Sqrt, bias=eps_ap, scale=scale), reads=[trk, T_c2], writes=[trk])
            op("dve", lambda e: e.reciprocal(dst, dst), reads=[trk], writes=[trk])

        rope_lag = Lag()
        kb.hooks.append(rope_lag.flush)

        rctr = [0]

        def rope_evac(bank, n, t0, dst, T_dst, rb, qraw, T_qraw, t1, T_t1, t2, T_t2):
            k = rctr[0] % 2
            rctr[0] += 1
            rb = 4 + k
            qraw, T_qraw, t1, T_t1, t2, T_t2 = qraw[k], T_qraw[k], t1[k], T_t1[k], t2[k], T_t2[k]
            op("act", lambda e: e.copy(qraw[:, 0:n], PB[bank][:, 0:n]), reads=[TPB[bank]], writes=[T_qraw])
            rope_lag.push(lambda: rope_rest(n, t0, dst, T_dst, rb, qraw, T_qraw, t1, T_t1, t2, T_t2))

        def rope_rest(n, t0, dst, T_dst, rb, qraw, T_qraw, t1, T_t1, t2, T_t2):
            op("pe", lambda e: e.matmul(PB[rb][:, 0:n], rmat[:], qraw[:, 0:n], start=True, stop=True),
               reads=[T_qraw, T_constp], writes=[TPB[rb]])
            op("dve", lambda e: e.tensor_tensor(t1[:, 0:n], qraw[:, 0:n], ropeT[:, 0, t0:t0 + n], ALU.mult),
               reads=[T_qraw, T_const], writes=[T_t1])
            op("dve", lambda e: e.tensor_tensor(t2[:, 0:n], PB[rb][:, 0:n], ropeT[:, 1, t0:t0 + n], ALU.mult),
               reads=[TPB[rb], T_const], writes=[T_t2])
            if not isinstance(dst, list):
                dst = [(dst, 0, 128)]
            for (dap, p0, p1) in dst:
                op("pool", lambda e: e.tensor_tensor(dap, t1[p0:p1, 0:n], t2[p0:p1, 0:n], ALU.add),
                   reads=[T_t1, T_t2], writes=[T_dst])

        def build_gate_bc(l, r, slot, gbc, diagf):
            for cc in range(8):
                dg = diagf[cc % 2]
                op("dve", lambda e: e.tensor_scalar(dg[:], ident[:], mod_ap(l, 16 + cc, r), None, ALU.mult),
                   reads=[T_const, T_mod], writes=[T_diagf[cc % 2]])
                op("pe", lambda e: e.matmul(PB[7][:, (cc % 4) * 128:(cc % 4 + 1) * 128], ones_f[:], dg[:],
                                            start=True, stop=True),
                   reads=[T_c2, T_diagf[cc % 2]], writes=[TPB[7]])
                if cc % 4 == 3:
                    h = cc // 4
                    op("act", lambda e: e.copy(gbc[:, slot, h * 512:(h + 1) * 512], PB[7][:, :]),
                       reads=[TPB[7]], writes=[T_gbc])

        def sload_dep(l, b, i):
            idx = sctr[0] % NS
            sctr[0] += 1
            rd = [T_h1[b][i]] if l == 1 else []
            kb.dma("sp", srct[idx][:], tile_rows(l, b, i), d_s[idx], reads=rd, writes=[T_s[idx]])
            return srct[idx], T_s[idx]

        def phase_transposes(l, b):
            cvt = Carver()
            for k in range(NS):
                srct[k] = cvt.get(D)
            for i in range(18):
                st, ts = sload_dep(l, b, i)
                r = 2 if i < 2 else b
                for half in range(2):
                    bank = (2 * i + half) % 4
                    for q in range(4):
                        j = half * 4 + q
                        op("pe", lambda e: e.transpose(PB[bank][:, q * 128:(q + 1) * 128], st[:, j * 128:(j + 1) * 128], ident[:]),
                           reads=[ts, T_const], writes=[TPB[bank]])
                    for q in range(4):
                        j = half * 4 + q
                        op("dve", lambda e: e.tensor_scalar(uT[:, j, i * 128:(i + 1) * 128], PB[bank][:, q * 128:(q + 1) * 128],
                                                           mod_ap(l, 8 + j, r), mod_ap(l, j, r), ALU.mult, ALU.add),
                           reads=[TPB[bank], T_mod], writes=[T_uT[i]])

        def phase_outproj(l, b, wout_d, tiles):
            kb.barrier()
            cv = Carver()
            for k in range(NS):
                srct[k] = cv.get(D)
            for k in range(NO):
                outt[k] = cv.get(D)
            zt = [cv.get(D) for _ in range(2)]
            T_z = [Trk(), Trk()]
            statsl = [cv.get(64) for _ in range(2)]
            T_stl = [Trk(), Trk()]
            lngb = cv.get(2 * D).rearrange("p (a d) -> p a d", a=2)
            gbc = cv.get(2 * D).rearrange("p (a d) -> p a d", a=2)
            diagf = [cv.get(128) for _ in range(2)]
            kb.dma("sp", lngb[:, 0, :], lng_d[l:l + 1, :].partition_broadcast(128), d_lngb, writes=[T_lngb])
            kb.dma("sp", lngb[:, 1, :], lnb_d[l:l + 1, :].partition_broadcast(128), d_lngb, writes=[T_lngb])
            build_gate_bc(l, b, 0, gbc, diagf)
            if l == 0:
                build_gate_bc(l, 2, 1, gbc, diagf)
            w0, tw0 = wload(wout_d[:, 0:512], 512)
            w1, tw1 = wload(wout_d[:, 512:1024], 512)
            for n_i, i in enumerate(tiles):
                st, ts = sload_dep(l, b, i)
                gs = 1 if i < 2 else 0
                pb0 = (n_i % 2) * 2
                for half, (ws, tw) in enumerate(((w0, tw0), (w1, tw1))):
                    for j in range(8):
                        op("pe", lambda e: e.matmul(PB[pb0 + half][:, :], mixT[:, j, i * 128:(i + 1) * 128], ws[:, j, :],
                                                    start=(j == 0), stop=(j == 7)),
                           reads=[T_mix, tw], writes=[TPB[pb0 + half]])
                z = zt[n_i % 2]
                tz = T_z[n_i % 2]
                stats = statsl[n_i % 2]
                T_st = T_stl[n_i % 2]
                for half in range(2):
                    hs = slice(half * 512, (half + 1) * 512)
                    op("dve", lambda e: e.tensor_tensor(z[:, hs], PB[pb0 + half][:, :], gbc[:, gs, hs], ALU.mult),
                       reads=[TPB[pb0 + half], T_gbc], writes=[tz])
                    op("dve", lambda e: e.scalar_tensor_tensor(z[:, hs], st[:, hs], ALPHA, z[:, hs], ALU.mult, ALU.add),
                       reads=[ts, tz], writes=[tz])
                    op("dve", lambda e: e.bn_stats(stats[:, half * 6:(half + 1) * 6], z[:, hs]), reads=[tz], writes=[T_st])
                op("dve", lambda e: e.bn_aggr(stats[:, 16:18], stats[:, 0:12].rearrange("p (a b) -> p a b", a=2)),
                   reads=[T_st], writes=[T_st])
                rsqrt_small(stats[:, 18:19], stats[:, 17:18], epsln, T_st)
                op("dve", lambda e: e.scalar_tensor_tensor(stats[:, 19:20], stats[:, 16:17], -1.0, stats[:, 18:19],
                                                          ALU.mult, ALU.mult), reads=[T_st], writes=[T_st])
                oi = octr[0] % NO
                octr[0] += 1
                ot, to = outt[oi], T_o[oi]
                op("act", lambda e: e.activation(z[:, :], z[:, :], AF.Identity, bias=stats[:, 19:20], scale=stats[:, 18:19]),
                   reads=[tz, T_st], writes=[tz])
                op("pool", lambda e: e.tensor_tensor(z[:, :], z[:, :], lngb[:, 0, :], ALU.mult), reads=[tz, T_lngb], writes=[tz])
                op("pool", lambda e: e.tensor_tensor(ot[:, :], z[:, :], lngb[:, 1, :], ALU.add), reads=[tz, T_lngb], writes=[to])
                if l == 0:
                    kb.dma("sp", h1_d[b, i * 128:(i + 1) * 128, :], ot[:, :], d_o[oi], reads=[to], writes=[T_h1[b][i]])
                else:
                    kb.dma("sp", out_d[b, (i - 2) * 128:(i - 1) * 128, :], ot[:, :], d_o[oi], reads=[to])

        def gelu2(dst, T_dst, bank, n, sq, T_sq, tt, T_tt):
            op("act", lambda e: e.activation(sq[:, 0:n], PB[bank][:, 0:n], AF.Square), reads=[TPB[bank]], writes=[T_sq])
            op("dve", lambda e: e.tensor_scalar(sq[:, 0:n], sq[:, 0:n], GC1, 1.0, ALU.mult, ALU.add), reads=[T_sq], writes=[T_sq])
            op("dve", lambda e: e.tensor_tensor(sq[:, 0:n], sq[:, 0:n], PB[bank][:, 0:n], ALU.mult),
               reads=[T_sq, TPB[bank]], writes=[T_sq])
            op("act", lambda e: e.activation(tt[:, 0:n], sq[:, 0:n], AF.Tanh, scale=GC0), reads=[T_sq], writes=[T_tt])
            return op("dve", lambda e: e.scalar_tensor_tensor(dst, tt[:, 0:n], 1.0, PB[bank][:, 0:n], ALU.add, ALU.mult),
                      reads=[T_tt, TPB[bank]], writes=[T_dst])

        def pass_layer0(b):
            l = 0
            kb.barrier()
            phase_transposes(l, b)
            chk(1)
            kb.barrier()
            cv = Carver()
            vn = cv.get(18 * 512, BF16)
            vn3 = vn.rearrange("p (i c) -> p i c", i=18)
            T_vn = [Trk() for _ in range(18)]
            Gu = cv.get(NT)
            T_Gu = Trk()
            sq = [cv.get(512) for _ in range(2)]
            T_sq = [Trk(), Trk()]
            tt = [cv.get(512) for _ in range(2)]
            T_tt = [Trk(), Trk()]
            g2 = [cv.get(512) for _ in range(2)]
            T_g2 = [Trk(), Trk()]
            statsA = [cv.get(64) for _ in range(2)]
            T_stA = [Trk(), Trk()]
            bsrep = cv.get(4 * 512).rearrange("p (g q) -> p g q", g=4)
            angb = cv.get(2 * 512).rearrange("p (a q) -> p a q", a=2)
            for rep in range(4):
                kb.dma("sp", bsrep[:, :, rep * 128:(rep + 1) * 128],
                       abs_d.rearrange("o (g q) -> o g q", g=4).partition_broadcast(128), d_ab, writes=[T_ab])
            kb.dma("sp", angb[:, 0, :], ang_d.partition_broadcast(128), d_ab, writes=[T_ab])
            kb.dma("sp", angb[:, 1, :], anb_d.partition_broadcast(128), d_ab, writes=[T_ab])
            wv, twv = wload(abwin_d[:, 512:1024], 512)
            wu, twu = wload(abwin_d[:, 0:512], 512)
            wg, twg = wload(abwin_d[:, 1024:1536], 512)
            for i in range(18):
                bank = i % 4
                k2 = i % 2
                stats, T_st = statsA[k2], T_stA[k2]
                proj_tm(wv, twv, 0, 512, i, bank)
                gelu2(g2[k2][:, :], T_g2[k2], bank, 512, sq[k2], T_sq[k2], tt[k2], T_tt[k2])
                op("dve", lambda e: e.bn_stats(stats[:, 0:6], g2[k2][:, :]), reads=[T_g2[k2]], writes=[T_st])
                op("dve", lambda e: e.bn_aggr(stats[:, 16:18], stats[:, 0:6]), reads=[T_st], writes=[T_st])
                rsqrt_small(stats[:, 18:19], stats[:, 17:18], epsln4, T_st)
                op("dve", lambda e: e.tensor_scalar(g2[k2][:, :], g2[k2][:, :], stats[:, 16:17], stats[:, 18:19],
                                                   ALU.subtract, ALU.mult), reads=[T_g2[k2], T_st], writes=[T_g2[k2]])
                op("pool", lambda e: e.tensor_tensor(g2[k2][:, :], g2[k2][:, :], angb[:, 0, :], ALU.mult),
                   reads=[T_g2[k2], T_ab], writes=[T_g2[k2]])
                op("pool", lambda e: e.tensor_tensor(vn3[:, i, :], g2[k2][:, :], angb[:, 1, :], ALU.add),
                   reads=[T_g2[k2], T_ab], writes=[T_vn[i]])
            for g in range(4):
                for si, (t0, n) in enumerate(SEGS):
                    bank = si % 4
                    k2 = si % 2
                    proj_fm(wu, twu, g * 128, t0, n, bank)
                    gelu2(Gu[:, t0:t0 + n], T_Gu, bank, n, sq[k2], T_sq[k2], tt[k2], T_tt[k2])
                for si, (t0, n) in enumerate(SEGS):
                    bank = si % 4
                    k2 = si % 2
                    proj_fm(wg, twg, g * 128, t0, n, bank)
                    op("act", lambda e: e.activation(tt[k2][:, 0:n], PB[bank][:, 0:n], AF.Tanh, scale=0.5),
                       reads=[TPB[bank]], writes=[T_tt[k2]])
                    op("dve", lambda e: e.scalar_tensor_tensor(tt[k2][:, 0:n], tt[k2][:, 0:n], 1.0, PB[bank][:, 0:n], ALU.add, ALU.mult),
                       reads=[T_tt[k2], TPB[bank]], writes=[T_tt[k2]])
                    op("dve", lambda e: e.scalar_tensor_tensor(Gu[:, t0:t0 + n], tt[k2][:, 0:n], 0.25, Gu[:, t0:t0 + n], ALU.mult, ALU.mult),
                       reads=[T_tt[k2], T_Gu], writes=[T_Gu])
                    mb = 4 + si % 2
                    tl = seg_tiles(t0, n)
                    for ci, c in enumerate(tl):
                        op("pe", lambda e: e.matmul(PB[mb][:, ci * 128:(ci + 1) * 128], vn3[:, c, g * 128:(g + 1) * 128], wsT[:, g, :],
                                                    start=True, stop=True),
                           reads=[T_vn[c], T_constp], writes=[TPB[mb]])
                    op("dve", lambda e: e.tensor_tensor(sq[k2][:, 0:n], PB[mb][:, 0:n], bsrep[:, g, 0:n], ALU.add),
                       reads=[TPB[mb], T_ab], writes=[T_sq[k2]])
                    op("pool", lambda e: e.tensor_tensor(mixT[:, g, t0:t0 + n], sq[k2][:, 0:n], Gu[:, t0:t0 + n], ALU.mult),
                       reads=[T_sq[k2], T_Gu], writes=[T_mix])
            chk(2)
            kb.barrier()
            cv = Carver()
            V1 = cv.get(18 * 4 * 144, BF16)
            V14 = V1.rearrange("p (i h c) -> p i h c", i=18, h=4)
            T_V = [Trk() for _ in range(18)]
            qT2f = cv.get(2 * NT, BF16)
            qT2 = qT2f.rearrange("p (c t) -> p c t", c=2)
            kT = cv.get(NT, BF16)
            T_qT, T_kT = Trk(), Trk()
            op("pool", lambda e: e.memset(qT2f[:, :], 0.0), writes=[T_qT])
            sbg = cv.get(NT)
            T_sbg = Trk()
            qraw = [cv.get(512, BF16) for _ in range(2)]
            T_qraw = [Trk(), Trk()]
            t1 = [cv.get(512) for _ in range(2)]
            T_t1 = [Trk(), Trk()]
            t2 = [cv.get(512) for _ in range(2)]
            T_t2 = [Trk(), Trk()]
            pT = [cv.get(512, BF16) for _ in range(4)]
            T_pT = [Trk() for _ in range(4)]
            o1n = [cv.get(128) for _ in range(2)]
            T_o1n = [Trk(), Trk()]
            oc_all = cv.get(18 * 128)
            oc3 = oc_all.rearrange("p (i c) -> p i c", i=18)
            T_oca = [Trk() for _ in range(18)]
            mv_all = cv.get(64)
            mv3 = mv_all[:, 0:36].rearrange("p (i c) -> p i c", i=18)
            msr = cv.get(64)
            T_mv = Trk()
            osc = [cv.get(128) for _ in range(2)]
            T_osc = [Trk(), Trk()]
            TQ3 = [Trk() for _ in range(4)]
            stats = cv.get(64)
            T_st = Trk()
            op("dve", lambda e: e.memset(V1[:, :], 1.0), writes=T_V)
            wv, twv = wload(abwin_d[:, 2560:3072], 512)
            wq, twq = wload(abwin_d[:, 1536:2048], 512)
            wk, twk = wload(abwin_d[:, 2048:2560], 512)
            for i in range(18):
                bank = i % 3
                proj_tm(wv, twv, 0, 512, i, bank)
                op("act", lambda e: e.copy(V14[:, i, :, 0:128], PB[bank][:, :].rearrange("p (h c) -> p h c", h=4)),
                   reads=[TPB[bank]], writes=[T_V[i]])
            chk(2.2)
            wg, twg = wload(abwin_d[:, 3072:3584], 512)
            pctr = [0]
            for h in range(4):
                for si, (t0, n) in enumerate(SEGS):
                    k2 = si % 2
                    proj_fm(wq, twq, h * 128, t0, n, si % 3)
                    rope_evac(si % 3, n, t0, [(qT2[0:64, 0, t0:t0 + n], 0, 64), (qT2[64:128, 1, t0:t0 + n], 64, 128)], T_qT, 4 + k2, qraw, T_qraw, t1, T_t1, t2, T_t2)
                for si, (t0, n) in enumerate(SEGS):
                    k2 = si % 2
                    proj_fm(wk, twk, h * 128, t0, n, si % 3)
                    rope_evac(si % 3, n, t0, kT[:, t0:t0 + n], T_kT, 4 + k2, qraw, T_qraw, t1, T_t1, t2, T_t2)
                rope_lag.flush()
                for si, (t0, n) in enumerate(SEGS):
                    k2 = si % 2
                    bank = si % 3
                    proj_fm(wg, twg, h * 128, t0, n, bank)
                    op("act", lambda e: e.activation(t1[k2][:, 0:n], PB[bank][:, 0:n], AF.Tanh, scale=0.5),
                       reads=[TPB[bank]], writes=[T_t1[k2]])
                    op("dve", lambda e: e.scalar_tensor_tensor(sbg[:, t0:t0 + n], t1[k2][:, 0:n], 1.0, PB[bank][:, 0:n], ALU.add, ALU.mult),
                       reads=[T_t1[k2], TPB[bank]], writes=[T_sbg])
                chk(2.4)
                qtiles = [(0, [0, 1])] + [(256 + 256 * qi, list(range(18))) for qi in range(8)]
                for qn, (q0, kts) in enumerate(qtiles):
                    ob = 4 + 2 * (qn % 2)

                    def qk(kt, sbank):
                        op("pe", lambda e: e.matmul(PB[sbank][:, :].rearrange("p (c q) -> p c q", c=2),
                                                    kT[:, kt * 128:(kt + 1) * 128],
                                                    qT2[:, :, q0:q0 + 256], start=True, stop=True),
                           reads=[T_kT, T_qT], writes=[TPB[sbank]])

                    for pre in range(min(2, len(kts))):
                        qk(kts[pre], pre % 3)
                    for ki, kt in enumerate(kts):
                        sbank = ki % 3
                        if ki + 2 < len(kts):
                            qk(kts[ki + 2], (ki + 2) % 3)
                        pi = pctr[0] % 4
                        pctr[0] += 1
                        op("act", lambda e: e.activation(pT[pi][:, :], PB[sbank][:, :], AF.Exp, scale=0.125),
                           reads=[TPB[sbank]], writes=[T_pT[pi]])
                        chk(2.5)
                        for c in range(2):
                            for s in range(2):
                                first = (ki == 0 and s == 0)
                                op("pe", lambda e: e.matmul(PB[ob + c][:, s * 144:s * 144 + 130],
                                                            pT[pi][:, c * 256 + s * 128:c * 256 + (s + 1) * 128],
                                                            V14[:, kt, h, 0:130], start=first, stop=(ki == len(kts) - 1),
                                                            skip_group_check=True),
                                   reads=[T_pT[pi], T_V[kt]], writes=[TPB[ob + c]])
                    chk(2.6)
                    for s in range(2):
                        k2 = s
                        idx = (q0 + s * 128) // 128
                        op("dve", lambda e: e.reciprocal(stats[:, 0:1], PB[ob][:, s * 144 + 128:s * 144 + 129]),
                           reads=[TPB[ob]], writes=[T_st])
                        op("dve", lambda e: e.reciprocal(stats[:, 1:2], PB[ob + 1][:, s * 144 + 128:s * 144 + 129]),
                           reads=[TPB[ob + 1]], writes=[T_st])
                        op("dve", lambda e: e.tensor_scalar(o1n[k2][:, :], PB[ob + 1][:, s * 144:s * 144 + 128], stats[:, 1:2], neglam,
                                                           ALU.mult, ALU.mult), reads=[TPB[ob + 1], T_st, T_c2], writes=[T_o1n[k2]])
                        op("dve", lambda e: e.scalar_tensor_tensor(oc3[:, idx, :], PB[ob][:, s * 144:s * 144 + 128], stats[:, 0:1], o1n[k2][:, :],
                                                                  ALU.mult, ALU.add), reads=[TPB[ob], T_st, T_o1n[k2]], writes=[T_oca[idx]])
                        op("dve", lambda e: e.bn_stats(stats[:, 8:14], oc3[:, idx, :]), reads=[T_oca[idx]], writes=[T_st])
                        op("dve", lambda e: e.bn_aggr(mv3[:, idx, :], stats[:, 8:14]), reads=[T_st], writes=[T_mv])
                op("dve", lambda e: e.tensor_tensor(msr[:, 0:18], mv3[:, :, 0], mv3[:, :, 0], ALU.mult), reads=[T_mv], writes=[T_mv])
                op("dve", lambda e: e.tensor_tensor(msr[:, 0:18], msr[:, 0:18], mv3[:, :, 1], ALU.add), reads=[T_mv], writes=[T_mv])
                rsqrt_small(msr[:, 32:50], msr[:, 0:18], epsrms, T_mv)
                for idx in range(18):
                    k2 = idx % 2
                    qd = idx % 4
                    op("dve", lambda e: e.scalar_tensor_tensor(osc[k2][:, :], oc3[:, idx, :], msr[:, 32 + idx:33 + idx], subg[:, :],
                                                              ALU.mult, ALU.mult), reads=[T_oca[idx], T_mv, T_c2], writes=[T_osc[k2]])
                    op("pe", lambda e: e.transpose(PB[3][:, qd * 128:(qd + 1) * 128], osc[k2][:, :], ident[:]),
                       reads=[T_osc[k2], T_const], writes=[TQ3[qd]])
                    op("dve", lambda e: e.tensor_tensor(mixT[:, 4 + h, idx * 128:(idx + 1) * 128], PB[3][:, qd * 128:(qd + 1) * 128],
                                                       sbg[:, idx * 128:(idx + 1) * 128], ALU.mult),
                       reads=[TQ3[qd], T_sbg], writes=[T_mix])
            chk(3)
            phase_outproj(l, b, abwout_d, list(range(18)))
            chk(4)

        def pass_layer1(b):
            l = 1
            kb.barrier()
            phase_transposes(l, b)
            kb.barrier()
            cv = Carver()
            yT = cv.get(4 * NLAT)
            yT3 = yT.rearrange("p (j t) -> p j t", j=4)
            T_yT = [Trk() for _ in range(4)]
            hpad = [cv.get(NLAT + 32, BF16) for _ in range(2)]
            T_hp = [Trk(), Trk()]
            caS = [cv.get(512) for _ in range(2)]
            T_ca = [Trk(), Trk()]
            tt = [cv.get(512) for _ in range(2)]
            T_tt = [Trk(), Trk()]
            diag = cv.get(31 * 128, BF16)
            diag3 = diag.rearrange("p (k c) -> p k c", k=31)
            T_dg = Trk()
            wa, twa = wload(cdwin_d[:, 0:512], 512)
            wb, twb = wload(cdwin_d[:, 512:1024], 512)
            for k2 in range(2):
                op("dve", lambda e: e.memset(hpad[k2][:, :], 0.0), writes=[T_hp[k2]])
            for j in range(4):
                hp = hpad[j % 2]
                thp = T_hp[j % 2]
                for k in range(31):
                    op("dve", lambda e: e.tensor_scalar(diag3[:, k, :], identb[:, :], dwT[:, j * 31 + k:j * 31 + k + 1], None, ALU.mult),
                       reads=[T_c2], writes=[T_dg])
                for si, (t0, n) in enumerate(LSEGS):
                    k2 = si % 2
                    lt0 = t0 - NCTX
                    proj_fm(wa, twa, j * 128, t0, n, si % 2)
                    op("act", lambda e: e.copy(caS[k2][:, :], PB[si % 2][:, :]), reads=[TPB[si % 2]], writes=[T_ca[k2]])
                    proj_fm(wb, twb, j * 128, t0, n, 2 + si % 2)
                    op("act", lambda e: e.activation(tt[k2][:, :], PB[2 + si % 2][:, :], AF.Tanh, scale=0.5),
                       reads=[TPB[2 + si % 2]], writes=[T_tt[k2]])
                    op("dve", lambda e: e.scalar_tensor_tensor(hp[:, 15 + lt0:15 + lt0 + n], tt[k2][:, :], 1.0, caS[k2][:, :], ALU.add, ALU.mult),
                       reads=[T_tt[k2], T_ca[k2]], writes=[thp])
                for si, (t0, n) in enumerate(LSEGS):
                    lt0 = t0 - NCTX
                    cb = 4 + si % 2
                    for k in range(31):
                        op("pe", lambda e: e.matmul(PB[cb][:, :], diag3[:, k, :], hp[:, lt0 + k:lt0 + k + 512],
                                                    start=(k == 0), stop=(k == 30)),
                           reads=[T_dg, thp], writes=[TPB[cb]])
                    op("act", lambda e: e.activation(yT3[:, j, lt0:lt0 + 512], PB[cb][:, :], AF.Identity, bias=cvecs[:, j:j + 1]),
                       reads=[TPB[cb], T_const], writes=[T_yT[j]])
            kb.barrier()
            cv2 = Carver()
            cv2.off = 4 * NLAT * 4
            mean_bc = cv2.get(NLAT)
            rstd_bc = cv2.get(NLAT)
            T_mr = Trk()
            ysq = [cv2.get(512) for _ in range(2)]
            T_ysq = [Trk(), Trk()]
            tt = [cv2.get(512) for _ in range(2)]
            T_tt = [Trk(), Trk()]
            sg = [cv2.get(512) for _ in range(2)]
            T_sg = [Trk(), Trk()]
            yn = [cv2.get(512) for _ in range(2)]
            T_yn = [Trk(), Trk()]
            for si, (t0, n) in enumerate(LSEGS):
                lt0 = t0 - NCTX
                ts_ = slice(lt0, lt0 + 512)
                mbk, sbk = 0 + 2 * (si % 2), 1 + 2 * (si % 2)
                for j in range(4):
                    op("pe", lambda e: e.matmul(PB[mbk][:, :], onesdiv[:, :], yT3[:, j, ts_], start=(j == 0), stop=(j == 3)),
                       reads=[T_c2, T_yT[j]], writes=[TPB[mbk]])
                for j in range(4):
                    k2 = j % 2
                    op("act", lambda e: e.activation(ysq[k2][:, :], yT3[:, j, ts_], AF.Square), reads=[T_yT[j]], writes=[T_ysq[k2]])
                    op("pe", lambda e: e.matmul(PB[sbk][:, :], onesdiv[:, :], ysq[k2][:, :], start=(j == 0), stop=(j == 3)),
                       reads=[T_c2, T_ysq[k2]], writes=[TPB[sbk]])
                op("act", lambda e: e.copy(mean_bc[:, ts_], PB[mbk][:, :]), reads=[TPB[mbk]], writes=[T_mr])
                op("dve", lambda e: e.tensor_tensor(rstd_bc[:, ts_], mean_bc[:, ts_], mean_bc[:, ts_], ALU.mult), reads=[T_mr], writes=[T_mr])
                op("dve", lambda e: e.tensor_tensor(rstd_bc[:, ts_], PB[sbk][:, :], rstd_bc[:, ts_], ALU.subtract),
                   reads=[TPB[sbk], T_mr], writes=[T_mr])
                op("act", lambda e: e.activation(rstd_bc[:, ts_], rstd_bc[:, ts_], AF.Sqrt, bias=epsln), reads=[T_mr, T_c2], writes=[T_mr])
                op("dve", lambda e: e.reciprocal(rstd_bc[:, ts_], rstd_bc[:, ts_]), reads=[T_mr], writes=[T_mr])
            wg, twg = wload(cdwin_d[:, 1024:1536], 512)
            cnt = 0
            for j in range(4):
                for si, (t0, n) in enumerate(LSEGS):
                    lt0 = t0 - NCTX
                    ts_ = slice(lt0, lt0 + 512)
                    k2 = cnt % 2
                    bank = 4 + cnt % 4
                    cnt += 1
                    proj_fm(wg, twg, j * 128, t0, n, bank)
                    op("act", lambda e: e.activation(tt[k2][:, :], PB[bank][:, :], AF.Tanh, scale=0.5), reads=[TPB[bank]], writes=[T_tt[k2]])
                    op("dve", lambda e: e.scalar_tensor_tensor(sg[k2][:, :], tt[k2][:, :], 1.0, PB[bank][:, :], ALU.add, ALU.mult),
                       reads=[T_tt[k2], TPB[bank]], writes=[T_sg[k2]])
                    op("pool", lambda e: e.tensor_tensor(yn[k2][:, :], yT3[:, j, ts_], mean_bc[:, ts_], ALU.subtract),
                       reads=[T_yT[j], T_mr], writes=[T_yn[k2]])
                    op("pool", lambda e: e.tensor_tensor(yn[k2][:, :], yn[k2][:, :], rstd_bc[:, ts_], ALU.mult),
                       reads=[T_yn[k2], T_mr], writes=[T_yn[k2]])
                    op("dve", lambda e: e.tensor_scalar(yn[k2][:, :], yn[k2][:, :], cvecs[:, 4 + j:5 + j], cvecs[:, 8 + j:9 + j], ALU.mult, ALU.add),
                       reads=[T_yn[k2], T_const], writes=[T_yn[k2]])
                    op("act", lambda e: e.activation(tt[k2][:, :], yn[k2][:, :], AF.Tanh, scale=0.5), reads=[T_yn[k2], T_sg[k2]], writes=[T_tt[k2]])
                    op("dve", lambda e: e.scalar_tensor_tensor(yn[k2][:, :], tt[k2][:, :], 1.0, yn[k2][:, :], ALU.add, ALU.mult),
                       reads=[T_tt[k2], T_yn[k2]], writes=[T_yn[k2]])
                    op("dve", lambda e: e.scalar_tensor_tensor(mixT[:, j, t0:t0 + n], yn[k2][:, :], 0.25, sg[k2][:, :], ALU.mult, ALU.mult),
                       reads=[T_yn[k2], T_sg[k2]], writes=[T_mix])
            kb.barrier()
            cv = Carver()
            V1 = cv.get(18 * 2 * 80, BF16)
            V14 = V1.rearrange("p (i h c) -> p i h c", i=18, h=2)
            T_V = [Trk() for _ in range(18)]
            kT2 = cv.get(2 * NT, BF16)
            kT23 = kT2.rearrange("p (h t) -> p h t", h=2)
            T_kT = Trk()
            qT2f = cv.get(2 * NLAT, BF16)
            qT2 = qT2f.rearrange("p (c t) -> p c t", c=2)
            T_qT = Trk()
            op("pool", lambda e: e.memset(qT2f[:, :], 0.0), writes=[T_qT])
            stats2 = [cv.get(64) for _ in range(2)]
            T_st2 = [Trk(), Trk()]
            sdg = cv.get(NLAT)
            T_sdg = Trk()
            wk2 = cv.get(8 * 2 * 128, BF16)
            wk24 = wk2.rearrange("p (j h c) -> p j h c", j=8, h=2)
            T_wk2 = Trk()
            d_wk2 = kb.dsem()
            qraw = [cv.get(512, BF16) for _ in range(2)]
            T_qraw = [Trk(), Trk()]
            t1 = [cv.get(512) for _ in range(2)]
            T_t1 = [Trk(), Trk()]
            t2 = [cv.get(512) for _ in range(2)]
            T_t2 = [Trk(), Trk()]
            pT = [cv.get(256, BF16) for _ in range(4)]
            T_pT = [Trk() for _ in range(4)]
            ocomb = [cv.get(128) for _ in range(2)]
            T_oc = [Trk(), Trk()]
            stats = cv.get(64)
            T_st = Trk()
            op("dve", lambda e: e.memset(V1[:, :], 1.0), writes=T_V)
            for kvh in range(2):
                for dup in range(2):
                    kb.dma("pool", wk24[:, :, kvh, dup * 64:(dup + 1) * 64],
                           cdwin_d[:, 2048 + kvh * 64:2048 + (kvh + 1) * 64].rearrange("(j p) c -> p j c", p=128),
                           d_wk2, writes=[T_wk2])
            wq, twq = wload(cdwin_d[:, 1536:2048], 512)
            wkv, twkv = wload(cdwin_d[:, 2048:2560], 512)
            wg2, twg2 = wload(cdwin_d[:, 2560:2816], 256)
            for i in range(18):
                bank = i % 4
                proj_tm(wkv, twkv, 128, 128, i, bank)
                op("act", lambda e: e.copy(V14[:, i, :, 0:64], PB[bank][:, 0:128].rearrange("p (h c) -> p h c", h=2)),
                   reads=[TPB[bank]], writes=[T_V[i]])
            for kvh in range(2):
                for si, (t0, n) in enumerate(SEGS):
                    k2 = si % 2
                    bank = si % 4
                    for j in range(8):
                        op("pe", lambda e: e.matmul(PB[bank][:, 0:n], wk24[:, j, kvh, :], uT[:, j, t0:t0 + n], start=(j == 0), stop=(j == 7)),
                           reads=[T_wk2] + [T_uT[i] for i in seg_tiles(t0, n)], writes=[TPB[bank]])
                    rope_evac(bank, n, t0, kT23[:, kvh, t0:t0 + n], T_kT, 4 + k2, qraw, T_qraw, t1, T_t1, t2, T_t2)
            rope_lag.flush()
            pctr = [0]
            att_lag = Lag()
            for cc in range(4):
                kvh = cc // 2
                for si, (t0, n) in enumerate(LSEGS):
                    k2 = si % 2
                    lt0 = t0 - NCTX
                    proj_fm(wq, twq, cc * 128, t0, n, si % 4)
                    rope_evac(si % 4, n, t0, [(qT2[0:64, 0, lt0:lt0 + n], 0, 64), (qT2[64:128, 1, lt0:lt0 + n], 64, 128)], T_qT, 4 + k2, qraw, T_qraw, t1, T_t1, t2, T_t2)
                rope_lag.flush()
                for si, (t0, n) in enumerate(LSEGS):
                    k2 = si % 2
                    bank = si % 4
                    lt0 = t0 - NCTX
                    if cc < 2:
                        proj_fm(wkv, twkv, 256 + cc * 128, t0, n, bank)
                    else:
                        proj_fm(wg2, twg2, (cc - 2) * 128, t0, n, bank)
                    op("act", lambda e: e.activation(t1[k2][:, 0:n], PB[bank][:, 0:n], AF.Tanh, scale=0.5),
                       reads=[TPB[bank]], writes=[T_t1[k2]])
                    op("dve", lambda e: e.scalar_tensor_tensor(sdg[:, lt0:lt0 + n], t1[k2][:, 0:n], 1.0, PB[bank][:, 0:n], ALU.add, ALU.mult),
                       reads=[T_t1[k2], TPB[bank]], writes=[T_sdg])
                steps = []
                for qb in range(16):
                    kts = []
                    if qb > 0:
                        kts.append((2 + qb - 1, 0))
                    kts.append((2 + qb, None))
                    if qb < 15:
                        kts.append((2 + qb + 1, 1))
                    kts += [(0, None), (1, None)]
                    for ki, (kt, mk) in enumerate(kts):
                        steps.append((qb, ki, kt, mk, ki == len(kts) - 1))

                def qk1(step, sbank):
                    qb, ki, kt, mk, last = step
                    q0 = qb * 128
                    op("pe", lambda e: e.matmul(PB[sbank][:, 0:256].rearrange("p (c q) -> p c q", c=2),
                                                kT23[:, kvh, kt * 128:(kt + 1) * 128],
                                                qT2[:, :, q0:q0 + 128], start=True, stop=(mk is None), skip_group_check=True),
                       reads=[T_kT, T_qT], writes=[TPB[sbank]])
                    if mk is not None:
                        op("pe", lambda e: e.matmul(PB[sbank][:, 0:256], identb[:, :], masks[:, mk, :], start=False, stop=True,
                                                    skip_group_check=True),
                           reads=[T_c2, T_constp], writes=[TPB[sbank]])

                for pre in range(2):
                    qk1(steps[pre], pre % 4)
                for i, step in enumerate(steps):
                    qb, ki, kt, mk, last = step
                    q0 = qb * 128
                    if i + 2 < len(steps):
                        qk1(steps[i + 2], (i + 2) % 4)
                    sbank = i % 4
                    ob = 4 + qb % 2
                    pi = pctr[0] % 4
                    pctr[0] += 1
                    op("act", lambda e: e.activation(pT[pi][:, :], PB[sbank][:, 0:256], AF.Exp, scale=0.125),
                       reads=[TPB[sbank]], writes=[T_pT[pi]])
                    for hl in range(2):
                        first = (ki == 0 and hl == 0)
                        op("pe", lambda e: e.matmul(PB[ob][:, hl * 80:hl * 80 + 66], pT[pi][:, hl * 128:(hl + 1) * 128],
                                                    V14[:, kt, kvh, 0:66], start=first, stop=last,
                                                    skip_group_check=True),
                           reads=[T_pT[pi], T_V[kt]], writes=[TPB[ob]])
                    if not last:
                        continue
                    k2 = qb % 2
                    for hl in range(2):
                        hd = cc * 2 + hl
                        op("dve", lambda e: e.tensor_scalar(stats2[k2][:, hl:hl + 1], PB[ob][:, hl * 80 + 64:hl * 80 + 65], sinkt[:, hd:hd + 1], None, ALU.add),
                           reads=[TPB[ob], T_c2], writes=[T_st2[k2]])
                        op("dve", lambda e: e.reciprocal(stats2[k2][:, 2 + hl:3 + hl], stats2[k2][:, hl:hl + 1]), reads=[T_st2[k2]], writes=[T_st2[k2]])
                        op("dve", lambda e: e.tensor_scalar(ocomb[k2][:, hl * 64:(hl + 1) * 64], PB[ob][:, hl * 80:hl * 80 + 64],
                                                           stats2[k2][:, 2 + hl:3 + hl], None, ALU.mult),
                           reads=[TPB[ob], T_st2[k2]], writes=[T_oc[k2]])

                    def post_b(cc=cc, qb=qb, q0=q0, k2=k2):
                        tb = 6 + qb % 2
                        op("pe", lambda e: e.transpose(PB[tb][:, 0:128], ocomb[k2][:, :], ident[:]), reads=[T_oc[k2], T_const], writes=[TPB[tb]])
                        op("dve", lambda e: e.scalar_tensor_tensor(mixT[:, 4 + cc, NCTX + q0:NCTX + q0 + 128], PB[tb][:, 0:128], 0.5,
                                                                  sdg[:, q0:q0 + 128], ALU.mult, ALU.mult),
                           reads=[TPB[tb], T_sdg], writes=[T_mix])
                    att_lag.push(post_b)
                att_lag.flush()
            phase_outproj(l, b, cdwout_d, list(range(2, 18)))

        try:
            chk(0)
            for b in range(2):
                pass_layer0(b)
            if not debug_h1:
                for b in range(2):
                    pass_layer1(b)
        except _Stop:
            pass
        kb.barrier()
        for ds in d_o:
            if ds.count:
                nc.sync.wait_ge(ds.sem, ds.count)
    return nc


def _consts():
    ident = np.eye(128, dtype=np.float32)
    rmat = np.zeros((128, 128), np.float32)
    for dp in range(128):
        partner = dp + 16 if (dp % 32) < 16 else dp - 16
        rmat[partner, dp] = 1.0
    m = 16
    inv = (10000.0 ** (-np.arange(m, dtype=np.float32) / m)).astype(np.float32)
    t = np.arange(NLAT)
    row = (t // 64).astype(np.float32)
    col = (t % 64).astype(np.float32)
    ang_r = (row[:, None] * inv[None, :]).astype(np.float32)
    ang_c = (col[:, None] * inv[None, :]).astype(np.float32)
    cos_t = np.ones((128, NT), np.float32)
    sin_t = np.zeros((128, NT), np.float32)
    for p in range(128):
        d = p % 64
        ang = ang_r if d < 32 else ang_c
        f = d % 16
        sign = -1.0 if (d % 32) < 16 else 1.0
        cos_t[p, NCTX:] = np.cos(ang[:, f])
        sin_t[p, NCTX:] = sign * np.sin(ang[:, f])
    ropeT = np.concatenate([cos_t, sin_t], axis=1)
    kk = np.arange(128)[:, None]
    qq = np.arange(128)[None, :]
    mp = np.where(kk >= qq, 0.0, -30000.0).astype(np.float32)
    mn = np.where(kk <= qq, 0.0, -30000.0).astype(np.float32)
    masks = np.concatenate([mp, mp, mn, mn], axis=1)
    return ident, rmat, ropeT, masks


_CACHE = {}


def _core_inputs(core, x, c, ctx, c_ctx, mod_w, mod_b, ln_g, ln_b, ab_w_in, ab_w_out, a_w_s, a_b_s,
                 a_norm_g, a_norm_b, b_lq1, b_lk1, b_lq2, b_lk2, b_subln_g, cd_w_in, cd_w_out,
                 c_dw_w, c_dw_b, c_norm_g, c_norm_b, d_sink, shared):
    b0 = 2 * core
    cvec = np.stack([c[b0], c[b0 + 1], c_ctx], axis=0)
    cvT = np.ascontiguousarray(cvec.reshape(3, 8, 128).transpose(2, 1, 0)).reshape(128, 24)
    d = dict(shared)
    d["x"] = np.ascontiguousarray(x[b0:b0 + 2])
    d["ctx"] = np.ascontiguousarray(ctx[b0:b0 + 2])
    d["cvT"] = cvT
    return d


def kernel(x, c, ctx, c_ctx, mod_w, mod_b, ln_g, ln_b, ab_w_in, ab_w_out, a_w_s, a_b_s,
           a_norm_g, a_norm_b, b_lq1, b_lk1, b_lq2, b_lk2, b_subln_g, cd_w_in, cd_w_out,
           c_dw_w, c_dw_b, c_norm_g, c_norm_b, d_sink, _debug_h1=False, _stage=99):
    f = lambda a: np.ascontiguousarray(np.asarray(a, dtype=np.float32))
    x, c, ctx, c_ctx = f(x), f(c), f(ctx), f(c_ctx)
    ident, rmat, ropeT, masks = _consts()
    shared = {
        "mod_w": f(mod_w),
        "mod_bT": np.ascontiguousarray(f(mod_b).reshape(2, 24, 128).transpose(2, 0, 1)).reshape(128, 48),
        "ln_g": f(ln_g), "ln_b": f(ln_b),
        "ab_w_in": f(ab_w_in)[0], "ab_w_out": f(ab_w_out)[0],
        "a_w_sT": np.ascontiguousarray(f(a_w_s)[0].transpose(2, 0, 1)).reshape(128, 512),
        "a_b_s": f(a_b_s)[0].reshape(1, 512),
        "a_norm_g": f(a_norm_g).reshape(1, 512), "a_norm_b": f(a_norm_b).reshape(1, 512),
        "lam_in": np.concatenate([f(b_lq1)[0], f(b_lk1)[0], f(b_lq2)[0], f(b_lk2)[0]]).reshape(1, 256),
        "b_subln_g": f(b_subln_g).reshape(1, 128),
        "cd_w_in": f(cd_w_in)[0], "cd_w_out": f(cd_w_out)[0],
        "dwT": np.ascontiguousarray(f(c_dw_w)[0].reshape(31, 4, 128).transpose(2, 1, 0)).reshape(128, 124),
        "cvecs": np.ascontiguousarray(np.stack([f(c_dw_b)[0], f(c_norm_g)[0], f(c_norm_b)[0]], 0)
                                      .reshape(3, 4, 128).transpose(2, 0, 1)).reshape(128, 12),
        "d_sink": f(d_sink).reshape(1, 8),
        "ident": ident, "rmat": rmat, "ropeT": ropeT, "masks": masks,
    }
    in_maps = []
    for core in range(8):
        in_maps.append(_core_inputs(core, x, c, ctx, c_ctx, None, None, None, None, None, None, None, None,
                                    None, None, None, None, None, None, None, None, None,
                                    None, None, None, None, None, shared))
    key = (bool(_debug_h1), _stage)
    if key not in _CACHE:
        _CACHE[key] = build_program(debug_h1=key[0], stage=_stage)
    nc = _CACHE[key]
    res = run_bass_kernel_spmd(nc, in_maps, core_ids=list(range(8)))
    if _debug_h1:
        return np.concatenate([r["h1"] for r in res.results], axis=0)
    return np.concatenate([r["out"] for r in res.results], axis=0).astype(np.float32)
```

```python
import math
import numpy as np
from contextlib import ExitStack
import concourse.bass as bass
import concourse.mybir as mybir
from concourse.bass_utils import run_bass_kernel_spmd

F32 = mybir.dt.float32
BF16 = mybir.dt.bfloat16
AF = mybir.ActivationFunctionType
ALU = mybir.AluOpType

NT = 2304
NCTX = 256
NLAT = 2048
D = 1024
LN_EPS = 1e-6
RMS_EPS = 1e-5
ALPHA = (2.0 * 2) ** 0.25
GC0 = math.sqrt(2.0 / math.pi)
GC1 = 0.044715
SEGS = [(0, 256), (256, 512), (768, 512), (1280, 512), (1792, 512)]
LSEGS = SEGS[1:]


class Trk:
    __slots__ = ("name", "w", "r")

    def __init__(self, name=""):
        self.name = name
        self.w = None
        self.r = {}


class DSem:
    def __init__(self, sem):
        self.sem = sem
        self.count = 0


class KB:
    def __init__(self, nc, es):
        self.nc = nc
        self.es = es
        self.eng = {"pe": nc.tensor, "act": nc.scalar, "dve": nc.vector, "pool": nc.gpsimd, "sp": nc.sync}
        self.esem = {}
        self.ecnt = {}
        for e in ("pe", "act", "dve", "pool"):
            self.esem[e] = es.enter_context(nc.semaphore("s_" + e))
            self.ecnt[e] = 0
        self.seen = {e: {} for e in self.eng}
        self.nsem = 0
        self.hooks = []
        self.bar_dsems = []

    def dsem(self):
        self.nsem += 1
        return DSem(self.es.enter_context(self.nc.semaphore(f"d{self.nsem}")))

    def _wait(self, e, reads, writes):
        evs = {}

        def add(ev):
            if ev is None:
                return
            k = id(ev[0])
            if k not in evs or evs[k][1] < ev[1]:
                evs[k] = ev

        for t in reads:
            add(t.w)
        for t in writes:
            add(t.w)
            for ev in t.r.values():
                add(ev)
        seen = self.seen[e]
        own = self.esem.get(e)
        for k, (sem, val) in evs.items():
            if e == "pe" and sem is own:
                continue
            if seen.get(k, 0) >= val:
                continue
            self.eng[e].wait_ge(sem, val)
            seen[k] = val

    def _post(self, ev, reads, writes):
        k = id(ev[0])
        for t in writes:
            t.w = ev
            t.r = {}
        for t in reads:
            t.r[k] = ev

    def op(self, e, fn, reads=(), writes=()):
        self._wait(e, reads, writes)
        ins = fn(self.eng[e])
        self.ecnt[e] += 1
        ins.then_inc(self.esem[e], 1)
        ev = (self.esem[e], self.ecnt[e])
        self._post(ev, reads, writes)
        return ev

    def dma(self, q, out, in_, ds, reads=(), writes=()):
        self._wait(q, reads, writes)
        ins = self.eng[q].dma_start(out=out, in_=in_)
        ds.count += 16
        ins.then_inc(ds.sem, 16)
        ev = (ds.sem, ds.count)
        self._post(ev, reads, writes)
        return ev

    def barrier(self):
        for h in self.hooks:
            h()
        for e in self.eng:
            seen = self.seen[e]
            for ds in self.bar_dsems:
                k = id(ds.sem)
                if ds.count and seen.get(k, 0) < ds.count:
                    self.eng[e].wait_ge(ds.sem, ds.count)
                    seen[k] = ds.count
            for f in ("pe", "act", "dve", "pool"):
                if f == e or self.ecnt[f] == 0:
                    continue
                k = id(self.esem[f])
                if seen.get(k, 0) >= self.ecnt[f]:
                    continue
                self.eng[e].wait_ge(self.esem[f], self.ecnt[f])
                seen[k] = self.ecnt[f]


class _Stop(Exception):
    pass


LAG_ON = True


class Lag:
    def __init__(self):
        self.p = None

    def push(self, fn):
        if not LAG_ON:
            fn()
            return
        old, self.p = self.p, fn
        if old:
            old()

    def flush(self):
        old, self.p = self.p, None
        if old:
            old()


def build_program(debug_h1=False, stage=99):
    nc = bass.Bass("TRN2", target_bir_lowering=False)

    def chk(n):
        if stage <= n:
            raise _Stop()

    def din(name, shape):
        return nc.dram_tensor(name, list(shape), F32, kind="ExternalInput").ap()

    x_d = din("x", [2, NLAT, D])
    ctx_d = din("ctx", [2, NCTX, D])
    cvT_d = din("cvT", [128, 24])
    modw_d = din("mod_w", [2, D, 3 * D])
    modbT_d = din("mod_bT", [128, 48])
    lng_d = din("ln_g", [2, D])
    lnb_d = din("ln_b", [2, D])
    abwin_d = din("ab_w_in", [D, 3584])
    abwout_d = din("ab_w_out", [D, D])
    awsT_d = din("a_w_sT", [128, 512])
    abs_d = din("a_b_s", [1, 512])
    ang_d = din("a_norm_g", [1, 512])
    anb_d = din("a_norm_b", [1, 512])
    lam_d = din("lam_in", [1, 256])
    subg_d = din("b_subln_g", [1, 128])
    cdwin_d = din("cd_w_in", [D, 2816])
    cdwout_d = din("cd_w_out", [D, D])
    dwT_d = din("dwT", [128, 124])
    cvec_d = din("cvecs", [128, 12])
    sink_d = din("d_sink", [1, 8])
    ident_d = din("ident", [128, 128])
    rmat_d = din("rmat", [128, 128])
    rope_d = din("ropeT", [128, 2 * NT])
    mask_d = din("masks", [128, 512])
    out_d = nc.dram_tensor("out", [2, NLAT, D], F32, kind="ExternalOutput").ap()
    if debug_h1:
        h1_d = nc.dram_tensor("h1", [2, NT, D], F32, kind="ExternalOutput").ap()
    else:
        h1_d = nc.dram_tensor("h1", [2, NT, D], F32).ap()

    with ExitStack() as es:
        kb = KB(nc, es)
        op = kb.op

        def sb(name, shape, dt=F32):
            return es.enter_context(nc.sbuf_tensor("sb_" + name, list(shape), dt))

        PB = [es.enter_context(nc.psum_tensor(f"pb{i}", [128, 512], F32)) for i in range(8)]
        TPB = [Trk(f"pb{i}") for i in range(8)]

        d_const = kb.dsem()
        T_const = Trk("const")
        d_constp = kb.dsem()
        T_constp = Trk("constp")
        ident = sb("ident", [128, 128])
        identb = sb("identb", [128, 128], BF16)
        rmat = sb("rmat", [128, 128], BF16)
        ropeT = sb("ropeT", [128, 2, NT])
        masks = sb("masks", [128, 2, 256], BF16)
        cvT = sb("cvT", [128, 24])
        modbT = sb("modbT", [128, 48])
        wsT = sb("wsT", [128, 4, 128], BF16)
        lamt = sb("lamt", [128, 256])
        subg = sb("subg", [128, 128])
        dwT = sb("dwT", [128, 124])
        cvecs = sb("cvecs", [128, 12])
        sinkt = sb("sinkt", [128, 8])
        ones_f = sb("ones_f", [128, 128])
        onesdiv = sb("onesdiv", [128, 128])
        sTb = sb("sTb", [128, 24], BF16)
        modT = sb("modT", [128, 144])
        small = sb("small", [128, 64])
        epsln = small[:, 0:1]
        epsrms = small[:, 1:2]
        epsln4 = small[:, 2:3]
        neglam = small[:, 3:4]

        def cdma(q, dst, src):
            if q == "pool":
                kb.dma(q, dst, src, d_constp, writes=[T_constp])
            else:
                kb.dma(q, dst, src, d_const, writes=[T_const])

        cdma("sp", ident[:], ident_d)
        cdma("pool", rmat[:], rmat_d)
        cdma("sp", ropeT[:], rope_d.rearrange("p (a t) -> p a t", a=2))
        cdma("pool", masks[:], mask_d.rearrange("p (a t) -> p a t", a=2))
        cdma("sp", cvT[:], cvT_d)
        cdma("sp", modbT[:], modbT_d)
        cdma("pool", wsT[:], awsT_d.rearrange("p (g q) -> p g q", g=4))
        cdma("sp", lamt[:], lam_d.partition_broadcast(128))
        cdma("sp", subg[:], subg_d.partition_broadcast(128))
        cdma("sp", dwT[:], dwT_d)
        cdma("sp", cvecs[:], cvec_d)
        cdma("sp", sinkt[:], sink_d.partition_broadcast(128))

        T_c2 = Trk("c2")
        RC = [T_const, T_c2]
        op("dve", lambda e: e.memset(ones_f[:], 1.0), writes=[T_c2])
        op("dve", lambda e: e.memset(onesdiv[:], 1.0 / 512.0), writes=[T_c2])
        op("dve", lambda e: e.memset(small[:, 0:1], LN_EPS), writes=[T_c2])
        op("dve", lambda e: e.memset(small[:, 1:2], RMS_EPS), writes=[T_c2])
        op("dve", lambda e: e.memset(small[:, 2:3], 4.0 * LN_EPS), writes=[T_c2])
        op("dve", lambda e: e.tensor_copy(identb[:], ident[:]), reads=[T_const], writes=[T_c2])
        lam_init0 = 0.8 - 0.6 * math.exp(-0.3 * 0)
        lprod = sb("lprod", [128, 128])
        op("dve", lambda e: e.tensor_tensor(lprod[:, 0:64], lamt[:, 0:64], lamt[:, 64:128], ALU.mult),
           reads=[T_const], writes=[T_c2])
        op("dve", lambda e: e.tensor_tensor(lprod[:, 64:128], lamt[:, 128:192], lamt[:, 192:256], ALU.mult),
           reads=[T_c2, T_const], writes=[T_c2])
        op("dve", lambda e: e.reduce_sum(small[:, 4:6], lprod[:].rearrange("p (a b) -> p a b", a=2),
                                        mybir.AxisListType.X), reads=[T_c2], writes=[T_c2])
        op("act", lambda e: e.activation(small[:, 6:8], small[:, 4:6], AF.Exp), reads=[T_c2], writes=[T_c2])
        op("dve", lambda e: e.scalar_tensor_tensor(small[:, 3:4], small[:, 7:8], -lam_init0, small[:, 6:7],
                                                  ALU.add, ALU.subtract), reads=[T_c2], writes=[T_c2])
        op("dve", lambda e: e.tensor_scalar(subg[:], subg[:], (1.0 - lam_init0) * 0.5, None, ALU.mult),
           reads=[T_const, T_c2], writes=[T_c2])
        op("act", lambda e: e.activation(sinkt[:], sinkt[:], AF.Exp), reads=[T_const, T_c2], writes=[T_c2])
        op("dve", lambda e: e.tensor_scalar(dwT[:], dwT[:], 0.5, None, ALU.mult), reads=[T_const, T_c2], writes=[T_c2])
        sct = sb("sct", [128, 24])
        op("act", lambda e: e.activation(sct[:], cvT[:], AF.Tanh, scale=0.5), reads=[T_const], writes=[T_c2])
        op("dve", lambda e: e.scalar_tensor_tensor(sct[:], sct[:], 1.0, cvT[:], ALU.add, ALU.mult),
           reads=[T_c2, T_const], writes=[T_c2])
        op("dve", lambda e: e.tensor_scalar(sTb[:], sct[:], 0.5, None, ALU.mult), reads=[T_c2], writes=[T_c2])

        NW = 3
        wslot = [sb(f"wslot{i}", [128, 8, 512], BF16) for i in range(NW)]
        T_w = [Trk(f"w{i}") for i in range(NW)]
        d_w = [kb.dsem() for _ in range(NW)]
        wctr = [0]

        def wload(src_ap, ncols):
            i = wctr[0] % NW
            wctr[0] += 1
            kb.dma("pool", wslot[i][:, :, 0:ncols], src_ap.rearrange("(j p) c -> p j c", p=128), d_w[i],
                   writes=[T_w[i]])
            return wslot[i], T_w[i]

        NS = 3
        srct = [None] * NS
        T_s = [Trk(f"s{i}") for i in range(NS)]
        d_s = [kb.dsem() for _ in range(NS)]
        sctr = [0]

        def sload(src_ap):
            i = sctr[0] % NS
            sctr[0] += 1
            kb.dma("sp", srct[i][:], src_ap, d_s[i], writes=[T_s[i]])
            return srct[i], T_s[i]

        NO = 3
        outt = [None] * NO
        T_o = [Trk(f"o{i}") for i in range(NO)]
        d_o = [kb.dsem() for _ in range(NO)]
        octr = [0]
        kb.bar_dsems = d_o
        T_h1 = [[Trk(f"h1_{b}_{i}") for i in range(18)] for b in range(2)]

        T_mod = Trk("mod")
        for l in range(2):
            for g in range(3):
                for half in range(2):
                    c0 = g * 1024 + half * 512
                    ws, tw = wload(modw_d[l, :, c0:c0 + 512], 512)
                    for cc in range(4):
                        for j in range(8):
                            op("pe", lambda e: e.matmul(PB[7][:, cc * 4:cc * 4 + 3], ws[:, j, cc * 128:(cc + 1) * 128],
                                                        sTb[:, j * 3:(j + 1) * 3], start=(j == 0), stop=(j == 7)),
                               reads=[tw, T_c2], writes=[TPB[7]])
                    for cc in range(4):
                        k = g * 8 + half * 4 + cc
                        o0 = (l * 24 + k) * 3
                        op("dve", lambda e: e.tensor_scalar(modT[:, o0:o0 + 3], PB[7][:, cc * 4:cc * 4 + 3],
                                                           modbT[:, l * 24 + k:l * 24 + k + 1],
                                                           1.0 if g == 1 else 0.0, ALU.add, ALU.add),
                           reads=[TPB[7], T_const], writes=[T_mod])

        def mod_ap(l, k, r):
            o0 = (l * 24 + k) * 3 + r
            return modT[:, o0:o0 + 1]

        uT = sb("uT", [128, 8, NT], BF16)
        T_uT = [Trk(f"uT{i}") for i in range(18)]
        mixT = sb("mixT", [128, 8, NT], BF16)
        T_mix = Trk("mixT")
        T_gbc = Trk("gbc")
        T_diagf = [Trk(), Trk()]
        T_lngb = Trk("lngb")
        d_lngb = kb.dsem()
        T_ab = Trk("ab")
        d_ab = kb.dsem()
        ARENA = 84 * 1024
        arena = sb("arena", [128, ARENA // 4])

        class Carver:
            def __init__(self):
                self.off = 0

            def get(self, nelem, dt=F32):
                nbytes = nelem * (4 if dt == F32 else 2)
                nbytes = (nbytes + 31) // 32 * 32
                assert self.off + nbytes <= ARENA, (self.off, nbytes)
                a = arena[:, self.off // 4:(self.off + nbytes) // 4]
                self.off += nbytes
                if dt == BF16:
                    a = a.bitcast(BF16)[:, 0:nelem]
                return a

        def tile_rows(l, b, i):
            if l == 0:
                return ctx_d[b, i * 128:(i + 1) * 128, :] if i < 2 else x_d[b, (i - 2) * 128:(i - 1) * 128, :]
            return h1_d[b, i * 128:(i + 1) * 128, :]

        def seg_tiles(t0, n):
            return list(range(t0 // 128, (t0 + n) // 128))

        def proj_fm(ws, tw, c0, t0, n, bank):
            for j in range(8):
                op("pe", lambda e: e.matmul(PB[bank][:, 0:n], ws[:, j, c0:c0 + 128], uT[:, j, t0:t0 + n],
                                            start=(j == 0), stop=(j == 7)),
                   reads=[tw] + [T_uT[i] for i in seg_tiles(t0, n)], writes=[TPB[bank]])

        def proj_tm(ws, tw, c0, ncols, i, bank):
            for j in range(8):
                op("pe", lambda e: e.matmul(PB[bank][:, 0:ncols], uT[:, j, i * 128:(i + 1) * 128], ws[:, j, c0:c0 + ncols],
                                            start=(j == 0), stop=(j == 7)),
                   reads=[tw, T_uT[i]], writes=[TPB[bank]])

        def rsqrt_small(dst, src, eps_ap, trk, scale=1.0):
            op("act", lambda e: e.activation(dst, src, AF.Sqrt, bias=eps_ap, scale=scale), reads=[trk, T_c2], writes=[trk])
            op("dve", lambda e: e.reciprocal(dst, dst), reads=[trk], writes=[trk])

        rope_lag = Lag()
        kb.hooks.append(rope_lag.flush)

        rctr = [0]

        def rope_evac(bank, n, t0, dst, T_dst, rb, qraw, T_qraw, t1, T_t1, t2, T_t2):
            k = rctr[0] % 3
            rctr[0] += 1
            rb = 4 + k
            qraw, T_qraw, t1, T_t1, t2, T_t2 = qraw[k], T_qraw[k], t1[k], T_t1[k], t2[k], T_t2[k]
            op("act", lambda e: e.copy(qraw[:, 0:n], PB[bank][:, 0:n]), reads=[TPB[bank]], writes=[T_qraw])
            rope_lag.push(lambda: rope_rest(n, t0, dst, T_dst, rb, qraw, T_qraw, t1, T_t1, t2, T_t2))

        def rope_rest(n, t0, dst, T_dst, rb, qraw, T_qraw, t1, T_t1, t2, T_t2):
            op("pe", lambda e: e.matmul(PB[rb][:, 0:n], rmat[:], qraw[:, 0:n], start=True, stop=True),
               reads=[T_qraw, T_constp], writes=[TPB[rb]])
            op("dve", lambda e: e.tensor_tensor(t1[:, 0:n], qraw[:, 0:n], ropeT[:, 0, t0:t0 + n], ALU.mult),
               reads=[T_qraw, T_const], writes=[T_t1])
            op("dve", lambda e: e.tensor_tensor(t2[:, 0:n], PB[rb][:, 0:n], ropeT[:, 1, t0:t0 + n], ALU.mult),
               reads=[TPB[rb], T_const], writes=[T_t2])
            if not isinstance(dst, list):
                dst = [(dst, 0, 128)]
            for (dap, p0, p1) in dst:
                op("pool", lambda e: e.tensor_tensor(dap, t1[p0:p1, 0:n], t2[p0:p1, 0:n], ALU.add),
                   reads=[T_t1, T_t2], writes=[T_dst])

        def build_gate_bc(l, r, slot, gbc, diagf):
            for cc in range(8):
                dg = diagf[cc % 2]
                op("dve", lambda e: e.tensor_scalar(dg[:], ident[:], mod_ap(l, 16 + cc, r), None, ALU.mult),
                   reads=[T_const, T_mod], writes=[T_diagf[cc % 2]])
                op("pe", lambda e: e.matmul(PB[7][:, (cc % 4) * 128:(cc % 4 + 1) * 128], ones_f[:], dg[:],
                                            start=True, stop=True),
                   reads=[T_c2, T_diagf[cc % 2]], writes=[TPB[7]])
                if cc % 4 == 3:
                    h = cc // 4
                    op("act", lambda e: e.copy(gbc[:, slot, h * 512:(h + 1) * 512], PB[7][:, :]),
                       reads=[TPB[7]], writes=[T_gbc])

        def sload_dep(l, b, i):
            idx = sctr[0] % NS
            sctr[0] += 1
            rd = [T_h1[b][i]] if l == 1 else []
            kb.dma("sp", srct[idx][:], tile_rows(l, b, i), d_s[idx], reads=rd, writes=[T_s[idx]])
            return srct[idx], T_s[idx]

        def phase_transposes(l, b):
            cvt = Carver()
            for k in range(NS):
                srct[k] = cvt.get(D)
            for i in range(18):
                st, ts = sload_dep(l, b, i)
                r = 2 if i < 2 else b
                for half in range(2):
                    bank = (2 * i + half) % 4
                    for q in range(4):
                        j = half * 4 + q
                        op("pe", lambda e: e.transpose(PB[bank][:, q * 128:(q + 1) * 128], st[:, j * 128:(j + 1) * 128], ident[:]),
                           reads=[ts, T_const], writes=[TPB[bank]])
                    for q in range(4):
                        j = half * 4 + q
                        op("dve", lambda e: e.tensor_scalar(uT[:, j, i * 128:(i + 1) * 128], PB[bank][:, q * 128:(q + 1) * 128],
                                                           mod_ap(l, 8 + j, r), mod_ap(l, j, r), ALU.mult, ALU.add),
                           reads=[TPB[bank], T_mod], writes=[T_uT[i]])

        def phase_outproj(l, b, wout_d, tiles):
            kb.barrier()
            cv = Carver()
            for k in range(NS):
                srct[k] = cv.get(D)
            for k in range(NO):
                outt[k] = cv.get(D)
            zt = [cv.get(D) for _ in range(4)]
            T_z = [Trk() for _ in range(4)]
            statsl = [cv.get(64) for _ in range(4)]
            T_stl = [Trk() for _ in range(4)]
            lngb = cv.get(2 * D).rearrange("p (a d) -> p a d", a=2)
            gbc = cv.get(2 * D).rearrange("p (a d) -> p a d", a=2)
            diagf = [cv.get(128) for _ in range(2)]
            kb.dma("sp", lngb[:, 0, :], lng_d[l:l + 1, :].partition_broadcast(128), d_lngb, writes=[T_lngb])
            kb.dma("sp", lngb[:, 1, :], lnb_d[l:l + 1, :].partition_broadcast(128), d_lngb, writes=[T_lngb])
            build_gate_bc(l, b, 0, gbc, diagf)
            if l == 0:
                build_gate_bc(l, 2, 1, gbc, diagf)
            w0, tw0 = wload(wout_d[:, 0:512], 512)
            w1, tw1 = wload(wout_d[:, 512:1024], 512)
            for n_i, i in enumerate(tiles):
                st, ts = sload_dep(l, b, i)
                gs = 1 if i < 2 else 0
                pb0 = (n_i % 3) * 2
                for half, (ws, tw) in enumerate(((w0, tw0), (w1, tw1))):
                    for j in range(8):
                        op("pe", lambda e: e.matmul(PB[pb0 + half][:, :], mixT[:, j, i * 128:(i + 1) * 128], ws[:, j, :],
                                                    start=(j == 0), stop=(j == 7)),
                           reads=[T_mix, tw], writes=[TPB[pb0 + half]])
                z = zt[n_i % 4]
                tz = T_z[n_i % 4]
                stats = statsl[n_i % 4]
                T_st = T_stl[n_i % 4]
                for half in range(2):
                    hs = slice(half * 512, (half + 1) * 512)
                    op("dve", lambda e: e.tensor_tensor(z[:, hs], PB[pb0 + half][:, :], gbc[:, gs, hs], ALU.mult),
                       reads=[TPB[pb0 + half], T_gbc], writes=[tz])
                    op("dve", lambda e: e.scalar_tensor_tensor(z[:, hs], st[:, hs], ALPHA, z[:, hs], ALU.mult, ALU.add),
                       reads=[ts, tz], writes=[tz])
                    op("dve", lambda e: e.bn_stats(stats[:, half * 6:(half + 1) * 6], z[:, hs]), reads=[tz], writes=[T_st])
                op("dve", lambda e: e.bn_aggr(stats[:, 16:18], stats[:, 0:12].rearrange("p (a b) -> p a b", a=2)),
                   reads=[T_st], writes=[T_st])
                rsqrt_small(stats[:, 18:19], stats[:, 17:18], epsln, T_st)
                op("dve", lambda e: e.scalar_tensor_tensor(stats[:, 19:20], stats[:, 16:17], -1.0, stats[:, 18:19],
                                                          ALU.mult, ALU.mult), reads=[T_st], writes=[T_st])
                oi = octr[0] % NO
                octr[0] += 1
                ot, to = outt[oi], T_o[oi]
                op("act", lambda e: e.activation(z[:, :], z[:, :], AF.Identity, bias=stats[:, 19:20], scale=stats[:, 18:19]),
                   reads=[tz, T_st], writes=[tz])
                op("pool", lambda e: e.tensor_tensor(z[:, :], z[:, :], lngb[:, 0, :], ALU.mult), reads=[tz, T_lngb], writes=[tz])
                op("pool", lambda e: e.tensor_tensor(ot[:, :], z[:, :], lngb[:, 1, :], ALU.add), reads=[tz, T_lngb], writes=[to])
                if l == 0:
                    kb.dma("sp", h1_d[b, i * 128:(i + 1) * 128, :], ot[:, :], d_o[oi], reads=[to], writes=[T_h1[b][i]])
                else:
                    kb.dma("sp", out_d[b, (i - 2) * 128:(i - 1) * 128, :], ot[:, :], d_o[oi], reads=[to])

        def gelu2(dst, T_dst, bank, n, sq, T_sq, tt, T_tt):
            op("act", lambda e: e.activation(sq[:, 0:n], PB[bank][:, 0:n], AF.Square), reads=[TPB[bank]], writes=[T_sq])
            op("dve", lambda e: e.tensor_scalar(sq[:, 0:n], sq[:, 0:n], GC1, 1.0, ALU.mult, ALU.add), reads=[T_sq], writes=[T_sq])
            op("dve", lambda e: e.tensor_tensor(sq[:, 0:n], sq[:, 0:n], PB[bank][:, 0:n], ALU.mult),
               reads=[T_sq, TPB[bank]], writes=[T_sq])
            op("act", lambda e: e.activation(tt[:, 0:n], sq[:, 0:n], AF.Tanh, scale=GC0), reads=[T_sq], writes=[T_tt])
            return op("dve", lambda e: e.scalar_tensor_tensor(dst, tt[:, 0:n], 1.0, PB[bank][:, 0:n], ALU.add, ALU.mult),
                      reads=[T_tt, TPB[bank]], writes=[T_dst])

        def pass_layer0(b):
            l = 0
            kb.barrier()
            phase_transposes(l, b)
            chk(1)
            kb.barrier()
            cv = Carver()
            vn = cv.get(18 * 512, BF16)
            vn3 = vn.rearrange("p (i c) -> p i c", i=18)
            T_vn = [Trk() for _ in range(18)]
            Gu = cv.get(NT)
            T_Gu = Trk()
            sq = [cv.get(512) for _ in range(4)]
            T_sq = [Trk() for _ in range(4)]
            tt = [cv.get(512) for _ in range(4)]
            T_tt = [Trk() for _ in range(4)]
            g2 = [cv.get(512) for _ in range(4)]
            T_g2 = [Trk() for _ in range(4)]
            statsA = [cv.get(64) for _ in range(4)]
            T_stA = [Trk() for _ in range(4)]
            actr = [0]
            bsrep = cv.get(4 * 512).rearrange("p (g q) -> p g q", g=4)
            angb = cv.get(2 * 512).rearrange("p (a q) -> p a q", a=2)
            for rep in range(4):
                kb.dma("sp", bsrep[:, :, rep * 128:(rep + 1) * 128],
                       abs_d.rearrange("o (g q) -> o g q", g=4).partition_broadcast(128), d_ab, writes=[T_ab])
            kb.dma("sp", angb[:, 0, :], ang_d.partition_broadcast(128), d_ab, writes=[T_ab])
            kb.dma("sp", angb[:, 1, :], anb_d.partition_broadcast(128), d_ab, writes=[T_ab])
            wv, twv = wload(abwin_d[:, 512:1024], 512)
            wu, twu = wload(abwin_d[:, 0:512], 512)
            wg, twg = wload(abwin_d[:, 1024:1536], 512)
            for i in range(18):
                bank = i % 4
                k2 = i % 4
                stats, T_st = statsA[k2], T_stA[k2]
                proj_tm(wv, twv, 0, 512, i, bank)
                gelu2(g2[k2][:, :], T_g2[k2], bank, 512, sq[k2], T_sq[k2], tt[k2], T_tt[k2])
                op("dve", lambda e: e.bn_stats(stats[:, 0:6], g2[k2][:, :]), reads=[T_g2[k2]], writes=[T_st])
                op("dve", lambda e: e.bn_aggr(stats[:, 16:18], stats[:, 0:6]), reads=[T_st], writes=[T_st])
                rsqrt_small(stats[:, 18:19], stats[:, 17:18], epsln4, T_st)
                op("dve", lambda e: e.tensor_scalar(g2[k2][:, :], g2[k2][:, :], stats[:, 16:17], stats[:, 18:19],
                                                   ALU.subtract, ALU.mult), reads=[T_g2[k2], T_st], writes=[T_g2[k2]])
                op("pool", lambda e: e.tensor_tensor(g2[k2][:, :], g2[k2][:, :], angb[:, 0, :], ALU.mult),
                   reads=[T_g2[k2], T_ab], writes=[T_g2[k2]])
                op("pool", lambda e: e.tensor_tensor(vn3[:, i, :], g2[k2][:, :], angb[:, 1, :], ALU.add),
                   reads=[T_g2[k2], T_ab], writes=[T_vn[i]])
            for g in range(4):
                for si, (t0, n) in enumerate(SEGS):
                    bank = actr[0] % 4
                    k2 = actr[0] % 4
                    actr[0] += 1
                    proj_fm(wu, twu, g * 128, t0, n, bank)
                    gelu2(Gu[:, t0:t0 + n], T_Gu, bank, n, sq[k2], T_sq[k2], tt[k2], T_tt[k2])
                for si, (t0, n) in enumerate(SEGS):
                    bank = actr[0] % 4
                    k2 = actr[0] % 4
                    actr[0] += 1
                    proj_fm(wg, twg, g * 128, t0, n, bank)
                    op("act", lambda e: e.activation(tt[k2][:, 0:n], PB[bank][:, 0:n], AF.Tanh, scale=0.5),
                       reads=[TPB[bank]], writes=[T_tt[k2]])
                    op("dve", lambda e: e.scalar_tensor_tensor(tt[k2][:, 0:n], tt[k2][:, 0:n], 1.0, PB[bank][:, 0:n], ALU.add, ALU.mult),
                       reads=[T_tt[k2], TPB[bank]], writes=[T_tt[k2]])
                    op("dve", lambda e: e.scalar_tensor_tensor(Gu[:, t0:t0 + n], tt[k2][:, 0:n], 0.25, Gu[:, t0:t0 + n], ALU.mult, ALU.mult),
                       reads=[T_tt[k2], T_Gu], writes=[T_Gu])
                    mb = 4 + actr[0] % 4
                    tl = seg_tiles(t0, n)
                    for ci, c in enumerate(tl):
                        op("pe", lambda e: e.matmul(PB[mb][:, ci * 128:(ci + 1) * 128], vn3[:, c, g * 128:(g + 1) * 128], wsT[:, g, :],
                                                    start=True, stop=True),
                           reads=[T_vn[c], T_constp], writes=[TPB[mb]])
                    op("dve", lambda e: e.tensor_tensor(sq[k2][:, 0:n], PB[mb][:, 0:n], bsrep[:, g, 0:n], ALU.add),
                       reads=[TPB[mb], T_ab], writes=[T_sq[k2]])
                    op("pool", lambda e: e.tensor_tensor(mixT[:, g, t0:t0 + n], sq[k2][:, 0:n], Gu[:, t0:t0 + n], ALU.mult),
                       reads=[T_sq[k2], T_Gu], writes=[T_mix])
            chk(2)
            kb.barrier()
            cv = Carver()
            V1 = cv.get(18 * 4 * 144, BF16)
            V14 = V1.rearrange("p (i h c) -> p i h c", i=18, h=4)
            T_V = [Trk() for _ in range(18)]
            qT2f = cv.get(2 * NT, BF16)
            qT2 = qT2f.rearrange("p (c t) -> p c t", c=2)
            kT = cv.get(NT, BF16)
            T_qT, T_kT = Trk(), Trk()
            op("pool", lambda e: e.memset(qT2f[:, :], 0.0), writes=[T_qT])
            sbg = cv.get(NT)
            T_sbg = Trk()
            qraw = [cv.get(512, BF16) for _ in range(3)]
            T_qraw = [Trk() for _ in range(3)]
            t1 = [cv.get(512) for _ in range(3)]
            T_t1 = [Trk() for _ in range(3)]
            t2 = [cv.get(512) for _ in range(3)]
            T_t2 = [Trk() for _ in range(3)]
            pT = [cv.get(512, BF16) for _ in range(4)]
            T_pT = [Trk() for _ in range(4)]
            o1n = [cv.get(128) for _ in range(2)]
            T_o1n = [Trk(), Trk()]
            oc_all = cv.get(18 * 128)
            oc3 = oc_all.rearrange("p (i c) -> p i c", i=18)
            T_oca = [Trk() for _ in range(18)]
            mv_all = cv.get(64)
            mv3 = mv_all[:, 0:36].rearrange("p (i c) -> p i c", i=18)
            msr = cv.get(64)
            T_mv = Trk()
            osc = [cv.get(128) for _ in range(2)]
            T_osc = [Trk(), Trk()]
            TQ3 = [Trk() for _ in range(4)]
            stats = cv.get(64)
            T_st = Trk()
            op("dve", lambda e: e.memset(V1[:, :], 1.0), writes=T_V)
            wv, twv = wload(abwin_d[:, 2560:3072], 512)
            wq, twq = wload(abwin_d[:, 1536:2048], 512)
            wk, twk = wload(abwin_d[:, 2048:2560], 512)
            for i in range(18):
                bank = i % 3
                proj_tm(wv, twv, 0, 512, i, bank)
                op("act", lambda e: e.copy(V14[:, i, :, 0:128], PB[bank][:, :].rearrange("p (h c) -> p h c", h=4)),
                   reads=[TPB[bank]], writes=[T_V[i]])
            chk(2.2)
            wg, twg = wload(abwin_d[:, 3072:3584], 512)
            pctr = [0]
            for h in range(4):
                for si, (t0, n) in enumerate(SEGS):
                    k2 = si % 2
                    proj_fm(wq, twq, h * 128, t0, n, si % 3)
                    rope_evac(si % 3, n, t0, [(qT2[0:64, 0, t0:t0 + n], 0, 64), (qT2[64:128, 1, t0:t0 + n], 64, 128)], T_qT, 4 + k2, qraw, T_qraw, t1, T_t1, t2, T_t2)
                for si, (t0, n) in enumerate(SEGS):
                    k2 = si % 2
                    proj_fm(wk, twk, h * 128, t0, n, si % 3)
                    rope_evac(si % 3, n, t0, kT[:, t0:t0 + n], T_kT, 4 + k2, qraw, T_qraw, t1, T_t1, t2, T_t2)
                rope_lag.flush()
                for si, (t0, n) in enumerate(SEGS):
                    k2 = si % 2
                    bank = si % 3
                    proj_fm(wg, twg, h * 128, t0, n, bank)
                    op("act", lambda e: e.activation(t1[k2][:, 0:n], PB[bank][:, 0:n], AF.Tanh, scale=0.5),
                       reads=[TPB[bank]], writes=[T_t1[k2]])
                    op("dve", lambda e: e.scalar_tensor_tensor(sbg[:, t0:t0 + n], t1[k2][:, 0:n], 1.0, PB[bank][:, 0:n], ALU.add, ALU.mult),
                       reads=[T_t1[k2], TPB[bank]], writes=[T_sbg])
                chk(2.4)
                qtiles = [(0, [0, 1])] + [(256 + 256 * qi, list(range(18))) for qi in range(8)]
                for qn, (q0, kts) in enumerate(qtiles):
                    ob = 4 + 2 * (qn % 2)

                    def qk(kt, sbank):
                        op("pe", lambda e: e.matmul(PB[sbank][:, :].rearrange("p (c q) -> p c q", c=2),
                                                    kT[:, kt * 128:(kt + 1) * 128],
                                                    qT2[:, :, q0:q0 + 256], start=True, stop=True),
                           reads=[T_kT, T_qT], writes=[TPB[sbank]])

                    for pre in range(min(2, len(kts))):
                        qk(kts[pre], pre % 3)
                    for ki, kt in enumerate(kts):
                        sbank = ki % 3
                        if ki + 2 < len(kts):
                            qk(kts[ki + 2], (ki + 2) % 3)
                        pi = pctr[0] % 4
                        pctr[0] += 1
                        op("act", lambda e: e.activation(pT[pi][:, :], PB[sbank][:, :], AF.Exp, scale=0.125),
                           reads=[TPB[sbank]], writes=[T_pT[pi]])
                        chk(2.5)
                        for c in range(2):
                            for s in range(2):
                                first = (ki == 0 and s == 0)
                                op("pe", lambda e: e.matmul(PB[ob + c][:, s * 144:s * 144 + 130],
                                                            pT[pi][:, c * 256 + s * 128:c * 256 + (s + 1) * 128],
                                                            V14[:, kt, h, 0:130], start=first, stop=(ki == len(kts) - 1),
                                                            skip_group_check=True),
                                   reads=[T_pT[pi], T_V[kt]], writes=[TPB[ob + c]])
                    chk(2.6)
                    for s in range(2):
                        k2 = s
                        idx = (q0 + s * 128) // 128
                        op("dve", lambda e: e.reciprocal(stats[:, 0:1], PB[ob][:, s * 144 + 128:s * 144 + 129]),
                           reads=[TPB[ob]], writes=[T_st])
                        op("dve", lambda e: e.reciprocal(stats[:, 1:2], PB[ob + 1][:, s * 144 + 128:s * 144 + 129]),
                           reads=[TPB[ob + 1]], writes=[T_st])
                        op("dve", lambda e: e.tensor_scalar(o1n[k2][:, :], PB[ob + 1][:, s * 144:s * 144 + 128], stats[:, 1:2], neglam,
                                                           ALU.mult, ALU.mult), reads=[TPB[ob + 1], T_st, T_c2], writes=[T_o1n[k2]])
                        op("dve", lambda e: e.scalar_tensor_tensor(oc3[:, idx, :], PB[ob][:, s * 144:s * 144 + 128], stats[:, 0:1], o1n[k2][:, :],
                                                                  ALU.mult, ALU.add), reads=[TPB[ob], T_st, T_o1n[k2]], writes=[T_oca[idx]])
                        op("dve", lambda e: e.bn_stats(stats[:, 8:14], oc3[:, idx, :]), reads=[T_oca[idx]], writes=[T_st])
                        op("dve", lambda e: e.bn_aggr(mv3[:, idx, :], stats[:, 8:14]), reads=[T_st], writes=[T_mv])
                op("dve", lambda e: e.tensor_tensor(msr[:, 0:18], mv3[:, :, 0], mv3[:, :, 0], ALU.mult), reads=[T_mv], writes=[T_mv])
                op("dve", lambda e: e.tensor_tensor(msr[:, 0:18], msr[:, 0:18], mv3[:, :, 1], ALU.add), reads=[T_mv], writes=[T_mv])
                rsqrt_small(msr[:, 32:50], msr[:, 0:18], epsrms, T_mv)
                for idx in range(18):
                    k2 = idx % 2
                    qd = idx % 4
                    op("dve", lambda e: e.scalar_tensor_tensor(osc[k2][:, :], oc3[:, idx, :], msr[:, 32 + idx:33 + idx], subg[:, :],
                                                              ALU.mult, ALU.mult), reads=[T_oca[idx], T_mv, T_c2], writes=[T_osc[k2]])
                    op("pe", lambda e: e.transpose(PB[3][:, qd * 128:(qd + 1) * 128], osc[k2][:, :], ident[:]),
                       reads=[T_osc[k2], T_const], writes=[TQ3[qd]])
                    op("dve", lambda e: e.tensor_tensor(mixT[:, 4 + h, idx * 128:(idx + 1) * 128], PB[3][:, qd * 128:(qd + 1) * 128],
                                                       sbg[:, idx * 128:(idx + 1) * 128], ALU.mult),
                       reads=[TQ3[qd], T_sbg], writes=[T_mix])
            chk(3)
            phase_outproj(l, b, abwout_d, list(range(18)))
            chk(4)

        def pass_layer1(b):
            l = 1
            kb.barrier()
            phase_transposes(l, b)
            kb.barrier()
            cv = Carver()
            yT = cv.get(4 * NLAT)
            yT3 = yT.rearrange("p (j t) -> p j t", j=4)
            T_yT = [Trk() for _ in range(4)]
            hpad = [cv.get(NLAT + 32, BF16) for _ in range(2)]
            T_hp = [Trk(), Trk()]
            caS = [cv.get(512) for _ in range(2)]
            T_ca = [Trk(), Trk()]
            tt = [cv.get(512) for _ in range(2)]
            T_tt = [Trk(), Trk()]
            diag = cv.get(31 * 128, BF16)
            diag3 = diag.rearrange("p (k c) -> p k c", k=31)
            T_dg = Trk()
            wa, twa = wload(cdwin_d[:, 0:512], 512)
            wb, twb = wload(cdwin_d[:, 512:1024], 512)
            for k2 in range(2):
                op("dve", lambda e: e.memset(hpad[k2][:, :], 0.0), writes=[T_hp[k2]])
            for j in range(4):
                hp = hpad[j % 2]
                thp = T_hp[j % 2]
                for k in range(31):
                    op("dve", lambda e: e.tensor_scalar(diag3[:, k, :], identb[:, :], dwT[:, j * 31 + k:j * 31 + k + 1], None, ALU.mult),
                       reads=[T_c2], writes=[T_dg])
                for si, (t0, n) in enumerate(LSEGS):
                    k2 = si % 2
                    lt0 = t0 - NCTX
                    proj_fm(wa, twa, j * 128, t0, n, si % 2)
                    op("act", lambda e: e.copy(caS[k2][:, :], PB[si % 2][:, :]), reads=[TPB[si % 2]], writes=[T_ca[k2]])
                    proj_fm(wb, twb, j * 128, t0, n, 2 + si % 2)
                    op("act", lambda e: e.activation(tt[k2][:, :], PB[2 + si % 2][:, :], AF.Tanh, scale=0.5),
                       reads=[TPB[2 + si % 2]], writes=[T_tt[k2]])
                    op("dve", lambda e: e.scalar_tensor_tensor(hp[:, 15 + lt0:15 + lt0 + n], tt[k2][:, :], 1.0, caS[k2][:, :], ALU.add, ALU.mult),
                       reads=[T_tt[k2], T_ca[k2]], writes=[thp])
                for si, (t0, n) in enumerate(LSEGS):
                    lt0 = t0 - NCTX
                    cb = 4 + si % 2
                    for k in range(31):
                        op("pe", lambda e: e.matmul(PB[cb][:, :], diag3[:, k, :], hp[:, lt0 + k:lt0 + k + 512],
                                                    start=(k == 0), stop=(k == 30)),
                           reads=[T_dg, thp], writes=[TPB[cb]])
                    op("act", lambda e: e.activation(yT3[:, j, lt0:lt0 + 512], PB[cb][:, :], AF.Identity, bias=cvecs[:, j:j + 1]),
                       reads=[TPB[cb], T_const], writes=[T_yT[j]])
            kb.barrier()
            cv2 = Carver()
            cv2.off = 4 * NLAT * 4
            mean_bc = cv2.get(NLAT)
            rstd_bc = cv2.get(NLAT)
            T_mr = Trk()
            ysq = [cv2.get(512) for _ in range(2)]
            T_ysq = [Trk(), Trk()]
            tt = [cv2.get(512) for _ in range(2)]
            T_tt = [Trk(), Trk()]
            sg = [cv2.get(512) for _ in range(2)]
            T_sg = [Trk(), Trk()]
            yn = [cv2.get(512) for _ in range(2)]
            T_yn = [Trk(), Trk()]
            for si, (t0, n) in enumerate(LSEGS):
                lt0 = t0 - NCTX
                ts_ = slice(lt0, lt0 + 512)
                mbk, sbk = 0 + 2 * (si % 2), 1 + 2 * (si % 2)
                for j in range(4):
                    op("pe", lambda e: e.matmul(PB[mbk][:, :], onesdiv[:, :], yT3[:, j, ts_], start=(j == 0), stop=(j == 3)),
                       reads=[T_c2, T_yT[j]], writes=[TPB[mbk]])
                for j in range(4):
                    k2 = j % 2
                    op("act", lambda e: e.activation(ysq[k2][:, :], yT3[:, j, ts_], AF.Square), reads=[T_yT[j]], writes=[T_ysq[k2]])
                    op("pe", lambda e: e.matmul(PB[sbk][:, :], onesdiv[:, :], ysq[k2][:, :], start=(j == 0), stop=(j == 3)),
                       reads=[T_c2, T_ysq[k2]], writes=[TPB[sbk]])
                op("act", lambda e: e.copy(mean_bc[:, ts_], PB[mbk][:, :]), reads=[TPB[mbk]], writes=[T_mr])
                op("dve", lambda e: e.tensor_tensor(rstd_bc[:, ts_], mean_bc[:, ts_], mean_bc[:, ts_], ALU.mult), reads=[T_mr], writes=[T_mr])
                op("dve", lambda e: e.tensor_tensor(rstd_bc[:, ts_], PB[sbk][:, :], rstd_bc[:, ts_], ALU.subtract),
                   reads=[TPB[sbk], T_mr], writes=[T_mr])
                op("act", lambda e: e.activation(rstd_bc[:, ts_], rstd_bc[:, ts_], AF.Sqrt, bias=epsln), reads=[T_mr, T_c2], writes=[T_mr])
                op("dve", lambda e: e.reciprocal(rstd_bc[:, ts_], rstd_bc[:, ts_]), reads=[T_mr], writes=[T_mr])
            wg, twg = wload(cdwin_d[:, 1024:1536], 512)
            cnt = 0
            for j in range(4):
                for si, (t0, n) in enumerate(LSEGS):
                    lt0 = t0 - NCTX
                    ts_ = slice(lt0, lt0 + 512)
                    k2 = cnt % 2
                    bank = 4 + cnt % 4
                    cnt += 1
                    proj_fm(wg, twg, j * 128, t0, n, bank)
                    op("act", lambda e: e.activation(tt[k2][:, :], PB[bank][:, :], AF.Tanh, scale=0.5), reads=[TPB[bank]], writes=[T_tt[k2]])
                    op("dve", lambda e: e.scalar_tensor_tensor(sg[k2][:, :], tt[k2][:, :], 1.0, PB[bank][:, :], ALU.add, ALU.mult),
                       reads=[T_tt[k2], TPB[bank]], writes=[T_sg[k2]])
                    op("pool", lambda e: e.tensor_tensor(yn[k2][:, :], yT3[:, j, ts_], mean_bc[:, ts_], ALU.subtract),
                       reads=[T_yT[j], T_mr], writes=[T_yn[k2]])
                    op("pool", lambda e: e.tensor_tensor(yn[k2][:, :], yn[k2][:, :], rstd_bc[:, ts_], ALU.mult),
                       reads=[T_yn[k2], T_mr], writes=[T_yn[k2]])
                    op("dve", lambda e: e.tensor_scalar(yn[k2][:, :], yn[k2][:, :], cvecs[:, 4 + j:5 + j], cvecs[:, 8 + j:9 + j], ALU.mult, ALU.add),
                       reads=[T_yn[k2], T_const], writes=[T_yn[k2]])
                    op("act", lambda e: e.activation(tt[k2][:, :], yn[k2][:, :], AF.Tanh, scale=0.5), reads=[T_yn[k2], T_sg[k2]], writes=[T_tt[k2]])
                    op("dve", lambda e: e.scalar_tensor_tensor(yn[k2][:, :], tt[k2][:, :], 1.0, yn[k2][:, :], ALU.add, ALU.mult),
                       reads=[T_tt[k2], T_yn[k2]], writes=[T_yn[k2]])
                    op("dve", lambda e: e.scalar_tensor_tensor(mixT[:, j, t0:t0 + n], yn[k2][:, :], 0.25, sg[k2][:, :], ALU.mult, ALU.mult),
                       reads=[T_yn[k2], T_sg[k2]], writes=[T_mix])
            kb.barrier()
            cv = Carver()
            V1 = cv.get(18 * 2 * 80, BF16)
            V14 = V1.rearrange("p (i h c) -> p i h c", i=18, h=2)
            T_V = [Trk() for _ in range(18)]
            kT2 = cv.get(2 * NT, BF16)
            kT23 = kT2.rearrange("p (h t) -> p h t", h=2)
            T_kT = Trk()
            qT2f = cv.get(2 * NLAT, BF16)
            qT2 = qT2f.rearrange("p (c t) -> p c t", c=2)
            T_qT = Trk()
            op("pool", lambda e: e.memset(qT2f[:, :], 0.0), writes=[T_qT])
            stats2 = [cv.get(64) for _ in range(2)]
            T_st2 = [Trk(), Trk()]
            sdg = cv.get(NLAT)
            T_sdg = Trk()
            wk2 = cv.get(8 * 2 * 128, BF16)
            wk24 = wk2.rearrange("p (j h c) -> p j h c", j=8, h=2)
            T_wk2 = Trk()
            d_wk2 = kb.dsem()
            qraw = [cv.get(512, BF16) for _ in range(3)]
            T_qraw = [Trk() for _ in range(3)]
            t1 = [cv.get(512) for _ in range(3)]
            T_t1 = [Trk() for _ in range(3)]
            t2 = [cv.get(512) for _ in range(3)]
            T_t2 = [Trk() for _ in range(3)]
            pT = [cv.get(256, BF16) for _ in range(4)]
            T_pT = [Trk() for _ in range(4)]
            ocomb = [cv.get(128) for _ in range(2)]
            T_oc = [Trk(), Trk()]
            stats = cv.get(64)
            T_st = Trk()
            op("dve", lambda e: e.memset(V1[:, :], 1.0), writes=T_V)
            for kvh in range(2):
                for dup in range(2):
                    kb.dma("pool", wk24[:, :, kvh, dup * 64:(dup + 1) * 64],
                           cdwin_d[:, 2048 + kvh * 64:2048 + (kvh + 1) * 64].rearrange("(j p) c -> p j c", p=128),
                           d_wk2, writes=[T_wk2])
            wq, twq = wload(cdwin_d[:, 1536:2048], 512)
            wkv, twkv = wload(cdwin_d[:, 2048:2560], 512)
            wg2, twg2 = wload(cdwin_d[:, 2560:2816], 256)
            for i in range(18):
                bank = i % 4
                proj_tm(wkv, twkv, 128, 128, i, bank)
                op("act", lambda e: e.copy(V14[:, i, :, 0:64], PB[bank][:, 0:128].rearrange("p (h c) -> p h c", h=2)),
                   reads=[TPB[bank]], writes=[T_V[i]])
            for kvh in range(2):
                for si, (t0, n) in enumerate(SEGS):
                    k2 = si % 2
                    bank = si % 4
                    for j in range(8):
                        op("pe", lambda e: e.matmul(PB[bank][:, 0:n], wk24[:, j, kvh, :], uT[:, j, t0:t0 + n], start=(j == 0), stop=(j == 7)),
                           reads=[T_wk2] + [T_uT[i] for i in seg_tiles(t0, n)], writes=[TPB[bank]])
                    rope_evac(bank, n, t0, kT23[:, kvh, t0:t0 + n], T_kT, 4 + k2, qraw, T_qraw, t1, T_t1, t2, T_t2)
            rope_lag.flush()
            pctr = [0]
            att_lag = Lag()
            for cc in range(4):
                kvh = cc // 2
                for si, (t0, n) in enumerate(LSEGS):
                    k2 = si % 2
                    lt0 = t0 - NCTX
                    proj_fm(wq, twq, cc * 128, t0, n, si % 4)
                    rope_evac(si % 4, n, t0, [(qT2[0:64, 0, lt0:lt0 + n], 0, 64), (qT2[64:128, 1, lt0:lt0 + n], 64, 128)], T_qT, 4 + k2, qraw, T_qraw, t1, T_t1, t2, T_t2)
                rope_lag.flush()
                for si, (t0, n) in enumerate(LSEGS):
                    k2 = si % 2
                    bank = si % 4
                    lt0 = t0 - NCTX
                    if cc < 2:
                        proj_fm(wkv, twkv, 256 + cc * 128, t0, n, bank)
                    else:
                        proj_fm(wg2, twg2, (cc - 2) * 128, t0, n, bank)
                    op("act", lambda e: e.activation(t1[k2][:, 0:n], PB[bank][:, 0:n], AF.Tanh, scale=0.5),
                       reads=[TPB[bank]], writes=[T_t1[k2]])
                    op("dve", lambda e: e.scalar_tensor_tensor(sdg[:, lt0:lt0 + n], t1[k2][:, 0:n], 1.0, PB[bank][:, 0:n], ALU.add, ALU.mult),
                       reads=[T_t1[k2], TPB[bank]], writes=[T_sdg])
                steps = []
                for qb in range(16):
                    kts = []
                    if qb > 0:
                        kts.append((2 + qb - 1, 0))
                    kts.append((2 + qb, None))
                    if qb < 15:
                        kts.append((2 + qb + 1, 1))
                    kts += [(0, None), (1, None)]
                    for ki, (kt, mk) in enumerate(kts):
                        steps.append((qb, ki, kt, mk, ki == len(kts) - 1))

                def qk1(step, sbank):
                    qb, ki, kt, mk, last = step
                    q0 = qb * 128
                    op("pe", lambda e: e.matmul(PB[sbank][:, 0:256].rearrange("p (c q) -> p c q", c=2),
                                                kT23[:, kvh, kt * 128:(kt + 1) * 128],
                                                qT2[:, :, q0:q0 + 128], start=True, stop=(mk is None), skip_group_check=True),
                       reads=[T_kT, T_qT], writes=[TPB[sbank]])
                    if mk is not None:
                        op("pe", lambda e: e.matmul(PB[sbank][:, 0:256], identb[:, :], masks[:, mk, :], start=False, stop=True,
                                                    skip_group_check=True),
                           reads=[T_c2, T_constp], writes=[TPB[sbank]])

                for pre in range(2):
                    qk1(steps[pre], pre % 4)
                for i, step in enumerate(steps):
                    qb, ki, kt, mk, last = step
                    q0 = qb * 128
                    if i + 2 < len(steps):
                        qk1(steps[i + 2], (i + 2) % 4)
                    sbank = i % 4
                    ob = 4 + qb % 2
                    pi = pctr[0] % 4
                    pctr[0] += 1
                    op("act", lambda e: e.activation(pT[pi][:, :], PB[sbank][:, 0:256], AF.Exp, scale=0.125),
                       reads=[TPB[sbank]], writes=[T_pT[pi]])
                    for hl in range(2):
                        first = (ki == 0 and hl == 0)
                        op("pe", lambda e: e.matmul(PB[ob][:, hl * 80:hl * 80 + 66], pT[pi][:, hl * 128:(hl + 1) * 128],
                                                    V14[:, kt, kvh, 0:66], start=first, stop=last,
                                                    skip_group_check=True),
                           reads=[T_pT[pi], T_V[kt]], writes=[TPB[ob]])
                    if not last:
                        continue
                    k2 = qb % 2
                    for hl in range(2):
                        hd = cc * 2 + hl
                        op("dve", lambda e: e.tensor_scalar(stats2[k2][:, hl:hl + 1], PB[ob][:, hl * 80 + 64:hl * 80 + 65], sinkt[:, hd:hd + 1], None, ALU.add),
                           reads=[TPB[ob], T_c2], writes=[T_st2[k2]])
                        op("dve", lambda e: e.reciprocal(stats2[k2][:, 2 + hl:3 + hl], stats2[k2][:, hl:hl + 1]), reads=[T_st2[k2]], writes=[T_st2[k2]])
                        op("dve", lambda e: e.tensor_scalar(ocomb[k2][:, hl * 64:(hl + 1) * 64], PB[ob][:, hl * 80:hl * 80 + 64],
                                                           stats2[k2][:, 2 + hl:3 + hl], None, ALU.mult),
                           reads=[TPB[ob], T_st2[k2]], writes=[T_oc[k2]])

                    def post_b(cc=cc, qb=qb, q0=q0, k2=k2):
                        tb = 6 + qb % 2
                        op("pe", lambda e: e.transpose(PB[tb][:, 0:128], ocomb[k2][:, :], ident[:]), reads=[T_oc[k2], T_const], writes=[TPB[tb]])
                        op("dve", lambda e: e.scalar_tensor_tensor(mixT[:, 4 + cc, NCTX + q0:NCTX + q0 + 128], PB[tb][:, 0:128], 0.5,
                                                                  sdg[:, q0:q0 + 128], ALU.mult, ALU.mult),
                           reads=[TPB[tb], T_sdg], writes=[T_mix])
                    att_lag.push(post_b)
                att_lag.flush()
            phase_outproj(l, b, cdwout_d, list(range(2, 18)))

        try:
            chk(0)
            for b in range(2):
                pass_layer0(b)
            if not debug_h1:
                for b in range(2):
                    pass_layer1(b)
        except _Stop:
            pass
        kb.barrier()
        for ds in d_o:
            if ds.count:
                nc.sync.wait_ge(ds.sem, ds.count)
    return nc


def _consts():
    ident = np.eye(128, dtype=np.float32)
    rmat = np.zeros((128, 128), np.float32)
    for dp in range(128):
        partner = dp + 16 if (dp % 32) < 16 else dp - 16
        rmat[partner, dp] = 1.0
    m = 16
    inv = (10000.0 ** (-np.arange(m, dtype=np.float32) / m)).astype(np.float32)
    t = np.arange(NLAT)
    row = (t // 64).astype(np.float32)
    col = (t % 64).astype(np.float32)
    ang_r = (row[:, None] * inv[None, :]).astype(np.float32)
    ang_c = (col[:, None] * inv[None, :]).astype(np.float32)
    cos_t = np.ones((128, NT), np.float32)
    sin_t = np.zeros((128, NT), np.float32)
    for p in range(128):
        d = p % 64
        ang = ang_r if d < 32 else ang_c
        f = d % 16
        sign = -1.0 if (d % 32) < 16 else 1.0
        cos_t[p, NCTX:] = np.cos(ang[:, f])
        sin_t[p, NCTX:] = sign * np.sin(ang[:, f])
    ropeT = np.concatenate([cos_t, sin_t], axis=1)
    kk = np.arange(128)[:, None]
    qq = np.arange(128)[None, :]
    mp = np.where(kk >= qq, 0.0, -30000.0).astype(np.float32)
    mn = np.where(kk <= qq, 0.0, -30000.0).astype(np.float32)
    masks = np.concatenate([mp, mp, mn, mn], axis=1)
    return ident, rmat, ropeT, masks


_CACHE = {}


def _core_inputs(core, x, c, ctx, c_ctx, mod_w, mod_b, ln_g, ln_b, ab_w_in, ab_w_out, a_w_s, a_b_s,
                 a_norm_g, a_norm_b, b_lq1, b_lk1, b_lq2, b_lk2, b_subln_g, cd_w_in, cd_w_out,
                 c_dw_w, c_dw_b, c_norm_g, c_norm_b, d_sink, shared):
    b0 = 2 * core
    cvec = np.stack([c[b0], c[b0 + 1], c_ctx], axis=0)
    cvT = np.ascontiguousarray(cvec.reshape(3, 8, 128).transpose(2, 1, 0)).reshape(128, 24)
    d = dict(shared)
    d["x"] = np.ascontiguousarray(x[b0:b0 + 2])
    d["ctx"] = np.ascontiguousarray(ctx[b0:b0 + 2])
    d["cvT"] = cvT
    return d


def kernel(x, c, ctx, c_ctx, mod_w, mod_b, ln_g, ln_b, ab_w_in, ab_w_out, a_w_s, a_b_s,
           a_norm_g, a_norm_b, b_lq1, b_lk1, b_lq2, b_lk2, b_subln_g, cd_w_in, cd_w_out,
           c_dw_w, c_dw_b, c_norm_g, c_norm_b, d_sink, _debug_h1=False, _stage=99):
    f = lambda a: np.ascontiguousarray(np.asarray(a, dtype=np.float32))
    x, c, ctx, c_ctx = f(x), f(c), f(ctx), f(c_ctx)
    ident, rmat, ropeT, masks = _consts()
    shared = {
        "mod_w": f(mod_w),
        "mod_bT": np.ascontiguousarray(f(mod_b).reshape(2, 24, 128).transpose(2, 0, 1)).reshape(128, 48),
        "ln_g": f(ln_g), "ln_b": f(ln_b),
        "ab_w_in": f(ab_w_in)[0], "ab_w_out": f(ab_w_out)[0],
        "a_w_sT": np.ascontiguousarray(f(a_w_s)[0].transpose(2, 0, 1)).reshape(128, 512),
        "a_b_s": f(a_b_s)[0].reshape(1, 512),
        "a_norm_g": f(a_norm_g).reshape(1, 512), "a_norm_b": f(a_norm_b).reshape(1, 512),
        "lam_in": np.concatenate([f(b_lq1)[0], f(b_lk1)[0], f(b_lq2)[0], f(b_lk2)[0]]).reshape(1, 256),
        "b_subln_g": f(b_subln_g).reshape(1, 128),
        "cd_w_in": f(cd_w_in)[0], "cd_w_out": f(cd_w_out)[0],
        "dwT": np.ascontiguousarray(f(c_dw_w)[0].reshape(31, 4, 128).transpose(2, 1, 0)).reshape(128, 124),
        "cvecs": np.ascontiguousarray(np.stack([f(c_dw_b)[0], f(c_norm_g)[0], f(c_norm_b)[0]], 0)
                                      .reshape(3, 4, 128).transpose(2, 0, 1)).reshape(128, 12),
        "d_sink": f(d_sink).reshape(1, 8),
        "ident": ident, "rmat": rmat, "ropeT": ropeT, "masks": masks,
    }
    in_maps = []
    for core in range(8):
        in_maps.append(_core_inputs(core, x, c, ctx, c_ctx, None, None, None, None, None, None, None, None,
                                    None, None, None, None, None, None, None, None, None,
                                    None, None, None, None, None, shared))
    key = (bool(_debug_h1), _stage)
    if key not in _CACHE:
        _CACHE[key] = build_program(debug_h1=key[0], stage=_stage)
    nc = _CACHE[key]
    res = run_bass_kernel_spmd(nc, in_maps, core_ids=list(range(8)))
    if _debug_h1:
        return np.concatenate([r["h1"] for r in res.results], axis=0)
    return np.concatenate([r["out"] for r in res.results], axis=0).astype(np.float32)
```

```python
import math
import numpy as np
from contextlib import ExitStack
import concourse.bass as bass
import concourse.mybir as mybir
from concourse.bass_utils import run_bass_kernel_spmd

F32 = mybir.dt.float32
BF16 = mybir.dt.bfloat16
AF = mybir.ActivationFunctionType
ALU = mybir.AluOpType

NT = 2304
NCTX = 256
NLAT = 2048
D = 1024
LN_EPS = 1e-6
RMS_EPS = 1e-5
ALPHA = (2.0 * 2) ** 0.25
GC0 = math.sqrt(2.0 / math.pi)
GC1 = 0.044715
SEGS = [(0, 256), (256, 512), (768, 512), (1280, 512), (1792, 512)]
LSEGS = SEGS[1:]


class Trk:
    __slots__ = ("name", "w", "r")

    def __init__(self, name=""):
        self.name = name
        self.w = None
        self.r = {}


class DSem:
    def __init__(self, sem):
        self.sem = sem
        self.count = 0


class KB:
    def __init__(self, nc, es):
        self.nc = nc
        self.es = es
        self.eng = {"pe": nc.tensor, "act": nc.scalar, "dve": nc.vector, "pool": nc.gpsimd, "sp": nc.sync}
        self.esem = {}
        self.ecnt = {}
        for e in ("pe", "act", "dve", "pool"):
            self.esem[e] = es.enter_context(nc.semaphore("s_" + e))
            self.ecnt[e] = 0
        self.seen = {e: {} for e in self.eng}
        self.nsem = 0
        self.hooks = []
        self.bar_dsems = []

    def dsem(self):
        self.nsem += 1
        return DSem(self.es.enter_context(self.nc.semaphore(f"d{self.nsem}")))

    def _wait(self, e, reads, writes):
        evs = {}

        def add(ev):
            if ev is None:
                return
            k = id(ev[0])
            if k not in evs or evs[k][1] < ev[1]:
                evs[k] = ev

        for t in reads:
            add(t.w)
        for t in writes:
            add(t.w)
            for ev in t.r.values():
                add(ev)
        seen = self.seen[e]
        own = self.esem.get(e)
        for k, (sem, val) in evs.items():
            if e == "pe" and sem is own:
                continue
            if seen.get(k, 0) >= val:
                continue
            self.eng[e].wait_ge(sem, val)
            seen[k] = val

    def _post(self, ev, reads, writes):
        k = id(ev[0])
        for t in writes:
            t.w = ev
            t.r = {}
        for t in reads:
            t.r[k] = ev

    def op(self, e, fn, reads=(), writes=()):
        self._wait(e, reads, writes)
        ins = fn(self.eng[e])
        self.ecnt[e] += 1
        ins.then_inc(self.esem[e], 1)
        ev = (self.esem[e], self.ecnt[e])
        self._post(ev, reads, writes)
        return ev

    def dma(self, q, out, in_, ds, reads=(), writes=()):
        self._wait(q, reads, writes)
        ins = self.eng[q].dma_start(out=out, in_=in_)
        ds.count += 16
        ins.then_inc(ds.sem, 16)
        ev = (ds.sem, ds.count)
        self._post(ev, reads, writes)
        return ev

    def barrier(self):
        for h in self.hooks:
            h()
        for e in self.eng:
            seen = self.seen[e]
            for ds in self.bar_dsems:
                k = id(ds.sem)
                if ds.count and seen.get(k, 0) < ds.count:
                    self.eng[e].wait_ge(ds.sem, ds.count)
                    seen[k] = ds.count
            for f in ("pe", "act", "dve", "pool"):
                if f == e or self.ecnt[f] == 0:
                    continue
                k = id(self.esem[f])
                if seen.get(k, 0) >= self.ecnt[f]:
                    continue
                self.eng[e].wait_ge(self.esem[f], self.ecnt[f])
                seen[k] = self.ecnt[f]


class _Stop(Exception):
    pass


LAG_ON = True


class Lag:
    def __init__(self):
        self.p = None

    def push(self, fn):
        if not LAG_ON:
            fn()
            return
        old, self.p = self.p, fn
        if old:
            old()

    def flush(self):
        old, self.p = self.p, None
        if old:
            old()


def build_program(debug_h1=False, stage=99):
    nc = bass.Bass("TRN2", target_bir_lowering=False)

    def chk(n):
        if stage <= n:
            raise _Stop()

    def din(name, shape):
        return nc.dram_tensor(name, list(shape), F32, kind="ExternalInput").ap()

    x_d = din("x", [2, NLAT, D])
    ctx_d = din("ctx", [2, NCTX, D])
    cvT_d = din("cvT", [128, 24])
    modw_d = din("mod_w", [2, D, 3 * D])
    modbT_d = din("mod_bT", [128, 48])
    lng_d = din("ln_g", [2, D])
    lnb_d = din("ln_b", [2, D])
    abwin_d = din("ab_w_in", [D, 3584])
    abwout_d = din("ab_w_out", [D, D])
    awsT_d = din("a_w_sT", [128, 512])
    abs_d = din("a_b_s", [1, 512])
    ang_d = din("a_norm_g", [1, 512])
    anb_d = din("a_norm_b", [1, 512])
    lam_d = din("lam_in", [1, 256])
    subg_d = din("b_subln_g", [1, 128])
    cdwin_d = din("cd_w_in", [D, 2816])
    cdwout_d = din("cd_w_out", [D, D])
    dwT_d = din("dwT", [128, 124])
    cvec_d = din("cvecs", [128, 12])
    sink_d = din("d_sink", [1, 8])
    ident_d = din("ident", [128, 128])
    rmat_d = din("rmat", [128, 128])
    rope_d = din("ropeT", [128, 2 * NT])
    mask_d = din("masks", [128, 512])
    out_d = nc.dram_tensor("out", [2, NLAT, D], F32, kind="ExternalOutput").ap()
    if debug_h1:
        h1_d = nc.dram_tensor("h1", [2, NT, D], F32, kind="ExternalOutput").ap()
    else:
        h1_d = nc.dram_tensor("h1", [2, NT, D], F32).ap()

    with ExitStack() as es:
        kb = KB(nc, es)
        op = kb.op

        def sb(name, shape, dt=F32):
            return es.enter_context(nc.sbuf_tensor("sb_" + name, list(shape), dt))

        PB = [es.enter_context(nc.psum_tensor(f"pb{i}", [128, 512], F32)) for i in range(8)]
        TPB = [Trk(f"pb{i}") for i in range(8)]

        d_const = kb.dsem()
        T_const = Trk("const")
        d_constp = kb.dsem()
        T_constp = Trk("constp")
        ident = sb("ident", [128, 128])
        identb = sb("identb", [128, 128], BF16)
        rmat = sb("rmat", [128, 128], BF16)
        ropeT = sb("ropeT", [128, 2, NT])
        masks = sb("masks", [128, 2, 256], BF16)
        cvT = sb("cvT", [128, 24])
        modbT = sb("modbT", [128, 48])
        wsT = sb("wsT", [128, 4, 128], BF16)
        lamt = sb("lamt", [128, 256])
        subg = sb("subg", [128, 128])
        dwT = sb("dwT", [128, 124])
        cvecs = sb("cvecs", [128, 12])
        sinkt = sb("sinkt", [128, 8])
        ones_f = sb("ones_f", [128, 128])
        onesdiv = sb("onesdiv", [128, 128])
        sTb = sb("sTb", [128, 24], BF16)
        modT = sb("modT", [128, 144])
        small = sb("small", [128, 64])
        epsln = small[:, 0:1]
        epsrms = small[:, 1:2]
        epsln4 = small[:, 2:3]
        neglam = small[:, 3:4]

        def cdma(q, dst, src):
            if q == "pool":
                kb.dma(q, dst, src, d_constp, writes=[T_constp])
            else:
                kb.dma(q, dst, src, d_const, writes=[T_const])

        cdma("sp", ident[:], ident_d)
        cdma("pool", rmat[:], rmat_d)
        cdma("sp", ropeT[:], rope_d.rearrange("p (a t) -> p a t", a=2))
        cdma("pool", masks[:], mask_d.rearrange("p (a t) -> p a t", a=2))
        cdma("sp", cvT[:], cvT_d)
        cdma("sp", modbT[:], modbT_d)
        cdma("pool", wsT[:], awsT_d.rearrange("p (g q) -> p g q", g=4))
        cdma("sp", lamt[:], lam_d.partition_broadcast(128))
        cdma("sp", subg[:], subg_d.partition_broadcast(128))
        cdma("sp", dwT[:], dwT_d)
        cdma("sp", cvecs[:], cvec_d)
        cdma("sp", sinkt[:], sink_d.partition_broadcast(128))

        T_c2 = Trk("c2")
        RC = [T_const, T_c2]
        op("dve", lambda e: e.memset(ones_f[:], 1.0), writes=[T_c2])
        op("dve", lambda e: e.memset(onesdiv[:], 1.0 / 512.0), writes=[T_c2])
        op("dve", lambda e: e.memset(small[:, 0:1], LN_EPS), writes=[T_c2])
        op("dve", lambda e: e.memset(small[:, 1:2], RMS_EPS), writes=[T_c2])
        op("dve", lambda e: e.memset(small[:, 2:3], 4.0 * LN_EPS), writes=[T_c2])
        op("dve", lambda e: e.tensor_copy(identb[:], ident[:]), reads=[T_const], writes=[T_c2])
        lam_init0 = 0.8 - 0.6 * math.exp(-0.3 * 0)
        lprod = sb("lprod", [128, 128])
        op("dve", lambda e: e.tensor_tensor(lprod[:, 0:64], lamt[:, 0:64], lamt[:, 64:128], ALU.mult),
           reads=[T_const], writes=[T_c2])
        op("dve", lambda e: e.tensor_tensor(lprod[:, 64:128], lamt[:, 128:192], lamt[:, 192:256], ALU.mult),
           reads=[T_c2, T_const], writes=[T_c2])
        op("dve", lambda e: e.reduce_sum(small[:, 4:6], lprod[:].rearrange("p (a b) -> p a b", a=2),
                                        mybir.AxisListType.X), reads=[T_c2], writes=[T_c2])
        op("act", lambda e: e.activation(small[:, 6:8], small[:, 4:6], AF.Exp), reads=[T_c2], writes=[T_c2])
        op("dve", lambda e: e.scalar_tensor_tensor(small[:, 3:4], small[:, 7:8], -lam_init0, small[:, 6:7],
                                                  ALU.add, ALU.subtract), reads=[T_c2], writes=[T_c2])
        op("dve", lambda e: e.tensor_scalar(subg[:], subg[:], (1.0 - lam_init0) * 0.5, None, ALU.mult),
           reads=[T_const, T_c2], writes=[T_c2])
        op("act", lambda e: e.activation(sinkt[:], sinkt[:], AF.Exp), reads=[T_const, T_c2], writes=[T_c2])
        op("dve", lambda e: e.tensor_scalar(dwT[:], dwT[:], 0.5, None, ALU.mult), reads=[T_const, T_c2], writes=[T_c2])
        sct = sb("sct", [128, 24])
        op("act", lambda e: e.activation(sct[:], cvT[:], AF.Tanh, scale=0.5), reads=[T_const], writes=[T_c2])
        op("dve", lambda e: e.scalar_tensor_tensor(sct[:], sct[:], 1.0, cvT[:], ALU.add, ALU.mult),
           reads=[T_c2, T_const], writes=[T_c2])
        op("dve", lambda e: e.tensor_scalar(sTb[:], sct[:], 0.5, None, ALU.mult), reads=[T_c2], writes=[T_c2])

        NW = 3
        wslot = [sb(f"wslot{i}", [128, 8, 512], BF16) for i in range(NW)]
        T_w = [Trk(f"w{i}") for i in range(NW)]
        d_w = [kb.dsem() for _ in range(NW)]
        wctr = [0]

        def wload(src_ap, ncols):
            i = wctr[0] % NW
            wctr[0] += 1
            kb.dma("pool", wslot[i][:, :, 0:ncols], src_ap.rearrange("(j p) c -> p j c", p=128), d_w[i],
                   writes=[T_w[i]])
            return wslot[i], T_w[i]

        NS = 3
        srct = [None] * NS
        T_s = [Trk(f"s{i}") for i in range(NS)]
        d_s = [kb.dsem() for _ in range(NS)]
        sctr = [0]
        T_s8 = [Trk(f"s8_{i}") for i in range(8)]
        d_s8 = [kb.dsem() for _ in range(8)]

        def sload(src_ap):
            i = sctr[0] % NS
            sctr[0] += 1
            kb.dma("sp", srct[i][:], src_ap, d_s[i], writes=[T_s[i]])
            return srct[i], T_s[i]

        NO = 3
        outt = [None] * NO
        T_o = [Trk(f"o{i}") for i in range(NO)]
        d_o = [kb.dsem() for _ in range(NO)]
        octr = [0]
        kb.bar_dsems = d_o
        T_h1 = [[Trk(f"h1_{b}_{i}") for i in range(18)] for b in range(2)]

        T_mod = Trk("mod")
        for l in range(2):
            for g in range(3):
                for half in range(2):
                    c0 = g * 1024 + half * 512
                    ws, tw = wload(modw_d[l, :, c0:c0 + 512], 512)
                    for cc in range(4):
                        for j in range(8):
                            op("pe", lambda e: e.matmul(PB[7][:, cc * 4:cc * 4 + 3], ws[:, j, cc * 128:(cc + 1) * 128],
                                                        sTb[:, j * 3:(j + 1) * 3], start=(j == 0), stop=(j == 7)),
                               reads=[tw, T_c2], writes=[TPB[7]])
                    for cc in range(4):
                        k = g * 8 + half * 4 + cc
                        o0 = (l * 24 + k) * 3
                        op("dve", lambda e: e.tensor_scalar(modT[:, o0:o0 + 3], PB[7][:, cc * 4:cc * 4 + 3],
                                                           modbT[:, l * 24 + k:l * 24 + k + 1],
                                                           1.0 if g == 1 else 0.0, ALU.add, ALU.add),
                           reads=[TPB[7], T_const], writes=[T_mod])

        def mod_ap(l, k, r):
            o0 = (l * 24 + k) * 3 + r
            return modT[:, o0:o0 + 1]

        uT = sb("uT", [128, 8, NT], BF16)
        T_uT = [Trk(f"uT{i}") for i in range(18)]
        mixT = sb("mixT", [128, 8, NT], BF16)
        T_mix = Trk("mixT")
        T_gbc = Trk("gbc")
        T_diagf = [Trk(), Trk()]
        T_lngb = Trk("lngb")
        d_lngb = kb.dsem()
        T_ab = Trk("ab")
        d_ab = kb.dsem()
        ARENA = 84 * 1024
        arena = sb("arena", [128, ARENA // 4])

        class Carver:
            def __init__(self):
                self.off = 0

            def get(self, nelem, dt=F32):
                nbytes = nelem * (4 if dt == F32 else 2)
                nbytes = (nbytes + 31) // 32 * 32
                assert self.off + nbytes <= ARENA, (self.off, nbytes)
                a = arena[:, self.off // 4:(self.off + nbytes) // 4]
                self.off += nbytes
                if dt == BF16:
                    a = a.bitcast(BF16)[:, 0:nelem]
                return a

        def tile_rows(l, b, i):
            if l == 0:
                return ctx_d[b, i * 128:(i + 1) * 128, :] if i < 2 else x_d[b, (i - 2) * 128:(i - 1) * 128, :]
            return h1_d[b, i * 128:(i + 1) * 128, :]

        def seg_tiles(t0, n):
            return list(range(t0 // 128, (t0 + n) // 128))

        def proj_fm(ws, tw, c0, t0, n, bank):
            for j in range(8):
                op("pe", lambda e: e.matmul(PB[bank][:, 0:n], ws[:, j, c0:c0 + 128], uT[:, j, t0:t0 + n],
                                            start=(j == 0), stop=(j == 7)),
                   reads=[tw] + [T_uT[i] for i in seg_tiles(t0, n)], writes=[TPB[bank]])

        def proj_tm(ws, tw, c0, ncols, i, bank):
            for j in range(8):
                op("pe", lambda e: e.matmul(PB[bank][:, 0:ncols], uT[:, j, i * 128:(i + 1) * 128], ws[:, j, c0:c0 + ncols],
                                            start=(j == 0), stop=(j == 7)),
                   reads=[tw, T_uT[i]], writes=[TPB[bank]])

        def rsqrt_small(dst, src, eps_ap, trk, scale=1.0):
            op("act", lambda e: e.activation(dst, src, AF.Sqrt, bias=eps_ap, scale=scale), reads=[trk, T_c2], writes=[trk])
            op("dve", lambda e: e.reciprocal(dst, dst), reads=[trk], writes=[trk])

        rope_lag = Lag()
        kb.hooks.append(rope_lag.flush)

        rctr = [0]

        def rope_evac(bank, n, t0, dst, T_dst, rb, qraw, T_qraw, t1, T_t1, t2, T_t2):
            k = rctr[0] % 3
            rctr[0] += 1
            rb = 4 + k
            qraw, T_qraw, t1, T_t1, t2, T_t2 = qraw[k], T_qraw[k], t1[k], T_t1[k], t2[k], T_t2[k]
            op("act", lambda e: e.copy(qraw[:, 0:n], PB[bank][:, 0:n]), reads=[TPB[bank]], writes=[T_qraw])
            rope_lag.push(lambda: rope_rest(n, t0, dst, T_dst, rb, qraw, T_qraw, t1, T_t1, t2, T_t2))

        def rope_rest(n, t0, dst, T_dst, rb, qraw, T_qraw, t1, T_t1, t2, T_t2):
            op("pe", lambda e: e.matmul(PB[rb][:, 0:n], rmat[:], qraw[:, 0:n], start=True, stop=True),
               reads=[T_qraw, T_constp], writes=[TPB[rb]])
            op("dve", lambda e: e.tensor_tensor(t1[:, 0:n], qraw[:, 0:n], ropeT[:, 0, t0:t0 + n], ALU.mult),
               reads=[T_qraw, T_const], writes=[T_t1])
            op("dve", lambda e: e.tensor_tensor(t2[:, 0:n], PB[rb][:, 0:n], ropeT[:, 1, t0:t0 + n], ALU.mult),
               reads=[TPB[rb], T_const], writes=[T_t2])
            if not isinstance(dst, list):
                dst = [(dst, 0, 128)]
            for (dap, p0, p1) in dst:
                op("pool", lambda e: e.tensor_tensor(dap, t1[p0:p1, 0:n], t2[p0:p1, 0:n], ALU.add),
                   reads=[T_t1, T_t2], writes=[T_dst])

        def build_gate_bc(l, r, slot, gbc, diagf):
            for cc in range(8):
                dg = diagf[cc % 2]
                op("dve", lambda e: e.tensor_scalar(dg[:], ident[:], mod_ap(l, 16 + cc, r), None, ALU.mult),
                   reads=[T_const, T_mod], writes=[T_diagf[cc % 2]])
                op("pe", lambda e: e.matmul(PB[7][:, (cc % 4) * 128:(cc % 4 + 1) * 128], ones_f[:], dg[:],
                                            start=True, stop=True),
                   reads=[T_c2, T_diagf[cc % 2]], writes=[TPB[7]])
                if cc % 4 == 3:
                    h = cc // 4
                    op("act", lambda e: e.copy(gbc[:, slot, h * 512:(h + 1) * 512], PB[7][:, :]),
                       reads=[TPB[7]], writes=[T_gbc])

        def sload_dep(l, b, i):
            idx = sctr[0] % NS
            sctr[0] += 1
            rd = [T_h1[b][i]] if l == 1 else []
            kb.dma("sp", srct[idx][:], tile_rows(l, b, i), d_s[idx], reads=rd, writes=[T_s[idx]])
            return srct[idx], T_s[idx]

        def phase_transposes(l, b):
            cvt = Carver()
            NSL = 8
            slots = [cvt.get(D) for _ in range(NSL)]
            groups = [[0, 1], [2, 3, 4, 5], [6, 7, 8, 9], [10, 11, 12, 13], [14, 15, 16, 17]]
            bctr = 0
            for gi, grp in enumerate(groups):
                r = 2 if grp[0] < 2 else b
                loaded = []
                for i in grp:
                    idx = sctr[0] % NSL
                    sctr[0] += 1
                    rd = [T_h1[b][i]] if l == 1 else []
                    kb.dma("sp", slots[idx][:, :], tile_rows(l, b, i), d_s8[idx], reads=rd, writes=[T_s8[idx]])
                    loaded.append((slots[idx], T_s8[idx]))
                ng = len(grp)
                t0 = grp[0] * 128
                for j in range(8):
                    bank = bctr % 8
                    bctr += 1
                    for q, (st, ts) in enumerate(loaded):
                        op("pe", lambda e: e.transpose(PB[bank][:, q * 128:(q + 1) * 128], st[:, j * 128:(j + 1) * 128], ident[:]),
                           reads=[ts, T_const], writes=[TPB[bank]])
                    wr = [T_uT[i] for i in grp]
                    if j % 2 == 0:
                        op("dve", lambda e: e.tensor_scalar(uT[:, j, t0:t0 + ng * 128], PB[bank][:, 0:ng * 128],
                                                           mod_ap(l, 8 + j, r), mod_ap(l, j, r), ALU.mult, ALU.add),
                           reads=[TPB[bank], T_mod], writes=wr)
                    else:
                        op("act", lambda e: e.activation(uT[:, j, t0:t0 + ng * 128], PB[bank][:, 0:ng * 128], AF.Identity,
                                                        bias=mod_ap(l, j, r), scale=mod_ap(l, 8 + j, r)),
                           reads=[TPB[bank], T_mod], writes=wr)

        def phase_outproj(l, b, wout_d, tiles):
            kb.barrier()
            cv = Carver()
            for k in range(NS):
                srct[k] = cv.get(D)
            for k in range(NO):
                outt[k] = cv.get(D)
            zt = [cv.get(D) for _ in range(4)]
            T_z = [Trk() for _ in range(4)]
            statsl = [cv.get(64) for _ in range(4)]
            T_stl = [Trk() for _ in range(4)]
            lngb = cv.get(2 * D).rearrange("p (a d) -> p a d", a=2)
            gbc = cv.get(2 * D).rearrange("p (a d) -> p a d", a=2)
            diagf = [cv.get(128) for _ in range(2)]
            kb.dma("sp", lngb[:, 0, :], lng_d[l:l + 1, :].partition_broadcast(128), d_lngb, writes=[T_lngb])
            kb.dma("sp", lngb[:, 1, :], lnb_d[l:l + 1, :].partition_broadcast(128), d_lngb, writes=[T_lngb])
            build_gate_bc(l, b, 0, gbc, diagf)
            if l == 0:
                build_gate_bc(l, 2, 1, gbc, diagf)
            w0, tw0 = wload(wout_d[:, 0:512], 512)
            w1, tw1 = wload(wout_d[:, 512:1024], 512)
            o_lag = Lag()
            for n_i, i in enumerate(tiles):
                st, ts = sload_dep(l, b, i)
                gs = 1 if i < 2 else 0
                pb0 = (n_i % 3) * 2
                for half, (ws, tw) in enumerate(((w0, tw0), (w1, tw1))):
                    for j in range(8):
                        op("pe", lambda e: e.matmul(PB[pb0 + half][:, :], mixT[:, j, i * 128:(i + 1) * 128], ws[:, j, :],
                                                    start=(j == 0), stop=(j == 7)),
                           reads=[T_mix, tw], writes=[TPB[pb0 + half]])
                z = zt[n_i % 4]
                tz = T_z[n_i % 4]
                stats = statsl[n_i % 4]
                T_st = T_stl[n_i % 4]
                for half in range(2):
                    hs = slice(half * 512, (half + 1) * 512)
                    op("dve", lambda e: e.tensor_tensor(z[:, hs], PB[pb0 + half][:, :], gbc[:, gs, hs], ALU.mult),
                       reads=[TPB[pb0 + half], T_gbc], writes=[tz])
                    op("dve", lambda e: e.scalar_tensor_tensor(z[:, hs], st[:, hs], ALPHA, z[:, hs], ALU.mult, ALU.add),
                       reads=[ts, tz], writes=[tz])
                    op("dve", lambda e: e.bn_stats(stats[:, half * 6:(half + 1) * 6], z[:, hs]), reads=[tz], writes=[T_st])
                op("dve", lambda e: e.bn_aggr(stats[:, 16:18], stats[:, 0:12].rearrange("p (a b) -> p a b", a=2)),
                   reads=[T_st], writes=[T_st])
                def tail(l=l, b=b, i=i, z=z, tz=tz, stats=stats, T_st=T_st):
                    rsqrt_small(stats[:, 18:19], stats[:, 17:18], epsln, T_st)
                    op("dve", lambda e: e.scalar_tensor_tensor(stats[:, 19:20], stats[:, 16:17], -1.0, stats[:, 18:19],
                                                              ALU.mult, ALU.mult), reads=[T_st], writes=[T_st])
                    oi = octr[0] % NO
                    octr[0] += 1
                    ot, to = outt[oi], T_o[oi]
                    op("act", lambda e: e.activation(z[:, :], z[:, :], AF.Identity, bias=stats[:, 19:20], scale=stats[:, 18:19]),
                       reads=[tz, T_st], writes=[tz])
                    op("dve", lambda e: e.tensor_tensor(z[:, :], z[:, :], lngb[:, 0, :], ALU.mult), reads=[tz, T_lngb], writes=[tz])
                    op("pool", lambda e: e.tensor_tensor(ot[:, :], z[:, :], lngb[:, 1, :], ALU.add), reads=[tz, T_lngb], writes=[to])
                    if l == 0:
                        kb.dma("sp", h1_d[b, i * 128:(i + 1) * 128, :], ot[:, :], d_o[oi], reads=[to], writes=[T_h1[b][i]])
                    else:
                        kb.dma("sp", out_d[b, (i - 2) * 128:(i - 1) * 128, :], ot[:, :], d_o[oi], reads=[to])
                o_lag.push(tail)
            o_lag.flush()

        def gelu2(dst, T_dst, bank, n, sq, T_sq, tt, T_tt):
            op("act", lambda e: e.activation(sq[:, 0:n], PB[bank][:, 0:n], AF.Square), reads=[TPB[bank]], writes=[T_sq])
            op("dve", lambda e: e.tensor_scalar(sq[:, 0:n], sq[:, 0:n], GC1, 1.0, ALU.mult, ALU.add), reads=[T_sq], writes=[T_sq])
            op("dve", lambda e: e.tensor_tensor(sq[:, 0:n], sq[:, 0:n], PB[bank][:, 0:n], ALU.mult),
               reads=[T_sq, TPB[bank]], writes=[T_sq])
            op("act", lambda e: e.activation(tt[:, 0:n], sq[:, 0:n], AF.Tanh, scale=GC0), reads=[T_sq], writes=[T_tt])
            return op("dve", lambda e: e.scalar_tensor_tensor(dst, tt[:, 0:n], 1.0, PB[bank][:, 0:n], ALU.add, ALU.mult),
                      reads=[T_tt, TPB[bank]], writes=[T_dst])

        def pass_layer0(b):
            l = 0
            kb.barrier()
            phase_transposes(l, b)
            chk(1)
            kb.barrier()
            cv = Carver()
            vn = cv.get(18 * 512, BF16)
            vn3 = vn.rearrange("p (i c) -> p i c", i=18)
            T_vn = [Trk() for _ in range(18)]
            Gu = cv.get(NT)
            T_Gu = Trk()
            sq = [cv.get(512) for _ in range(4)]
            T_sq = [Trk() for _ in range(4)]
            tt = [cv.get(512) for _ in range(4)]
            T_tt = [Trk() for _ in range(4)]
            g2 = [cv.get(512) for _ in range(4)]
            T_g2 = [Trk() for _ in range(4)]
            statsA = [cv.get(64) for _ in range(4)]
            T_stA = [Trk() for _ in range(4)]
            actr = [0]
            a_lag = Lag()
            bsrep = cv.get(4 * 512).rearrange("p (g q) -> p g q", g=4)
            angb = cv.get(2 * 512).rearrange("p (a q) -> p a q", a=2)
            for rep in range(4):
                kb.dma("sp", bsrep[:, :, rep * 128:(rep + 1) * 128],
                       abs_d.rearrange("o (g q) -> o g q", g=4).partition_broadcast(128), d_ab, writes=[T_ab])
            kb.dma("sp", angb[:, 0, :], ang_d.partition_broadcast(128), d_ab, writes=[T_ab])
            kb.dma("sp", angb[:, 1, :], anb_d.partition_broadcast(128), d_ab, writes=[T_ab])
            wv, twv = wload(abwin_d[:, 512:1024], 512)
            wu, twu = wload(abwin_d[:, 0:512], 512)
            wg, twg = wload(abwin_d[:, 1024:1536], 512)
            for i in range(18):
                bank = i % 4
                k2 = i % 4
                stats, T_st = statsA[k2], T_stA[k2]
                proj_tm(wv, twv, 0, 512, i, bank)
                gelu2(g2[k2][:, :], T_g2[k2], bank, 512, sq[k2], T_sq[k2], tt[k2], T_tt[k2])
                op("dve", lambda e: e.bn_stats(stats[:, 0:6], g2[k2][:, :]), reads=[T_g2[k2]], writes=[T_st])
                op("dve", lambda e: e.bn_aggr(stats[:, 16:18], stats[:, 0:6]), reads=[T_st], writes=[T_st])

                def ln_tail(i=i, k2=k2, stats=stats, T_st=T_st):
                    rsqrt_small(stats[:, 18:19], stats[:, 17:18], epsln4, T_st)
                    op("dve", lambda e: e.tensor_scalar(g2[k2][:, :], g2[k2][:, :], stats[:, 16:17], stats[:, 18:19],
                                                       ALU.subtract, ALU.mult), reads=[T_g2[k2], T_st], writes=[T_g2[k2]])
                    op("pool", lambda e: e.tensor_tensor(g2[k2][:, :], g2[k2][:, :], angb[:, 0, :], ALU.mult),
                       reads=[T_g2[k2], T_ab], writes=[T_g2[k2]])
                    op("pool", lambda e: e.tensor_tensor(vn3[:, i, :], g2[k2][:, :], angb[:, 1, :], ALU.add),
                       reads=[T_g2[k2], T_ab], writes=[T_vn[i]])
                a_lag.push(ln_tail)
            a_lag.flush()
            for g in range(4):
                for si, (t0, n) in enumerate(SEGS):
                    bank = actr[0] % 4
                    k2 = actr[0] % 4
                    actr[0] += 1
                    proj_fm(wu, twu, g * 128, t0, n, bank)
                    gelu2(Gu[:, t0:t0 + n], T_Gu, bank, n, sq[k2], T_sq[k2], tt[k2], T_tt[k2])
                for si, (t0, n) in enumerate(SEGS):
                    bank = actr[0] % 4
                    k2 = actr[0] % 4
                    actr[0] += 1
                    proj_fm(wg, twg, g * 128, t0, n, bank)
                    op("act", lambda e: e.activation(tt[k2][:, 0:n], PB[bank][:, 0:n], AF.Tanh, scale=0.5),
                       reads=[TPB[bank]], writes=[T_tt[k2]])
                    op("dve", lambda e: e.scalar_tensor_tensor(tt[k2][:, 0:n], tt[k2][:, 0:n], 1.0, PB[bank][:, 0:n], ALU.add, ALU.mult),
                       reads=[T_tt[k2], TPB[bank]], writes=[T_tt[k2]])
                    op("dve", lambda e: e.scalar_tensor_tensor(Gu[:, t0:t0 + n], tt[k2][:, 0:n], 0.25, Gu[:, t0:t0 + n], ALU.mult, ALU.mult),
                       reads=[T_tt[k2], T_Gu], writes=[T_Gu])
                    mb = 4 + actr[0] % 4
                    tl = seg_tiles(t0, n)
                    for ci, c in enumerate(tl):
                        op("pe", lambda e: e.matmul(PB[mb][:, ci * 128:(ci + 1) * 128], vn3[:, c, g * 128:(g + 1) * 128], wsT[:, g, :],
                                                    start=True, stop=True),
                           reads=[T_vn[c], T_constp], writes=[TPB[mb]])
                    op("dve", lambda e: e.tensor_tensor(sq[k2][:, 0:n], PB[mb][:, 0:n], bsrep[:, g, 0:n], ALU.add),
                       reads=[TPB[mb], T_ab], writes=[T_sq[k2]])
                    op("pool", lambda e: e.tensor_tensor(mixT[:, g, t0:t0 + n], sq[k2][:, 0:n], Gu[:, t0:t0 + n], ALU.mult),
                       reads=[T_sq[k2], T_Gu], writes=[T_mix])
            chk(2)
            kb.barrier()
            cv = Carver()
            V1 = cv.get(18 * 4 * 144, BF16)
            V14 = V1.rearrange("p (i h c) -> p i h c", i=18, h=4)
            T_V = [Trk() for _ in range(18)]
            qT2f = cv.get(2 * NT, BF16)
            qT2 = qT2f.rearrange("p (c t) -> p c t", c=2)
            kT = cv.get(NT, BF16)
            T_qT, T_kT = Trk(), Trk()
            op("pool", lambda e: e.memset(qT2f[:, :], 0.0), writes=[T_qT])
            sbg = cv.get(NT)
            T_sbg = Trk()
            qraw = [cv.get(512, BF16) for _ in range(3)]
            T_qraw = [Trk() for _ in range(3)]
            t1 = [cv.get(512) for _ in range(3)]
            T_t1 = [Trk() for _ in range(3)]
            t2 = [cv.get(512) for _ in range(3)]
            T_t2 = [Trk() for _ in range(3)]
            pT = [cv.get(512, BF16) for _ in range(4)]
            T_pT = [Trk() for _ in range(4)]
            o1n = [cv.get(128) for _ in range(2)]
            T_o1n = [Trk(), Trk()]
            oc_all = cv.get(18 * 128)
            oc3 = oc_all.rearrange("p (i c) -> p i c", i=18)
            T_oca = [Trk() for _ in range(18)]
            mv_all = cv.get(64)
            mv3 = mv_all[:, 0:36].rearrange("p (i c) -> p i c", i=18)
            msr = cv.get(64)
            T_mv = Trk()
            osc = [cv.get(128) for _ in range(2)]
            T_osc = [Trk(), Trk()]
            TQ3 = [Trk() for _ in range(4)]
            stats = cv.get(64)
            T_st = Trk()
            op("dve", lambda e: e.memset(V1[:, :], 1.0), writes=T_V)
            wv, twv = wload(abwin_d[:, 2560:3072], 512)
            wq, twq = wload(abwin_d[:, 1536:2048], 512)
            wk, twk = wload(abwin_d[:, 2048:2560], 512)
            for i in range(18):
                bank = i % 3
                proj_tm(wv, twv, 0, 512, i, bank)
                op("act", lambda e: e.copy(V14[:, i, :, 0:128], PB[bank][:, :].rearrange("p (h c) -> p h c", h=4)),
                   reads=[TPB[bank]], writes=[T_V[i]])
            chk(2.2)
            wg, twg = wload(abwin_d[:, 3072:3584], 512)
            pctr = [0]
            for h in range(4):
                for si, (t0, n) in enumerate(SEGS):
                    k2 = si % 2
                    proj_fm(wq, twq, h * 128, t0, n, si % 3)
                    rope_evac(si % 3, n, t0, [(qT2[0:64, 0, t0:t0 + n], 0, 64), (qT2[64:128, 1, t0:t0 + n], 64, 128)], T_qT, 4 + k2, qraw, T_qraw, t1, T_t1, t2, T_t2)
                for si, (t0, n) in enumerate(SEGS):
                    k2 = si % 2
                    proj_fm(wk, twk, h * 128, t0, n, si % 3)
                    rope_evac(si % 3, n, t0, kT[:, t0:t0 + n], T_kT, 4 + k2, qraw, T_qraw, t1, T_t1, t2, T_t2)
                rope_lag.flush()
                for si, (t0, n) in enumerate(SEGS):
                    k2 = si % 2
                    bank = si % 3
                    proj_fm(wg, twg, h * 128, t0, n, bank)
                    op("act", lambda e: e.activation(t1[k2][:, 0:n], PB[bank][:, 0:n], AF.Tanh, scale=0.5),
                       reads=[TPB[bank]], writes=[T_t1[k2]])
                    op("dve", lambda e: e.scalar_tensor_tensor(sbg[:, t0:t0 + n], t1[k2][:, 0:n], 1.0, PB[bank][:, 0:n], ALU.add, ALU.mult),
                       reads=[T_t1[k2], TPB[bank]], writes=[T_sbg])
                chk(2.4)
                qtiles = [(0, [0, 1])] + [(256 + 256 * qi, list(range(18))) for qi in range(8)]
                for qn, (q0, kts) in enumerate(qtiles):
                    ob = 4 + 2 * (qn % 2)

                    def qk(kt, sbank):
                        op("pe", lambda e: e.matmul(PB[sbank][:, :].rearrange("p (c q) -> p c q", c=2),
                                                    kT[:, kt * 128:(kt + 1) * 128],
                                                    qT2[:, :, q0:q0 + 256], start=True, stop=True),
                           reads=[T_kT, T_qT], writes=[TPB[sbank]])

                    for pre in range(min(2, len(kts))):
                        qk(kts[pre], pre % 3)
                    for ki, kt in enumerate(kts):
                        sbank = ki % 3
                        if ki + 2 < len(kts):
                            qk(kts[ki + 2], (ki + 2) % 3)
                        pi = pctr[0] % 4
                        pctr[0] += 1
                        op("act", lambda e: e.activation(pT[pi][:, :], PB[sbank][:, :], AF.Exp, scale=0.125),
                           reads=[TPB[sbank]], writes=[T_pT[pi]])
                        chk(2.5)
                        for c in range(2):
                            for s in range(2):
                                first = (ki == 0 and s == 0)
                                op("pe", lambda e: e.matmul(PB[ob + c][:, s * 144:s * 144 + 130],
                                                            pT[pi][:, c * 256 + s * 128:c * 256 + (s + 1) * 128],
                                                            V14[:, kt, h, 0:130], start=first, stop=(ki == len(kts) - 1),
                                                            skip_group_check=True),
                                   reads=[T_pT[pi], T_V[kt]], writes=[TPB[ob + c]])
                    chk(2.6)
                    for s in range(2):
                        k2 = s
                        idx = (q0 + s * 128) // 128
                        op("dve", lambda e: e.reciprocal(stats[:, 0:1], PB[ob][:, s * 144 + 128:s * 144 + 129]),
                           reads=[TPB[ob]], writes=[T_st])
                        op("dve", lambda e: e.reciprocal(stats[:, 1:2], PB[ob + 1][:, s * 144 + 128:s * 144 + 129]),
                           reads=[TPB[ob + 1]], writes=[T_st])
                        op("dve", lambda e: e.tensor_scalar(o1n[k2][:, :], PB[ob + 1][:, s * 144:s * 144 + 128], stats[:, 1:2], neglam,
                                                           ALU.mult, ALU.mult), reads=[TPB[ob + 1], T_st, T_c2], writes=[T_o1n[k2]])
                        op("dve", lambda e: e.scalar_tensor_tensor(oc3[:, idx, :], PB[ob][:, s * 144:s * 144 + 128], stats[:, 0:1], o1n[k2][:, :],
                                                                  ALU.mult, ALU.add), reads=[TPB[ob], T_st, T_o1n[k2]], writes=[T_oca[idx]])
                        op("dve", lambda e: e.bn_stats(stats[:, 8:14], oc3[:, idx, :]), reads=[T_oca[idx]], writes=[T_st])
                        op("dve", lambda e: e.bn_aggr(mv3[:, idx, :], stats[:, 8:14]), reads=[T_st], writes=[T_mv])
                op("dve", lambda e: e.tensor_tensor(msr[:, 0:18], mv3[:, :, 0], mv3[:, :, 0], ALU.mult), reads=[T_mv], writes=[T_mv])
                op("dve", lambda e: e.tensor_tensor(msr[:, 0:18], msr[:, 0:18], mv3[:, :, 1], ALU.add), reads=[T_mv], writes=[T_mv])
                rsqrt_small(msr[:, 32:50], msr[:, 0:18], epsrms, T_mv)
                for idx in range(18):
                    k2 = idx % 2
                    qd = idx % 4
                    op("dve", lambda e: e.scalar_tensor_tensor(osc[k2][:, :], oc3[:, idx, :], msr[:, 32 + idx:33 + idx], subg[:, :],
                                                              ALU.mult, ALU.mult), reads=[T_oca[idx], T_mv, T_c2], writes=[T_osc[k2]])
                    op("pe", lambda e: e.transpose(PB[3][:, qd * 128:(qd + 1) * 128], osc[k2][:, :], ident[:]),
                       reads=[T_osc[k2], T_const], writes=[TQ3[qd]])
                    op("dve", lambda e: e.tensor_tensor(mixT[:, 4 + h, idx * 128:(idx + 1) * 128], PB[3][:, qd * 128:(qd + 1) * 128],
                                                       sbg[:, idx * 128:(idx + 1) * 128], ALU.mult),
                       reads=[TQ3[qd], T_sbg], writes=[T_mix])
            chk(3)
            phase_outproj(l, b, abwout_d, list(range(18)))
            chk(4)

        def pass_layer1(b):
            l = 1
            kb.barrier()
            phase_transposes(l, b)
            kb.barrier()
            cv = Carver()
            yT = cv.get(4 * NLAT)
            yT3 = yT.rearrange("p (j t) -> p j t", j=4)
            T_yT = [Trk() for _ in range(4)]
            hpad = [cv.get(NLAT + 32, BF16) for _ in range(2)]
            T_hp = [Trk(), Trk()]
            caS = [cv.get(512) for _ in range(2)]
            T_ca = [Trk(), Trk()]
            tt = [cv.get(512) for _ in range(2)]
            T_tt = [Trk(), Trk()]
            diag = cv.get(31 * 128, BF16)
            diag3 = diag.rearrange("p (k c) -> p k c", k=31)
            T_dg = Trk()
            wa, twa = wload(cdwin_d[:, 0:512], 512)
            wb, twb = wload(cdwin_d[:, 512:1024], 512)
            for k2 in range(2):
                op("dve", lambda e: e.memset(hpad[k2][:, :], 0.0), writes=[T_hp[k2]])
            for j in range(4):
                hp = hpad[j % 2]
                thp = T_hp[j % 2]
                for k in range(31):
                    op("dve", lambda e: e.tensor_scalar(diag3[:, k, :], identb[:, :], dwT[:, j * 31 + k:j * 31 + k + 1], None, ALU.mult),
                       reads=[T_c2], writes=[T_dg])
                for si, (t0, n) in enumerate(LSEGS):
                    k2 = si % 2
                    lt0 = t0 - NCTX
                    proj_fm(wa, twa, j * 128, t0, n, si % 2)
                    op("act", lambda e: e.copy(caS[k2][:, :], PB[si % 2][:, :]), reads=[TPB[si % 2]], writes=[T_ca[k2]])
                    proj_fm(wb, twb, j * 128, t0, n, 2 + si % 2)
                    op("act", lambda e: e.activation(tt[k2][:, :], PB[2 + si % 2][:, :], AF.Tanh, scale=0.5),
                       reads=[TPB[2 + si % 2]], writes=[T_tt[k2]])
                    op("dve", lambda e: e.scalar_tensor_tensor(hp[:, 15 + lt0:15 + lt0 + n], tt[k2][:, :], 1.0, caS[k2][:, :], ALU.add, ALU.mult),
                       reads=[T_tt[k2], T_ca[k2]], writes=[thp])
                for si, (t0, n) in enumerate(LSEGS):
                    lt0 = t0 - NCTX
                    cb = 4 + si % 2
                    for k in range(31):
                        op("pe", lambda e: e.matmul(PB[cb][:, :], diag3[:, k, :], hp[:, lt0 + k:lt0 + k + 512],
                                                    start=(k == 0), stop=(k == 30)),
                           reads=[T_dg, thp], writes=[TPB[cb]])
                    op("act", lambda e: e.activation(yT3[:, j, lt0:lt0 + 512], PB[cb][:, :], AF.Identity, bias=cvecs[:, j:j + 1]),
                       reads=[TPB[cb], T_const], writes=[T_yT[j]])
            kb.barrier()
            cv2 = Carver()
            cv2.off = 4 * NLAT * 4
            mean_bc = cv2.get(NLAT)
            rstd_bc = cv2.get(NLAT)
            T_mr = Trk()
            ysq = [cv2.get(512) for _ in range(2)]
            T_ysq = [Trk(), Trk()]
            tt = [cv2.get(512) for _ in range(2)]
            T_tt = [Trk(), Trk()]
            sg = [cv2.get(512) for _ in range(2)]
            T_sg = [Trk(), Trk()]
            yn = [cv2.get(512) for _ in range(2)]
            T_yn = [Trk(), Trk()]
            for si, (t0, n) in enumerate(LSEGS):
                lt0 = t0 - NCTX
                ts_ = slice(lt0, lt0 + 512)
                mbk, sbk = 0 + 2 * (si % 2), 1 + 2 * (si % 2)
                for j in range(4):
                    op("pe", lambda e: e.matmul(PB[mbk][:, :], onesdiv[:, :], yT3[:, j, ts_], start=(j == 0), stop=(j == 3)),
                       reads=[T_c2, T_yT[j]], writes=[TPB[mbk]])
                for j in range(4):
                    k2 = j % 2
                    op("act", lambda e: e.activation(ysq[k2][:, :], yT3[:, j, ts_], AF.Square), reads=[T_yT[j]], writes=[T_ysq[k2]])
                    op("pe", lambda e: e.matmul(PB[sbk][:, :], onesdiv[:, :], ysq[k2][:, :], start=(j == 0), stop=(j == 3)),
                       reads=[T_c2, T_ysq[k2]], writes=[TPB[sbk]])
                op("act", lambda e: e.copy(mean_bc[:, ts_], PB[mbk][:, :]), reads=[TPB[mbk]], writes=[T_mr])
                op("dve", lambda e: e.tensor_tensor(rstd_bc[:, ts_], mean_bc[:, ts_], mean_bc[:, ts_], ALU.mult), reads=[T_mr], writes=[T_mr])
                op("dve", lambda e: e.tensor_tensor(rstd_bc[:, ts_], PB[sbk][:, :], rstd_bc[:, ts_], ALU.subtract),
                   reads=[TPB[sbk], T_mr], writes=[T_mr])
                op("act", lambda e: e.activation(rstd_bc[:, ts_], rstd_bc[:, ts_], AF.Sqrt, bias=epsln), reads=[T_mr, T_c2], writes=[T_mr])
                op("dve", lambda e: e.reciprocal(rstd_bc[:, ts_], rstd_bc[:, ts_]), reads=[T_mr], writes=[T_mr])
            wg, twg = wload(cdwin_d[:, 1024:1536], 512)
            cnt = 0
            for j in range(4):
                for si, (t0, n) in enumerate(LSEGS):
                    lt0 = t0 - NCTX
                    ts_ = slice(lt0, lt0 + 512)
                    k2 = cnt % 2
                    bank = 4 + cnt % 4
                    cnt += 1
                    proj_fm(wg, twg, j * 128, t0, n, bank)
                    op("act", lambda e: e.activation(tt[k2][:, :], PB[bank][:, :], AF.Tanh, scale=0.5), reads=[TPB[bank]], writes=[T_tt[k2]])
                    op("dve", lambda e: e.scalar_tensor_tensor(sg[k2][:, :], tt[k2][:, :], 1.0, PB[bank][:, :], ALU.add, ALU.mult),
                       reads=[T_tt[k2], TPB[bank]], writes=[T_sg[k2]])
                    op("pool", lambda e: e.tensor_tensor(yn[k2][:, :], yT3[:, j, ts_], mean_bc[:, ts_], ALU.subtract),
                       reads=[T_yT[j], T_mr], writes=[T_yn[k2]])
                    op("pool", lambda e: e.tensor_tensor(yn[k2][:, :], yn[k2][:, :], rstd_bc[:, ts_], ALU.mult),
                       reads=[T_yn[k2], T_mr], writes=[T_yn[k2]])
                    op("dve", lambda e: e.tensor_scalar(yn[k2][:, :], yn[k2][:, :], cvecs[:, 4 + j:5 + j], cvecs[:, 8 + j:9 + j], ALU.mult, ALU.add),
                       reads=[T_yn[k2], T_const], writes=[T_yn[k2]])
                    op("act", lambda e: e.activation(tt[k2][:, :], yn[k2][:, :], AF.Tanh, scale=0.5), reads=[T_yn[k2], T_sg[k2]], writes=[T_tt[k2]])
                    op("dve", lambda e: e.scalar_tensor_tensor(yn[k2][:, :], tt[k2][:, :], 1.0, yn[k2][:, :], ALU.add, ALU.mult),
                       reads=[T_tt[k2], T_yn[k2]], writes=[T_yn[k2]])
                    op("dve", lambda e: e.scalar_tensor_tensor(mixT[:, j, t0:t0 + n], yn[k2][:, :], 0.25, sg[k2][:, :], ALU.mult, ALU.mult),
                       reads=[T_yn[k2], T_sg[k2]], writes=[T_mix])
            kb.barrier()
            cv = Carver()
            V1 = cv.get(18 * 2 * 80, BF16)
            V14 = V1.rearrange("p (i h c) -> p i h c", i=18, h=2)
            T_V = [Trk() for _ in range(18)]
            kT2 = cv.get(2 * NT, BF16)
            kT23 = kT2.rearrange("p (h t) -> p h t", h=2)
            T_kT = Trk()
            qT2f = cv.get(2 * NLAT, BF16)
            qT2 = qT2f.rearrange("p (c t) -> p c t", c=2)
            T_qT = Trk()
            op("pool", lambda e: e.memset(qT2f[:, :], 0.0), writes=[T_qT])
            stats2 = [cv.get(64) for _ in range(2)]
            T_st2 = [Trk(), Trk()]
            sdg = cv.get(NLAT)
            T_sdg = Trk()
            wk2 = cv.get(8 * 2 * 128, BF16)
            wk24 = wk2.rearrange("p (j h c) -> p j h c", j=8, h=2)
            T_wk2 = Trk()
            d_wk2 = kb.dsem()
            qraw = [cv.get(512, BF16) for _ in range(3)]
            T_qraw = [Trk() for _ in range(3)]
            t1 = [cv.get(512) for _ in range(3)]
            T_t1 = [Trk() for _ in range(3)]
            t2 = [cv.get(512) for _ in range(3)]
            T_t2 = [Trk() for _ in range(3)]
            pT = [cv.get(256, BF16) for _ in range(4)]
            T_pT = [Trk() for _ in range(4)]
            ocomb = [cv.get(128) for _ in range(2)]
            T_oc = [Trk(), Trk()]
            stats = cv.get(64)
            T_st = Trk()
            op("dve", lambda e: e.memset(V1[:, :], 1.0), writes=T_V)
            for kvh in range(2):
                for dup in range(2):
                    kb.dma("pool", wk24[:, :, kvh, dup * 64:(dup + 1) * 64],
                           cdwin_d[:, 2048 + kvh * 64:2048 + (kvh + 1) * 64].rearrange("(j p) c -> p j c", p=128),
                           d_wk2, writes=[T_wk2])
            wq, twq = wload(cdwin_d[:, 1536:2048], 512)
            wkv, twkv = wload(cdwin_d[:, 2048:2560], 512)
            wg2, twg2 = wload(cdwin_d[:, 2560:2816], 256)
            for i in range(18):
                bank = i % 4
                proj_tm(wkv, twkv, 128, 128, i, bank)
                op("act", lambda e: e.copy(V14[:, i, :, 0:64], PB[bank][:, 0:128].rearrange("p (h c) -> p h c", h=2)),
                   reads=[TPB[bank]], writes=[T_V[i]])
            for kvh in range(2):
                for si, (t0, n) in enumerate(SEGS):
                    k2 = si % 2
                    bank = si % 4
                    for j in range(8):
                        op("pe", lambda e: e.matmul(PB[bank][:, 0:n], wk24[:, j, kvh, :], uT[:, j, t0:t0 + n], start=(j == 0), stop=(j == 7)),
                           reads=[T_wk2] + [T_uT[i] for i in seg_tiles(t0, n)], writes=[TPB[bank]])
                    rope_evac(bank, n, t0, kT23[:, kvh, t0:t0 + n], T_kT, 4 + k2, qraw, T_qraw, t1, T_t1, t2, T_t2)
            rope_lag.flush()
            pctr = [0]
            att_lag = Lag()
            for cc in range(4):
                kvh = cc // 2
                for si, (t0, n) in enumerate(LSEGS):
                    k2 = si % 2
                    lt0 = t0 - NCTX
                    proj_fm(wq, twq, cc * 128, t0, n, si % 4)
                    rope_evac(si % 4, n, t0, [(qT2[0:64, 0, lt0:lt0 + n], 0, 64), (qT2[64:128, 1, lt0:lt0 + n], 64, 128)], T_qT, 4 + k2, qraw, T_qraw, t1, T_t1, t2, T_t2)
                rope_lag.flush()
                for si, (t0, n) in enumerate(LSEGS):
                    k2 = si % 2
                    bank = si % 4
                    lt0 = t0 - NCTX
                    if cc < 2:
                        proj_fm(wkv, twkv, 256 + cc * 128, t0, n, bank)
                    else:
                        proj_fm(wg2, twg2, (cc - 2) * 128, t0, n, bank)
                    op("act", lambda e: e.activation(t1[k2][:, 0:n], PB[bank][:, 0:n], AF.Tanh, scale=0.5),
                       reads=[TPB[bank]], writes=[T_t1[k2]])
                    op("dve", lambda e: e.scalar_tensor_tensor(sdg[:, lt0:lt0 + n], t1[k2][:, 0:n], 1.0, PB[bank][:, 0:n], ALU.add, ALU.mult),
                       reads=[T_t1[k2], TPB[bank]], writes=[T_sdg])
                steps = []
                for qb in range(16):
                    kts = []
                    if qb > 0:
                        kts.append((2 + qb - 1, 0))
                    kts.append((2 + qb, None))
                    if qb < 15:
                        kts.append((2 + qb + 1, 1))
                    kts += [(0, None), (1, None)]
                    for ki, (kt, mk) in enumerate(kts):
                        steps.append((qb, ki, kt, mk, ki == len(kts) - 1))

                def qk1(step, sbank):
                    qb, ki, kt, mk, last = step
                    q0 = qb * 128
                    op("pe", lambda e: e.matmul(PB[sbank][:, 0:256].rearrange("p (c q) -> p c q", c=2),
                                                kT23[:, kvh, kt * 128:(kt + 1) * 128],
                                                qT2[:, :, q0:q0 + 128], start=True, stop=(mk is None), skip_group_check=True),
                       reads=[T_kT, T_qT], writes=[TPB[sbank]])
                    if mk is not None:
                        op("pe", lambda e: e.matmul(PB[sbank][:, 0:256], identb[:, :], masks[:, mk, :], start=False, stop=True,
                                                    skip_group_check=True),
                           reads=[T_c2, T_constp], writes=[TPB[sbank]])

                for pre in range(2):
                    qk1(steps[pre], pre % 4)
                for i, step in enumerate(steps):
                    qb, ki, kt, mk, last = step
                    q0 = qb * 128
                    if i + 2 < len(steps):
                        qk1(steps[i + 2], (i + 2) % 4)
                    sbank = i % 4
                    ob = 4 + qb % 2
                    pi = pctr[0] % 4
                    pctr[0] += 1
                    op("act", lambda e: e.activation(pT[pi][:, :], PB[sbank][:, 0:256], AF.Exp, scale=0.125),
                       reads=[TPB[sbank]], writes=[T_pT[pi]])
                    for hl in range(2):
                        first = (ki == 0 and hl == 0)
                        op("pe", lambda e: e.matmul(PB[ob][:, hl * 80:hl * 80 + 66], pT[pi][:, hl * 128:(hl + 1) * 128],
                                                    V14[:, kt, kvh, 0:66], start=first, stop=last,
                                                    skip_group_check=True),
                           reads=[T_pT[pi], T_V[kt]], writes=[TPB[ob]])
                    if not last:
                        continue
                    k2 = qb % 2
                    for hl in range(2):
                        hd = cc * 2 + hl
                        op("dve", lambda e: e.tensor_scalar(stats2[k2][:, hl:hl + 1], PB[ob][:, hl * 80 + 64:hl * 80 + 65], sinkt[:, hd:hd + 1], None, ALU.add),
                           reads=[TPB[ob], T_c2], writes=[T_st2[k2]])
                        op("dve", lambda e: e.reciprocal(stats2[k2][:, 2 + hl:3 + hl], stats2[k2][:, hl:hl + 1]), reads=[T_st2[k2]], writes=[T_st2[k2]])
                        op("dve", lambda e: e.tensor_scalar(ocomb[k2][:, hl * 64:(hl + 1) * 64], PB[ob][:, hl * 80:hl * 80 + 64],
                                                           stats2[k2][:, 2 + hl:3 + hl], None, ALU.mult),
                           reads=[TPB[ob], T_st2[k2]], writes=[T_oc[k2]])

                    def post_b(cc=cc, qb=qb, q0=q0, k2=k2):
                        tb = 6 + qb % 2
                        op("pe", lambda e: e.transpose(PB[tb][:, 0:128], ocomb[k2][:, :], ident[:]), reads=[T_oc[k2], T_const], writes=[TPB[tb]])
                        op("dve", lambda e: e.scalar_tensor_tensor(mixT[:, 4 + cc, NCTX + q0:NCTX + q0 + 128], PB[tb][:, 0:128], 0.5,
                                                                  sdg[:, q0:q0 + 128], ALU.mult, ALU.mult),
                           reads=[TPB[tb], T_sdg], writes=[T_mix])
                    att_lag.push(post_b)
                att_lag.flush()
            phase_outproj(l, b, cdwout_d, list(range(2, 18)))

        try:
            chk(0)
            for b in range(2):
                pass_layer0(b)
            if not debug_h1:
                for b in range(2):
                    pass_layer1(b)
        except _Stop:
            pass
        kb.barrier()
        for ds in d_o:
            if ds.count:
                nc.sync.wait_ge(ds.sem, ds.count)
    return nc


def _consts():
    ident = np.eye(128, dtype=np.float32)
    rmat = np.zeros((128, 128), np.float32)
    for dp in range(128):
        partner = dp + 16 if (dp % 32) < 16 else dp - 16
        rmat[partner, dp] = 1.0
    m = 16
    inv = (10000.0 ** (-np.arange(m, dtype=np.float32) / m)).astype(np.float32)
    t = np.arange(NLAT)
    row = (t // 64).astype(np.float32)
    col = (t % 64).astype(np.float32)
    ang_r = (row[:, None] * inv[None, :]).astype(np.float32)
    ang_c = (col[:, None] * inv[None, :]).astype(np.float32)
    cos_t = np.ones((128, NT), np.float32)
    sin_t = np.zeros((128, NT), np.float32)
    for p in range(128):
        d = p % 64
        ang = ang_r if d < 32 else ang_c
        f = d % 16
        sign = -1.0 if (d % 32) < 16 else 1.0
        cos_t[p, NCTX:] = np.cos(ang[:, f])
        sin_t[p, NCTX:] = sign * np.sin(ang[:, f])
    ropeT = np.concatenate([cos_t, sin_t], axis=1)
    kk = np.arange(128)[:, None]
    qq = np.arange(128)[None, :]
    mp = np.where(kk >= qq, 0.0, -30000.0).astype(np.float32)
    mn = np.where(kk <= qq, 0.0, -30000.0).astype(np.float32)
    masks = np.concatenate([mp, mp, mn, mn], axis=1)
    return ident, rmat, ropeT, masks


_CACHE = {}


def _core_inputs(core, x, c, ctx, c_ctx, mod_w, mod_b, ln_g, ln_b, ab_w_in, ab_w_out, a_w_s, a_b_s,
                 a_norm_g, a_norm_b, b_lq1, b_lk1, b_lq2, b_lk2, b_subln_g, cd_w_in, cd_w_out,
                 c_dw_w, c_dw_b, c_norm_g, c_norm_b, d_sink, shared):
    b0 = 2 * core
    cvec = np.stack([c[b0], c[b0 + 1], c_ctx], axis=0)
    cvT = np.ascontiguousarray(cvec.reshape(3, 8, 128).transpose(2, 1, 0)).reshape(128, 24)
    d = dict(shared)
    d["x"] = np.ascontiguousarray(x[b0:b0 + 2])
    d["ctx"] = np.ascontiguousarray(ctx[b0:b0 + 2])
    d["cvT"] = cvT
    return d


def kernel(x, c, ctx, c_ctx, mod_w, mod_b, ln_g, ln_b, ab_w_in, ab_w_out, a_w_s, a_b_s,
           a_norm_g, a_norm_b, b_lq1, b_lk1, b_lq2, b_lk2, b_subln_g, cd_w_in, cd_w_out,
           c_dw_w, c_dw_b, c_norm_g, c_norm_b, d_sink, _debug_h1=False, _stage=99):
    f = lambda a: np.ascontiguousarray(np.asarray(a, dtype=np.float32))
    x, c, ctx, c_ctx = f(x), f(c), f(ctx), f(c_ctx)
    ident, rmat, ropeT, masks = _consts()
    shared = {
        "mod_w": f(mod_w),
        "mod_bT": np.ascontiguousarray(f(mod_b).reshape(2, 24, 128).transpose(2, 0, 1)).reshape(128, 48),
        "ln_g": f(ln_g), "ln_b": f(ln_b),
        "ab_w_in": f(ab_w_in)[0], "ab_w_out": f(ab_w_out)[0],
        "a_w_sT": np.ascontiguousarray(f(a_w_s)[0].transpose(2, 0, 1)).reshape(128, 512),
        "a_b_s": f(a_b_s)[0].reshape(1, 512),
        "a_norm_g": f(a_norm_g).reshape(1, 512), "a_norm_b": f(a_norm_b).reshape(1, 512),
        "lam_in": np.concatenate([f(b_lq1)[0], f(b_lk1)[0], f(b_lq2)[0], f(b_lk2)[0]]).reshape(1, 256),
        "b_subln_g": f(b_subln_g).reshape(1, 128),
        "cd_w_in": f(cd_w_in)[0], "cd_w_out": f(cd_w_out)[0],
        "dwT": np.ascontiguousarray(f(c_dw_w)[0].reshape(31, 4, 128).transpose(2, 1, 0)).reshape(128, 124),
        "cvecs": np.ascontiguousarray(np.stack([f(c_dw_b)[0], f(c_norm_g)[0], f(c_norm_b)[0]], 0)
                                      .reshape(3, 4, 128).transpose(2, 0, 1)).reshape(128, 12),
        "d_sink": f(d_sink).reshape(1, 8),
        "ident": ident, "rmat": rmat, "ropeT": ropeT, "masks": masks,
    }
    in_maps = []
    for core in range(8):
        in_maps.append(_core_inputs(core, x, c, ctx, c_ctx, None, None, None, None, None, None, None, None,
                                    None, None, None, None, None, None, None, None, None,
                                    None, None, None, None, None, shared))
    key = (bool(_debug_h1), _stage)
    if key not in _CACHE:
        _CACHE[key] = build_program(debug_h1=key[0], stage=_stage)
    nc = _CACHE[key]
    res = run_bass_kernel_spmd(nc, in_maps, core_ids=list(range(8)))
    if _debug_h1:
        return np.concatenate([r["h1"] for r in res.results], axis=0)
    return np.concatenate([r["out"] for r in res.results], axis=0).astype(np.float32)
```

```python
import math
import numpy as np
from contextlib import ExitStack
import concourse.bass as bass
import concourse.mybir as mybir
from concourse.bass_utils import run_bass_kernel_spmd

F32 = mybir.dt.float32
BF16 = mybir.dt.bfloat16
AF = mybir.ActivationFunctionType
ALU = mybir.AluOpType

NT = 2304
NCTX = 256
NLAT = 2048
D = 1024
LN_EPS = 1e-6
RMS_EPS = 1e-5
ALPHA = (2.0 * 2) ** 0.25
GC0 = math.sqrt(2.0 / math.pi)
GC1 = 0.044715
SEGS = [(0, 256), (256, 512), (768, 512), (1280, 512), (1792, 512)]
LSEGS = SEGS[1:]


class Trk:
    __slots__ = ("name", "w", "r")

    def __init__(self, name=""):
        self.name = name
        self.w = None
        self.r = {}


class DSem:
    def __init__(self, sem):
        self.sem = sem
        self.count = 0


class KB:
    def __init__(self, nc, es):
        self.nc = nc
        self.es = es
        self.eng = {"pe": nc.tensor, "act": nc.scalar, "dve": nc.vector, "pool": nc.gpsimd, "sp": nc.sync}
        self.esem = {}
        self.ecnt = {}
        for e in ("pe", "act", "dve", "pool"):
            self.esem[e] = es.enter_context(nc.semaphore("s_" + e))
            self.ecnt[e] = 0
        self.seen = {e: {} for e in self.eng}
        self.nsem = 0
        self.hooks = []
        self.bar_dsems = []

    def dsem(self):
        self.nsem += 1
        return DSem(self.es.enter_context(self.nc.semaphore(f"d{self.nsem}")))

    def _wait(self, e, reads, writes):
        evs = {}

        def add(ev):
            if ev is None:
                return
            k = id(ev[0])
            if k not in evs or evs[k][1] < ev[1]:
                evs[k] = ev

        for t in reads:
            add(t.w)
        for t in writes:
            add(t.w)
            for ev in t.r.values():
                add(ev)
        seen = self.seen[e]
        own = self.esem.get(e)
        for k, (sem, val) in evs.items():
            if e == "pe" and sem is own:
                continue
            if seen.get(k, 0) >= val:
                continue
            self.eng[e].wait_ge(sem, val)
            seen[k] = val

    def _post(self, ev, reads, writes):
        k = id(ev[0])
        for t in writes:
            t.w = ev
            t.r = {}
        for t in reads:
            t.r[k] = ev

    def op(self, e, fn, reads=(), writes=(), sig=True):
        self._wait(e, reads, writes)
        ins = fn(self.eng[e])
        if sig:
            self.ecnt[e] += 1
            ins.then_inc(self.esem[e], 1)
            ev = (self.esem[e], self.ecnt[e])
        else:
            ev = (self.esem[e], self.ecnt[e] + 1)
        self._post(ev, reads, writes)
        return ev

    def dma(self, q, out, in_, ds, reads=(), writes=()):
        self._wait(q, reads, writes)
        ins = self.eng[q].dma_start(out=out, in_=in_)
        ds.count += 16
        ins.then_inc(ds.sem, 16)
        ev = (ds.sem, ds.count)
        self._post(ev, reads, writes)
        return ev

    def barrier(self):
        for h in self.hooks:
            h()
        for e in self.eng:
            seen = self.seen[e]
            for ds in self.bar_dsems:
                k = id(ds.sem)
                if ds.count and seen.get(k, 0) < ds.count:
                    self.eng[e].wait_ge(ds.sem, ds.count)
                    seen[k] = ds.count
            for f in ("pe", "act", "dve", "pool"):
                if f == e or self.ecnt[f] == 0:
                    continue
                k = id(self.esem[f])
                if seen.get(k, 0) >= self.ecnt[f]:
                    continue
                self.eng[e].wait_ge(self.esem[f], self.ecnt[f])
                seen[k] = self.ecnt[f]


class _Stop(Exception):
    pass


LAG_ON = True


class Lag:
    def __init__(self):
        self.p = None

    def push(self, fn):
        if not LAG_ON:
            fn()
            return
        old, self.p = self.p, fn
        if old:
            old()

    def flush(self):
        old, self.p = self.p, None
        if old:
            old()


def build_program(debug_h1=False, stage=99):
    nc = bass.Bass("TRN2", target_bir_lowering=False)

    def chk(n):
        if stage <= n:
            raise _Stop()

    def din(name, shape):
        return nc.dram_tensor(name, list(shape), F32, kind="ExternalInput").ap()

    x_d = din("x", [2, NLAT, D])
    ctx_d = din("ctx", [2, NCTX, D])
    cvT_d = din("cvT", [128, 24])
    modw_d = din("mod_w", [2, D, 3 * D])
    modbT_d = din("mod_bT", [128, 48])
    lng_d = din("ln_g", [2, D])
    lnb_d = din("ln_b", [2, D])
    abwin_d = din("ab_w_in", [D, 3584])
    abwout_d = din("ab_w_out", [D, D])
    awsT_d = din("a_w_sT", [128, 512])
    abs_d = din("a_b_s", [1, 512])
    ang_d = din("a_norm_g", [1, 512])
    anb_d = din("a_norm_b", [1, 512])
    lam_d = din("lam_in", [1, 256])
    subg_d = din("b_subln_g", [1, 128])
    cdwin_d = din("cd_w_in", [D, 2816])
    cdwout_d = din("cd_w_out", [D, D])
    dwT_d = din("dwT", [128, 124])
    cvec_d = din("cvecs", [128, 12])
    sink_d = din("d_sink", [1, 8])
    ident_d = din("ident", [128, 128])
    rmat_d = din("rmat", [128, 128])
    rope_d = din("ropeT", [128, 2 * NT])
    mask_d = din("masks", [128, 512])
    out_d = nc.dram_tensor("out", [2, NLAT, D], F32, kind="ExternalOutput").ap()
    if debug_h1:
        h1_d = nc.dram_tensor("h1", [2, NT, D], F32, kind="ExternalOutput").ap()
    else:
        h1_d = nc.dram_tensor("h1", [2, NT, D], F32).ap()

    with ExitStack() as es:
        kb = KB(nc, es)
        op = kb.op

        def sb(name, shape, dt=F32):
            return es.enter_context(nc.sbuf_tensor("sb_" + name, list(shape), dt))

        PB = [es.enter_context(nc.psum_tensor(f"pb{i}", [128, 512], F32)) for i in range(8)]
        TPB = [Trk(f"pb{i}") for i in range(8)]

        d_const = kb.dsem()
        T_const = Trk("const")
        d_constp = kb.dsem()
        T_constp = Trk("constp")
        ident = sb("ident", [128, 128])
        identb = sb("identb", [128, 128], BF16)
        rmat = sb("rmat", [128, 128], BF16)
        ropeT = sb("ropeT", [128, 2, NT])
        masks = sb("masks", [128, 2, 256], BF16)
        cvT = sb("cvT", [128, 24])
        modbT = sb("modbT", [128, 48])
        wsT = sb("wsT", [128, 4, 128], BF16)
        lamt = sb("lamt", [128, 256])
        subg = sb("subg", [128, 128])
        dwT = sb("dwT", [128, 124])
        cvecs = sb("cvecs", [128, 12])
        sinkt = sb("sinkt", [128, 8])
        ones_f = sb("ones_f", [128, 128])
        onesdiv = sb("onesdiv", [128, 128])
        sTb = sb("sTb", [128, 24], BF16)
        modT = sb("modT", [128, 144])
        small = sb("small", [128, 64])
        epsln = small[:, 0:1]
        epsrms = small[:, 1:2]
        epsln4 = small[:, 2:3]
        neglam = small[:, 3:4]

        def cdma(q, dst, src):
            if q == "pool":
                kb.dma(q, dst, src, d_constp, writes=[T_constp])
            else:
                kb.dma(q, dst, src, d_const, writes=[T_const])

        cdma("sp", ident[:], ident_d)
        cdma("pool", rmat[:], rmat_d)
        cdma("sp", ropeT[:], rope_d.rearrange("p (a t) -> p a t", a=2))
        cdma("pool", masks[:], mask_d.rearrange("p (a t) -> p a t", a=2))
        cdma("sp", cvT[:], cvT_d)
        cdma("sp", modbT[:], modbT_d)
        cdma("pool", wsT[:], awsT_d.rearrange("p (g q) -> p g q", g=4))
        cdma("sp", lamt[:], lam_d.partition_broadcast(128))
        cdma("sp", subg[:], subg_d.partition_broadcast(128))
        cdma("sp", dwT[:], dwT_d)
        cdma("sp", cvecs[:], cvec_d)
        cdma("sp", sinkt[:], sink_d.partition_broadcast(128))

        T_c2 = Trk("c2")
        RC = [T_const, T_c2]
        op("dve", lambda e: e.memset(ones_f[:], 1.0), writes=[T_c2])
        op("dve", lambda e: e.memset(onesdiv[:], 1.0 / 512.0), writes=[T_c2])
        op("dve", lambda e: e.memset(small[:, 0:1], LN_EPS), writes=[T_c2])
        op("dve", lambda e: e.memset(small[:, 1:2], RMS_EPS), writes=[T_c2])
        op("dve", lambda e: e.memset(small[:, 2:3], 4.0 * LN_EPS), writes=[T_c2])
        op("dve", lambda e: e.tensor_copy(identb[:], ident[:]), reads=[T_const], writes=[T_c2])
        lam_init0 = 0.8 - 0.6 * math.exp(-0.3 * 0)
        lprod = sb("lprod", [128, 128])
        op("dve", lambda e: e.tensor_tensor(lprod[:, 0:64], lamt[:, 0:64], lamt[:, 64:128], ALU.mult),
           reads=[T_const], writes=[T_c2])
        op("dve", lambda e: e.tensor_tensor(lprod[:, 64:128], lamt[:, 128:192], lamt[:, 192:256], ALU.mult),
           reads=[T_c2, T_const], writes=[T_c2])
        op("dve", lambda e: e.reduce_sum(small[:, 4:6], lprod[:].rearrange("p (a b) -> p a b", a=2),
                                        mybir.AxisListType.X), reads=[T_c2], writes=[T_c2])
        op("act", lambda e: e.activation(small[:, 6:8], small[:, 4:6], AF.Exp), reads=[T_c2], writes=[T_c2])
        op("dve", lambda e: e.scalar_tensor_tensor(small[:, 3:4], small[:, 7:8], -lam_init0, small[:, 6:7],
                                                  ALU.add, ALU.subtract), reads=[T_c2], writes=[T_c2])
        op("dve", lambda e: e.tensor_scalar(subg[:], subg[:], (1.0 - lam_init0) * 0.5, None, ALU.mult),
           reads=[T_const, T_c2], writes=[T_c2])
        op("act", lambda e: e.activation(sinkt[:], sinkt[:], AF.Exp), reads=[T_const, T_c2], writes=[T_c2])
        op("dve", lambda e: e.tensor_scalar(dwT[:], dwT[:], 0.5, None, ALU.mult), reads=[T_const, T_c2], writes=[T_c2])
        sct = sb("sct", [128, 24])
        op("act", lambda e: e.activation(sct[:], cvT[:], AF.Tanh, scale=0.5), reads=[T_const], writes=[T_c2])
        op("dve", lambda e: e.scalar_tensor_tensor(sct[:], sct[:], 1.0, cvT[:], ALU.add, ALU.mult),
           reads=[T_c2, T_const], writes=[T_c2])
        op("dve", lambda e: e.tensor_scalar(sTb[:], sct[:], 0.5, None, ALU.mult), reads=[T_c2], writes=[T_c2])

        NW = 3
        wslot = [sb(f"wslot{i}", [128, 8, 512], BF16) for i in range(NW)]
        T_w = [Trk(f"w{i}") for i in range(NW)]
        d_w = [kb.dsem() for _ in range(NW)]
        wctr = [0]

        def wload(src_ap, ncols):
            i = wctr[0] % NW
            wctr[0] += 1
            kb.dma("pool", wslot[i][:, :, 0:ncols], src_ap.rearrange("(j p) c -> p j c", p=128), d_w[i],
                   writes=[T_w[i]])
            return wslot[i], T_w[i]

        NS = 3
        srct = [None] * NS
        T_s = [Trk(f"s{i}") for i in range(NS)]
        d_s = [kb.dsem() for _ in range(NS)]
        sctr = [0]
        T_s8 = [Trk(f"s8_{i}") for i in range(8)]
        d_s8 = [kb.dsem() for _ in range(8)]

        def sload(src_ap):
            i = sctr[0] % NS
            sctr[0] += 1
            kb.dma("sp", srct[i][:], src_ap, d_s[i], writes=[T_s[i]])
            return srct[i], T_s[i]

        NO = 3
        outt = [None] * NO
        T_o = [Trk(f"o{i}") for i in range(NO)]
        d_o = [kb.dsem() for _ in range(NO)]
        octr = [0]
        kb.bar_dsems = d_o
        T_h1 = [[Trk(f"h1_{b}_{i}") for i in range(18)] for b in range(2)]

        T_mod = Trk("mod")
        for l in range(2):
            for g in range(3):
                for half in range(2):
                    c0 = g * 1024 + half * 512
                    ws, tw = wload(modw_d[l, :, c0:c0 + 512], 512)
                    for cc in range(4):
                        for j in range(8):
                            op("pe", lambda e: e.matmul(PB[7][:, cc * 4:cc * 4 + 3], ws[:, j, cc * 128:(cc + 1) * 128],
                                                        sTb[:, j * 3:(j + 1) * 3], start=(j == 0), stop=(j == 7)),
                               reads=[tw, T_c2], writes=[TPB[7]])
                    for cc in range(4):
                        k = g * 8 + half * 4 + cc
                        o0 = (l * 24 + k) * 3
                        op("dve", lambda e: e.tensor_scalar(modT[:, o0:o0 + 3], PB[7][:, cc * 4:cc * 4 + 3],
                                                           modbT[:, l * 24 + k:l * 24 + k + 1],
                                                           1.0 if g == 1 else 0.0, ALU.add, ALU.add),
                           reads=[TPB[7], T_const], writes=[T_mod])

        def mod_ap(l, k, r):
            o0 = (l * 24 + k) * 3 + r
            return modT[:, o0:o0 + 1]

        uT = sb("uT", [128, 8, NT], BF16)
        T_uT = [Trk(f"uT{i}") for i in range(18)]
        mixT = sb("mixT", [128, 8, NT], BF16)
        T_mix = Trk("mixT")
        T_gbc = Trk("gbc")
        T_diagf = [Trk(), Trk()]
        T_lngb = Trk("lngb")
        d_lngb = kb.dsem()
        T_ab = Trk("ab")
        d_ab = kb.dsem()
        ARENA = 84 * 1024
        arena = sb("arena", [128, ARENA // 4])

        class Carver:
            def __init__(self):
                self.off = 0

            def get(self, nelem, dt=F32):
                nbytes = nelem * (4 if dt == F32 else 2)
                nbytes = (nbytes + 31) // 32 * 32
                assert self.off + nbytes <= ARENA, (self.off, nbytes)
                a = arena[:, self.off // 4:(self.off + nbytes) // 4]
                self.off += nbytes
                if dt == BF16:
                    a = a.bitcast(BF16)[:, 0:nelem]
                return a

        def tile_rows(l, b, i):
            if l == 0:
                return ctx_d[b, i * 128:(i + 1) * 128, :] if i < 2 else x_d[b, (i - 2) * 128:(i - 1) * 128, :]
            return h1_d[b, i * 128:(i + 1) * 128, :]

        def seg_tiles(t0, n):
            return list(range(t0 // 128, (t0 + n) // 128))

        def proj_fm(ws, tw, c0, t0, n, bank):
            for j in range(8):
                op("pe", lambda e: e.matmul(PB[bank][:, 0:n], ws[:, j, c0:c0 + 128], uT[:, j, t0:t0 + n],
                                            start=(j == 0), stop=(j == 7)),
                   reads=[tw] + [T_uT[i] for i in seg_tiles(t0, n)], writes=[TPB[bank]], sig=(j == 7))

        def proj_tm(ws, tw, c0, ncols, i, bank):
            for j in range(8):
                op("pe", lambda e: e.matmul(PB[bank][:, 0:ncols], uT[:, j, i * 128:(i + 1) * 128], ws[:, j, c0:c0 + ncols],
                                            start=(j == 0), stop=(j == 7)),
                   reads=[tw, T_uT[i]], writes=[TPB[bank]], sig=(j == 7))

        def rsqrt_small(dst, src, eps_ap, trk, scale=1.0):
            op("act", lambda e: e.activation(dst, src, AF.Sqrt, bias=eps_ap, scale=scale), reads=[trk, T_c2], writes=[trk])
            op("dve", lambda e: e.reciprocal(dst, dst), reads=[trk], writes=[trk])

        rope_lag = Lag()
        kb.hooks.append(rope_lag.flush)

        rctr = [0]

        def rope_evac(bank, n, t0, dst, T_dst, rb, qraw, T_qraw, t1, T_t1, t2, T_t2):
            k = rctr[0] % 3
            rctr[0] += 1
            rb = 4 + k
            qraw, T_qraw, t1, T_t1, t2, T_t2 = qraw[k], T_qraw[k], t1[k], T_t1[k], t2[k], T_t2[k]
            op("act", lambda e: e.copy(qraw[:, 0:n], PB[bank][:, 0:n]), reads=[TPB[bank]], writes=[T_qraw])
            rope_lag.push(lambda: rope_rest(n, t0, dst, T_dst, rb, qraw, T_qraw, t1, T_t1, t2, T_t2))

        def rope_rest(n, t0, dst, T_dst, rb, qraw, T_qraw, t1, T_t1, t2, T_t2):
            op("pe", lambda e: e.matmul(PB[rb][:, 0:n], rmat[:], qraw[:, 0:n], start=True, stop=True),
               reads=[T_qraw, T_constp], writes=[TPB[rb]])
            op("dve", lambda e: e.tensor_tensor(t1[:, 0:n], qraw[:, 0:n], ropeT[:, 0, t0:t0 + n], ALU.mult),
               reads=[T_qraw, T_const], writes=[T_t1])
            op("dve", lambda e: e.tensor_tensor(t2[:, 0:n], PB[rb][:, 0:n], ropeT[:, 1, t0:t0 + n], ALU.mult),
               reads=[TPB[rb], T_const], writes=[T_t2])
            if not isinstance(dst, list):
                dst = [(dst, 0, 128)]
            for (dap, p0, p1) in dst:
                op("pool", lambda e: e.tensor_tensor(dap, t1[p0:p1, 0:n], t2[p0:p1, 0:n], ALU.add),
                   reads=[T_t1, T_t2], writes=[T_dst])

        def build_gate_bc(l, r, slot, gbc, diagf):
            for cc in range(8):
                dg = diagf[cc % 2]
                op("dve", lambda e: e.tensor_scalar(dg[:], ident[:], mod_ap(l, 16 + cc, r), None, ALU.mult),
                   reads=[T_const, T_mod], writes=[T_diagf[cc % 2]])
                op("pe", lambda e: e.matmul(PB[7][:, (cc % 4) * 128:(cc % 4 + 1) * 128], ones_f[:], dg[:],
                                            start=True, stop=True),
                   reads=[T_c2, T_diagf[cc % 2]], writes=[TPB[7]])
                if cc % 4 == 3:
                    h = cc // 4
                    op("act", lambda e: e.copy(gbc[:, slot, h * 512:(h + 1) * 512], PB[7][:, :]),
                       reads=[TPB[7]], writes=[T_gbc])

        def sload_dep(l, b, i):
            idx = sctr[0] % NS
            sctr[0] += 1
            rd = [T_h1[b][i]] if l == 1 else []
            kb.dma("sp", srct[idx][:], tile_rows(l, b, i), d_s[idx], reads=rd, writes=[T_s[idx]])
            return srct[idx], T_s[idx]

        def phase_transposes(l, b):
            cvt = Carver()
            NSL = 8
            slots = [cvt.get(D) for _ in range(NSL)]
            groups = [[0, 1], [2, 3, 4, 5], [6, 7, 8, 9], [10, 11, 12, 13], [14, 15, 16, 17]]
            bctr = 0
            for gi, grp in enumerate(groups):
                r = 2 if grp[0] < 2 else b
                loaded = []
                for i in grp:
                    idx = sctr[0] % NSL
                    sctr[0] += 1
                    rd = [T_h1[b][i]] if l == 1 else []
                    kb.dma("sp", slots[idx][:, :], tile_rows(l, b, i), d_s8[idx], reads=rd, writes=[T_s8[idx]])
                    loaded.append((slots[idx], T_s8[idx]))
                ng = len(grp)
                t0 = grp[0] * 128
                for j in range(8):
                    bank = bctr % 8
                    bctr += 1
                    for q, (st, ts) in enumerate(loaded):
                        op("pe", lambda e: e.transpose(PB[bank][:, q * 128:(q + 1) * 128], st[:, j * 128:(j + 1) * 128], ident[:]),
                           reads=[ts, T_const], writes=[TPB[bank]], sig=(q == len(loaded) - 1))
                    wr = [T_uT[i] for i in grp]
                    if j % 2 == 0:
                        op("dve", lambda e: e.tensor_scalar(uT[:, j, t0:t0 + ng * 128], PB[bank][:, 0:ng * 128],
                                                           mod_ap(l, 8 + j, r), mod_ap(l, j, r), ALU.mult, ALU.add),
                           reads=[TPB[bank], T_mod], writes=wr)
                    else:
                        op("act", lambda e: e.activation(uT[:, j, t0:t0 + ng * 128], PB[bank][:, 0:ng * 128], AF.Identity,
                                                        bias=mod_ap(l, j, r), scale=mod_ap(l, 8 + j, r)),
                           reads=[TPB[bank], T_mod], writes=wr)

        def phase_outproj(l, b, wout_d, tiles):
            kb.barrier()
            cv = Carver()
            for k in range(NS):
                srct[k] = cv.get(D)
            for k in range(NO):
                outt[k] = cv.get(D)
            zt = [cv.get(D) for _ in range(4)]
            T_z = [Trk() for _ in range(4)]
            statsl = [cv.get(64) for _ in range(4)]
            T_stl = [Trk() for _ in range(4)]
            lngb = cv.get(2 * D).rearrange("p (a d) -> p a d", a=2)
            gbc = cv.get(2 * D).rearrange("p (a d) -> p a d", a=2)
            diagf = [cv.get(128) for _ in range(2)]
            kb.dma("sp", lngb[:, 0, :], lng_d[l:l + 1, :].partition_broadcast(128), d_lngb, writes=[T_lngb])
            kb.dma("sp", lngb[:, 1, :], lnb_d[l:l + 1, :].partition_broadcast(128), d_lngb, writes=[T_lngb])
            build_gate_bc(l, b, 0, gbc, diagf)
            if l == 0:
                build_gate_bc(l, 2, 1, gbc, diagf)
            w0, tw0 = wload(wout_d[:, 0:512], 512)
            w1, tw1 = wload(wout_d[:, 512:1024], 512)
            o_lag = Lag()
            for n_i, i in enumerate(tiles):
                st, ts = sload_dep(l, b, i)
                gs = 1 if i < 2 else 0
                pb0 = (n_i % 3) * 2
                for half, (ws, tw) in enumerate(((w0, tw0), (w1, tw1))):
                    for j in range(8):
                        op("pe", lambda e: e.matmul(PB[pb0 + half][:, :], mixT[:, j, i * 128:(i + 1) * 128], ws[:, j, :],
                                                    start=(j == 0), stop=(j == 7)),
                           reads=[T_mix, tw], writes=[TPB[pb0 + half]], sig=(j == 7))
                z = zt[n_i % 4]
                tz = T_z[n_i % 4]
                stats = statsl[n_i % 4]
                T_st = T_stl[n_i % 4]
                for half in range(2):
                    hs = slice(half * 512, (half + 1) * 512)
                    op("dve", lambda e: e.tensor_tensor(z[:, hs], PB[pb0 + half][:, :], gbc[:, gs, hs], ALU.mult),
                       reads=[TPB[pb0 + half], T_gbc], writes=[tz])
                    op("dve", lambda e: e.scalar_tensor_tensor(z[:, hs], st[:, hs], ALPHA, z[:, hs], ALU.mult, ALU.add),
                       reads=[ts, tz], writes=[tz])
                    op("dve", lambda e: e.bn_stats(stats[:, half * 6:(half + 1) * 6], z[:, hs]), reads=[tz], writes=[T_st])
                op("dve", lambda e: e.bn_aggr(stats[:, 16:18], stats[:, 0:12].rearrange("p (a b) -> p a b", a=2)),
                   reads=[T_st], writes=[T_st])
                def tail(l=l, b=b, i=i, z=z, tz=tz, stats=stats, T_st=T_st):
                    rsqrt_small(stats[:, 18:19], stats[:, 17:18], epsln, T_st)
                    op("dve", lambda e: e.scalar_tensor_tensor(stats[:, 19:20], stats[:, 16:17], -1.0, stats[:, 18:19],
                                                              ALU.mult, ALU.mult), reads=[T_st], writes=[T_st])
                    oi = octr[0] % NO
                    octr[0] += 1
                    ot, to = outt[oi], T_o[oi]
                    op("act", lambda e: e.activation(z[:, :], z[:, :], AF.Identity, bias=stats[:, 19:20], scale=stats[:, 18:19]),
                       reads=[tz, T_st], writes=[tz])
                    op("dve", lambda e: e.tensor_tensor(z[:, :], z[:, :], lngb[:, 0, :], ALU.mult), reads=[tz, T_lngb], writes=[tz])
                    op("pool", lambda e: e.tensor_tensor(ot[:, :], z[:, :], lngb[:, 1, :], ALU.add), reads=[tz, T_lngb], writes=[to])
                    if l == 0:
                        kb.dma("sp", h1_d[b, i * 128:(i + 1) * 128, :], ot[:, :], d_o[oi], reads=[to], writes=[T_h1[b][i]])
                    else:
                        kb.dma("sp", out_d[b, (i - 2) * 128:(i - 1) * 128, :], ot[:, :], d_o[oi], reads=[to])
                o_lag.push(tail)
            o_lag.flush()

        def gelu2(dst, T_dst, bank, n, sq, T_sq, tt, T_tt):
            op("act", lambda e: e.activation(sq[:, 0:n], PB[bank][:, 0:n], AF.Square, scale=math.sqrt(GC1)), reads=[TPB[bank]], writes=[T_sq])
            op("dve", lambda e: e.scalar_tensor_tensor(sq[:, 0:n], sq[:, 0:n], 1.0, PB[bank][:, 0:n], ALU.add, ALU.mult),
               reads=[T_sq, TPB[bank]], writes=[T_sq])
            op("act", lambda e: e.activation(tt[:, 0:n], sq[:, 0:n], AF.Tanh, scale=GC0), reads=[T_sq], writes=[T_tt])
            return op("dve", lambda e: e.scalar_tensor_tensor(dst, tt[:, 0:n], 1.0, PB[bank][:, 0:n], ALU.add, ALU.mult),
                      reads=[T_tt, TPB[bank]], writes=[T_dst])

        def pass_layer0(b):
            l = 0
            kb.barrier()
            phase_transposes(l, b)
            chk(1)
            kb.barrier()
            cv = Carver()
            vn = cv.get(18 * 512, BF16)
            vn3 = vn.rearrange("p (i c) -> p i c", i=18)
            T_vn = [Trk() for _ in range(18)]
            Gu = cv.get(NT)
            T_Gu = Trk()
            sq = [cv.get(512) for _ in range(4)]
            T_sq = [Trk() for _ in range(4)]
            tt = [cv.get(512) for _ in range(4)]
            T_tt = [Trk() for _ in range(4)]
            g2 = [cv.get(512) for _ in range(4)]
            T_g2 = [Trk() for _ in range(4)]
            statsA = [cv.get(64) for _ in range(4)]
            T_stA = [Trk() for _ in range(4)]
            actr = [0]
            a_lag = Lag()
            bsrep = cv.get(4 * 512).rearrange("p (g q) -> p g q", g=4)
            angb = cv.get(2 * 512).rearrange("p (a q) -> p a q", a=2)
            for rep in range(4):
                kb.dma("sp", bsrep[:, :, rep * 128:(rep + 1) * 128],
                       abs_d.rearrange("o (g q) -> o g q", g=4).partition_broadcast(128), d_ab, writes=[T_ab])
            kb.dma("sp", angb[:, 0, :], ang_d.partition_broadcast(128), d_ab, writes=[T_ab])
            kb.dma("sp", angb[:, 1, :], anb_d.partition_broadcast(128), d_ab, writes=[T_ab])
            wv, twv = wload(abwin_d[:, 512:1024], 512)
            wu, twu = wload(abwin_d[:, 0:512], 512)
            wg, twg = wload(abwin_d[:, 1024:1536], 512)
            for i in range(18):
                bank = i % 4
                k2 = i % 4
                stats, T_st = statsA[k2], T_stA[k2]
                proj_tm(wv, twv, 0, 512, i, bank)
                gelu2(g2[k2][:, :], T_g2[k2], bank, 512, sq[k2], T_sq[k2], tt[k2], T_tt[k2])
                op("dve", lambda e: e.bn_stats(stats[:, 0:6], g2[k2][:, :]), reads=[T_g2[k2]], writes=[T_st])
                op("dve", lambda e: e.bn_aggr(stats[:, 16:18], stats[:, 0:6]), reads=[T_st], writes=[T_st])

                def ln_tail(i=i, k2=k2, stats=stats, T_st=T_st):
                    rsqrt_small(stats[:, 18:19], stats[:, 17:18], epsln4, T_st)
                    op("dve", lambda e: e.tensor_scalar(g2[k2][:, :], g2[k2][:, :], stats[:, 16:17], stats[:, 18:19],
                                                       ALU.subtract, ALU.mult), reads=[T_g2[k2], T_st], writes=[T_g2[k2]])
                    op("pool", lambda e: e.tensor_tensor(g2[k2][:, :], g2[k2][:, :], angb[:, 0, :], ALU.mult),
                       reads=[T_g2[k2], T_ab], writes=[T_g2[k2]])
                    op("pool", lambda e: e.tensor_tensor(vn3[:, i, :], g2[k2][:, :], angb[:, 1, :], ALU.add),
                       reads=[T_g2[k2], T_ab], writes=[T_vn[i]])
                a_lag.push(ln_tail)
            a_lag.flush()
            for g in range(4):
                for si, (t0, n) in enumerate(SEGS):
                    bank = actr[0] % 4
                    k2 = actr[0] % 4
                    actr[0] += 1
                    proj_fm(wu, twu, g * 128, t0, n, bank)
                    gelu2(Gu[:, t0:t0 + n], T_Gu, bank, n, sq[k2], T_sq[k2], tt[k2], T_tt[k2])
                for si, (t0, n) in enumerate(SEGS):
                    bank = actr[0] % 4
                    k2 = actr[0] % 4
                    actr[0] += 1
                    proj_fm(wg, twg, g * 128, t0, n, bank)
                    op("act", lambda e: e.activation(tt[k2][:, 0:n], PB[bank][:, 0:n], AF.Tanh, scale=0.5),
                       reads=[TPB[bank]], writes=[T_tt[k2]])
                    op("dve", lambda e: e.scalar_tensor_tensor(tt[k2][:, 0:n], tt[k2][:, 0:n], 1.0, PB[bank][:, 0:n], ALU.add, ALU.mult),
                       reads=[T_tt[k2], TPB[bank]], writes=[T_tt[k2]])
                    op("dve", lambda e: e.scalar_tensor_tensor(Gu[:, t0:t0 + n], tt[k2][:, 0:n], 0.25, Gu[:, t0:t0 + n], ALU.mult, ALU.mult),
                       reads=[T_tt[k2], T_Gu], writes=[T_Gu])
                    mb = 4 + actr[0] % 4
                    tl = seg_tiles(t0, n)
                    for ci, c in enumerate(tl):
                        op("pe", lambda e: e.matmul(PB[mb][:, ci * 128:(ci + 1) * 128], vn3[:, c, g * 128:(g + 1) * 128], wsT[:, g, :],
                                                    start=True, stop=True),
                           reads=[T_vn[c], T_constp], writes=[TPB[mb]])
                    op("dve", lambda e: e.tensor_tensor(sq[k2][:, 0:n], PB[mb][:, 0:n], bsrep[:, g, 0:n], ALU.add),
                       reads=[TPB[mb], T_ab], writes=[T_sq[k2]])
                    op("pool", lambda e: e.tensor_tensor(mixT[:, g, t0:t0 + n], sq[k2][:, 0:n], Gu[:, t0:t0 + n], ALU.mult),
                       reads=[T_sq[k2], T_Gu], writes=[T_mix])
            chk(2)
            kb.barrier()
            cv = Carver()
            V1 = cv.get(18 * 4 * 144, BF16)
            V14 = V1.rearrange("p (i h c) -> p i h c", i=18, h=4)
            T_V = [Trk() for _ in range(18)]
            qT2f = cv.get(2 * NT, BF16)
            qT2 = qT2f.rearrange("p (c t) -> p c t", c=2)
            kT = cv.get(NT, BF16)
            T_qT, T_kT = Trk(), Trk()
            op("pool", lambda e: e.memset(qT2f[:, :], 0.0), writes=[T_qT])
            sbg = cv.get(NT)
            T_sbg = Trk()
            qraw = [cv.get(512, BF16) for _ in range(3)]
            T_qraw = [Trk() for _ in range(3)]
            t1 = [cv.get(512) for _ in range(3)]
            T_t1 = [Trk() for _ in range(3)]
            t2 = [cv.get(512) for _ in range(3)]
            T_t2 = [Trk() for _ in range(3)]
            pT = [cv.get(512, BF16) for _ in range(4)]
            T_pT = [Trk() for _ in range(4)]
            o1n = [cv.get(128) for _ in range(2)]
            T_o1n = [Trk(), Trk()]
            oc_all = cv.get(18 * 128)
            oc3 = oc_all.rearrange("p (i c) -> p i c", i=18)
            T_oca = [Trk() for _ in range(18)]
            mv_all = cv.get(64)
            mv3 = mv_all[:, 0:36].rearrange("p (i c) -> p i c", i=18)
            msr = cv.get(64)
            T_mv = Trk()
            osc = [cv.get(128) for _ in range(2)]
            T_osc = [Trk(), Trk()]
            TQ3 = [Trk() for _ in range(4)]
            stats = cv.get(64)
            T_st = Trk()
            op("dve", lambda e: e.memset(V1[:, :], 1.0), writes=T_V)
            wv, twv = wload(abwin_d[:, 2560:3072], 512)
            wq, twq = wload(abwin_d[:, 1536:2048], 512)
            wk, twk = wload(abwin_d[:, 2048:2560], 512)
            for i in range(18):
                bank = i % 3
                proj_tm(wv, twv, 0, 512, i, bank)
                op("act", lambda e: e.copy(V14[:, i, :, 0:128], PB[bank][:, :].rearrange("p (h c) -> p h c", h=4)),
                   reads=[TPB[bank]], writes=[T_V[i]])
            chk(2.2)
            wg, twg = wload(abwin_d[:, 3072:3584], 512)
            pctr = [0]
            for h in range(4):
                for si, (t0, n) in enumerate(SEGS):
                    k2 = si % 2
                    proj_fm(wq, twq, h * 128, t0, n, si % 3)
                    rope_evac(si % 3, n, t0, [(qT2[0:64, 0, t0:t0 + n], 0, 64), (qT2[64:128, 1, t0:t0 + n], 64, 128)], T_qT, 4 + k2, qraw, T_qraw, t1, T_t1, t2, T_t2)
                for si, (t0, n) in enumerate(SEGS):
                    k2 = si % 2
                    proj_fm(wk, twk, h * 128, t0, n, si % 3)
                    rope_evac(si % 3, n, t0, kT[:, t0:t0 + n], T_kT, 4 + k2, qraw, T_qraw, t1, T_t1, t2, T_t2)
                rope_lag.flush()
                for si, (t0, n) in enumerate(SEGS):
                    k2 = si % 2
                    bank = si % 3
                    proj_fm(wg, twg, h * 128, t0, n, bank)
                    op("act", lambda e: e.activation(t1[k2][:, 0:n], PB[bank][:, 0:n], AF.Tanh, scale=0.5),
                       reads=[TPB[bank]], writes=[T_t1[k2]])
                    op("dve", lambda e: e.scalar_tensor_tensor(sbg[:, t0:t0 + n], t1[k2][:, 0:n], 1.0, PB[bank][:, 0:n], ALU.add, ALU.mult),
                       reads=[T_t1[k2], TPB[bank]], writes=[T_sbg])
                chk(2.4)
                qtiles = [(0, [0, 1])] + [(256 + 256 * qi, list(range(18))) for qi in range(8)]
                for qn, (q0, kts) in enumerate(qtiles):
                    ob = 4 + 2 * (qn % 2)

                    def qk(kt, sbank):
                        op("pe", lambda e: e.matmul(PB[sbank][:, :].rearrange("p (c q) -> p c q", c=2),
                                                    kT[:, kt * 128:(kt + 1) * 128],
                                                    qT2[:, :, q0:q0 + 256], start=True, stop=True),
                           reads=[T_kT, T_qT], writes=[TPB[sbank]])

                    for pre in range(min(2, len(kts))):
                        qk(kts[pre], pre % 3)
                    for ki, kt in enumerate(kts):
                        sbank = ki % 3
                        if ki + 2 < len(kts):
                            qk(kts[ki + 2], (ki + 2) % 3)
                        pi = pctr[0] % 4
                        pctr[0] += 1
                        op("act", lambda e: e.activation(pT[pi][:, :], PB[sbank][:, :], AF.Exp, scale=0.125),
                           reads=[TPB[sbank]], writes=[T_pT[pi]])
                        chk(2.5)
                        for c in range(2):
                            for s in range(2):
                                first = (ki == 0 and s == 0)
                                op("pe", lambda e: e.matmul(PB[ob + c][:, s * 144:s * 144 + 130],
                                                            pT[pi][:, c * 256 + s * 128:c * 256 + (s + 1) * 128],
                                                            V14[:, kt, h, 0:130], start=first, stop=(ki == len(kts) - 1),
                                                            skip_group_check=True),
                                   reads=[T_pT[pi], T_V[kt]], writes=[TPB[ob + c]], sig=(c == 1 and s == 1))
                    chk(2.6)
                    for s in range(2):
                        k2 = s
                        idx = (q0 + s * 128) // 128
                        op("dve", lambda e: e.reciprocal(stats[:, 0:1], PB[ob][:, s * 144 + 128:s * 144 + 129]),
                           reads=[TPB[ob]], writes=[T_st])
                        op("dve", lambda e: e.reciprocal(stats[:, 1:2], PB[ob + 1][:, s * 144 + 128:s * 144 + 129]),
                           reads=[TPB[ob + 1]], writes=[T_st])
                        op("dve", lambda e: e.tensor_scalar(o1n[k2][:, :], PB[ob + 1][:, s * 144:s * 144 + 128], stats[:, 1:2], neglam,
                                                           ALU.mult, ALU.mult), reads=[TPB[ob + 1], T_st, T_c2], writes=[T_o1n[k2]])
                        op("dve", lambda e: e.scalar_tensor_tensor(oc3[:, idx, :], PB[ob][:, s * 144:s * 144 + 128], stats[:, 0:1], o1n[k2][:, :],
                                                                  ALU.mult, ALU.add), reads=[TPB[ob], T_st, T_o1n[k2]], writes=[T_oca[idx]])
                        op("dve", lambda e: e.bn_stats(stats[:, 8:14], oc3[:, idx, :]), reads=[T_oca[idx]], writes=[T_st])
                        op("dve", lambda e: e.bn_aggr(mv3[:, idx, :], stats[:, 8:14]), reads=[T_st], writes=[T_mv])
                op("dve", lambda e: e.tensor_tensor(msr[:, 0:18], mv3[:, :, 0], mv3[:, :, 0], ALU.mult), reads=[T_mv], writes=[T_mv])
                op("dve", lambda e: e.tensor_tensor(msr[:, 0:18], msr[:, 0:18], mv3[:, :, 1], ALU.add), reads=[T_mv], writes=[T_mv])
                rsqrt_small(msr[:, 32:50], msr[:, 0:18], epsrms, T_mv)
                for idx in range(18):
                    k2 = idx % 2
                    qd = idx % 4
                    op("dve", lambda e: e.scalar_tensor_tensor(osc[k2][:, :], oc3[:, idx, :], msr[:, 32 + idx:33 + idx], subg[:, :],
                                                              ALU.mult, ALU.mult), reads=[T_oca[idx], T_mv, T_c2], writes=[T_osc[k2]])
                    op("pe", lambda e: e.transpose(PB[3][:, qd * 128:(qd + 1) * 128], osc[k2][:, :], ident[:]),
                       reads=[T_osc[k2], T_const], writes=[TQ3[qd]])
                    op("dve", lambda e: e.tensor_tensor(mixT[:, 4 + h, idx * 128:(idx + 1) * 128], PB[3][:, qd * 128:(qd + 1) * 128],
                                                       sbg[:, idx * 128:(idx + 1) * 128], ALU.mult),
                       reads=[TQ3[qd], T_sbg], writes=[T_mix])
            chk(3)
            phase_outproj(l, b, abwout_d, list(range(18)))
            chk(4)

        def pass_layer1(b):
            l = 1
            kb.barrier()
            phase_transposes(l, b)
            kb.barrier()
            cv = Carver()
            yT = cv.get(4 * NLAT)
            yT3 = yT.rearrange("p (j t) -> p j t", j=4)
            T_yT = [Trk() for _ in range(4)]
            hpad = [cv.get(NLAT + 32, BF16) for _ in range(2)]
            T_hp = [Trk(), Trk()]
            caS = [cv.get(512) for _ in range(2)]
            T_ca = [Trk(), Trk()]
            tt = [cv.get(512) for _ in range(2)]
            T_tt = [Trk(), Trk()]
            diag = cv.get(31 * 128, BF16)
            diag3 = diag.rearrange("p (k c) -> p k c", k=31)
            T_dg = Trk()
            wa, twa = wload(cdwin_d[:, 0:512], 512)
            wb, twb = wload(cdwin_d[:, 512:1024], 512)
            for k2 in range(2):
                op("dve", lambda e: e.memset(hpad[k2][:, :], 0.0), writes=[T_hp[k2]])
            for j in range(4):
                hp = hpad[j % 2]
                thp = T_hp[j % 2]
                for k in range(31):
                    op("dve", lambda e: e.tensor_scalar(diag3[:, k, :], identb[:, :], dwT[:, j * 31 + k:j * 31 + k + 1], None, ALU.mult),
                       reads=[T_c2], writes=[T_dg])
                for si, (t0, n) in enumerate(LSEGS):
                    k2 = si % 2
                    lt0 = t0 - NCTX
                    proj_fm(wa, twa, j * 128, t0, n, si % 2)
                    op("act", lambda e: e.copy(caS[k2][:, :], PB[si % 2][:, :]), reads=[TPB[si % 2]], writes=[T_ca[k2]])
                    proj_fm(wb, twb, j * 128, t0, n, 2 + si % 2)
                    op("act", lambda e: e.activation(tt[k2][:, :], PB[2 + si % 2][:, :], AF.Tanh, scale=0.5),
                       reads=[TPB[2 + si % 2]], writes=[T_tt[k2]])
                    op("dve", lambda e: e.scalar_tensor_tensor(hp[:, 15 + lt0:15 + lt0 + n], tt[k2][:, :], 1.0, caS[k2][:, :], ALU.add, ALU.mult),
                       reads=[T_tt[k2], T_ca[k2]], writes=[thp])
                for si, (t0, n) in enumerate(LSEGS):
                    lt0 = t0 - NCTX
                    cb = 4 + si % 2
                    for k in range(31):
                        op("pe", lambda e: e.matmul(PB[cb][:, :], diag3[:, k, :], hp[:, lt0 + k:lt0 + k + 512],
                                                    start=(k == 0), stop=(k == 30)),
                           reads=[T_dg, thp], writes=[TPB[cb]], sig=(k == 30))
                    op("act", lambda e: e.activation(yT3[:, j, lt0:lt0 + 512], PB[cb][:, :], AF.Identity, bias=cvecs[:, j:j + 1]),
                       reads=[TPB[cb], T_const], writes=[T_yT[j]])
            kb.barrier()
            cv2 = Carver()
            cv2.off = 4 * NLAT * 4
            mean_bc = cv2.get(NLAT)
            rstd_bc = cv2.get(NLAT)
            T_mr = Trk()
            ysq = [cv2.get(512) for _ in range(2)]
            T_ysq = [Trk(), Trk()]
            tt = [cv2.get(512) for _ in range(2)]
            T_tt = [Trk(), Trk()]
            sg = [cv2.get(512) for _ in range(2)]
            T_sg = [Trk(), Trk()]
            yn = [cv2.get(512) for _ in range(2)]
            T_yn = [Trk(), Trk()]
            for si, (t0, n) in enumerate(LSEGS):
                lt0 = t0 - NCTX
                ts_ = slice(lt0, lt0 + 512)
                mbk, sbk = 0 + 2 * (si % 2), 1 + 2 * (si % 2)
                for j in range(4):
                    op("pe", lambda e: e.matmul(PB[mbk][:, :], onesdiv[:, :], yT3[:, j, ts_], start=(j == 0), stop=(j == 3)),
                       reads=[T_c2, T_yT[j]], writes=[TPB[mbk]])
                for j in range(4):
                    k2 = j % 2
                    op("act", lambda e: e.activation(ysq[k2][:, :], yT3[:, j, ts_], AF.Square), reads=[T_yT[j]], writes=[T_ysq[k2]])
                    op("pe", lambda e: e.matmul(PB[sbk][:, :], onesdiv[:, :], ysq[k2][:, :], start=(j == 0), stop=(j == 3)),
                       reads=[T_c2, T_ysq[k2]], writes=[TPB[sbk]])
                op("act", lambda e: e.copy(mean_bc[:, ts_], PB[mbk][:, :]), reads=[TPB[mbk]], writes=[T_mr])
                op("dve", lambda e: e.tensor_tensor(rstd_bc[:, ts_], mean_bc[:, ts_], mean_bc[:, ts_], ALU.mult), reads=[T_mr], writes=[T_mr])
                op("dve", lambda e: e.tensor_tensor(rstd_bc[:, ts_], PB[sbk][:, :], rstd_bc[:, ts_], ALU.subtract),
                   reads=[TPB[sbk], T_mr], writes=[T_mr])
                op("act", lambda e: e.activation(rstd_bc[:, ts_], rstd_bc[:, ts_], AF.Sqrt, bias=epsln), reads=[T_mr, T_c2], writes=[T_mr])
                op("dve", lambda e: e.reciprocal(rstd_bc[:, ts_], rstd_bc[:, ts_]), reads=[T_mr], writes=[T_mr])
            wg, twg = wload(cdwin_d[:, 1024:1536], 512)
            cnt = 0
            for j in range(4):
                for si, (t0, n) in enumerate(LSEGS):
                    lt0 = t0 - NCTX
                    ts_ = slice(lt0, lt0 + 512)
                    k2 = cnt % 2
                    bank = 4 + cnt % 4
                    cnt += 1
                    proj_fm(wg, twg, j * 128, t0, n, bank)
                    op("act", lambda e: e.activation(tt[k2][:, :], PB[bank][:, :], AF.Tanh, scale=0.5), reads=[TPB[bank]], writes=[T_tt[k2]])
                    op("dve", lambda e: e.scalar_tensor_tensor(sg[k2][:, :], tt[k2][:, :], 1.0, PB[bank][:, :], ALU.add, ALU.mult),
                       reads=[T_tt[k2], TPB[bank]], writes=[T_sg[k2]])
                    op("pool", lambda e: e.tensor_tensor(yn[k2][:, :], yT3[:, j, ts_], mean_bc[:, ts_], ALU.subtract),
                       reads=[T_yT[j], T_mr], writes=[T_yn[k2]])
                    op("pool", lambda e: e.tensor_tensor(yn[k2][:, :], yn[k2][:, :], rstd_bc[:, ts_], ALU.mult),
                       reads=[T_yn[k2], T_mr], writes=[T_yn[k2]])
                    op("dve", lambda e: e.tensor_scalar(yn[k2][:, :], yn[k2][:, :], cvecs[:, 4 + j:5 + j], cvecs[:, 8 + j:9 + j], ALU.mult, ALU.add),
                       reads=[T_yn[k2], T_const], writes=[T_yn[k2]])
                    op("act", lambda e: e.activation(tt[k2][:, :], yn[k2][:, :], AF.Tanh, scale=0.5), reads=[T_yn[k2], T_sg[k2]], writes=[T_tt[k2]])
                    op("dve", lambda e: e.scalar_tensor_tensor(yn[k2][:, :], tt[k2][:, :], 1.0, yn[k2][:, :], ALU.add, ALU.mult),
                       reads=[T_tt[k2], T_yn[k2]], writes=[T_yn[k2]])
                    op("dve", lambda e: e.scalar_tensor_tensor(mixT[:, j, t0:t0 + n], yn[k2][:, :], 0.25, sg[k2][:, :], ALU.mult, ALU.mult),
                       reads=[T_yn[k2], T_sg[k2]], writes=[T_mix])
            kb.barrier()
            cv = Carver()
            V1 = cv.get(18 * 2 * 80, BF16)
            V14 = V1.rearrange("p (i h c) -> p i h c", i=18, h=2)
            T_V = [Trk() for _ in range(18)]
            kT2 = cv.get(2 * NT, BF16)
            kT23 = kT2.rearrange("p (h t) -> p h t", h=2)
            T_kT = Trk()
            qT2f = cv.get(2 * NLAT, BF16)
            qT2 = qT2f.rearrange("p (c t) -> p c t", c=2)
            T_qT = Trk()
            op("pool", lambda e: e.memset(qT2f[:, :], 0.0), writes=[T_qT])
            stats2 = [cv.get(64) for _ in range(2)]
            T_st2 = [Trk(), Trk()]
            sdg = cv.get(NLAT)
            T_sdg = Trk()
            wk2 = cv.get(8 * 2 * 128, BF16)
            wk24 = wk2.rearrange("p (j h c) -> p j h c", j=8, h=2)
            T_wk2 = Trk()
            d_wk2 = kb.dsem()
            qraw = [cv.get(512, BF16) for _ in range(3)]
            T_qraw = [Trk() for _ in range(3)]
            t1 = [cv.get(512) for _ in range(3)]
            T_t1 = [Trk() for _ in range(3)]
            t2 = [cv.get(512) for _ in range(3)]
            T_t2 = [Trk() for _ in range(3)]
            pT = [cv.get(256, BF16) for _ in range(4)]
            T_pT = [Trk() for _ in range(4)]
            ocomb = [cv.get(128) for _ in range(2)]
            T_oc = [Trk(), Trk()]
            stats = cv.get(64)
            T_st = Trk()
            op("dve", lambda e: e.memset(V1[:, :], 1.0), writes=T_V)
            for kvh in range(2):
                for dup in range(2):
                    kb.dma("pool", wk24[:, :, kvh, dup * 64:(dup + 1) * 64],
                           cdwin_d[:, 2048 + kvh * 64:2048 + (kvh + 1) * 64].rearrange("(j p) c -> p j c", p=128),
                           d_wk2, writes=[T_wk2])
            wq, twq = wload(cdwin_d[:, 1536:2048], 512)
            wkv, twkv = wload(cdwin_d[:, 2048:2560], 512)
            wg2, twg2 = wload(cdwin_d[:, 2560:2816], 256)
            for i in range(18):
                bank = i % 4
                proj_tm(wkv, twkv, 128, 128, i, bank)
                op("act", lambda e: e.copy(V14[:, i, :, 0:64], PB[bank][:, 0:128].rearrange("p (h c) -> p h c", h=2)),
                   reads=[TPB[bank]], writes=[T_V[i]])
            for kvh in range(2):
                for si, (t0, n) in enumerate(SEGS):
                    k2 = si % 2
                    bank = si % 4
                    for j in range(8):
                        op("pe", lambda e: e.matmul(PB[bank][:, 0:n], wk24[:, j, kvh, :], uT[:, j, t0:t0 + n], start=(j == 0), stop=(j == 7)),
                           reads=[T_wk2] + [T_uT[i] for i in seg_tiles(t0, n)], writes=[TPB[bank]], sig=(j == 7))
                    rope_evac(bank, n, t0, kT23[:, kvh, t0:t0 + n], T_kT, 4 + k2, qraw, T_qraw, t1, T_t1, t2, T_t2)
            rope_lag.flush()
            pctr = [0]
            att_lag = Lag()
            for cc in range(4):
                kvh = cc // 2
                for si, (t0, n) in enumerate(LSEGS):
                    k2 = si % 2
                    lt0 = t0 - NCTX
                    proj_fm(wq, twq, cc * 128, t0, n, si % 4)
                    rope_evac(si % 4, n, t0, [(qT2[0:64, 0, lt0:lt0 + n], 0, 64), (qT2[64:128, 1, lt0:lt0 + n], 64, 128)], T_qT, 4 + k2, qraw, T_qraw, t1, T_t1, t2, T_t2)
                rope_lag.flush()
                for si, (t0, n) in enumerate(LSEGS):
                    k2 = si % 2
                    bank = si % 4
                    lt0 = t0 - NCTX
                    if cc < 2:
                        proj_fm(wkv, twkv, 256 + cc * 128, t0, n, bank)
                    else:
                        proj_fm(wg2, twg2, (cc - 2) * 128, t0, n, bank)
                    op("act", lambda e: e.activation(t1[k2][:, 0:n], PB[bank][:, 0:n], AF.Tanh, scale=0.5),
                       reads=[TPB[bank]], writes=[T_t1[k2]])
                    op("dve", lambda e: e.scalar_tensor_tensor(sdg[:, lt0:lt0 + n], t1[k2][:, 0:n], 1.0, PB[bank][:, 0:n], ALU.add, ALU.mult),
                       reads=[T_t1[k2], TPB[bank]], writes=[T_sdg])
                steps = []
                for qb in range(16):
                    kts = []
                    if qb > 0:
                        kts.append((2 + qb - 1, 0))
                    kts.append((2 + qb, None))
                    if qb < 15:
                        kts.append((2 + qb + 1, 1))
                    kts += [(0, None), (1, None)]
                    for ki, (kt, mk) in enumerate(kts):
                        steps.append((qb, ki, kt, mk, ki == len(kts) - 1))

                def qk1(step, sbank):
                    qb, ki, kt, mk, last = step
                    q0 = qb * 128
                    op("pe", lambda e: e.matmul(PB[sbank][:, 0:256].rearrange("p (c q) -> p c q", c=2),
                                                kT23[:, kvh, kt * 128:(kt + 1) * 128],
                                                qT2[:, :, q0:q0 + 128], start=True, stop=(mk is None), skip_group_check=True),
                       reads=[T_kT, T_qT], writes=[TPB[sbank]])
                    if mk is not None:
                        op("pe", lambda e: e.matmul(PB[sbank][:, 0:256], identb[:, :], masks[:, mk, :], start=False, stop=True,
                                                    skip_group_check=True),
                           reads=[T_c2, T_constp], writes=[TPB[sbank]])

                for pre in range(2):
                    qk1(steps[pre], pre % 4)
                for i, step in enumerate(steps):
                    qb, ki, kt, mk, last = step
                    q0 = qb * 128
                    if i + 2 < len(steps):
                        qk1(steps[i + 2], (i + 2) % 4)
                    sbank = i % 4
                    ob = 4 + qb % 2
                    pi = pctr[0] % 4
                    pctr[0] += 1
                    op("act", lambda e: e.activation(pT[pi][:, :], PB[sbank][:, 0:256], AF.Exp, scale=0.125),
                       reads=[TPB[sbank]], writes=[T_pT[pi]])
                    for hl in range(2):
                        first = (ki == 0 and hl == 0)
                        op("pe", lambda e: e.matmul(PB[ob][:, hl * 80:hl * 80 + 66], pT[pi][:, hl * 128:(hl + 1) * 128],
                                                    V14[:, kt, kvh, 0:66], start=first, stop=last,
                                                    skip_group_check=True),
                           reads=[T_pT[pi], T_V[kt]], writes=[TPB[ob]], sig=(hl == 1))
                    if not last:
                        continue
                    k2 = qb % 2
                    for hl in range(2):
                        hd = cc * 2 + hl
                        op("dve", lambda e: e.tensor_scalar(stats2[k2][:, hl:hl + 1], PB[ob][:, hl * 80 + 64:hl * 80 + 65], sinkt[:, hd:hd + 1], None, ALU.add),
                           reads=[TPB[ob], T_c2], writes=[T_st2[k2]])
                        op("dve", lambda e: e.reciprocal(stats2[k2][:, 2 + hl:3 + hl], stats2[k2][:, hl:hl + 1]), reads=[T_st2[k2]], writes=[T_st2[k2]])
                        op("dve", lambda e: e.tensor_scalar(ocomb[k2][:, hl * 64:(hl + 1) * 64], PB[ob][:, hl * 80:hl * 80 + 64],
                                                           stats2[k2][:, 2 + hl:3 + hl], None, ALU.mult),
                           reads=[TPB[ob], T_st2[k2]], writes=[T_oc[k2]])

                    def post_b(cc=cc, qb=qb, q0=q0, k2=k2):
                        tb = 6 + qb % 2
                        op("pe", lambda e: e.transpose(PB[tb][:, 0:128], ocomb[k2][:, :], ident[:]), reads=[T_oc[k2], T_const], writes=[TPB[tb]])
                        op("dve", lambda e: e.scalar_tensor_tensor(mixT[:, 4 + cc, NCTX + q0:NCTX + q0 + 128], PB[tb][:, 0:128], 0.5,
                                                                  sdg[:, q0:q0 + 128], ALU.mult, ALU.mult),
                           reads=[TPB[tb], T_sdg], writes=[T_mix])
                    att_lag.push(post_b)
                att_lag.flush()
            phase_outproj(l, b, cdwout_d, list(range(2, 18)))

        try:
            chk(0)
            for b in range(2):
                pass_layer0(b)
            if not debug_h1:
                for b in range(2):
                    pass_layer1(b)
        except _Stop:
            pass
        kb.barrier()
        for ds in d_o:
            if ds.count:
                nc.sync.wait_ge(ds.sem, ds.count)
    return nc


def _consts():
    ident = np.eye(128, dtype=np.float32)
    rmat = np.zeros((128, 128), np.float32)
    for dp in range(128):
        partner = dp + 16 if (dp % 32) < 16 else dp - 16
        rmat[partner, dp] = 1.0
    m = 16
    inv = (10000.0 ** (-np.arange(m, dtype=np.float32) / m)).astype(np.float32)
    t = np.arange(NLAT)
    row = (t // 64).astype(np.float32)
    col = (t % 64).astype(np.float32)
    ang_r = (row[:, None] * inv[None, :]).astype(np.float32)
    ang_c = (col[:, None] * inv[None, :]).astype(np.float32)
    cos_t = np.ones((128, NT), np.float32)
    sin_t = np.zeros((128, NT), np.float32)
    for p in range(128):
        d = p % 64
        ang = ang_r if d < 32 else ang_c
        f = d % 16
        sign = -1.0 if (d % 32) < 16 else 1.0
        cos_t[p, NCTX:] = np.cos(ang[:, f])
        sin_t[p, NCTX:] = sign * np.sin(ang[:, f])
    ropeT = np.concatenate([cos_t, sin_t], axis=1)
    kk = np.arange(128)[:, None]
    qq = np.arange(128)[None, :]
    mp = np.where(kk >= qq, 0.0, -30000.0).astype(np.float32)
    mn = np.where(kk <= qq, 0.0, -30000.0).astype(np.float32)
    masks = np.concatenate([mp, mp, mn, mn], axis=1)
    return ident, rmat, ropeT, masks


_CACHE = {}


def _core_inputs(core, x, c, ctx, c_ctx, mod_w, mod_b, ln_g, ln_b, ab_w_in, ab_w_out, a_w_s, a_b_s,
                 a_norm_g, a_norm_b, b_lq1, b_lk1, b_lq2, b_lk2, b_subln_g, cd_w_in, cd_w_out,
                 c_dw_w, c_dw_b, c_norm_g, c_norm_b, d_sink, shared):
    b0 = 2 * core
    cvec = np.stack([c[b0], c[b0 + 1], c_ctx], axis=0)
    cvT = np.ascontiguousarray(cvec.reshape(3, 8, 128).transpose(2, 1, 0)).reshape(128, 24)
    d = dict(shared)
    d["x"] = np.ascontiguousarray(x[b0:b0 + 2])
    d["ctx"] = np.ascontiguousarray(ctx[b0:b0 + 2])
    d["cvT"] = cvT
    return d


def kernel(x, c, ctx, c_ctx, mod_w, mod_b, ln_g, ln_b, ab_w_in, ab_w_out, a_w_s, a_b_s,
           a_norm_g, a_norm_b, b_lq1, b_lk1, b_lq2, b_lk2, b_subln_g, cd_w_in, cd_w_out,
           c_dw_w, c_dw_b, c_norm_g, c_norm_b, d_sink, _debug_h1=False, _stage=99):
    f = lambda a: np.ascontiguousarray(np.asarray(a, dtype=np.float32))
    x, c, ctx, c_ctx = f(x), f(c), f(ctx), f(c_ctx)
    ident, rmat, ropeT, masks = _consts()
    shared = {
        "mod_w": f(mod_w),
        "mod_bT": np.ascontiguousarray(f(mod_b).reshape(2, 24, 128).transpose(2, 0, 1)).reshape(128, 48),
        "ln_g": f(ln_g), "ln_b": f(ln_b),
        "ab_w_in": f(ab_w_in)[0], "ab_w_out": f(ab_w_out)[0],
        "a_w_sT": np.ascontiguousarray(f(a_w_s)[0].transpose(2, 0, 1)).reshape(128, 512),
        "a_b_s": f(a_b_s)[0].reshape(1, 512),
        "a_norm_g": f(a_norm_g).reshape(1, 512), "a_norm_b": f(a_norm_b).reshape(1, 512),
        "lam_in": np.concatenate([f(b_lq1)[0], f(b_lk1)[0], f(b_lq2)[0], f(b_lk2)[0]]).reshape(1, 256),
        "b_subln_g": f(b_subln_g).reshape(1, 128),
        "cd_w_in": f(cd_w_in)[0], "cd_w_out": f(cd_w_out)[0],
        "dwT": np.ascontiguousarray(f(c_dw_w)[0].reshape(31, 4, 128).transpose(2, 1, 0)).reshape(128, 124),
        "cvecs": np.ascontiguousarray(np.stack([f(c_dw_b)[0], f(c_norm_g)[0], f(c_norm_b)[0]], 0)
                                      .reshape(3, 4, 128).transpose(2, 0, 1)).reshape(128, 12),
        "d_sink": f(d_sink).reshape(1, 8),
        "ident": ident, "rmat": rmat, "ropeT": ropeT, "masks": masks,
    }
    in_maps = []
    for core in range(8):
        in_maps.append(_core_inputs(core, x, c, ctx, c_ctx, None, None, None, None, None, None, None, None,
                                    None, None, None, None, None, None, None, None, None,
                                    None, None, None, None, None, shared))
    key = (bool(_debug_h1), _stage)
    if key not in _CACHE:
        _CACHE[key] = build_program(debug_h1=key[0], stage=_stage)
    nc = _CACHE[key]
    res = run_bass_kernel_spmd(nc, in_maps, core_ids=list(range(8)))
    if _debug_h1:
        return np.concatenate([r["h1"] for r in res.results], axis=0)
    return np.concatenate([r["out"] for r in res.results], axis=0).astype(np.float32)
```

```python
import math
import numpy as np
from contextlib import ExitStack
import concourse.bass as bass
import concourse.mybir as mybir
from concourse.bass_utils import run_bass_kernel_spmd

F32 = mybir.dt.float32
BF16 = mybir.dt.bfloat16
AF = mybir.ActivationFunctionType
ALU = mybir.AluOpType

NT = 2304
NCTX = 256
NLAT = 2048
D = 1024
LN_EPS = 1e-6
RMS_EPS = 1e-5
ALPHA = (2.0 * 2) ** 0.25
GC0 = math.sqrt(2.0 / math.pi)
GC1 = 0.044715
SEGS = [(0, 256), (256, 512), (768, 512), (1280, 512), (1792, 512)]
LSEGS = SEGS[1:]


class Trk:
    __slots__ = ("name", "w", "r")

    def __init__(self, name=""):
        self.name = name
        self.w = None
        self.r = {}


class DSem:
    def __init__(self, sem):
        self.sem = sem
        self.count = 0


class KB:
    def __init__(self, nc, es):
        self.nc = nc
        self.es = es
        self.eng = {"pe": nc.tensor, "act": nc.scalar, "dve": nc.vector, "pool": nc.gpsimd, "sp": nc.sync}
        self.esem = {}
        self.ecnt = {}
        for e in ("pe", "act", "dve", "pool"):
            self.esem[e] = es.enter_context(nc.semaphore("s_" + e))
            self.ecnt[e] = 0
        self.seen = {e: {} for e in self.eng}
        self.nsem = 0
        self.hooks = []
        self.bar_dsems = []

    def dsem(self):
        self.nsem += 1
        return DSem(self.es.enter_context(self.nc.semaphore(f"d{self.nsem}")))

    def _wait(self, e, reads, writes):
        evs = {}

        def add(ev):
            if ev is None:
                return
            k = id(ev[0])
            if k not in evs or evs[k][1] < ev[1]:
                evs[k] = ev

        for t in reads:
            add(t.w)
        for t in writes:
            add(t.w)
            for ev in t.r.values():
                add(ev)
        seen = self.seen[e]
        own = self.esem.get(e)
        for k, (sem, val) in evs.items():
            if e == "pe" and sem is own:
                continue
            if seen.get(k, 0) >= val:
                continue
            self.eng[e].wait_ge(sem, val)
            seen[k] = val

    def _post(self, ev, reads, writes):
        k = id(ev[0])
        for t in writes:
            t.w = ev
            t.r = {}
        for t in reads:
            t.r[k] = ev

    def op(self, e, fn, reads=(), writes=(), sig=True):
        self._wait(e, reads, writes)
        ins = fn(self.eng[e])
        if sig:
            self.ecnt[e] += 1
            ins.then_inc(self.esem[e], 1)
            ev = (self.esem[e], self.ecnt[e])
        else:
            ev = (self.esem[e], self.ecnt[e] + 1)
        self._post(ev, reads, writes)
        return ev

    def dma(self, q, out, in_, ds, reads=(), writes=()):
        self._wait(q, reads, writes)
        ins = self.eng[q].dma_start(out=out, in_=in_)
        ds.count += 16
        ins.then_inc(ds.sem, 16)
        ev = (ds.sem, ds.count)
        self._post(ev, reads, writes)
        return ev

    def barrier(self):
        for h in self.hooks:
            h()
        for e in self.eng:
            seen = self.seen[e]
            for ds in self.bar_dsems:
                k = id(ds.sem)
                if ds.count and seen.get(k, 0) < ds.count:
                    self.eng[e].wait_ge(ds.sem, ds.count)
                    seen[k] = ds.count
            for f in ("pe", "act", "dve", "pool"):
                if f == e or self.ecnt[f] == 0:
                    continue
                k = id(self.esem[f])
                if seen.get(k, 0) >= self.ecnt[f]:
                    continue
                self.eng[e].wait_ge(self.esem[f], self.ecnt[f])
                seen[k] = self.ecnt[f]


class _Stop(Exception):
    pass


LAG_ON = True


class Lag:
    def __init__(self):
        self.p = None

    def push(self, fn):
        if not LAG_ON:
            fn()
            return
        old, self.p = self.p, fn
        if old:
            old()

    def flush(self):
        old, self.p = self.p, None
        if old:
            old()


def build_program(debug_h1=False, stage=99):
    nc = bass.Bass("TRN2", target_bir_lowering=False)

    def chk(n):
        if stage <= n:
            raise _Stop()

    def din(name, shape):
        return nc.dram_tensor(name, list(shape), F32, kind="ExternalInput").ap()

    x_d = din("x", [2, NLAT, D])
    ctx_d = din("ctx", [2, NCTX, D])
    cvT_d = din("cvT", [128, 24])
    modw_d = din("mod_w", [2, D, 3 * D])
    modbT_d = din("mod_bT", [128, 48])
    lng_d = din("ln_g", [2, D])
    lnb_d = din("ln_b", [2, D])
    abwin_d = din("ab_w_in", [D, 3584])
    abwout_d = din("ab_w_out", [D, D])
    awsT_d = din("a_w_sT", [128, 512])
    abs_d = din("a_b_s", [1, 512])
    ang_d = din("a_norm_g", [1, 512])
    anb_d = din("a_norm_b", [1, 512])
    lam_d = din("lam_in", [1, 256])
    subg_d = din("b_subln_g", [1, 128])
    cdwin_d = din("cd_w_in", [D, 2816])
    cdwout_d = din("cd_w_out", [D, D])
    dwT_d = din("dwT", [128, 124])
    cvec_d = din("cvecs", [128, 12])
    sink_d = din("d_sink", [1, 8])
    ident_d = din("ident", [128, 128])
    rmat_d = din("rmat", [128, 128])
    rope_d = din("ropeT", [128, 2 * NT])
    mask_d = din("masks", [128, 512])
    out_d = nc.dram_tensor("out", [2, NLAT, D], F32, kind="ExternalOutput").ap()
    if debug_h1:
        h1_d = nc.dram_tensor("h1", [2, NT, D], F32, kind="ExternalOutput").ap()
    else:
        h1_d = nc.dram_tensor("h1", [2, NT, D], F32).ap()

    with ExitStack() as es:
        kb = KB(nc, es)
        op = kb.op

        def sb(name, shape, dt=F32):
            return es.enter_context(nc.sbuf_tensor("sb_" + name, list(shape), dt))

        PB = [es.enter_context(nc.psum_tensor(f"pb{i}", [128, 512], F32)) for i in range(8)]
        TPB = [Trk(f"pb{i}") for i in range(8)]

        d_const = kb.dsem()
        T_const = Trk("const")
        d_constp = kb.dsem()
        T_constp = Trk("constp")
        ident = sb("ident", [128, 128])
        identb = sb("identb", [128, 128], BF16)
        rmat = sb("rmat", [128, 128], BF16)
        ropeT = sb("ropeT", [128, 2, NT])
        masks = sb("masks", [128, 2, 256], BF16)
        cvT = sb("cvT", [128, 24])
        modbT = sb("modbT", [128, 48])
        wsT = sb("wsT", [128, 4, 128], BF16)
        lamt = sb("lamt", [128, 256])
        subg = sb("subg", [128, 128])
        dwT = sb("dwT", [128, 124])
        cvecs = sb("cvecs", [128, 12])
        sinkt = sb("sinkt", [128, 8])
        ones_f = sb("ones_f", [128, 128])
        onesdiv = sb("onesdiv", [128, 128])
        sTb = sb("sTb", [128, 24], BF16)
        modT = sb("modT", [128, 144])
        small = sb("small", [128, 64])
        epsln = small[:, 0:1]
        epsrms = small[:, 1:2]
        epsln4 = small[:, 2:3]
        neglam = small[:, 3:4]

        def cdma(q, dst, src):
            if q == "pool":
                kb.dma(q, dst, src, d_constp, writes=[T_constp])
            else:
                kb.dma(q, dst, src, d_const, writes=[T_const])

        cdma("sp", ident[:], ident_d)
        cdma("pool", rmat[:], rmat_d)
        cdma("sp", ropeT[:], rope_d.rearrange("p (a t) -> p a t", a=2))
        cdma("pool", masks[:], mask_d.rearrange("p (a t) -> p a t", a=2))
        cdma("sp", cvT[:], cvT_d)
        cdma("sp", modbT[:], modbT_d)
        cdma("pool", wsT[:], awsT_d.rearrange("p (g q) -> p g q", g=4))
        cdma("sp", lamt[:], lam_d.partition_broadcast(128))
        cdma("sp", subg[:], subg_d.partition_broadcast(128))
        cdma("sp", dwT[:], dwT_d)
        cdma("sp", cvecs[:], cvec_d)
        cdma("sp", sinkt[:], sink_d.partition_broadcast(128))

        T_c2 = Trk("c2")
        RC = [T_const, T_c2]
        op("dve", lambda e: e.memset(ones_f[:], 1.0), writes=[T_c2])
        op("dve", lambda e: e.memset(onesdiv[:], 1.0 / 512.0), writes=[T_c2])
        op("dve", lambda e: e.memset(small[:, 0:1], LN_EPS), writes=[T_c2])
        op("dve", lambda e: e.memset(small[:, 1:2], RMS_EPS), writes=[T_c2])
        op("dve", lambda e: e.memset(small[:, 2:3], 4.0 * LN_EPS), writes=[T_c2])
        op("dve", lambda e: e.tensor_copy(identb[:], ident[:]), reads=[T_const], writes=[T_c2])
        lam_init0 = 0.8 - 0.6 * math.exp(-0.3 * 0)
        lprod = sb("lprod", [128, 128])
        op("dve", lambda e: e.tensor_tensor(lprod[:, 0:64], lamt[:, 0:64], lamt[:, 64:128], ALU.mult),
           reads=[T_const], writes=[T_c2])
        op("dve", lambda e: e.tensor_tensor(lprod[:, 64:128], lamt[:, 128:192], lamt[:, 192:256], ALU.mult),
           reads=[T_c2, T_const], writes=[T_c2])
        op("dve", lambda e: e.reduce_sum(small[:, 4:6], lprod[:].rearrange("p (a b) -> p a b", a=2),
                                        mybir.AxisListType.X), reads=[T_c2], writes=[T_c2])
        op("act", lambda e: e.activation(small[:, 6:8], small[:, 4:6], AF.Exp), reads=[T_c2], writes=[T_c2])
        op("dve", lambda e: e.scalar_tensor_tensor(small[:, 3:4], small[:, 7:8], -lam_init0, small[:, 6:7],
                                                  ALU.add, ALU.subtract), reads=[T_c2], writes=[T_c2])
        op("dve", lambda e: e.tensor_scalar(subg[:], subg[:], (1.0 - lam_init0) * 0.5, None, ALU.mult),
           reads=[T_const, T_c2], writes=[T_c2])
        op("act", lambda e: e.activation(sinkt[:], sinkt[:], AF.Exp), reads=[T_const, T_c2], writes=[T_c2])
        op("dve", lambda e: e.tensor_scalar(dwT[:], dwT[:], 0.5, None, ALU.mult), reads=[T_const, T_c2], writes=[T_c2])
        sct = sb("sct", [128, 24])
        op("act", lambda e: e.activation(sct[:], cvT[:], AF.Tanh, scale=0.5), reads=[T_const], writes=[T_c2])
        op("dve", lambda e: e.scalar_tensor_tensor(sct[:], sct[:], 1.0, cvT[:], ALU.add, ALU.mult),
           reads=[T_c2, T_const], writes=[T_c2])
        op("dve", lambda e: e.tensor_scalar(sTb[:], sct[:], 0.5, None, ALU.mult), reads=[T_c2], writes=[T_c2])

        NW = 3
        wslot = [sb(f"wslot{i}", [128, 8, 512], BF16) for i in range(NW)]
        T_w = [Trk(f"w{i}") for i in range(NW)]
        d_w = [kb.dsem() for _ in range(NW)]
        wctr = [0]

        def wload(src_ap, ncols):
            i = wctr[0] % NW
            wctr[0] += 1
            kb.dma("pool", wslot[i][:, :, 0:ncols], src_ap.rearrange("(j p) c -> p j c", p=128), d_w[i],
                   writes=[T_w[i]])
            return wslot[i], T_w[i]

        NS = 3
        srct = [None] * NS
        T_s = [Trk(f"s{i}") for i in range(NS)]
        d_s = [kb.dsem() for _ in range(NS)]
        sctr = [0]
        T_s8 = [Trk(f"s8_{i}") for i in range(8)]
        d_s8 = [kb.dsem() for _ in range(8)]

        def sload(src_ap):
            i = sctr[0] % NS
            sctr[0] += 1
            kb.dma("sp", srct[i][:], src_ap, d_s[i], writes=[T_s[i]])
            return srct[i], T_s[i]

        NO = 3
        outt = [None] * NO
        T_o = [Trk(f"o{i}") for i in range(NO)]
        d_o = [kb.dsem() for _ in range(NO)]
        octr = [0]
        kb.bar_dsems = d_o
        T_h1 = [[Trk(f"h1_{b}_{i}") for i in range(18)] for b in range(2)]

        T_mod = Trk("mod")
        for l in range(2):
            for g in range(3):
                for half in range(2):
                    c0 = g * 1024 + half * 512
                    ws, tw = wload(modw_d[l, :, c0:c0 + 512], 512)
                    for cc in range(4):
                        for j in range(8):
                            op("pe", lambda e: e.matmul(PB[7][:, cc * 4:cc * 4 + 3], ws[:, j, cc * 128:(cc + 1) * 128],
                                                        sTb[:, j * 3:(j + 1) * 3], start=(j == 0), stop=(j == 7)),
                               reads=[tw, T_c2], writes=[TPB[7]])
                    for cc in range(4):
                        k = g * 8 + half * 4 + cc
                        o0 = (l * 24 + k) * 3
                        op("dve", lambda e: e.tensor_scalar(modT[:, o0:o0 + 3], PB[7][:, cc * 4:cc * 4 + 3],
                                                           modbT[:, l * 24 + k:l * 24 + k + 1],
                                                           1.0 if g == 1 else 0.0, ALU.add, ALU.add),
                           reads=[TPB[7], T_const], writes=[T_mod])

        def mod_ap(l, k, r):
            o0 = (l * 24 + k) * 3 + r
            return modT[:, o0:o0 + 1]

        uT = sb("uT", [128, 8, NT], BF16)
        T_uT = [Trk(f"uT{i}") for i in range(18)]
        mixT = sb("mixT", [128, 8, NT], BF16)
        T_mix = Trk("mixT")
        T_gbc = Trk("gbc")
        T_diagf = [Trk(), Trk()]
        T_lngb = Trk("lngb")
        d_lngb = kb.dsem()
        T_ab = Trk("ab")
        d_ab = kb.dsem()
        ARENA = 84 * 1024
        arena = sb("arena", [128, ARENA // 4])

        class Carver:
            def __init__(self):
                self.off = 0

            def get(self, nelem, dt=F32):
                nbytes = nelem * (4 if dt == F32 else 2)
                nbytes = (nbytes + 31) // 32 * 32
                assert self.off + nbytes <= ARENA, (self.off, nbytes)
                a = arena[:, self.off // 4:(self.off + nbytes) // 4]
                self.off += nbytes
                if dt == BF16:
                    a = a.bitcast(BF16)[:, 0:nelem]
                return a

        def tile_rows(l, b, i):
            if l == 0:
                return ctx_d[b, i * 128:(i + 1) * 128, :] if i < 2 else x_d[b, (i - 2) * 128:(i - 1) * 128, :]
            return h1_d[b, i * 128:(i + 1) * 128, :]

        def seg_tiles(t0, n):
            return list(range(t0 // 128, (t0 + n) // 128))

        def proj_fm(ws, tw, c0, t0, n, bank):
            for j in range(8):
                op("pe", lambda e: e.matmul(PB[bank][:, 0:n], ws[:, j, c0:c0 + 128], uT[:, j, t0:t0 + n],
                                            start=(j == 0), stop=(j == 7)),
                   reads=[tw] + [T_uT[i] for i in seg_tiles(t0, n)], writes=[TPB[bank]], sig=(j == 7))

        def proj_tm(ws, tw, c0, ncols, i, bank):
            for j in range(8):
                op("pe", lambda e: e.matmul(PB[bank][:, 0:ncols], uT[:, j, i * 128:(i + 1) * 128], ws[:, j, c0:c0 + ncols],
                                            start=(j == 0), stop=(j == 7)),
                   reads=[tw, T_uT[i]], writes=[TPB[bank]], sig=(j == 7))

        def rsqrt_small(dst, src, eps_ap, trk, scale=1.0):
            op("act", lambda e: e.activation(dst, src, AF.Sqrt, bias=eps_ap, scale=scale), reads=[trk, T_c2], writes=[trk])
            op("dve", lambda e: e.reciprocal(dst, dst), reads=[trk], writes=[trk])

        rope_lag = Lag()
        kb.hooks.append(rope_lag.flush)

        rctr = [0]

        def rope_evac(bank, n, t0, dst, T_dst, rb, qraw, T_qraw, t1, T_t1, t2, T_t2):
            k = rctr[0] % 3
            rctr[0] += 1
            rb = 4 + k
            qraw, T_qraw, t1, T_t1, t2, T_t2 = qraw[k], T_qraw[k], t1[k], T_t1[k], t2[k], T_t2[k]
            op("act", lambda e: e.copy(qraw[:, 0:n], PB[bank][:, 0:n]), reads=[TPB[bank]], writes=[T_qraw])
            rope_lag.push(lambda: rope_rest(n, t0, dst, T_dst, rb, qraw, T_qraw, t1, T_t1, t2, T_t2))

        def rope_rest(n, t0, dst, T_dst, rb, qraw, T_qraw, t1, T_t1, t2, T_t2):
            op("pe", lambda e: e.matmul(PB[rb][:, 0:n], rmat[:], qraw[:, 0:n], start=True, stop=True),
               reads=[T_qraw, T_constp], writes=[TPB[rb]])
            op("dve", lambda e: e.tensor_tensor(t1[:, 0:n], qraw[:, 0:n], ropeT[:, 0, t0:t0 + n], ALU.mult),
               reads=[T_qraw, T_const], writes=[T_t1])
            op("dve", lambda e: e.tensor_tensor(t2[:, 0:n], PB[rb][:, 0:n], ropeT[:, 1, t0:t0 + n], ALU.mult),
               reads=[TPB[rb], T_const], writes=[T_t2])
            if not isinstance(dst, list):
                dst = [(dst, 0, 128)]
            for (dap, p0, p1) in dst:
                op("pool", lambda e: e.tensor_tensor(dap, t1[p0:p1, 0:n], t2[p0:p1, 0:n], ALU.add),
                   reads=[T_t1, T_t2], writes=[T_dst])

        def build_gate_bc(l, r, slot, gbc, diagf):
            for cc in range(8):
                dg = diagf[cc % 2]
                op("dve", lambda e: e.tensor_scalar(dg[:], ident[:], mod_ap(l, 16 + cc, r), None, ALU.mult),
                   reads=[T_const, T_mod], writes=[T_diagf[cc % 2]])
                op("pe", lambda e: e.matmul(PB[7][:, (cc % 4) * 128:(cc % 4 + 1) * 128], ones_f[:], dg[:],
                                            start=True, stop=True),
                   reads=[T_c2, T_diagf[cc % 2]], writes=[TPB[7]])
                if cc % 4 == 3:
                    h = cc // 4
                    op("act", lambda e: e.copy(gbc[:, slot, h * 512:(h + 1) * 512], PB[7][:, :]),
                       reads=[TPB[7]], writes=[T_gbc])

        def sload_dep(l, b, i):
            idx = sctr[0] % NS
            sctr[0] += 1
            rd = [T_h1[b][i]] if l == 1 else []
            kb.dma("sp", srct[idx][:], tile_rows(l, b, i), d_s[idx], reads=rd, writes=[T_s[idx]])
            return srct[idx], T_s[idx]

        def phase_transposes(l, b):
            cvt = Carver()
            NSL = 8
            slots = [cvt.get(D) for _ in range(NSL)]
            groups = [[0, 1], [2, 3, 4, 5], [6, 7, 8, 9], [10, 11, 12, 13], [14, 15, 16, 17]]
            bctr = 0
            for gi, grp in enumerate(groups):
                r = 2 if grp[0] < 2 else b
                loaded = []
                for i in grp:
                    idx = sctr[0] % NSL
                    sctr[0] += 1
                    rd = [T_h1[b][i]] if l == 1 else []
                    kb.dma("sp", slots[idx][:, :], tile_rows(l, b, i), d_s8[idx], reads=rd, writes=[T_s8[idx]])
                    loaded.append((slots[idx], T_s8[idx]))
                ng = len(grp)
                t0 = grp[0] * 128
                for j in range(8):
                    bank = bctr % 8
                    bctr += 1
                    for q, (st, ts) in enumerate(loaded):
                        op("pe", lambda e: e.transpose(PB[bank][:, q * 128:(q + 1) * 128], st[:, j * 128:(j + 1) * 128], ident[:]),
                           reads=[ts, T_const], writes=[TPB[bank]], sig=(q == len(loaded) - 1))
                    wr = [T_uT[i] for i in grp]
                    if j % 2 == 0:
                        op("dve", lambda e: e.tensor_scalar(uT[:, j, t0:t0 + ng * 128], PB[bank][:, 0:ng * 128],
                                                           mod_ap(l, 8 + j, r), mod_ap(l, j, r), ALU.mult, ALU.add),
                           reads=[TPB[bank], T_mod], writes=wr)
                    else:
                        op("act", lambda e: e.activation(uT[:, j, t0:t0 + ng * 128], PB[bank][:, 0:ng * 128], AF.Identity,
                                                        bias=mod_ap(l, j, r), scale=mod_ap(l, 8 + j, r)),
                           reads=[TPB[bank], T_mod], writes=wr)

        def phase_outproj(l, b, wout_d, tiles):
            kb.barrier()
            cv = Carver()
            for k in range(NS):
                srct[k] = cv.get(D)
            for k in range(NO):
                outt[k] = cv.get(D)
            zt = [cv.get(D) for _ in range(4)]
            T_z = [Trk() for _ in range(4)]
            statsl = [cv.get(64) for _ in range(4)]
            T_stl = [Trk() for _ in range(4)]
            lngb = cv.get(2 * D).rearrange("p (a d) -> p a d", a=2)
            gbc = cv.get(2 * D).rearrange("p (a d) -> p a d", a=2)
            diagf = [cv.get(128) for _ in range(2)]
            kb.dma("sp", lngb[:, 0, :], lng_d[l:l + 1, :].partition_broadcast(128), d_lngb, writes=[T_lngb])
            kb.dma("sp", lngb[:, 1, :], lnb_d[l:l + 1, :].partition_broadcast(128), d_lngb, writes=[T_lngb])
            build_gate_bc(l, b, 0, gbc, diagf)
            if l == 0:
                build_gate_bc(l, 2, 1, gbc, diagf)
            w0, tw0 = wload(wout_d[:, 0:512], 512)
            w1, tw1 = wload(wout_d[:, 512:1024], 512)
            o_lag = Lag()
            for n_i, i in enumerate(tiles):
                st, ts = sload_dep(l, b, i)
                gs = 1 if i < 2 else 0
                pb0 = (n_i % 3) * 2
                for half, (ws, tw) in enumerate(((w0, tw0), (w1, tw1))):
                    for j in range(8):
                        op("pe", lambda e: e.matmul(PB[pb0 + half][:, :], mixT[:, j, i * 128:(i + 1) * 128], ws[:, j, :],
                                                    start=(j == 0), stop=(j == 7)),
                           reads=[T_mix, tw], writes=[TPB[pb0 + half]], sig=(j == 7))
                z = zt[n_i % 4]
                tz = T_z[n_i % 4]
                stats = statsl[n_i % 4]
                T_st = T_stl[n_i % 4]
                for half in range(2):
                    hs = slice(half * 512, (half + 1) * 512)
                    op("dve", lambda e: e.tensor_tensor(z[:, hs], PB[pb0 + half][:, :], gbc[:, gs, hs], ALU.mult),
                       reads=[TPB[pb0 + half], T_gbc], writes=[tz])
                    op("dve", lambda e: e.scalar_tensor_tensor(z[:, hs], st[:, hs], ALPHA, z[:, hs], ALU.mult, ALU.add),
                       reads=[ts, tz], writes=[tz])
                    op("dve", lambda e: e.bn_stats(stats[:, half * 6:(half + 1) * 6], z[:, hs]), reads=[tz], writes=[T_st])
                op("dve", lambda e: e.bn_aggr(stats[:, 16:18], stats[:, 0:12].rearrange("p (a b) -> p a b", a=2)),
                   reads=[T_st], writes=[T_st])
                def tail(l=l, b=b, i=i, z=z, tz=tz, stats=stats, T_st=T_st):
                    rsqrt_small(stats[:, 18:19], stats[:, 17:18], epsln, T_st)
                    op("dve", lambda e: e.scalar_tensor_tensor(stats[:, 19:20], stats[:, 16:17], -1.0, stats[:, 18:19],
                                                              ALU.mult, ALU.mult), reads=[T_st], writes=[T_st])
                    oi = octr[0] % NO
                    octr[0] += 1
                    ot, to = outt[oi], T_o[oi]
                    op("act", lambda e: e.activation(z[:, :], z[:, :], AF.Identity, bias=stats[:, 19:20], scale=stats[:, 18:19]),
                       reads=[tz, T_st], writes=[tz])
                    op("dve", lambda e: e.tensor_tensor(z[:, :], z[:, :], lngb[:, 0, :], ALU.mult), reads=[tz, T_lngb], writes=[tz])
                    op("pool", lambda e: e.tensor_tensor(ot[:, :], z[:, :], lngb[:, 1, :], ALU.add), reads=[tz, T_lngb], writes=[to])
                    if l == 0:
                        kb.dma("sp", h1_d[b, i * 128:(i + 1) * 128, :], ot[:, :], d_o[oi], reads=[to], writes=[T_h1[b][i]])
                    else:
                        kb.dma("sp", out_d[b, (i - 2) * 128:(i - 1) * 128, :], ot[:, :], d_o[oi], reads=[to])
                o_lag.push(tail)
            o_lag.flush()

        def gelu2(dst, T_dst, bank, n, sq, T_sq, tt, T_tt):
            op("act", lambda e: e.activation(sq[:, 0:n], PB[bank][:, 0:n], AF.Square, scale=math.sqrt(GC1)), reads=[TPB[bank]], writes=[T_sq])
            op("dve", lambda e: e.scalar_tensor_tensor(sq[:, 0:n], sq[:, 0:n], 1.0, PB[bank][:, 0:n], ALU.add, ALU.mult),
               reads=[T_sq, TPB[bank]], writes=[T_sq])
            op("act", lambda e: e.activation(tt[:, 0:n], sq[:, 0:n], AF.Tanh, scale=GC0), reads=[T_sq], writes=[T_tt])
            return op("dve", lambda e: e.scalar_tensor_tensor(dst, tt[:, 0:n], 1.0, PB[bank][:, 0:n], ALU.add, ALU.mult),
                      reads=[T_tt, TPB[bank]], writes=[T_dst])

        def pass_layer0(b):
            l = 0
            kb.barrier()
            phase_transposes(l, b)
            chk(1)
            kb.barrier()
            cv = Carver()
            vn = cv.get(18 * 512, BF16)
            vn3 = vn.rearrange("p (i c) -> p i c", i=18)
            T_vn = [Trk() for _ in range(18)]
            Gu = cv.get(NT)
            T_Gu = Trk()
            sq = [cv.get(512) for _ in range(4)]
            T_sq = [Trk() for _ in range(4)]
            tt = [cv.get(512) for _ in range(4)]
            T_tt = [Trk() for _ in range(4)]
            g2 = [cv.get(512) for _ in range(4)]
            T_g2 = [Trk() for _ in range(4)]
            statsA = [cv.get(64) for _ in range(4)]
            T_stA = [Trk() for _ in range(4)]
            actr = [0]
            a_lag = Lag()
            bsrep = cv.get(4 * 512).rearrange("p (g q) -> p g q", g=4)
            angb = cv.get(2 * 512).rearrange("p (a q) -> p a q", a=2)
            for rep in range(4):
                kb.dma("sp", bsrep[:, :, rep * 128:(rep + 1) * 128],
                       abs_d.rearrange("o (g q) -> o g q", g=4).partition_broadcast(128), d_ab, writes=[T_ab])
            kb.dma("sp", angb[:, 0, :], ang_d.partition_broadcast(128), d_ab, writes=[T_ab])
            kb.dma("sp", angb[:, 1, :], anb_d.partition_broadcast(128), d_ab, writes=[T_ab])
            wv, twv = wload(abwin_d[:, 512:1024], 512)
            wu, twu = wload(abwin_d[:, 0:512], 512)
            wg, twg = wload(abwin_d[:, 1024:1536], 512)
            for i in range(18):
                bank = i % 4
                k2 = i % 4
                stats, T_st = statsA[k2], T_stA[k2]
                proj_tm(wv, twv, 0, 512, i, bank)
                gelu2(g2[k2][:, :], T_g2[k2], bank, 512, sq[k2], T_sq[k2], tt[k2], T_tt[k2])
                op("dve", lambda e: e.bn_stats(stats[:, 0:6], g2[k2][:, :]), reads=[T_g2[k2]], writes=[T_st])
                op("dve", lambda e: e.bn_aggr(stats[:, 16:18], stats[:, 0:6]), reads=[T_st], writes=[T_st])

                def ln_tail(i=i, k2=k2, stats=stats, T_st=T_st):
                    rsqrt_small(stats[:, 18:19], stats[:, 17:18], epsln4, T_st)
                    op("dve", lambda e: e.tensor_scalar(g2[k2][:, :], g2[k2][:, :], stats[:, 16:17], stats[:, 18:19],
                                                       ALU.subtract, ALU.mult), reads=[T_g2[k2], T_st], writes=[T_g2[k2]])
                    op("pool", lambda e: e.tensor_tensor(g2[k2][:, :], g2[k2][:, :], angb[:, 0, :], ALU.mult),
                       reads=[T_g2[k2], T_ab], writes=[T_g2[k2]])
                    op("pool", lambda e: e.tensor_tensor(vn3[:, i, :], g2[k2][:, :], angb[:, 1, :], ALU.add),
                       reads=[T_g2[k2], T_ab], writes=[T_vn[i]])
                a_lag.push(ln_tail)
            a_lag.flush()
            for g in range(4):
                for si, (t0, n) in enumerate(SEGS):
                    bank = actr[0] % 4
                    k2 = actr[0] % 4
                    actr[0] += 1
                    proj_fm(wu, twu, g * 128, t0, n, bank)
                    gelu2(Gu[:, t0:t0 + n], T_Gu, bank, n, sq[k2], T_sq[k2], tt[k2], T_tt[k2])
                for si, (t0, n) in enumerate(SEGS):
                    bank = actr[0] % 4
                    k2 = actr[0] % 4
                    actr[0] += 1
                    proj_fm(wg, twg, g * 128, t0, n, bank)
                    op("act", lambda e: e.activation(tt[k2][:, 0:n], PB[bank][:, 0:n], AF.Tanh, scale=0.5),
                       reads=[TPB[bank]], writes=[T_tt[k2]])
                    op("dve", lambda e: e.scalar_tensor_tensor(tt[k2][:, 0:n], tt[k2][:, 0:n], 1.0, PB[bank][:, 0:n], ALU.add, ALU.mult),
                       reads=[T_tt[k2], TPB[bank]], writes=[T_tt[k2]])
                    op("dve", lambda e: e.scalar_tensor_tensor(Gu[:, t0:t0 + n], tt[k2][:, 0:n], 0.25, Gu[:, t0:t0 + n], ALU.mult, ALU.mult),
                       reads=[T_tt[k2], T_Gu], writes=[T_Gu])
                    mb = 4 + actr[0] % 4
                    tl = seg_tiles(t0, n)
                    for ci, c in enumerate(tl):
                        op("pe", lambda e: e.matmul(PB[mb][:, ci * 128:(ci + 1) * 128], vn3[:, c, g * 128:(g + 1) * 128], wsT[:, g, :],
                                                    start=True, stop=True),
                           reads=[T_vn[c], T_constp], writes=[TPB[mb]])
                    op("dve", lambda e: e.tensor_tensor(sq[k2][:, 0:n], PB[mb][:, 0:n], bsrep[:, g, 0:n], ALU.add),
                       reads=[TPB[mb], T_ab], writes=[T_sq[k2]])
                    op("pool", lambda e: e.tensor_tensor(mixT[:, g, t0:t0 + n], sq[k2][:, 0:n], Gu[:, t0:t0 + n], ALU.mult),
                       reads=[T_sq[k2], T_Gu], writes=[T_mix])
            chk(2)
            kb.barrier()
            cv = Carver()
            V1 = cv.get(18 * 4 * 144, BF16)
            V14 = V1.rearrange("p (i h c) -> p i h c", i=18, h=4)
            T_V = [Trk() for _ in range(18)]
            qT2f = cv.get(2 * NT, BF16)
            qT2 = qT2f.rearrange("p (c t) -> p c t", c=2)
            kT = cv.get(NT, BF16)
            T_qT, T_kT = Trk(), Trk()
            op("pool", lambda e: e.memset(qT2f[:, :], 0.0), writes=[T_qT])
            sbg = cv.get(NT)
            T_sbg = Trk()
            qraw = [cv.get(512, BF16) for _ in range(3)]
            T_qraw = [Trk() for _ in range(3)]
            t1 = [cv.get(512) for _ in range(3)]
            T_t1 = [Trk() for _ in range(3)]
            t2 = [cv.get(512) for _ in range(3)]
            T_t2 = [Trk() for _ in range(3)]
            pT = [cv.get(512, BF16) for _ in range(4)]
            T_pT = [Trk() for _ in range(4)]
            o1n = [cv.get(128) for _ in range(2)]
            T_o1n = [Trk(), Trk()]
            oc_all = cv.get(18 * 128)
            oc3 = oc_all.rearrange("p (i c) -> p i c", i=18)
            T_oca = [Trk() for _ in range(18)]
            mv_all = cv.get(64)
            mv3 = mv_all[:, 0:36].rearrange("p (i c) -> p i c", i=18)
            msr = cv.get(64)
            T_mv = Trk()
            osc = [cv.get(128) for _ in range(4)]
            T_osc = [Trk() for _ in range(4)]
            pending_eoh = [None]
            TQ3 = [Trk() for _ in range(4)]
            stats = cv.get(64)
            T_st = Trk()
            op("dve", lambda e: e.memset(V1[:, :], 1.0), writes=T_V)
            wv, twv = wload(abwin_d[:, 2560:3072], 512)
            wq, twq = wload(abwin_d[:, 1536:2048], 512)
            wk, twk = wload(abwin_d[:, 2048:2560], 512)
            for i in range(18):
                bank = i % 3
                proj_tm(wv, twv, 0, 512, i, bank)
                op("act", lambda e: e.copy(V14[:, i, :, 0:128], PB[bank][:, :].rearrange("p (h c) -> p h c", h=4)),
                   reads=[TPB[bank]], writes=[T_V[i]])
            chk(2.2)
            wg, twg = wload(abwin_d[:, 3072:3584], 512)
            pctr = [0]
            for h in range(4):
                for si, (t0, n) in enumerate(SEGS):
                    k2 = si % 2
                    proj_fm(wq, twq, h * 128, t0, n, si % 3)
                    rope_evac(si % 3, n, t0, [(qT2[0:64, 0, t0:t0 + n], 0, 64), (qT2[64:128, 1, t0:t0 + n], 64, 128)], T_qT, 4 + k2, qraw, T_qraw, t1, T_t1, t2, T_t2)
                for si, (t0, n) in enumerate(SEGS):
                    k2 = si % 2
                    proj_fm(wk, twk, h * 128, t0, n, si % 3)
                    rope_evac(si % 3, n, t0, kT[:, t0:t0 + n], T_kT, 4 + k2, qraw, T_qraw, t1, T_t1, t2, T_t2)
                rope_lag.flush()
                if pending_eoh[0] is not None:
                    pending_eoh[0]()
                    pending_eoh[0] = None
                for si, (t0, n) in enumerate(SEGS):
                    k2 = si % 2
                    bank = si % 3
                    proj_fm(wg, twg, h * 128, t0, n, bank)
                    op("act", lambda e: e.activation(t1[k2][:, 0:n], PB[bank][:, 0:n], AF.Tanh, scale=0.5),
                       reads=[TPB[bank]], writes=[T_t1[k2]])
                    op("dve", lambda e: e.scalar_tensor_tensor(sbg[:, t0:t0 + n], t1[k2][:, 0:n], 1.0, PB[bank][:, 0:n], ALU.add, ALU.mult),
                       reads=[T_t1[k2], TPB[bank]], writes=[T_sbg])
                chk(2.4)
                qtiles = [(0, [0, 1])] + [(256 + 256 * qi, list(range(18))) for qi in range(8)]
                for qn, (q0, kts) in enumerate(qtiles):
                    ob = 4 + 2 * (qn % 2)

                    def qk(kt, sbank):
                        op("pe", lambda e: e.matmul(PB[sbank][:, :].rearrange("p (c q) -> p c q", c=2),
                                                    kT[:, kt * 128:(kt + 1) * 128],
                                                    qT2[:, :, q0:q0 + 256], start=True, stop=True),
                           reads=[T_kT, T_qT], writes=[TPB[sbank]])

                    for pre in range(min(2, len(kts))):
                        qk(kts[pre], pre % 3)
                    for ki, kt in enumerate(kts):
                        sbank = ki % 3
                        if ki + 2 < len(kts):
                            qk(kts[ki + 2], (ki + 2) % 3)
                        pi = pctr[0] % 4
                        pctr[0] += 1
                        op("act", lambda e: e.activation(pT[pi][:, :], PB[sbank][:, :], AF.Exp, scale=0.125),
                           reads=[TPB[sbank]], writes=[T_pT[pi]])
                        chk(2.5)
                        for c in range(2):
                            for s in range(2):
                                first = (ki == 0 and s == 0)
                                op("pe", lambda e: e.matmul(PB[ob + c][:, s * 144:s * 144 + 130],
                                                            pT[pi][:, c * 256 + s * 128:c * 256 + (s + 1) * 128],
                                                            V14[:, kt, h, 0:130], start=first, stop=(ki == len(kts) - 1),
                                                            skip_group_check=True),
                                   reads=[T_pT[pi], T_V[kt]], writes=[TPB[ob + c]], sig=(c == 1 and s == 1))
                    chk(2.6)
                    for s in range(2):
                        k2 = s
                        idx = (q0 + s * 128) // 128
                        op("dve", lambda e: e.reciprocal(stats[:, 0:1], PB[ob][:, s * 144 + 128:s * 144 + 129]),
                           reads=[TPB[ob]], writes=[T_st])
                        op("dve", lambda e: e.reciprocal(stats[:, 1:2], PB[ob + 1][:, s * 144 + 128:s * 144 + 129]),
                           reads=[TPB[ob + 1]], writes=[T_st])
                        op("dve", lambda e: e.tensor_scalar(o1n[k2][:, :], PB[ob + 1][:, s * 144:s * 144 + 128], stats[:, 1:2], neglam,
                                                           ALU.mult, ALU.mult), reads=[TPB[ob + 1], T_st, T_c2], writes=[T_o1n[k2]])
                        op("dve", lambda e: e.scalar_tensor_tensor(oc3[:, idx, :], PB[ob][:, s * 144:s * 144 + 128], stats[:, 0:1], o1n[k2][:, :],
                                                                  ALU.mult, ALU.add), reads=[TPB[ob], T_st, T_o1n[k2]], writes=[T_oca[idx]])
                        op("dve", lambda e: e.bn_stats(stats[:, 8:14], oc3[:, idx, :]), reads=[T_oca[idx]], writes=[T_st])
                        op("dve", lambda e: e.bn_aggr(mv3[:, idx, :], stats[:, 8:14]), reads=[T_st], writes=[T_mv])
                def eoh(h=h):
                    op("dve", lambda e: e.tensor_tensor(msr[:, 0:18], mv3[:, :, 0], mv3[:, :, 0], ALU.mult), reads=[T_mv], writes=[T_mv])
                    op("dve", lambda e: e.tensor_tensor(msr[:, 0:18], msr[:, 0:18], mv3[:, :, 1], ALU.add), reads=[T_mv], writes=[T_mv])
                    rsqrt_small(msr[:, 32:50], msr[:, 0:18], epsrms, T_mv)
                    e_lag = Lag()
                    for idx in range(18):
                        k2 = idx % 4
                        qd = idx % 4
                        op("dve", lambda e: e.scalar_tensor_tensor(osc[k2][:, :], oc3[:, idx, :], msr[:, 32 + idx:33 + idx], subg[:, :],
                                                                  ALU.mult, ALU.mult), reads=[T_oca[idx], T_mv, T_c2], writes=[T_osc[k2]])

                        def te(idx=idx, k2=k2, qd=qd):
                            tb = 3 if idx % 2 == 0 else 7
                            op("pe", lambda e: e.transpose(PB[tb][:, 0:128], osc[k2][:, :], ident[:]),
                               reads=[T_osc[k2], T_const], writes=[TPB[tb]])
                            op("dve", lambda e: e.tensor_tensor(mixT[:, 4 + h, idx * 128:(idx + 1) * 128], PB[tb][:, 0:128],
                                                               sbg[:, idx * 128:(idx + 1) * 128], ALU.mult),
                               reads=[TPB[tb], T_sbg], writes=[T_mix])
                        e_lag.push(te)
                    e_lag.flush()
                pending_eoh[0] = eoh
            pending_eoh[0]()
            chk(3)
            phase_outproj(l, b, abwout_d, list(range(18)))
            chk(4)

        def pass_layer1(b):
            l = 1
            kb.barrier()
            phase_transposes(l, b)
            kb.barrier()
            cv = Carver()
            yT = cv.get(4 * NLAT)
            yT3 = yT.rearrange("p (j t) -> p j t", j=4)
            T_yT = [Trk() for _ in range(4)]
            hpad = [cv.get(NLAT + 32, BF16) for _ in range(2)]
            T_hp = [Trk(), Trk()]
            caS = [cv.get(512) for _ in range(2)]
            T_ca = [Trk(), Trk()]
            tt = [cv.get(512) for _ in range(2)]
            T_tt = [Trk(), Trk()]
            diag = cv.get(31 * 128, BF16)
            diag3 = diag.rearrange("p (k c) -> p k c", k=31)
            T_dg = Trk()
            wa, twa = wload(cdwin_d[:, 0:512], 512)
            wb, twb = wload(cdwin_d[:, 512:1024], 512)
            for k2 in range(2):
                op("dve", lambda e: e.memset(hpad[k2][:, :], 0.0), writes=[T_hp[k2]])
            for j in range(4):
                hp = hpad[j % 2]
                thp = T_hp[j % 2]
                for k in range(31):
                    op("dve", lambda e: e.tensor_scalar(diag3[:, k, :], identb[:, :], dwT[:, j * 31 + k:j * 31 + k + 1], None, ALU.mult),
                       reads=[T_c2], writes=[T_dg])
                for si, (t0, n) in enumerate(LSEGS):
                    k2 = si % 2
                    lt0 = t0 - NCTX
                    proj_fm(wa, twa, j * 128, t0, n, si % 2)
                    op("act", lambda e: e.copy(caS[k2][:, :], PB[si % 2][:, :]), reads=[TPB[si % 2]], writes=[T_ca[k2]])
                    proj_fm(wb, twb, j * 128, t0, n, 2 + si % 2)
                    op("act", lambda e: e.activation(tt[k2][:, :], PB[2 + si % 2][:, :], AF.Tanh, scale=0.5),
                       reads=[TPB[2 + si % 2]], writes=[T_tt[k2]])
                    op("dve", lambda e: e.scalar_tensor_tensor(hp[:, 15 + lt0:15 + lt0 + n], tt[k2][:, :], 1.0, caS[k2][:, :], ALU.add, ALU.mult),
                       reads=[T_tt[k2], T_ca[k2]], writes=[thp])
                for si, (t0, n) in enumerate(LSEGS):
                    lt0 = t0 - NCTX
                    cb = 4 + si % 2
                    for k in range(31):
                        op("pe", lambda e: e.matmul(PB[cb][:, :], diag3[:, k, :], hp[:, lt0 + k:lt0 + k + 512],
                                                    start=(k == 0), stop=(k == 30)),
                           reads=[T_dg, thp], writes=[TPB[cb]], sig=(k == 30))
                    op("act", lambda e: e.activation(yT3[:, j, lt0:lt0 + 512], PB[cb][:, :], AF.Identity, bias=cvecs[:, j:j + 1]),
                       reads=[TPB[cb], T_const], writes=[T_yT[j]])
            kb.barrier()
            cv2 = Carver()
            cv2.off = 4 * NLAT * 4
            mean_bc = cv2.get(NLAT)
            rstd_bc = cv2.get(NLAT)
            T_mr = Trk()
            ysq = [cv2.get(512) for _ in range(2)]
            T_ysq = [Trk(), Trk()]
            tt = [cv2.get(512) for _ in range(2)]
            T_tt = [Trk(), Trk()]
            sg = [cv2.get(512) for _ in range(2)]
            T_sg = [Trk(), Trk()]
            yn = [cv2.get(512) for _ in range(2)]
            T_yn = [Trk(), Trk()]
            for si, (t0, n) in enumerate(LSEGS):
                lt0 = t0 - NCTX
                ts_ = slice(lt0, lt0 + 512)
                mbk, sbk = 0 + 2 * (si % 2), 1 + 2 * (si % 2)
                for j in range(4):
                    op("pe", lambda e: e.matmul(PB[mbk][:, :], onesdiv[:, :], yT3[:, j, ts_], start=(j == 0), stop=(j == 3)),
                       reads=[T_c2, T_yT[j]], writes=[TPB[mbk]])
                for j in range(4):
                    k2 = j % 2
                    op("act", lambda e: e.activation(ysq[k2][:, :], yT3[:, j, ts_], AF.Square), reads=[T_yT[j]], writes=[T_ysq[k2]])
                    op("pe", lambda e: e.matmul(PB[sbk][:, :], onesdiv[:, :], ysq[k2][:, :], start=(j == 0), stop=(j == 3)),
                       reads=[T_c2, T_ysq[k2]], writes=[TPB[sbk]])
                op("act", lambda e: e.copy(mean_bc[:, ts_], PB[mbk][:, :]), reads=[TPB[mbk]], writes=[T_mr])
                op("dve", lambda e: e.tensor_tensor(rstd_bc[:, ts_], mean_bc[:, ts_], mean_bc[:, ts_], ALU.mult), reads=[T_mr], writes=[T_mr])
                op("dve", lambda e: e.tensor_tensor(rstd_bc[:, ts_], PB[sbk][:, :], rstd_bc[:, ts_], ALU.subtract),
                   reads=[TPB[sbk], T_mr], writes=[T_mr])
                op("act", lambda e: e.activation(rstd_bc[:, ts_], rstd_bc[:, ts_], AF.Sqrt, bias=epsln), reads=[T_mr, T_c2], writes=[T_mr])
                op("dve", lambda e: e.reciprocal(rstd_bc[:, ts_], rstd_bc[:, ts_]), reads=[T_mr], writes=[T_mr])
            wg, twg = wload(cdwin_d[:, 1024:1536], 512)
            cnt = 0
            for j in range(4):
                for si, (t0, n) in enumerate(LSEGS):
                    lt0 = t0 - NCTX
                    ts_ = slice(lt0, lt0 + 512)
                    k2 = cnt % 2
                    bank = 4 + cnt % 4
                    cnt += 1
                    proj_fm(wg, twg, j * 128, t0, n, bank)
                    op("act", lambda e: e.activation(tt[k2][:, :], PB[bank][:, :], AF.Tanh, scale=0.5), reads=[TPB[bank]], writes=[T_tt[k2]])
                    op("dve", lambda e: e.scalar_tensor_tensor(sg[k2][:, :], tt[k2][:, :], 1.0, PB[bank][:, :], ALU.add, ALU.mult),
                       reads=[T_tt[k2], TPB[bank]], writes=[T_sg[k2]])
                    op("pool", lambda e: e.tensor_tensor(yn[k2][:, :], yT3[:, j, ts_], mean_bc[:, ts_], ALU.subtract),
                       reads=[T_yT[j], T_mr], writes=[T_yn[k2]])
                    op("pool", lambda e: e.tensor_tensor(yn[k2][:, :], yn[k2][:, :], rstd_bc[:, ts_], ALU.mult),
                       reads=[T_yn[k2], T_mr], writes=[T_yn[k2]])
                    op("dve", lambda e: e.tensor_scalar(yn[k2][:, :], yn[k2][:, :], cvecs[:, 4 + j:5 + j], cvecs[:, 8 + j:9 + j], ALU.mult, ALU.add),
                       reads=[T_yn[k2], T_const], writes=[T_yn[k2]])
                    op("act", lambda e: e.activation(tt[k2][:, :], yn[k2][:, :], AF.Tanh, scale=0.5), reads=[T_yn[k2], T_sg[k2]], writes=[T_tt[k2]])
                    op("dve", lambda e: e.scalar_tensor_tensor(yn[k2][:, :], tt[k2][:, :], 1.0, yn[k2][:, :], ALU.add, ALU.mult),
                       reads=[T_tt[k2], T_yn[k2]], writes=[T_yn[k2]])
                    op("dve", lambda e: e.scalar_tensor_tensor(mixT[:, j, t0:t0 + n], yn[k2][:, :], 0.25, sg[k2][:, :], ALU.mult, ALU.mult),
                       reads=[T_yn[k2], T_sg[k2]], writes=[T_mix])
            kb.barrier()
            cv = Carver()
            V1 = cv.get(18 * 2 * 80, BF16)
            V14 = V1.rearrange("p (i h c) -> p i h c", i=18, h=2)
            T_V = [Trk() for _ in range(18)]
            kT2 = cv.get(2 * NT, BF16)
            kT23 = kT2.rearrange("p (h t) -> p h t", h=2)
            T_kT = Trk()
            qT2f = cv.get(2 * NLAT, BF16)
            qT2 = qT2f.rearrange("p (c t) -> p c t", c=2)
            T_qT = Trk()
            op("pool", lambda e: e.memset(qT2f[:, :], 0.0), writes=[T_qT])
            stats2 = [cv.get(64) for _ in range(2)]
            T_st2 = [Trk(), Trk()]
            sdg = cv.get(NLAT)
            T_sdg = Trk()
            wk2 = cv.get(8 * 2 * 128, BF16)
            wk24 = wk2.rearrange("p (j h c) -> p j h c", j=8, h=2)
            T_wk2 = Trk()
            d_wk2 = kb.dsem()
            qraw = [cv.get(512, BF16) for _ in range(3)]
            T_qraw = [Trk() for _ in range(3)]
            t1 = [cv.get(512) for _ in range(3)]
            T_t1 = [Trk() for _ in range(3)]
            t2 = [cv.get(512) for _ in range(3)]
            T_t2 = [Trk() for _ in range(3)]
            pT = [cv.get(256, BF16) for _ in range(4)]
            T_pT = [Trk() for _ in range(4)]
            ocomb = [cv.get(128) for _ in range(2)]
            T_oc = [Trk(), Trk()]
            stats = cv.get(64)
            T_st = Trk()
            op("dve", lambda e: e.memset(V1[:, :], 1.0), writes=T_V)
            for kvh in range(2):
                for dup in range(2):
                    kb.dma("pool", wk24[:, :, kvh, dup * 64:(dup + 1) * 64],
                           cdwin_d[:, 2048 + kvh * 64:2048 + (kvh + 1) * 64].rearrange("(j p) c -> p j c", p=128),
                           d_wk2, writes=[T_wk2])
            wq, twq = wload(cdwin_d[:, 1536:2048], 512)
            wkv, twkv = wload(cdwin_d[:, 2048:2560], 512)
            wg2, twg2 = wload(cdwin_d[:, 2560:2816], 256)
            for i in range(18):
                bank = i % 4
                proj_tm(wkv, twkv, 128, 128, i, bank)
                op("act", lambda e: e.copy(V14[:, i, :, 0:64], PB[bank][:, 0:128].rearrange("p (h c) -> p h c", h=2)),
                   reads=[TPB[bank]], writes=[T_V[i]])
            for kvh in range(2):
                for si, (t0, n) in enumerate(SEGS):
                    k2 = si % 2
                    bank = si % 4
                    for j in range(8):
                        op("pe", lambda e: e.matmul(PB[bank][:, 0:n], wk24[:, j, kvh, :], uT[:, j, t0:t0 + n], start=(j == 0), stop=(j == 7)),
                           reads=[T_wk2] + [T_uT[i] for i in seg_tiles(t0, n)], writes=[TPB[bank]], sig=(j == 7))
                    rope_evac(bank, n, t0, kT23[:, kvh, t0:t0 + n], T_kT, 4 + k2, qraw, T_qraw, t1, T_t1, t2, T_t2)
            rope_lag.flush()
            pctr = [0]
            att_lag = Lag()
            for cc in range(4):
                kvh = cc // 2
                for si, (t0, n) in enumerate(LSEGS):
                    k2 = si % 2
                    lt0 = t0 - NCTX
                    proj_fm(wq, twq, cc * 128, t0, n, si % 4)
                    rope_evac(si % 4, n, t0, [(qT2[0:64, 0, lt0:lt0 + n], 0, 64), (qT2[64:128, 1, lt0:lt0 + n], 64, 128)], T_qT, 4 + k2, qraw, T_qraw, t1, T_t1, t2, T_t2)
                rope_lag.flush()
                for si, (t0, n) in enumerate(LSEGS):
                    k2 = si % 2
                    bank = si % 4
                    lt0 = t0 - NCTX
                    if cc < 2:
                        proj_fm(wkv, twkv, 256 + cc * 128, t0, n, bank)
                    else:
                        proj_fm(wg2, twg2, (cc - 2) * 128, t0, n, bank)
                    op("act", lambda e: e.activation(t1[k2][:, 0:n], PB[bank][:, 0:n], AF.Tanh, scale=0.5),
                       reads=[TPB[bank]], writes=[T_t1[k2]])
                    op("dve", lambda e: e.scalar_tensor_tensor(sdg[:, lt0:lt0 + n], t1[k2][:, 0:n], 1.0, PB[bank][:, 0:n], ALU.add, ALU.mult),
                       reads=[T_t1[k2], TPB[bank]], writes=[T_sdg])
                steps = []
                for qb in range(16):
                    kts = []
                    if qb > 0:
                        kts.append((2 + qb - 1, 0))
                    kts.append((2 + qb, None))
                    if qb < 15:
                        kts.append((2 + qb + 1, 1))
                    kts += [(0, None), (1, None)]
                    for ki, (kt, mk) in enumerate(kts):
                        steps.append((qb, ki, kt, mk, ki == len(kts) - 1))

                def qk1(step, sbank):
                    qb, ki, kt, mk, last = step
                    q0 = qb * 128
                    op("pe", lambda e: e.matmul(PB[sbank][:, 0:256].rearrange("p (c q) -> p c q", c=2),
                                                kT23[:, kvh, kt * 128:(kt + 1) * 128],
                                                qT2[:, :, q0:q0 + 128], start=True, stop=(mk is None), skip_group_check=True),
                       reads=[T_kT, T_qT], writes=[TPB[sbank]])
                    if mk is not None:
                        op("pe", lambda e: e.matmul(PB[sbank][:, 0:256], identb[:, :], masks[:, mk, :], start=False, stop=True,
                                                    skip_group_check=True),
                           reads=[T_c2, T_constp], writes=[TPB[sbank]])

                for pre in range(2):
                    qk1(steps[pre], pre % 4)
                for i, step in enumerate(steps):
                    qb, ki, kt, mk, last = step
                    q0 = qb * 128
                    if i + 2 < len(steps):
                        qk1(steps[i + 2], (i + 2) % 4)
                    sbank = i % 4
                    ob = 4 + qb % 2
                    pi = pctr[0] % 4
                    pctr[0] += 1
                    op("act", lambda e: e.activation(pT[pi][:, :], PB[sbank][:, 0:256], AF.Exp, scale=0.125),
                       reads=[TPB[sbank]], writes=[T_pT[pi]])
                    for hl in range(2):
                        first = (ki == 0 and hl == 0)
                        op("pe", lambda e: e.matmul(PB[ob][:, hl * 80:hl * 80 + 66], pT[pi][:, hl * 128:(hl + 1) * 128],
                                                    V14[:, kt, kvh, 0:66], start=first, stop=last,
                                                    skip_group_check=True),
                           reads=[T_pT[pi], T_V[kt]], writes=[TPB[ob]], sig=(hl == 1))
                    if not last:
                        continue
                    k2 = qb % 2
                    for hl in range(2):
                        hd = cc * 2 + hl
                        op("dve", lambda e: e.tensor_scalar(stats2[k2][:, hl:hl + 1], PB[ob][:, hl * 80 + 64:hl * 80 + 65], sinkt[:, hd:hd + 1], None, ALU.add),
                           reads=[TPB[ob], T_c2], writes=[T_st2[k2]])
                        op("dve", lambda e: e.reciprocal(stats2[k2][:, 2 + hl:3 + hl], stats2[k2][:, hl:hl + 1]), reads=[T_st2[k2]], writes=[T_st2[k2]])
                        op("dve", lambda e: e.tensor_scalar(ocomb[k2][:, hl * 64:(hl + 1) * 64], PB[ob][:, hl * 80:hl * 80 + 64],
                                                           stats2[k2][:, 2 + hl:3 + hl], None, ALU.mult),
                           reads=[TPB[ob], T_st2[k2]], writes=[T_oc[k2]])

                    def post_b(cc=cc, qb=qb, q0=q0, k2=k2):
                        tb = 6 + qb % 2
                        op("pe", lambda e: e.transpose(PB[tb][:, 0:128], ocomb[k2][:, :], ident[:]), reads=[T_oc[k2], T_const], writes=[TPB[tb]])
                        op("dve", lambda e: e.scalar_tensor_tensor(mixT[:, 4 + cc, NCTX + q0:NCTX + q0 + 128], PB[tb][:, 0:128], 0.5,
                                                                  sdg[:, q0:q0 + 128], ALU.mult, ALU.mult),
                           reads=[TPB[tb], T_sdg], writes=[T_mix])
                    att_lag.push(post_b)
                att_lag.flush()
            phase_outproj(l, b, cdwout_d, list(range(2, 18)))

        try:
            chk(0)
            for b in range(2):
                pass_layer0(b)
            if not debug_h1:
                for b in range(2):
                    pass_layer1(b)
        except _Stop:
            pass
        kb.barrier()
        for ds in d_o:
            if ds.count:
                nc.sync.wait_ge(ds.sem, ds.count)
    return nc


def _consts():
    ident = np.eye(128, dtype=np.float32)
    rmat = np.zeros((128, 128), np.float32)
    for dp in range(128):
        partner = dp + 16 if (dp % 32) < 16 else dp - 16
        rmat[partner, dp] = 1.0
    m = 16
    inv = (10000.0 ** (-np.arange(m, dtype=np.float32) / m)).astype(np.float32)
    t = np.arange(NLAT)
    row = (t // 64).astype(np.float32)
    col = (t % 64).astype(np.float32)
    ang_r = (row[:, None] * inv[None, :]).astype(np.float32)
    ang_c = (col[:, None] * inv[None, :]).astype(np.float32)
    cos_t = np.ones((128, NT), np.float32)
    sin_t = np.zeros((128, NT), np.float32)
    for p in range(128):
        d = p % 64
        ang = ang_r if d < 32 else ang_c
        f = d % 16
        sign = -1.0 if (d % 32) < 16 else 1.0
        cos_t[p, NCTX:] = np.cos(ang[:, f])
        sin_t[p, NCTX:] = sign * np.sin(ang[:, f])
    ropeT = np.concatenate([cos_t, sin_t], axis=1)
    kk = np.arange(128)[:, None]
    qq = np.arange(128)[None, :]
    mp = np.where(kk >= qq, 0.0, -30000.0).astype(np.float32)
    mn = np.where(kk <= qq, 0.0, -30000.0).astype(np.float32)
    masks = np.concatenate([mp, mp, mn, mn], axis=1)
    return ident, rmat, ropeT, masks


_CACHE = {}


def _core_inputs(core, x, c, ctx, c_ctx, mod_w, mod_b, ln_g, ln_b, ab_w_in, ab_w_out, a_w_s, a_b_s,
                 a_norm_g, a_norm_b, b_lq1, b_lk1, b_lq2, b_lk2, b_subln_g, cd_w_in, cd_w_out,
                 c_dw_w, c_dw_b, c_norm_g, c_norm_b, d_sink, shared):
    b0 = 2 * core
    cvec = np.stack([c[b0], c[b0 + 1], c_ctx], axis=0)
    cvT = np.ascontiguousarray(cvec.reshape(3, 8, 128).transpose(2, 1, 0)).reshape(128, 24)
    d = dict(shared)
    d["x"] = np.ascontiguousarray(x[b0:b0 + 2])
    d["ctx"] = np.ascontiguousarray(ctx[b0:b0 + 2])
    d["cvT"] = cvT
    return d


def kernel(x, c, ctx, c_ctx, mod_w, mod_b, ln_g, ln_b, ab_w_in, ab_w_out, a_w_s, a_b_s,
           a_norm_g, a_norm_b, b_lq1, b_lk1, b_lq2, b_lk2, b_subln_g, cd_w_in, cd_w_out,
           c_dw_w, c_dw_b, c_norm_g, c_norm_b, d_sink, _debug_h1=False, _stage=99):
    f = lambda a: np.ascontiguousarray(np.asarray(a, dtype=np.float32))
    x, c, ctx, c_ctx = f(x), f(c), f(ctx), f(c_ctx)
    ident, rmat, ropeT, masks = _consts()
    shared = {
        "mod_w": f(mod_w),
        "mod_bT": np.ascontiguousarray(f(mod_b).reshape(2, 24, 128).transpose(2, 0, 1)).reshape(128, 48),
        "ln_g": f(ln_g), "ln_b": f(ln_b),
        "ab_w_in": f(ab_w_in)[0], "ab_w_out": f(ab_w_out)[0],
        "a_w_sT": np.ascontiguousarray(f(a_w_s)[0].transpose(2, 0, 1)).reshape(128, 512),
        "a_b_s": f(a_b_s)[0].reshape(1, 512),
        "a_norm_g": f(a_norm_g).reshape(1, 512), "a_norm_b": f(a_norm_b).reshape(1, 512),
        "lam_in": np.concatenate([f(b_lq1)[0], f(b_lk1)[0], f(b_lq2)[0], f(b_lk2)[0]]).reshape(1, 256),
        "b_subln_g": f(b_subln_g).reshape(1, 128),
        "cd_w_in": f(cd_w_in)[0], "cd_w_out": f(cd_w_out)[0],
        "dwT": np.ascontiguousarray(f(c_dw_w)[0].reshape(31, 4, 128).transpose(2, 1, 0)).reshape(128, 124),
        "cvecs": np.ascontiguousarray(np.stack([f(c_dw_b)[0], f(c_norm_g)[0], f(c_norm_b)[0]], 0)
                                      .reshape(3, 4, 128).transpose(2, 0, 1)).reshape(128, 12),
        "d_sink": f(d_sink).reshape(1, 8),
        "ident": ident, "rmat": rmat, "ropeT": ropeT, "masks": masks,
    }
    in_maps = []
    for core in range(8):
        in_maps.append(_core_inputs(core, x, c, ctx, c_ctx, None, None, None, None, None, None, None, None,
                                    None, None, None, None, None, None, None, None, None,
                                    None, None, None, None, None, shared))
    key = (bool(_debug_h1), _stage)
    if key not in _CACHE:
        _CACHE[key] = build_program(debug_h1=key[0], stage=_stage)
    nc = _CACHE[key]
    res = run_bass_kernel_spmd(nc, in_maps, core_ids=list(range(8)))
    if _debug_h1:
        return np.concatenate([r["h1"] for r in res.results], axis=0)
    return np.concatenate([r["out"] for r in res.results], axis=0).astype(np.float32)
```
